# Optimizing a Trainium2 kernel written in Bass

```python
import math
import jax, jax.numpy as jnp
from jax import lax
import numpy as np

D_MODEL = 1024
BATCH = 4
SEQ = 8192
DEPTH = 2

PLE_DIM = 256
N_ATTN_HEADS = 4
ATTN_HALF_DIM = 64
ATTN_V_DIM = 2 * ATTN_HALF_DIM
ATTN_WIDTH = N_ATTN_HEADS * 2 * ATTN_HALF_DIM
N_RWKV_HEADS = 8
RWKV_HEAD_DIM = 64
RWKV_WIDTH = N_RWKV_HEADS * RWKV_HEAD_DIM
DECAY_LORA = 64
AAA_LORA = 64
GATE_LORA = 128
RWKV_COLS = 3 * RWKV_WIDTH + DECAY_LORA + AAA_LORA + GATE_LORA
N_BRANCHES = 2
IN_COLS = 3 * ATTN_WIDTH + RWKV_COLS + N_BRANCHES * D_MODEL
D_FF = -(-8 * D_MODEL // (3 * 256)) * 256
NUM_BUCKETS = 32
MAX_DISTANCE = 128
Q_BLOCK = 128
NORM_EPS = 1e-6
SUBLN_EPS = 1e-5
GN_EPS = 64e-5

kernel_name = "hybrid_diffattn_rwkv7_gated_block"


def rmsnorm(x, g, eps=NORM_EPS):
    xf = x.astype(jnp.float32)
    y = xf * lax.rsqrt(jnp.mean(xf * xf, axis=-1, keepdims=True) + eps)
    return (y * g.astype(jnp.float32)).astype(x.dtype)


def token_shift(t):
    return jnp.pad(t, ((0, 0), (1, 0), (0, 0)))[:, :-1]


def t5_bucket(dist):
    n = jnp.maximum(dist, 0)
    max_exact = NUM_BUCKETS // 2
    nf = jnp.maximum(n, max_exact).astype(jnp.float32)
    large = max_exact + (jnp.log(nf / max_exact) / math.log(MAX_DISTANCE / max_exact)
                         * (NUM_BUCKETS - max_exact)).astype(jnp.int32)
    large = jnp.minimum(large, NUM_BUCKETS - 1)
    return jnp.where(n < max_exact, n, large)


def diff_attention(q, k, v, rel_bias, lam):
    B, S, H, _, Dh = q.shape
    nb = S // Q_BLOCK
    scale = Dh ** -0.5
    kt = k.transpose(0, 2, 3, 1, 4)
    vt = v.transpose(0, 2, 1, 3)
    qb = (q * scale).reshape(B, nb, Q_BLOCK, H, 2, Dh).transpose(1, 0, 3, 4, 2, 5)
    k_pos = jnp.arange(S)
    lam32 = lam.astype(jnp.float32)

    def block(args):
        q_blk, start = args
        s = jnp.einsum('bhcqd,bhckd->bhcqk', q_blk, kt).astype(jnp.float32)
        dist = (start + jnp.arange(Q_BLOCK))[:, None] - k_pos[None, :]
        bias = rel_bias[t5_bucket(dist)].astype(jnp.float32)
        s = s + bias.transpose(2, 3, 0, 1)[None]
        s = jnp.where(dist >= 0, s, -jnp.inf)
        pr = jax.nn.softmax(s, axis=-1)
        wts = pr[:, :, 0] - lam32 * pr[:, :, 1]
        return jnp.einsum('bhqk,bhkd->bhqd', wts.astype(vt.dtype), vt)

    starts = jnp.arange(nb) * Q_BLOCK
    out = lax.map(block, (qb, starts))
    return out.transpose(1, 0, 3, 2, 4).reshape(B, S, H, v.shape[-1])


def rwkv7_scan(r, w, k, v, a, b):
    dt = r.dtype
    B, S, H, N = r.shape
    to_t = lambda t: t.astype(jnp.float32).transpose(1, 0, 2, 3)

    def step(state, inp):
        r_t, w_t, k_t, v_t, a_t, b_t = inp
        sa = jnp.einsum('bhvk,bhk->bhv', state, a_t)
        state = (state * w_t[:, :, None, :] + sa[..., None] * b_t[:, :, None, :]
                 + v_t[..., None] * k_t[:, :, None, :])
        return state, jnp.einsum('bhvk,bhk->bhv', state, r_t)

    s0 = jnp.zeros((B, H, N, N), jnp.float32)
    _, y = lax.scan(step, s0, (to_t(r), to_t(w), to_t(k), to_t(v), to_t(a), to_t(b)))
    return y.transpose(1, 0, 2, 3).astype(dt)


def setup_inputs(seed: int = 0) -> dict:
    key = jax.random.key(seed)
    ks = iter(jax.random.split(key, 40))
    nrm = lambda shape, scale: jax.random.normal(next(ks), shape, jnp.float32) * scale
    gain = lambda shape: 1.0 + nrm(shape, 0.02)
    L, D = DEPTH, D_MODEL
    return {
        "x": nrm((BATCH, SEQ, D), 1.0),
        "p": nrm((DEPTH, BATCH, SEQ, PLE_DIM), 1.0),
        "rel_bias": nrm((NUM_BUCKETS, N_ATTN_HEADS, 2), 0.5),
        "norm_mix": gain((L, D)),
        "w_in": nrm((L, D, IN_COLS), D ** -0.5),
        "lam_q1": nrm((L, ATTN_HALF_DIM), 0.1),
        "lam_k1": nrm((L, ATTN_HALF_DIM), 0.1),
        "lam_q2": nrm((L, ATTN_HALF_DIM), 0.1),
        "lam_k2": nrm((L, ATTN_HALF_DIM), 0.1),
        "attn_subln": gain((L, ATTN_V_DIM)),
        "rwkv_mu": jax.random.uniform(next(ks), (L, RWKV_COLS), jnp.float32, 0.0, 1.0),
        "rwkv_w0": jax.random.uniform(next(ks), (L, RWKV_WIDTH), jnp.float32, -6.0, -1.0),
        "rwkv_w2": nrm((L, DECAY_LORA, RWKV_WIDTH), 0.1 * DECAY_LORA ** -0.5),
        "rwkv_a0": nrm((L, RWKV_WIDTH), 0.5),
        "rwkv_a2": nrm((L, AAA_LORA, RWKV_WIDTH), 0.5 * AAA_LORA ** -0.5),
        "rwkv_g2": nrm((L, GATE_LORA, RWKV_WIDTH), GATE_LORA ** -0.5),
        "rwkv_kk": 0.85 + nrm((L, RWKV_WIDTH), 0.05),
        "rwkv_ka": 1.0 + nrm((L, RWKV_WIDTH), 0.05),
        "rwkv_rk": nrm((L, N_RWKV_HEADS, RWKV_HEAD_DIM), 0.1),
        "rwkv_lnx_w": gain((L, RWKV_WIDTH)),
        "rwkv_lnx_b": nrm((L, RWKV_WIDTH), 0.02),
        "w_out_attn": nrm((L, ATTN_WIDTH, D), ATTN_WIDTH ** -0.5),
        "w_out_rwkv": nrm((L, RWKV_WIDTH, D), RWKV_WIDTH ** -0.5),
        "w_out": nrm((L, D, D), D ** -0.5),
        "norm_ffn": gain((L, D)),
        "w_ffn_gate": nrm((L, D, D_FF), D ** -0.5),
        "w_ffn_up": nrm((L, D, D_FF), D ** -0.5),
        "w_ffn_down": nrm((L, D_FF, D), D_FF ** -0.5),
        "norm_ple": gain((L, D)),
        "w_ple": nrm((L, PLE_DIM, D), PLE_DIM ** -0.5),
        "w_ple_gate": nrm((L, D, D), D ** -0.5),
        "norm_final": gain((D,)),
    }


def reference(x, p, rel_bias, norm_mix, w_in, lam_q1, lam_k1, lam_q2, lam_k2, attn_subln,
              rwkv_mu, rwkv_w0, rwkv_w2, rwkv_a0, rwkv_a2, rwkv_g2, rwkv_kk, rwkv_ka, rwkv_rk,
              rwkv_lnx_w, rwkv_lnx_b, w_out_attn, w_out_rwkv, w_out, norm_ffn, w_ffn_gate,
              w_ffn_up, w_ffn_down, norm_ple, w_ple, w_ple_gate, norm_final):
    B, S, D = x.shape
    H, Dh, Dv = N_ATTN_HEADS, ATTN_HALF_DIM, ATTN_V_DIM
    NH, N = N_RWKV_HEADS, RWKV_HEAD_DIM
    heads = lambda t: t.reshape(B, S, NH, N)
    for i in range(DEPTH):
        h = rmsnorm(x, norm_mix[i])
        z = h @ w_in[i]
        o = 0
        qa = z[..., o:o + ATTN_WIDTH].reshape(B, S, H, 2, Dh); o += ATTN_WIDTH
        ka = z[..., o:o + ATTN_WIDTH].reshape(B, S, H, 2, Dh); o += ATTN_WIDTH
        va = z[..., o:o + ATTN_WIDTH].reshape(B, S, H, Dv); o += ATTN_WIDTH
        zr = z[..., o:o + RWKV_COLS]; o += RWKV_COLS
        gate_a = jax.nn.sigmoid(z[..., o:o + D]); o += D
        gate_r = jax.nn.sigmoid(z[..., o:o + D])

        lam_init = 0.8 - 0.6 * math.exp(-0.3 * i)
        lam = (jnp.exp(jnp.sum(lam_q1[i] * lam_k1[i])) - jnp.exp(jnp.sum(lam_q2[i] * lam_k2[i]))
               + lam_init)
        oa = diff_attention(qa, ka, va, rel_bias, lam)
        oa = rmsnorm(oa, attn_subln[i], SUBLN_EPS) * (1.0 - lam_init)
        y_attn = oa.reshape(B, S, ATTN_WIDTH) @ w_out_attn[i]

        zs = zr + (token_shift(zr) - zr) * rwkv_mu[i]
        c = 0
        r = zs[..., c:c + RWKV_WIDTH]; c += RWKV_WIDTH
        kr = zs[..., c:c + RWKV_WIDTH]; c += RWKV_WIDTH
        vr = zs[..., c:c + RWKV_WIDTH]; c += RWKV_WIDTH
        xw = zs[..., c:c + DECAY_LORA]; c += DECAY_LORA
        xa = zs[..., c:c + AAA_LORA]; c += AAA_LORA
        xg = zs[..., c:c + GATE_LORA]
        wlog = -jax.nn.softplus(-(rwkv_w0[i] + jnp.tanh(xw) @ rwkv_w2[i])) - 0.5
        decay = jnp.exp(-jnp.exp(wlog.astype(jnp.float32)))
        a = jax.nn.sigmoid(rwkv_a0[i] + xa @ rwkv_a2[i])
        g = jax.nn.sigmoid(xg) @ rwkv_g2[i]
        kk = heads(kr * rwkv_kk[i]).astype(jnp.float32)
        kk = kk / jnp.maximum(jnp.sqrt(jnp.sum(kk * kk, axis=-1, keepdims=True)), 1e-12)
        kr = kr * (1.0 + (a - 1.0) * rwkv_ka[i])
        rh, kh, vh, ah = heads(r), heads(kr), heads(vr), heads(a).astype(jnp.float32)
        yr = rwkv7_scan(rh, heads(decay), kh, vh, -kk, kk * ah)
        yf = yr.astype(jnp.float32)
        mu = jnp.mean(yf, axis=-1, keepdims=True)
        var = jnp.mean(jnp.square(yf - mu), axis=-1, keepdims=True)
        yn = ((yf - mu) * lax.rsqrt(var + GN_EPS)).reshape(B, S, RWKV_WIDTH)
        yn = (yn * rwkv_lnx_w[i].astype(jnp.float32) + rwkv_lnx_b[i].astype(jnp.float32)).astype(x.dtype)
        bonus = jnp.sum(rh * kh * rwkv_rk[i], axis=-1, keepdims=True) * vh
        y_rwkv = ((yn + bonus.reshape(B, S, RWKV_WIDTH)) * g) @ w_out_rwkv[i]

        x = x + (gate_a * y_attn + gate_r * y_rwkv) @ w_out[i]

        h2 = rmsnorm(x, norm_ffn[i])
        x = x + (jax.nn.silu(h2 @ w_ffn_gate[i]) * (h2 @ w_ffn_up[i])) @ w_ffn_down[i]

        e = p[i] @ w_ple[i]
        gp = jax.nn.sigmoid(rmsnorm(x, norm_ple[i]) @ w_ple_gate[i])
        x = x + gp * e
    return rmsnorm(x, norm_final)
```

```python
import math
import numpy as np
import concourse.bass as bass
import concourse.mybir as mybir
from concourse.bass_utils import run_bass_kernel_spmd

F32 = mybir.dt.float32
BF16 = mybir.dt.bfloat16
AF = mybir.ActivationFunctionType
ALU = mybir.AluOpType
AX = mybir.AxisListType

EPOCH = 30000
DMA_RING = 8
DMA_EPOCH = 700


class KB:
    def __init__(self, nc):
        self.nc = nc
        self.eng = {"pe": nc.tensor, "act": nc.scalar, "dve": nc.vector,
                    "pool": nc.gpsimd, "sp": nc.sync}
        self.sem = {}
        self.cnt = {}
        self.nsem = 0
        for e in ("pe", "act", "dve", "pool"):
            self._new_sem(e)
        self.seen = {e: {} for e in self.eng}
        self.semobj = {}
        self.lastw = {}
        self.readers = {}
        self.dq = {}
        for q in ("sp", "pool", "act"):
            self.dq[q] = {"n": 0, "sems": None, "vals": None, "tok": [None] * DMA_RING}
        self.ninst = {e: 0 for e in self.eng}

    def _alloc_sem(self, name):
        self.nsem += 1
        s = self.nc.alloc_semaphore(f"{name}_{self.nsem}")
        return s

    def _new_sem(self, e):
        self.sem[e] = self._alloc_sem("s" + e)
        self.cnt[e] = 0

    def _wait(self, e, tok):
        if tok is None:
            return
        s, v, pe = tok
        sid = id(s)
        if self.seen[e].get(sid, 0) >= v:
            return
        self.eng[e].wait_ge(s, v)
        self.seen[e][sid] = v

    def _deps(self, e, reads, writes):
        toks = []
        for k in list(reads) + list(writes):
            t = self.lastw.get(k)
            if t is not None:
                toks.append(t)
        for k in writes:
            toks.extend(self.readers.get(k, ()))
        for t in toks:
            if e == "pe" and t[2] == "pe":
                continue
            self._wait(e, t)

    def _record(self, tok, reads, writes):
        for k in writes:
            self.lastw[k] = tok
            self.readers[k] = []
        for k in reads:
            self.readers.setdefault(k, []).append(tok)

    def emit(self, e, fn, reads=(), writes=()):
        self._deps(e, reads, writes)
        if self.cnt[e] >= EPOCH:
            self._new_sem(e)
        ins = fn()
        self.cnt[e] += 1
        ins.then_inc(self.sem[e], 1)
        tok = (self.sem[e], self.cnt[e], e)
        self._record(tok, reads, writes)
        self.ninst[e] += 1
        return tok

    def dma(self, q, out, in_, reads=(), writes=(), **kw):
        d = self.dq[q]
        n = d["n"]
        slot = n % DMA_RING
        if n % (DMA_RING * DMA_EPOCH) == 0:
            d["sems"] = [self._alloc_sem("d" + q) for _ in range(DMA_RING)]
            d["vals"] = [0] * DMA_RING
            for t in d["tok"]:
                self._wait(q, t)
        self._wait(q, d["tok"][slot])
        self._deps(q, reads, writes)
        ins = self.eng[q].dma_start(out=out, in_=in_, **kw)
        d["vals"][slot] += 16
        ins.then_inc(d["sems"][slot], 16)
        tok = (d["sems"][slot], d["vals"][slot], "dma_" + q)
        d["tok"][slot] = tok
        d["n"] = n + 1
        self._record(tok, reads, writes)
        self.ninst[q] += 1
        return tok

    def finish(self, q="sp"):
        for qq, d in self.dq.items():
            for t in d["tok"]:
                self._wait(q, t)

    def mm(self, out, lhsT, rhs, start, stop, reads, writes, **kw):
        return self.emit("pe", lambda: self.nc.tensor.matmul(out, lhsT, rhs, start=start, stop=stop, **kw),
                         reads, writes)

    def tr(self, out, in_, ident, reads, writes):
        return self.emit("pe", lambda: self.nc.tensor.transpose(out, in_, ident), reads, writes)

    def act(self, out, in_, func, reads, writes, **kw):
        return self.emit("act", lambda: self.nc.scalar.activation(out, in_, func, **kw), reads, writes)

    def tt(self, e, out, a, b, op, reads, writes):
        return self.emit(e, lambda: self.eng[e].tensor_tensor(out, a, b, op), reads, writes)

    def ts(self, e, out, a, s1, s2, op0, op1, reads, writes, **kw):
        return self.emit(e, lambda: self.eng[e].tensor_scalar(out, a, s1, s2, op0, op1, **kw), reads, writes)

    def ts1(self, e, out, a, s1, op, reads, writes):
        return self.emit(e, lambda: self.eng[e].tensor_single_scalar(out, a, s1, op), reads, writes)

    def cp(self, e, out, a, reads, writes):
        if e == "act":
            return self.emit(e, lambda: self.nc.scalar.copy(out, a), reads, writes)
        return self.emit(e, lambda: self.eng[e].tensor_copy(out, a), reads, writes)

    def memset(self, e, ap, val, writes):
        return self.emit(e, lambda: self.eng[e].memset(ap, val), (), writes)


import contextlib


class Alloc:
    def __init__(self, nc):
        self.nc = nc
        self.st = contextlib.ExitStack()
    def __enter__(self):
        self.st.__enter__()
        return self
    def __exit__(self, *a):
        return self.st.__exit__(*a)
    def sb(self, name, shape, dt):
        return self.st.enter_context(self.nc.sbuf_tensor(name, list(shape), dt))
    def ps(self, name, shape, dt):
        return self.st.enter_context(self.nc.psum_tensor(name, list(shape), dt))


def kb_barrier(self):
    toks = []
    for e in ("pe", "act", "dve", "pool"):
        if self.cnt[e] > 0:
            toks.append((self.sem[e], self.cnt[e], e))
    for q, d in self.dq.items():
        for t in d["tok"]:
            if t is not None:
                toks.append(t)
    for e in ("pe", "act", "dve", "pool", "sp"):
        for t in toks:
            self._wait(e, t)
    self.lastw.clear()
    self.readers.clear()

KB.barrier = kb_barrier

D = 1024
INC = 5376
TB = 512

def build_A(NT):
    nc = bass.Bass("TRN2", target_bir_lowering=False)
    x = nc.dram_tensor("x", [NT, D], F32, kind="ExternalInput").ap()
    w = nc.dram_tensor("w_in", [D, INC], F32, kind="ExternalInput").ap()
    gam = nc.dram_tensor("gam", [128, 8], F32, kind="ExternalInput").ap()
    ident = nc.dram_tensor("ident", [128, 128], F32, kind="ExternalInput").ap()
    qT = nc.dram_tensor("qT", [512, NT], BF16, kind="ExternalOutput").ap()
    kT = nc.dram_tensor("kT", [512, NT], BF16, kind="ExternalOutput").ap()
    vo = nc.dram_tensor("v", [NT, 512], BF16, kind="ExternalOutput").ap()
    zrT = nc.dram_tensor("zrT", [1792, NT], F32, kind="ExternalOutput").ap()
    gT = nc.dram_tensor("gT", [2048, NT], BF16, kind="ExternalOutput").ap()
    k = KB(nc)
    with Alloc(nc) as al:
        emit_A(nc, k, al, NT, x, w, gam, ident, qT, kT, vo, zrT, gT)
        k.finish("sp")
    return nc


def emit_A(nc, k, al, NT, x, w, gam, ident, qT, kT, vo, zrT, gT, pfx="A"):
    nblk = NT // TB
    wb = al.sb(pfx + "wb", [128, 8, INC], BF16)
    wst = [al.sb(pfx + f"wst{i}", [128, 8, 512], F32) for i in range(2)]
    gt = al.sb(pfx + "gt", [128, 8], F32)
    idf = al.sb(pfx + "idf", [128, 128], F32)
    idb = al.sb(pfx + "idb", [128, 128], BF16)
    xt = [al.sb(pfx + f"xt{i}", [128, 4, D], F32) for i in range(2)]
    junk = al.sb(pfx + "junk", [128, D], BF16)
    ss = al.sb(pfx + "ss", [128, 4], F32)
    rstd = al.sb(pfx + "rstd", [128, 4], F32)
    xn = [al.sb(pfx + f"xn{i}", [128, D], BF16) for i in range(2)]
    hT = al.sb(pfx + "hT", [128, 8, TB], BF16)
    NOB = 4
    ob16 = [al.sb(pfx + f"ob16_{i}", [128, 512], BF16) for i in range(NOB)]
    ob32 = [al.sb(pfx + f"ob32_{i}", [128, 512], F32) for i in range(NOB)]
    pT = [al.ps(pfx + f"pT{i}", [128, 8, 128], BF16) for i in range(2)]
    NPZ = 4
    pz = [al.ps(pfx + f"pz{i}", [128, 512], F32) for i in range(NPZ)]

    k.dma("sp", gt[:], gam, writes=["gt"])
    k.dma("sp", idf[:], ident, writes=["idf"])
    k.cp("dve", idb[:], idf[:], ["idf"], ["idb"])
    wv = w.rearrange("(kc p) n -> p kc n", p=128)
    xv = x.rearrange("(b s p) d -> b p s d", p=128, s=4)

    def load_x(b):
        k.dma("sp", xt[b % 2][:], xv[b], writes=[f"xt{b%2}"])

    load_x(0)
    ncc = INC // 512 + (1 if INC % 512 else 0)
    for c in range(ncc):
        c0 = c * 512
        cw = min(512, INC - c0)
        st = wst[c % 2]
        k.dma("sp", st[:, :, :cw], wv[:, :, c0:c0 + cw], writes=[f"wst{c%2}"])
        for kc in range(8):
            e = "dve" if kc % 2 == 0 else "pool"
            k.ts1(e, wb[:, kc, c0:c0 + cw], st[:, kc, :cw], gt[:, kc:kc + 1], ALU.mult,
                  [f"wst{c%2}", "gt"], ["wb"])

    evn = [0]
    for b in range(nblk):
        if b + 1 < nblk:
            load_x(b + 1)
        xb = xt[b % 2]
        xk = f"xt{b%2}"
        for s in range(4):
            k.act(junk[:], xb[:, s, :], AF.Square, [xk], ["junk", "ss"], accum_out=ss[:, s:s + 1])
        k.ts("dve", rstd[:], ss[:], 1.0 / D, 1e-6, ALU.mult, ALU.add, ["ss"], ["rstd"])
        k.act(rstd[:], rstd[:], AF.Sqrt, ["rstd"], ["rstd"])
        k.emit("dve", lambda: nc.vector.reciprocal(rstd[:], rstd[:]), ["rstd"], ["rstd"])
        for s in range(4):
            xnk = f"xn{s%2}"
            k.ts1("dve", xn[s % 2][:], xb[:, s, :], rstd[:, s:s + 1], ALU.mult, [xk, "rstd"], [xnk])
            pk = f"pT{s%2}"
            for kc in range(8):
                k.tr(pT[s % 2][:, kc, :], xn[s % 2][:, kc * 128:(kc + 1) * 128], idb[:], [xnk, "idb"], [pk])
            k.cp("act", hT[:, :, s * 128:(s + 1) * 128], pT[s % 2][:], [pk], ["hT"])
        t0 = b * TB
        chunks = []
        for cc in range(4):
            chunks.append(("q", cc, cc * 128))
        for cc in range(4):
            chunks.append(("k", cc, 512 + cc * 128))
        for cc in range(14):
            chunks.append(("r", cc, 1536 + cc * 128))
        for cc in range(16):
            chunks.append(("g", cc, 1536 + 1792 + cc * 128))
        for (kind, cc, col) in chunks:
            i = evn[0]; evn[0] += 1
            pzi = pz[i % NPZ]; pk = f"pz{i%NPZ}"
            for kc in range(8):
                k.mm(pzi[:], wb[:, kc, col:col + 128], hT[:, kc, :], kc == 0, kc == 7, ["wb", "hT"], [pk])
            oi = i % NOB
            if kind == "q":
                k.act(ob16[oi][:], pzi[:], AF.Copy, [pk], [f"ob16_{oi}"], scale=0.125)
                k.dma("pool", qT[cc * 128:(cc + 1) * 128, t0:t0 + TB], ob16[oi][:], reads=[f"ob16_{oi}"])
            elif kind == "k":
                k.cp("dve", ob16[oi][:], pzi[:], [pk], [f"ob16_{oi}"])
                k.dma("pool", kT[cc * 128:(cc + 1) * 128, t0:t0 + TB], ob16[oi][:], reads=[f"ob16_{oi}"])
            elif kind == "r":
                e = "dve" if cc % 2 == 0 else "act"
                k.cp(e, ob32[oi][:], pzi[:], [pk], [f"ob32_{oi}"])
                k.dma("pool", zrT[cc * 128:(cc + 1) * 128, t0:t0 + TB], ob32[oi][:], reads=[f"ob32_{oi}"])
            else:
                k.act(ob16[oi][:], pzi[:], AF.Sigmoid, [pk], [f"ob16_{oi}"])
                k.dma("pool", gT[cc * 128:(cc + 1) * 128, t0:t0 + TB], ob16[oi][:], reads=[f"ob16_{oi}"])
        for s in range(4):
            i = evn[0]; evn[0] += 1
            pzi = pz[i % NPZ]; pk = f"pz{i%NPZ}"
            for kc in range(8):
                k.mm(pzi[:], hT[:, kc, s * 128:(s + 1) * 128], wb[:, kc, 1024:1536], kc == 0, kc == 7,
                     ["wb", "hT"], [pk])
            oi = i % NOB
            k.cp("dve", ob16[oi][:], pzi[:], [pk], [f"ob16_{oi}"])
            k.dma("pool", vo[t0 + s * 128:t0 + (s + 1) * 128, :], ob16[oi][:], reads=[f"ob16_{oi}"])


NEGM = -30000.0

def t5_bucket_np(n):
    n = np.maximum(n, 0)
    nf = np.maximum(n, 16).astype(np.float32)
    large = 16 + (np.log(nf / np.float32(16)) / np.float32(math.log(128 / 16)) * np.float32(16)).astype(np.int32)
    large = np.minimum(large, 31)
    return np.where(n < 16, n, large)

def toeplitz_idx():
    j = np.arange(128)[:, None]
    i = np.arange(256)[None, :]
    return t5_bucket_np(i - j)

def neg_mask():
    j = np.arange(128)[:, None]
    i = np.arange(256)[None, :]
    return np.where(i >= j, 0.0, NEGM).astype(np.float32)

def build_B(S, lam_init):
    nc = bass.Bass("TRN2", target_bir_lowering=False)
    ins = dict(
        qT=nc.dram_tensor("qT", [256, S], BF16, kind="ExternalInput").ap(),
        kT=nc.dram_tensor("kT", [256, S], BF16, kind="ExternalInput").ap(),
        v=nc.dram_tensor("v", [S, 256], BF16, kind="ExternalInput").ap(),
        G=nc.dram_tensor("G", [4, 128, 256], F32, kind="ExternalInput").ap(),
        b31=nc.dram_tensor("b31", [128, 4], F32, kind="ExternalInput").ap(),
        neg=nc.dram_tensor("neg", [128, 256], F32, kind="ExternalInput").ap(),
        lamv=nc.dram_tensor("lamv", [128, 4, 64], F32, kind="ExternalInput").ap(),
        sgain=nc.dram_tensor("sgain", [128, 128], F32, kind="ExternalInput").ap(),
        ident=nc.dram_tensor("ident", [128, 128], F32, kind="ExternalInput").ap(),
    )
    oaT = nc.dram_tensor("oaT", [256, S], BF16, kind="ExternalOutput").ap()
    dbg = nc.dram_tensor("dbg", [4, 128, 512], F32, kind="ExternalOutput").ap()
    k = KB(nc)
    with Alloc(nc) as al:
        emit_B(nc, k, al, S, lam_init, ins, oaT, dbg=dbg)
        k.finish("sp")
    return nc


def emit_B(nc, k, al, S, lam_init, ins, oaT, pfx="B", dbg=None):
    NJ = S // 128
    NG = S // 512
    VW = 136
    qs = al.sb(pfx + "qs", [128, 2, S], BF16)
    ks = al.sb(pfx + "ks", [128, 2, S], BF16)
    vaug = al.sb(pfx + "vaug", [128, NJ, 2, VW], BF16)
    Gs = al.sb(pfx + "Gs", [128, 4, 256], F32)
    negs = al.sb(pfx + "negs", [128, 256], F32)
    b31s = al.sb(pfx + "b31s", [128, 4], F32)
    Tb = al.sb(pfx + "Tb", [128, 4, 256], BF16)
    lamt = al.sb(pfx + "lamt", [128, 4, 64], F32)
    lprod = al.sb(pfx + "lprod", [128, 2, 64], F32)
    lsum = al.sb(pfx + "lsum", [128, 2], F32)
    lam = al.sb(pfx + "lam", [128, 1], F32)
    sg = al.sb(pfx + "sg", [128, 128], F32)
    idf = al.sb(pfx + "idf", [128, 128], F32)
    idb = al.sb(pfx + "idb", [128, 128], BF16)
    NB = 3
    PT = [al.sb(pfx + f"PT{i}", [128, 512], BF16) for i in range(NB)]
    rr = al.sb(pfx + "rr", [128, 2], F32)
    rl = al.sb(pfx + "rl", [128, 1], F32)
    t1 = al.sb(pfx + "t1", [128, 128], F32)
    ot = al.sb(pfx + "ot", [128, 128], F32)
    junk = al.sb(pfx + "junk", [128, 128], F32)
    ms = al.sb(pfx + "ms", [128, 1], F32)
    onb = al.sb(pfx + "onb", [128, 128], BF16)
    oT = [al.sb(pfx + f"oT{i}", [128, 512], BF16) for i in range(2)]
    ST = [al.ps(pfx + f"ST{i}", [128, 512], F32) for i in range(NB)]
    accb = al.ps(pfx + "accb", [128, 4, 512], F32)
    pTr = al.ps(pfx + "pTr", [128, 128], BF16)

    for hl in range(2):
        for part in range(max(1, S // 2048)):
            w = min(S, 2048)
            k.dma("sp", qs[:, hl, part * w:(part + 1) * w], ins["qT"][hl * 128:(hl + 1) * 128, part * w:(part + 1) * w], writes=["qs"])
            k.dma("sp", ks[:, hl, part * w:(part + 1) * w], ins["kT"][hl * 128:(hl + 1) * 128, part * w:(part + 1) * w], writes=["ks"])
    vv = ins["v"].rearrange("(J p) (h d) -> p J h d", p=128, h=2)
    JB = 8
    for j0 in range(0, NJ, JB):
        j1 = min(NJ, j0 + JB)
        for h2 in range(2):
            k.dma("sp", vaug[:, j0:j1, h2, 0:128], vv[:, j0:j1, h2, :], writes=["vaug"])
    k.memset("pool", vaug[:, :, :, 128:129], 1.0, ["vaug"])
    k.dma("sp", Gs[:], ins["G"].rearrange("a p i -> p a i"), writes=["Gs"])
    k.dma("sp", negs[:], ins["neg"], writes=["negs"])
    k.dma("sp", b31s[:], ins["b31"], writes=["b31s"])
    k.dma("sp", lamt[:], ins["lamv"], writes=["lamt"])
    k.dma("sp", sg[:], ins["sgain"], writes=["sg"])
    k.dma("sp", idf[:], ins["ident"], writes=["idf"])
    k.cp("dve", idb[:], idf[:], ["idf"], ["idb"])
    for hc in range(4):
        k.ts1("dve", Gs[:, hc, :], Gs[:, hc, :], b31s[:, hc:hc + 1], ALU.subtract, ["Gs", "b31s"], ["Gs"])
        k.tt("dve", Tb[:, hc, :], Gs[:, hc, :], negs[:], ALU.add, ["Gs", "negs"], ["Tb"])
    k.tt("dve", lprod[:, 0, :], lamt[:, 0, :], lamt[:, 1, :], ALU.mult, ["lamt"], ["lprod"])
    k.tt("dve", lprod[:, 1, :], lamt[:, 2, :], lamt[:, 3, :], ALU.mult, ["lamt"], ["lprod"])
    k.emit("dve", lambda: nc.vector.reduce_sum(lsum[:], lprod[:], AX.X), ["lprod"], ["lsum"])
    k.act(lsum[:], lsum[:], AF.Exp, ["lsum"], ["lsum"])
    k.tt("dve", lam[:], lsum[:, 0:1], lsum[:, 1:2], ALU.subtract, ["lsum"], ["lam"])
    k.ts1("dve", lam[:], lam[:], float(lam_init), ALU.add, ["lam"], ["lam"])
    k.ts1("dve", sg[:], sg[:], float(1.0 - lam_init), ALU.mult, ["sg"], ["sg"])

    def acc(c, I):
        return accb[:, I, c * 256:c * 256 + 129]

    for hl in range(2):
        steps = [(g, J, c) for g in range(NG) for J in range(4 * g + 4) for c in range(2)]
        N = len(steps)

        def qk(n):
            g, J, c = steps[n]
            buf = n % NB
            r = J - 4 * g
            c0 = max(0, r) * 128
            a = 128 * r
            has_t = r >= -1
            k.mm(ST[buf][:, c0:512], ks[c * 64:(c + 1) * 64, hl, J * 128:(J + 1) * 128],
                 qs[c * 64:(c + 1) * 64, hl, g * 512 + c0:(g + 1) * 512], True, not has_t,
                 ["ks", "qs"], [f"ST{buf}"])
            if has_t:
                tc0 = max(a, 0); tc1 = min(a + 256, 512)
                i0 = tc0 - a
                k.mm(ST[buf][:, tc0:tc1], idb[:], Tb[:, hl * 2 + c, i0:i0 + (tc1 - tc0)], False, True,
                     ["idb", "Tb"], [f"ST{buf}"])

        def ex_pv(n):
            g, J, c = steps[n]
            buf = n % NB
            r = J - 4 * g
            c0 = max(0, r) * 128
            k.act(PT[buf][:, c0:512], ST[buf][:, c0:512], AF.Exp, [f"ST{buf}"], [f"PT{buf}"])
            for I in range(4):
                Ia = 4 * g + I
                if Ia >= J:
                    k.mm(acc(c, I), PT[buf][:, I * 128:(I + 1) * 128], vaug[:, J, hl, 0:129], (J == 0 and c == 0), J == Ia,
                         [f"PT{buf}", "vaug"], [f"acc{c}_{I}"], skip_group_check=True)

        def fin(g):
            ob = oT[g % 2]; obk = f"oT{g%2}"
            for I in range(4):
                ak = [f"acc0_{I}", f"acc1_{I}"]
                k.emit("dve", lambda: nc.vector.reciprocal(rr[:, 0:1], accb[:, I, 128:129]), [ak[0]], ["rr"])
                k.emit("dve", lambda: nc.vector.reciprocal(rr[:, 1:2], accb[:, I, 256 + 128:256 + 129]), [ak[1]], ["rr"])
                k.tt("dve", rl[:], rr[:, 1:2], lam[:], ALU.mult, ["rr", "lam"], ["rl"])
                k.ts1("dve", t1[:], accb[:, I, 256:256 + 128], rl[:, 0:1], ALU.mult, [ak[1], "rl"], ["t1"])
                k.emit("dve", lambda: nc.vector.scalar_tensor_tensor(ot[:], accb[:, I, 0:128], rr[:, 0:1], t1[:], ALU.mult, ALU.subtract),
                       [ak[0], "rr", "t1"], ["ot"])
                k.act(junk[:], ot[:], AF.Square, ["ot"], ["junk", "ms"], accum_out=ms[:])
                k.ts("dve", ms[:], ms[:], 1.0 / 128, 1e-5, ALU.mult, ALU.add, ["ms"], ["ms"])
                k.act(ms[:], ms[:], AF.Sqrt, ["ms"], ["ms"])
                k.emit("dve", lambda: nc.vector.reciprocal(ms[:], ms[:]), ["ms"], ["ms"])
                k.emit("dve", lambda: nc.vector.scalar_tensor_tensor(onb[:], ot[:], ms[:, 0:1], sg[:], ALU.mult, ALU.mult),
                       ["ot", "ms", "sg"], ["onb"])
                if dbg is not None and hl == 0 and g == 0 and I == 0:
                    k.cp("dve", dt[:, 3, :], accb[:, 0, :], ["acc0_0", "acc1_0"], ["dbgt"])
                    k.cp("dve", dt[:, 2, 0:128], ot[:], ["ot"], ["dbgt"])
                    k.cp("dve", dt[:, 2, 128:256], onb[:], ["onb"], ["dbgt"])
                    k.dma("pool", dbg.rearrange("a p n -> p a n"), dt[:], reads=["dbgt"])
                k.tr(pTr[:], onb[:], idb[:], ["onb", "idb"], ["pTr"])
                k.cp("act", ob[:, I * 128:(I + 1) * 128], pTr[:], ["pTr"], [obk])
            k.dma("pool", oaT[hl * 128:(hl + 1) * 128, g * 512:(g + 1) * 512], ob[:], reads=[obk])

        LA = 2
        if dbg is not None and hl == 0:
            dt = al.sb(pfx + "dbgt", [128, 4, 512], F32)
            k.memset("pool", dt[:], 0.0, ["dbgt"])
            qk(0)
            k.cp("dve", dt[:, 0, :], ST[0][:], ["ST0"], ["dbgt"])
            k.act(dt[:, 1, :], ST[0][:], AF.Exp, ["ST0"], ["dbgt"])
            k.cp("dve", dt[:, 2, 0:256], Tb[:, 0, :], ["Tb"], ["dbgt"])
            k.cp("dve", dt[:, 2, 256:257], lam[:], ["lam"], ["dbgt"])
        for n in range(min(LA, N)):
            qk(n)
        for n in range(N):
            if n + LA < N:
                qk(n + LA)
            ex_pv(n)
            g, J, c = steps[n]
            if J == 4 * g + 3 and c == 1:
                fin(g)


CDEC = math.exp(-0.5)
GN_EPS = 64e-5

def consts_C():
    s = np.arange(128)[:, None]; t = np.arange(128)[None, :]
    c = dict(
        ident=np.eye(128, dtype=np.float32),
        triI=np.where(s <= t, -CDEC, 0.0).astype(np.float32),
        triE=np.where(s < t, -CDEC, 0.0).astype(np.float32),
        triR=np.where(s > t, -CDEC, 0.0).astype(np.float32),
        mstrict=(t > s).astype(np.float32),
        mask3=np.concatenate([(t >= s), (t > s), (t >= s)], 1).astype(np.float32),
        mlow=(s > t).astype(np.float32),
        bones=(s // 64 == t // 64).astype(np.float32),
    )
    return c

C_IN = [("zc", [1024, None], F32), ("mu", [128, 8], F32), ("w2c", [64, 256], F32), ("w0c", [1, 256], F32),
        ("a2c", [64, 256], F32), ("a0c", [128, 2], F32), ("g2c", [128, 256], F32), ("kkw", [128, 2], F32),
        ("ka", [128, 2], F32), ("rkb", [128, 2, 128], F32), ("lnw", [128, 2], F32), ("lnb", [128, 2], F32),
        ("ident", [128, 128], F32), ("triI", [128, 128], F32), ("triE", [128, 128], F32), ("triR", [128, 128], F32),
        ("mstrict", [128, 128], F32), ("mask3", [128, 384], F32), ("mlow", [128, 128], F32), ("bones", [128, 128], F32)]

def build_C(S):
    nc = bass.Bass("TRN2", target_bir_lowering=False)
    ins = {}
    for nm, shp, dt in C_IN:
        shp = [S if x is None else x for x in shp]
        ins[nm] = nc.dram_tensor(nm, shp, dt, kind="ExternalInput").ap()
    yT = nc.dram_tensor("yT", [256, S], BF16, kind="ExternalOutput").ap()
    k = KB(nc)
    with Alloc(nc) as al:
        emit_C(nc, k, al, S, ins, yT)
        k.finish("sp")
    return nc


def emit_C(nc, k, al, S, ins, yT, pfx="C", zrows=None):
    if zrows is None:
        zrows = [j * 128 for j in range(8)]
    NBLK = S // 512
    A = lambda nm, shp, dt=F32: al.sb(pfx + nm, shp, dt)
    P = lambda nm, shp, dt=F32: al.ps(pfx + nm, shp, dt)
    cs = {}
    for nm, shp, dt in C_IN[1:]:
        cs[nm] = A("c_" + nm, shp)
        k.dma("sp", cs[nm][:], ins[nm], writes=["c_" + nm])
    wa = A("wa", [128, 256])
    k.dma("sp", wa[0:64, :], ins["w2c"], writes=["wa"])
    k.dma("sp", wa[64:128, :], ins["a2c"], writes=["wa"])
    ones1 = A("ones1", [1, 128])
    k.memset("pool", ones1[:], 1.0, ["ones1"])
    omk = A("omk", [128, 2])
    k.ts("dve", omk[:], cs["ka"][:], -1.0, 1.0, ALU.mult, ALU.add, ["c_ka"], ["omk"])
    ident = cs["ident"]

    zin = [A(f"zin{i}", [128, 513]) for i in range(3)]
    dtmp = [A(f"dtmp{i}", [128, 512]) for i in range(2)]
    zs = A("zs", [128, 8, 512])
    tw = A("tw", [64, 512])
    sgx = A("sgx", [128, 512])
    aT = A("aT", [128, 2, 512])
    kkT = A("kkT", [128, 2, 512])
    kpT = A("kpT", [128, 2, 512])
    kkaT = A("kkaT", [128, 2, 512])
    bonT = A("bonT", [128, 2, 512])
    gTs = A("gTs", [128, 2, 512])
    tmpA = A("tmpA", [128, 512])
    tmpB = A("tmpB", [128, 512])
    sig = A("sig", [128, 256])
    Gi = A("Gi", [128, 2, 128]); iG = A("iG", [128, 2, 128]); Ge = A("Ge", [128, 2, 128])
    Gr = A("Gr", [128, 256])
    ARt = A("ARt", [128, 2, 2, 128])
    KtT = A("KtT", [128, 2, 128]); BtT = A("BtT", [128, 2, 128])
    Vtok = A("Vtok", [128, 256]); Khat = A("Khat", [128, 256]); Bhat = A("Bhat", [128, 256])
    ATs = A("ATs", [128, 4, 3, 128])
    PTs = [A(f"PTs{i}", [128, 4, 2, 128]) for i in range(2)]
    Qs = [A(f"Qs{i}", [128, 4, 128]) for i in range(2)]
    Hst = A("Hst", [128, 2, 64])
    X0 = A("X0", [128, 256]); Up = A("Up", [128, 256])
    Ysb = A("Ysb", [128, 4, 64]); Ysq = A("Ysq", [128, 4, 64])
    st1 = A("st1", [128, 4]); st2 = A("st2", [128, 4]); mean = A("mean", [128, 4]); rstd = A("rstd", [128, 4])
    msq = A("msq", [128, 4])
    Yn = A("Yn", [128, 256])
    fin = A("fin", [128, 2, 128])
    outb = [A(f"outb{i}", [128, 2, 512], BF16) for i in range(2)]
    SA = P("SA", [128, 4, 256]); SB = P("SB", [128, 4, 128])
    ATp = P("ATp", [128, 512]); Q0p = P("Q0p", [128, 4, 128])
    M = [P(f"M{i}", [128, 512]) for i in range(3)]
    mi = [0]
    def misc():
        i = mi[0] % 3; mi[0] += 1
        return M[i], f"M{i}"

    k.memset("pool", Hst[:], 0.0, ["Hst"])
    k.memset("pool", zin[0][:, 0:1], 0.0, ["zin0"])

    zi = [0]
    for b in range(NBLK):
        t0 = b * 512
        for j in range(8):
            zb = zin[zi[0] % 3]; zk = f"zin{zi[0]%3}"; zi[0] += 1
            if b == 0:
                k.memset("pool", zb[:, 0:1], 0.0, [zk])
                k.dma("sp", zb[:, 1:513], ins["zc"][zrows[j]:zrows[j] + 128, 0:512], writes=[zk])
            else:
                k.dma("sp", zb[:, 0:513], ins["zc"][zrows[j]:zrows[j] + 128, t0 - 1:t0 + 512], writes=[zk])
            e = "pool"
            dt_ = dtmp[j % 2]; dk = f"dtmp{j%2}"
            k.tt(e, dt_[:], zb[:, 0:512], zb[:, 1:513], ALU.subtract, [zk], [dk])
            k.emit("dve", lambda: nc.vector.scalar_tensor_tensor(zs[:, j, :], dt_[:], cs["mu"][:, j:j + 1], zb[:, 1:513], ALU.mult, ALU.add),
                   [dk, zk, "c_mu"], [f"zs{j}"])
        k.act(tw[:], zs[0:64, 6, :], AF.Tanh, ["zs6"], ["tw"])
        k.act(sgx[:], zs[:, 7, :], AF.Sigmoid, ["zs7"], ["sgx"])
        for hp in range(2):
            m, mk = misc()
            k.mm(m[:], wa[64:128, hp * 128:(hp + 1) * 128], zs[64:128, 6, :], True, True, ["wa", "zs6"], [mk])
            k.act(aT[:, hp, :], m[:], AF.Sigmoid, [mk, "c_a0c"], [f"aT{hp}"], bias=cs["a0c"][:, hp:hp + 1])
            k.ts1("pool", tmpA[:], zs[:, 2 + hp, :], cs["kkw"][:, hp:hp + 1], ALU.mult, [f"zs{2+hp}", "c_kkw"], ["tmpA"])
            k.tt("pool", tmpB[:], tmpA[:], tmpA[:], ALU.mult, ["tmpA"], ["tmpB"])
            m, mk = misc()
            k.mm(m[:], cs["bones"][:], tmpB[:], True, True, ["c_bones", "tmpB"], [mk])
            k.act(tmpB[:], m[:], AF.Sqrt, [mk], ["tmpB"])
            k.ts1("dve", tmpB[:], tmpB[:], 1e-12, ALU.max, ["tmpB"], ["tmpB"])
            k.emit("dve", lambda: nc.vector.reciprocal(tmpB[:], tmpB[:]), ["tmpB"], ["tmpB"])
            k.tt("dve", kkT[:, hp, :], tmpA[:], tmpB[:], ALU.mult, ["tmpA", "tmpB"], [f"kkT{hp}"])
            k.ts("dve", tmpA[:], aT[:, hp, :], cs["ka"][:, hp:hp + 1], omk[:, hp:hp + 1], ALU.mult, ALU.add,
                 [f"aT{hp}", "c_ka", "omk"], ["tmpA"])
            k.tt("pool", kpT[:, hp, :], tmpA[:], zs[:, 2 + hp, :], ALU.mult, ["tmpA", f"zs{2+hp}"], [f"kpT{hp}"])
            k.tt("pool", kkaT[:, hp, :], kkT[:, hp, :], aT[:, hp, :], ALU.mult, [f"kkT{hp}", f"aT{hp}"], [f"kkaT{hp}"])
            k.tt("pool", tmpA[:], zs[:, hp, :], kpT[:, hp, :], ALU.mult, [f"zs{hp}", f"kpT{hp}"], ["tmpA"])
            m, mk = misc()
            k.mm(m[:], cs["rkb"][:, hp, :], tmpA[:], True, True, ["c_rkb", "tmpA"], [mk])
            k.tt("dve", bonT[:, hp, :], m[:], zs[:, 4 + hp, :], ALU.mult, [mk, f"zs{4+hp}"], [f"bonT{hp}"])
            m, mk = misc()
            k.mm(m[:], cs["g2c"][:, hp * 128:(hp + 1) * 128], sgx[:], True, True, ["c_g2c", "sgx"], [mk])
            k.cp("act", gTs[:, hp, :], m[:], [mk], [f"gTs{hp}"])

        ob = outb[b % 2]; obk = f"outb{b%2}"
        for ch in range(4):
            cs0 = ch * 128
            csl = slice(cs0, cs0 + 128)
            m, mk = misc()
            k.mm(m[:, 0:256], tw[:, csl], wa[0:64, :], True, False, ["tw", "wa"], [mk])
            k.mm(m[:, 0:256], ones1[:], cs["w0c"][:], False, True, ["ones1", "c_w0c"], [mk])
            k.act(sig[:], m[:, 0:256], AF.Sigmoid, [mk], ["sig"])
            m, mk = misc()
            for hp in range(2):
                k.mm(m[:, hp * 128:(hp + 1) * 128], sig[:, hp * 128:(hp + 1) * 128], cs["triI"][:], hp == 0, False, ["sig", "c_triI"], [mk], skip_group_check=True)
            for hp in range(2):
                k.mm(m[:, 256 + hp * 128:256 + (hp + 1) * 128], sig[:, hp * 128:(hp + 1) * 128], cs["triE"][:], False, hp == 1, ["sig", "c_triE"], [mk], skip_group_check=True)
            k.act(Gi[:], m[:, 0:256], AF.Exp, [mk], ["Gi"])
            k.act(iG[:], m[:, 0:256], AF.Exp, [mk], ["iG"], scale=-1.0)
            k.act(Ge[:], m[:, 256:512], AF.Exp, [mk], ["Ge"])
            m, mk = misc()
            k.mm(m[:, 0:256], cs["triR"][:], sig[:], True, True, ["sig", "c_triR"], [mk])
            k.act(Gr[:], m[:, 0:256], AF.Exp, [mk], ["Gr"])
            for hp in range(2):
                k.emit("dve", lambda: nc.vector.scalar_tensor_tensor(ARt[:, hp, 0, :], kkT[:, hp, csl], -1.0, Ge[:, hp, :], ALU.mult, ALU.mult),
                       [f"kkT{hp}", "Ge"], ["ARt"])
                k.tt("pool", ARt[:, hp, 1, :], zs[:, hp, csl], Gi[:, hp, :], ALU.mult, [f"zs{hp}", "Gi"], ["ARt"])
                k.tt("dve", KtT[:, hp, :], kpT[:, hp, csl], iG[:, hp, :], ALU.mult, [f"kpT{hp}", "iG"], ["KtT"])
                k.tt("dve", BtT[:, hp, :], kkaT[:, hp, csl], iG[:, hp, :], ALU.mult, [f"kkaT{hp}", "iG"], ["BtT"])
            m, mk = misc()
            for hp in range(2):
                k.tr(m[:, hp * 128:(hp + 1) * 128], zs[:, 4 + hp, csl], ident[:], [f"zs{4+hp}", "c_ident"], [mk])
            k.cp("act", Vtok[:], m[:, 0:256], [mk], ["Vtok"])
            m, mk = misc()
            for hp in range(2):
                k.tr(m[:, hp * 128:(hp + 1) * 128], kpT[:, hp, csl], ident[:], [f"kpT{hp}", "c_ident"], [mk])
                k.tr(m[:, 256 + hp * 128:256 + (hp + 1) * 128], kkaT[:, hp, csl], ident[:], [f"kkaT{hp}", "c_ident"], [mk])
            k.tt("dve", Khat[:], m[:, 0:256], Gr[:], ALU.mult, [mk, "Gr"], ["Khat"])
            k.tt("dve", Bhat[:], m[:, 256:512], Gr[:], ALU.mult, [mk, "Gr"], ["Bhat"])
            for h in range(4):
                hp, e = h // 2, h % 2
                ps = slice(e * 64, (e + 1) * 64)
                k.mm(ATp[:, 0:256], BtT[ps, hp, :], ARt[ps, hp, :, :], True, False, ["BtT", "ARt"], ["ATp"], skip_group_check=True)
                k.mm(ATp[:, 256:512], KtT[ps, hp, :], ARt[ps, hp, :, :], False, True, ["KtT", "ARt"], ["ATp"], skip_group_check=True)
                k.mm(Q0p[:, h, :], ARt[ps, hp, 0, :], BtT[ps, hp, :], h == 0, h == 3, ["ARt", "BtT"], ["Q0p"], skip_group_check=True)
                k.tt("dve", PTs[0][:, h, 0, :], ATp[:, 0:128], cs["mstrict"][:], ALU.mult, ["ATp", "c_mstrict"], ["PTs0"])
                k.tt("dve", ATs[:, h, :, :], ATp[:, 128:512], cs["mask3"][:], ALU.mult, ["ATp", "c_mask3"], [f"ATs{h}"])
                k.cp("pool", PTs[0][:, h, 1, :], ident[:], ["c_ident"], ["PTs0"])
            for h in range(4):
                k.tt("dve", Qs[0][:, h, :], Q0p[:, h, :], cs["mlow"][:], ALU.mult, ["Q0p", "c_mlow"], ["Qs0"])
            for lv in range(7):
                cur, nxt = lv % 2, (lv + 1) % 2
                for h in range(4):
                    if lv < 6:
                        k.mm(SA[:, h, :], Qs[cur][:, h, :], PTs[cur][:, h, :, :], h % 2 == 0, h % 2 == 1,
                             [f"Qs{cur}", f"PTs{cur}"], ["SA"], skip_group_check=True)
                        k.mm(SB[:, h, :], PTs[cur][:, h, 0, :], Qs[cur][:, h, :], h == 0, h == 3,
                             [f"Qs{cur}", f"PTs{cur}"], ["SB"], skip_group_check=True)
                    else:
                        k.mm(SA[:, h, 128:256], Qs[cur][:, h, :], PTs[cur][:, h, 1, :], h % 2 == 0, h % 2 == 1,
                             [f"Qs{cur}", f"PTs{cur}"], ["SA"], skip_group_check=True)
                if lv < 6:
                    k.cp("act", PTs[nxt][:, :, 0, :], SA[:, :, 0:128], ["SA"], [f"PTs{nxt}"])
                    k.cp("act", Qs[nxt][:], SB[:], ["SB"], [f"Qs{nxt}"])
                k.tt("dve", PTs[nxt][:, :, 1, :], SA[:, :, 128:256], PTs[cur][:, :, 1, :], ALU.add, ["SA", f"PTs{cur}"], [f"PTs{nxt}"])
            TT = PTs[1]; TTk = "PTs1"
            m, mk = misc()
            for h in range(4):
                hp, e = h // 2, h % 2
                ps = slice(e * 64, (e + 1) * 64)
                k.mm(m[:, h * 64:(h + 1) * 64], ARt[ps, hp, 0, :], Hst[ps, hp, :], h == 0, False, ["ARt", "Hst"], [mk], skip_group_check=True)
                k.mm(m[:, h * 64:(h + 1) * 64], ATs[:, h, 1, :], Vtok[:, h * 64:(h + 1) * 64], False, h == 3, [f"ATs{h}", "Vtok"], [mk], skip_group_check=True)
            k.cp("act", X0[:], m[:, 0:256], [mk], ["X0"])
            m, mk = misc()
            for h in range(4):
                k.mm(m[:, h * 64:(h + 1) * 64], TT[:, h, 1, :], X0[:, h * 64:(h + 1) * 64], h == 0, h == 3, [TTk, "X0"], [mk], skip_group_check=True)
            k.cp("act", Up[:], m[:, 0:256], [mk], ["Up"])
            my, myk = misc()
            for h in range(4):
                hp, e = h // 2, h % 2
                ps = slice(e * 64, (e + 1) * 64)
                k.mm(my[:, h * 64:(h + 1) * 64], ARt[ps, hp, 1, :], Hst[ps, hp, :], h == 0, False, ["ARt", "Hst"], [myk], skip_group_check=True)
                k.mm(my[:, h * 64:(h + 1) * 64], ATs[:, h, 0, :], Up[:, h * 64:(h + 1) * 64], False, False, [f"ATs{h}", "Up"], [myk], skip_group_check=True)
                k.mm(my[:, h * 64:(h + 1) * 64], ATs[:, h, 2, :], Vtok[:, h * 64:(h + 1) * 64], False, h == 3, [f"ATs{h}", "Vtok"], [myk], skip_group_check=True)
            m, mk = misc()
            for hp in range(2):
                k.mm(m[:, hp * 128:(hp + 1) * 128], Bhat[:, hp * 128:(hp + 1) * 128], Up[:, hp * 128:(hp + 1) * 128], hp == 0, False, ["Bhat", "Up"], [mk], skip_group_check=True)
                k.mm(m[:, hp * 128:(hp + 1) * 128], Khat[:, hp * 128:(hp + 1) * 128], Vtok[:, hp * 128:(hp + 1) * 128], False, hp == 1, ["Khat", "Vtok"], [mk], skip_group_check=True)
            for h in range(4):
                hp, e = h // 2, h % 2
                ps = slice(e * 64, (e + 1) * 64)
                k.emit("dve", lambda: nc.vector.scalar_tensor_tensor(Hst[ps, hp, :], Hst[ps, hp, :], Gi[ps, hp, 127:128],
                                                                    m[ps, hp * 128 + e * 64:hp * 128 + (e + 1) * 64], ALU.mult, ALU.add),
                       ["Hst", "Gi", mk], ["Hst"])
            k.cp("act", Ysb[:], my[:, 0:256], [myk], ["Ysb"])
            k.emit("dve", lambda: nc.vector.reduce_sum(st1[:], Ysb[:], AX.X), ["Ysb"], ["st1"])
            k.tt("pool", Ysq[:], Ysb[:], Ysb[:], ALU.mult, ["Ysb"], ["Ysq"])
            k.emit("dve", lambda: nc.vector.reduce_sum(st2[:], Ysq[:], AX.X), ["Ysq"], ["st2"])
            k.ts1("dve", mean[:], st1[:], 1.0 / 64, ALU.mult, ["st1"], ["mean"])
            k.tt("dve", msq[:], mean[:], mean[:], ALU.mult, ["mean"], ["msq"])
            k.emit("dve", lambda: nc.vector.scalar_tensor_tensor(rstd[:], st2[:], 1.0 / 64, msq[:], ALU.mult, ALU.subtract), ["st2", "msq"], ["rstd"])
            k.ts1("dve", rstd[:], rstd[:], GN_EPS, ALU.add, ["rstd"], ["rstd"])
            k.act(rstd[:], rstd[:], AF.Sqrt, ["rstd"], ["rstd"])
            k.emit("dve", lambda: nc.vector.reciprocal(rstd[:], rstd[:]), ["rstd"], ["rstd"])
            for h in range(4):
                e_ = "dve" if h % 2 == 0 else "pool"
                k.ts(e_, Yn[:, h * 64:(h + 1) * 64], Ysb[:, h, :], mean[:, h:h + 1], rstd[:, h:h + 1], ALU.subtract, ALU.mult,
                     ["Ysb", "mean", "rstd"], ["Yn"])
            m, mk = misc()
            for hp in range(2):
                k.tr(m[:, hp * 128:(hp + 1) * 128], Yn[:, hp * 128:(hp + 1) * 128], ident[:], ["Yn", "c_ident"], [mk])
            for hp in range(2):
                k.act(fin[:, hp, :], m[:, hp * 128:(hp + 1) * 128], AF.Identity, [mk, "c_lnw", "c_lnb"], ["fin"],
                      bias=cs["lnb"][:, hp:hp + 1], scale=cs["lnw"][:, hp:hp + 1])
                k.tt("pool", fin[:, hp, :], fin[:, hp, :], bonT[:, hp, csl], ALU.add, ["fin", f"bonT{hp}"], ["fin"])
                k.tt("pool", ob[:, hp, csl], fin[:, hp, :], gTs[:, hp, csl], ALU.mult, ["fin", f"gTs{hp}"], [obk])
        for hp in range(2):
            k.dma("pool", yT[hp * 128:(hp + 1) * 128, t0:t0 + 512], ob[:, hp, :], reads=[obk])


WST = 2048
D = 1024
DFF = 2816
NF = DFF // 128
TB = 512

def rmsnorm_T(nc, k, xb, xk, hT, ss, rstd, junk, xn, pT, idb, eps, pfx):
    for s in range(4):
        k.act(junk[:], xb[:, s, :], AF.Square, [xk], [pfx + "junk", pfx + "ss"], accum_out=ss[:, s:s + 1])
    k.ts("dve", rstd[:], ss[:], 1.0 / D, eps, ALU.mult, ALU.add, [pfx + "ss"], [pfx + "rstd"])
    k.act(rstd[:], rstd[:], AF.Sqrt, [pfx + "rstd"], [pfx + "rstd"])
    k.emit("dve", lambda: nc.vector.reciprocal(rstd[:], rstd[:]), [pfx + "rstd"], [pfx + "rstd"])
    for s in range(4):
        xnk = pfx + f"xn{s%2}"
        k.ts1("dve", xn[s % 2][:], xb[:, s, :], rstd[:, s:s + 1], ALU.mult, [xk, pfx + "rstd"], [xnk])
        pk = pfx + f"pT{s%2}"
        for kc in range(8):
            k.tr(pT[s % 2][:, kc, :], xn[s % 2][:, kc * 128:(kc + 1) * 128], idb[:], [xnk, pfx + "idb"], [pk])
        k.cp("act", hT[:, :, s * 128:(s + 1) * 128], pT[s % 2][:], [pk], [pfx + "hT"])


def load_w_bf16(nc, k, dst, dkey, src, K, N, wst, wstk, scale_ap=None, skey=None):
    wv = src.rearrange("(kc p) n -> p kc n", p=128)
    step = WST // K
    i = 0
    for c0 in range(0, N, step):
        cw = min(step, N - c0)
        st = wst[i % 2]; sk = wstk + str(i % 2); i += 1
        stv = st[:, 0:K * cw].rearrange("p (kc n) -> p kc n", kc=K)
        k.dma("sp", stv, wv[:, :, c0:c0 + cw], writes=[sk])
        for kc in range(K):
            e = "dve" if kc % 2 == 0 else "pool"
            if scale_ap is not None:
                k.ts1(e, dst[:, kc, c0:c0 + cw], stv[:, kc, :], scale_ap[:, kc:kc + 1], ALU.mult, [sk, skey], [dkey])
            else:
                k.cp(e, dst[:, kc, c0:c0 + cw], stv[:, kc, :], [sk], [dkey])


def build_D1(NT):
    nc = bass.Bass("TRN2", target_bir_lowering=False)
    I = lambda nm, shp, dt=F32: nc.dram_tensor(nm, shp, dt, kind="ExternalInput").ap()
    ins = dict(x=I("x", [NT, D]), oaT=I("oaT", [512, NT], BF16), yrT=I("yrT", [512, NT], BF16), gT=I("gT", [2048, NT], BF16),
               woa=I("woa", [512, D]), wor=I("wor", [512, D]), wo=I("wo", [D, D]))
    xo = nc.dram_tensor("xo", [NT, D], F32, kind="ExternalOutput").ap()
    k = KB(nc)
    with Alloc(nc) as al:
        emit_D1(nc, k, al, NT, ins, xo)
        k.finish("sp")
    return nc


def emit_D1(nc, k, al, NT, ins, xo, pfx="E"):
    A = lambda nm, shp, dt=F32: al.sb(pfx + nm, shp, dt)
    P = lambda nm, shp, dt=F32: al.ps(pfx + nm, shp, dt)
    woa = A("woa", [128, 4, D], BF16); wor = A("wor", [128, 4, D], BF16); wo = A("wo", [128, 8, D], BF16)
    wst = [A(f"wst{i}", [128, WST]) for i in range(2)]
    load_w_bf16(nc, k, woa, pfx + "woa", ins["woa"], 4, D, wst, pfx + "wst")
    load_w_bf16(nc, k, wor, pfx + "wor", ins["wor"], 4, D, wst, pfx + "wst")
    load_w_bf16(nc, k, wo, pfx + "wo", ins["wo"], 8, D, wst, pfx + "wst")
    oab = [A(f"oab{i}", [128, 4, TB], BF16) for i in range(2)]
    yrb = [A(f"yrb{i}", [128, 4, TB], BF16) for i in range(2)]
    xb = [A(f"xb{i}", [128, 4, D]) for i in range(2)]
    gab = [A(f"gab{i}", [128, TB], BF16) for i in range(3)]
    grb = [A(f"grb{i}", [128, TB], BF16) for i in range(3)]
    t1 = [A(f"t1_{i}", [128, TB]) for i in range(2)]
    t2 = [A(f"t2_{i}", [128, TB]) for i in range(2)]
    mT = A("mT", [128, 8, TB], BF16)
    pa = [P(f"pa{i}", [128, 512]) for i in range(2)]
    pr = [P(f"pr{i}", [128, 512]) for i in range(2)]
    po = [P(f"po{i}", [128, 512]) for i in range(3)]
    nblk = NT // TB
    oav = ins["oaT"].rearrange("(kc p) t -> p kc t", p=128)
    yrv = ins["yrT"].rearrange("(kc p) t -> p kc t", p=128)
    xv = ins["x"].rearrange("(b s p) d -> b p s d", p=128, s=4)
    xov = xo.rearrange("(b s p) d -> b p s d", p=128, s=4)

    def load_blk(b):
        t0 = b * TB
        k.dma("sp", oab[b % 2][:], oav[:, :, t0:t0 + TB], writes=[pfx + f"oab{b%2}"])
        k.dma("sp", yrb[b % 2][:], yrv[:, :, t0:t0 + TB], writes=[pfx + f"yrb{b%2}"])
        k.dma("sp", xb[b % 2][:], xv[b], writes=[pfx + f"xb{b%2}"])
    load_blk(0)
    gi = [0]; oi = [0]
    for b in range(nblk):
        t0 = b * TB
        if b + 1 < nblk:
            load_blk(b + 1)
        for oc in range(8):
            i = gi[0]; gi[0] += 1
            ga = gab[i % 3]; gr = grb[i % 3]
            k.dma("sp", ga[:], ins["gT"][oc * 128:(oc + 1) * 128, t0:t0 + TB], writes=[pfx + f"gab{i%3}"])
            k.dma("sp", gr[:], ins["gT"][1024 + oc * 128:1024 + (oc + 1) * 128, t0:t0 + TB], writes=[pfx + f"grb{i%3}"])
            pai = pa[i % 2]; pri = pr[i % 2]
            for kc in range(4):
                k.mm(pai[:], woa[:, kc, oc * 128:(oc + 1) * 128], oab[b % 2][:, kc, :], kc == 0, kc == 3,
                     [pfx + "woa", pfx + f"oab{b%2}"], [pfx + f"pa{i%2}"])
            for kc in range(4):
                k.mm(pri[:], wor[:, kc, oc * 128:(oc + 1) * 128], yrb[b % 2][:, kc, :], kc == 0, kc == 3,
                     [pfx + "wor", pfx + f"yrb{b%2}"], [pfx + f"pr{i%2}"])
            k.tt("dve", t1[i % 2][:], pai[:], ga[:], ALU.mult, [pfx + f"pa{i%2}", pfx + f"gab{i%3}"], [pfx + f"t1_{i%2}"])
            k.tt("dve", t2[i % 2][:], pri[:], gr[:], ALU.mult, [pfx + f"pr{i%2}", pfx + f"grb{i%3}"], [pfx + f"t2_{i%2}"])
            k.tt("pool", mT[:, oc, :], t1[i % 2][:], t2[i % 2][:], ALU.add, [pfx + f"t1_{i%2}", pfx + f"t2_{i%2}"], [pfx + "mT"])
        for s in range(4):
            for n in range(2):
                j = oi[0]; oi[0] += 1
                pj = po[j % 3]
                for kc in range(8):
                    k.mm(pj[:], mT[:, kc, s * 128:(s + 1) * 128], wo[:, kc, n * 512:(n + 1) * 512], kc == 0, kc == 7,
                         [pfx + "mT", pfx + "wo"], [pfx + f"po{j%3}"])
                k.tt("dve", xb[b % 2][:, s, n * 512:(n + 1) * 512], pj[:], xb[b % 2][:, s, n * 512:(n + 1) * 512], ALU.add,
                     [pfx + f"po{j%3}", pfx + f"xb{b%2}"], [pfx + f"xb{b%2}"])
        k.dma("pool", xov[b], xb[b % 2][:], reads=[pfx + f"xb{b%2}"])


def build_D2(NT, last):
    nc = bass.Bass("TRN2", target_bir_lowering=False)
    I = lambda nm, shp, dt=F32: nc.dram_tensor(nm, shp, dt, kind="ExternalInput").ap()
    ins = dict(x=I("x", [NT, D]), p=I("p", [NT, 256]), wg=I("wg", [D, DFF]), wu=I("wu", [D, DFF]), wd=I("wd", [DFF, D]),
               wple=I("wple", [256, D]), wpg=I("wpg", [D, D]), gffn=I("gffn", [128, 8]), gple=I("gple", [128, 8]),
               gfin=I("gfin", [128, D]), ident=I("ident", [128, 128]))
    xo = nc.dram_tensor("xo", [NT, D], F32, kind="ExternalOutput").ap()
    wgS = nc.dram_tensor("wgS", [NF, 128, 1024], BF16).ap()
    wuS = nc.dram_tensor("wuS", [NF, 128, 1024], BF16).ap()
    k = KB(nc)
    with Alloc(nc) as al:
        emit_D2(nc, k, al, NT, last, ins, xo, wgS, wuS)
        k.finish("sp")
    return nc


def emit_D2(nc, k, al, NT, last, ins, xo, wgS, wuS, pfx="F"):
    A = lambda nm, shp, dt=F32: al.sb(pfx + nm, shp, dt)
    P = lambda nm, shp, dt=F32: al.ps(pfx + nm, shp, dt)
    gffn = A("gffn", [128, 8]); gple = A("gple", [128, 8]); idf = A("idf", [128, 128]); idb = A("idb", [128, 128], BF16)
    k.dma("sp", gffn[:], ins["gffn"], writes=[pfx + "gffn"])
    k.dma("sp", gple[:], ins["gple"], writes=[pfx + "gple"])
    k.dma("sp", idf[:], ins["ident"], writes=[pfx + "idf"])
    k.cp("dve", idb[:], idf[:], [pfx + "idf"], [pfx + "idb"])
    if last:
        gfin = A("gfin", [128, D])
        k.dma("sp", gfin[:], ins["gfin"], writes=[pfx + "gfin"])
    wd = A("wd", [128, NF, D], BF16); wple = A("wple", [128, 2, D], BF16); wpg = A("wpg", [128, 8, D], BF16)
    wst = [A(f"wst{i}", [128, WST]) for i in range(2)]
    load_w_bf16(nc, k, wple, pfx + "wple", ins["wple"], 2, D, wst, pfx + "wst")
    load_w_bf16(nc, k, wpg, pfx + "wpg", ins["wpg"], 8, D, wst, pfx + "wst", gple, pfx + "gple")
    wdv = ins["wd"].rearrange("(f p) n -> p f n", p=128)
    i = 0
    for f0 in range(0, NF, 2):
        fw_ = min(2, NF - f0)
        st = wst[i % 2]; sk = pfx + f"wst{i%2}"; i += 1
        stv = st[:, 0:fw_ * D].rearrange("p (f n) -> p f n", f=fw_)
        k.dma("sp", stv, wdv[:, f0:f0 + fw_, :], writes=[sk])
        for f in range(fw_):
            e = "dve" if f % 2 == 0 else "pool"
            k.cp(e, wd[:, f0 + f, :], stv[:, f, :], [sk], [pfx + "wd"])
    wcb = [A(f"wcb{i}", [128, 8, 256], BF16) for i in range(2)]
    ci = 0
    for (src, dstS, nm) in ((ins["wg"], wgS, "wgS"), (ins["wu"], wuS, "wuS")):
        wv = src.rearrange("(kc p) n -> p kc n", p=128)
        for c0 in range(0, DFF, 256):
            cw = min(256, DFF - c0)
            st = wst[i % 2]; sk = pfx + f"wst{i%2}"; i += 1
            stv = st[:, 0:8 * cw].rearrange("p (kc n) -> p kc n", kc=8)
            k.dma("sp", stv, wv[:, :, c0:c0 + cw], writes=[sk])
            cb = wcb[ci % 2]; cbk = pfx + f"wcb{ci%2}"; ci += 1
            for kc in range(8):
                e = "dve" if kc % 2 == 0 else "pool"
                k.ts1(e, cb[:, kc, 0:cw], stv[:, kc, :], gffn[:, kc:kc + 1], ALU.mult, [sk, pfx + "gffn"], [cbk])
            for j in range(cw // 128):
                f = c0 // 128 + j
                k.dma("pool", dstS[f].rearrange("p (kc n) -> p kc n", kc=8), cb[:, :, j * 128:(j + 1) * 128], reads=[cbk], writes=[nm])

    xb = [A(f"xb{i}", [128, 4, D]) for i in range(2)]
    pb = [A(f"pb{i}", [128, 4, 256]) for i in range(2)]
    junk = A("junk", [128, D], BF16)
    ss = A("ss", [128, 4]); rstd = A("rstd", [128, 4])
    xn = [A(f"xn{i}", [128, D], BF16) for i in range(2)]
    hT = A("hT", [128, 8, TB], BF16)
    ppT = A("ppT", [128, 2, TB], BF16)
    aT = A("aT", [128, NF, TB], BF16)
    NWB = 3
    wgb = [A(f"wgb{i}", [128, 8, 128], BF16) for i in range(NWB)]
    wub = [A(f"wub{i}", [128, 8, 128], BF16) for i in range(NWB)]
    sl = [A(f"sl{i}", [128, TB]) for i in range(2)]
    tmp = [A(f"tmp{i}", [128, 512]) for i in range(2)]
    pT = [P(f"pT{i}", [128, 8, 128], BF16) for i in range(2)]
    pg = [P(f"pg{i}", [128, 512]) for i in range(2)]
    pu = [P(f"pu{i}", [128, 512]) for i in range(2)]
    pd = [P(f"pd{i}", [128, 512]) for i in range(2)]
    nblk = NT // TB
    xv = ins["x"].rearrange("(b s p) d -> b p s d", p=128, s=4)
    pv = ins["p"].rearrange("(b s p) d -> b p s d", p=128, s=4)
    xov = xo.rearrange("(b s p) d -> b p s d", p=128, s=4)

    def load_blk(b):
        k.dma("sp", xb[b % 2][:], xv[b], writes=[pfx + f"xb{b%2}"])
        k.dma("sp", pb[b % 2][:], pv[b], writes=[pfx + f"pb{b%2}"])
    load_blk(0)
    wi = [0]; di = [0]
    for b in range(nblk):
        if b + 1 < nblk:
            load_blk(b + 1)
        X = xb[b % 2]; xk = pfx + f"xb{b%2}"
        rmsnorm_T(nc, k, X, xk, hT, ss, rstd, junk, xn, pT, idb, 1e-6, pfx)
        def load_w(f):
            i = f % NWB
            k.dma("sp", wgb[i][:], wgS[f].rearrange("p (kc n) -> p kc n", kc=8), reads=["wgS"], writes=[pfx + f"wgb{i}"])
            k.dma("sp", wub[i][:], wuS[f].rearrange("p (kc n) -> p kc n", kc=8), reads=["wuS"], writes=[pfx + f"wub{i}"])
        load_w(0); load_w(1)
        for f in range(NF):
            if f + 2 < NF:
                load_w(f + 2)
            i = f % NWB
            j = wi[0]; wi[0] += 1
            for kc in range(8):
                k.mm(pg[j % 2][:], wgb[i][:, kc, :], hT[:, kc, :], kc == 0, kc == 7, [pfx + f"wgb{i}", pfx + "hT"], [pfx + f"pg{j%2}"])
            for kc in range(8):
                k.mm(pu[j % 2][:], wub[i][:, kc, :], hT[:, kc, :], kc == 0, kc == 7, [pfx + f"wub{i}", pfx + "hT"], [pfx + f"pu{j%2}"])
            k.act(sl[j % 2][:], pg[j % 2][:], AF.Silu, [pfx + f"pg{j%2}"], [pfx + f"sl{j%2}"])
            k.tt("dve", aT[:, f, :], pu[j % 2][:], sl[j % 2][:], ALU.mult, [pfx + f"pu{j%2}", pfx + f"sl{j%2}"], [pfx + "aT"])
        for s in range(4):
            for n in range(2):
                j = di[0]; di[0] += 1
                for f in range(NF):
                    k.mm(pd[j % 2][:], aT[:, f, s * 128:(s + 1) * 128], wd[:, f, n * 512:(n + 1) * 512], f == 0, f == NF - 1,
                         [pfx + "aT", pfx + "wd"], [pfx + f"pd{j%2}"])
                k.tt("dve", X[:, s, n * 512:(n + 1) * 512], pd[j % 2][:], X[:, s, n * 512:(n + 1) * 512], ALU.add,
                     [pfx + f"pd{j%2}", xk], [xk])
        rmsnorm_T(nc, k, X, xk, hT, ss, rstd, junk, xn, pT, idb, 1e-6, pfx)
        PB = pb[b % 2]; pbk = pfx + f"pb{b%2}"
        for s in range(4):
            k.cp("pool", xn[s % 2][:, 0:256], PB[:, s, :], [pbk], [pfx + f"xn{s%2}"])
            for kc in range(2):
                k.tr(pT[s % 2][:, kc, :], xn[s % 2][:, kc * 128:(kc + 1) * 128], idb[:], [pfx + f"xn{s%2}", pfx + "idb"], [pfx + f"pT{s%2}"])
            k.cp("act", ppT[:, :, s * 128:(s + 1) * 128], pT[s % 2][:, 0:2, :], [pfx + f"pT{s%2}"], [pfx + "ppT"])
        for s in range(4):
            for n in range(2):
                j = di[0]; di[0] += 1
                for kc in range(8):
                    k.mm(pg[j % 2][:], hT[:, kc, s * 128:(s + 1) * 128], wpg[:, kc, n * 512:(n + 1) * 512], kc == 0, kc == 7,
                         [pfx + "hT", pfx + "wpg"], [pfx + f"pg{j%2}"])
                for kc in range(2):
                    k.mm(pu[j % 2][:], ppT[:, kc, s * 128:(s + 1) * 128], wple[:, kc, n * 512:(n + 1) * 512], kc == 0, kc == 1,
                         [pfx + "ppT", pfx + "wple"], [pfx + f"pu{j%2}"])
                k.act(tmp[j % 2][:], pg[j % 2][:], AF.Sigmoid, [pfx + f"pg{j%2}"], [pfx + f"tmp{j%2}"])
                k.tt("dve", tmp[j % 2][:], pu[j % 2][:], tmp[j % 2][:], ALU.mult, [pfx + f"pu{j%2}", pfx + f"tmp{j%2}"], [pfx + f"tmp{j%2}"])
                k.tt("pool", X[:, s, n * 512:(n + 1) * 512], X[:, s, n * 512:(n + 1) * 512], tmp[j % 2][:], ALU.add,
                     [xk, pfx + f"tmp{j%2}"], [xk])
        if last:
            for s in range(4):
                k.act(junk[:], X[:, s, :], AF.Square, [xk], [pfx + "junk", pfx + "ss"], accum_out=ss[:, s:s + 1])
            k.ts("dve", rstd[:], ss[:], 1.0 / D, 1e-6, ALU.mult, ALU.add, [pfx + "ss"], [pfx + "rstd"])
            k.act(rstd[:], rstd[:], AF.Sqrt, [pfx + "rstd"], [pfx + "rstd"])
            k.emit("dve", lambda: nc.vector.reciprocal(rstd[:], rstd[:]), [pfx + "rstd"], [pfx + "rstd"])
            for s in range(4):
                k.emit("dve", lambda: nc.vector.scalar_tensor_tensor(X[:, s, :], X[:, s, :], rstd[:, s:s + 1], gfin[:], ALU.mult, ALU.mult),
                       [xk, pfx + "rstd", pfx + "gfin"], [xk])
        k.dma("pool", xov[b], X[:], reads=[xk])

CPAR = [e for e in C_IN[1:12]]
CCON = [e for e in C_IN[12:]]


def build_fused(S, depth):
    nc = bass.Bass("TRN2", target_bir_lowering=False)
    I = lambda nm, shp, dt=F32: nc.dram_tensor(nm, list(shp), dt, kind="ExternalInput").ap()
    Sc = lambda nm, shp, dt: nc.dram_tensor(nm, list(shp), dt).ap()
    x = I("x", [S, D]); p = I("p", [depth, S, 256])
    w_in = I("w_in", [depth, D, INC]); gam = I("gam", [depth, 128, 8]); ident = I("ident", [128, 128])
    G = I("G", [8, 128, 256]); b31 = I("b31", [128, 8]); neg = I("neg", [128, 256])
    lamv = I("lamv", [depth, 128, 4, 64]); sgain = I("sgain", [depth, 128, 128])
    cpar = {nm: I("c_" + nm, [depth, 2] + list(shp)) for nm, shp, dt in CPAR}
    ccon = {nm: (ident if nm == "ident" else I("k_" + nm, shp)) for nm, shp, dt in CCON}
    woa = I("woa", [depth, 512, D]); wor = I("wor", [depth, 512, D]); wo = I("wo", [depth, D, D])
    wg = I("wg", [depth, D, DFF]); wu = I("wu", [depth, D, DFF]); wd = I("wd", [depth, DFF, D])
    wple = I("wple", [depth, 256, D]); wpg = I("wpg", [depth, D, D])
    gffn = I("gffn", [depth, 128, 8]); gple = I("gple", [depth, 128, 8]); gfin = I("gfin", [128, D])
    out = nc.dram_tensor("out", [S, D], F32, kind="ExternalOutput").ap()
    qT = Sc("s_qT", [512, S], BF16); kT = Sc("s_kT", [512, S], BF16); vS = Sc("s_v", [S, 512], BF16)
    zrT = Sc("s_zrT", [1792, S], F32); gT = Sc("s_gT", [2048, S], BF16)
    oaT = Sc("s_oaT", [512, S], BF16); yT = Sc("s_yT", [512, S], BF16)
    xmid = Sc("s_xmid", [S, D], F32); x1 = Sc("s_x1", [S, D], F32)
    wgS = Sc("s_wgS", [NF, 128, 1024], BF16); wuS = Sc("s_wuS", [NF, 128, 1024], BF16)
    k = KB(nc)
    for i in range(depth):
        last = (i == depth - 1)
        xin = x if i == 0 else x1
        lam_init = 0.8 - 0.6 * math.exp(-0.3 * i)
        with Alloc(nc) as al:
            emit_A(nc, k, al, S, xin, w_in[i], gam[i], ident, qT, kT, vS, zrT, gT, pfx=f"A{i}")
        k.barrier()
        for hh in range(2):
            with Alloc(nc) as al:
                insB = dict(qT=qT[hh * 256:(hh + 1) * 256, :], kT=kT[hh * 256:(hh + 1) * 256, :],
                            v=vS[:, hh * 256:(hh + 1) * 256], G=G[hh * 4:(hh + 1) * 4], b31=b31[:, hh * 4:(hh + 1) * 4],
                            neg=neg, lamv=lamv[i], sgain=sgain[i], ident=ident)
                emit_B(nc, k, al, S, lam_init, insB, oaT[hh * 256:(hh + 1) * 256, :], pfx=f"B{i}{hh}")
            k.barrier()
        for hh in range(2):
            with Alloc(nc) as al:
                insC = {nm: cpar[nm][i, hh] for nm, shp, dt in CPAR}
                insC.update(ccon)
                insC["zc"] = zrT
                zrows = [hh * 256, hh * 256 + 128, 512 + hh * 256, 512 + hh * 256 + 128,
                         1024 + hh * 256, 1024 + hh * 256 + 128, 1536, 1664]
                emit_C(nc, k, al, S, insC, yT[hh * 256:(hh + 1) * 256, :], pfx=f"C{i}{hh}", zrows=zrows)
            k.barrier()
        with Alloc(nc) as al:
            emit_D1(nc, k, al, S, dict(x=xin, oaT=oaT, yrT=yT, gT=gT, woa=woa[i], wor=wor[i], wo=wo[i]), xmid, pfx=f"E{i}")
        k.barrier()
        with Alloc(nc) as al:
            emit_D2(nc, k, al, S, last, dict(x=xmid, p=p[i], wg=wg[i], wu=wu[i], wd=wd[i], wple=wple[i], wpg=wpg[i],
                                             gffn=gffn[i], gple=gple[i], gfin=gfin, ident=ident),
                    out if last else x1, wgS, wuS, pfx=f"F{i}")
        k.barrier()
    k.finish("sp")
    return nc


def _pp(v):
    return np.ascontiguousarray(np.asarray(v, np.float32).reshape(-1, 128).T)


def kernel(x, p, rel_bias, norm_mix, w_in, lam_q1, lam_k1, lam_q2, lam_k2, attn_subln,
           rwkv_mu, rwkv_w0, rwkv_w2, rwkv_a0, rwkv_a2, rwkv_g2, rwkv_kk, rwkv_ka, rwkv_rk,
           rwkv_lnx_w, rwkv_lnx_b, w_out_attn, w_out_rwkv, w_out, norm_ffn, w_ffn_gate,
           w_ffn_up, w_ffn_down, norm_ple, w_ple, w_ple_gate, norm_final):
    f32 = np.float32
    A_ = lambda a: np.ascontiguousarray(np.asarray(a, f32))
    x = A_(x); p = A_(p); rel_bias = A_(rel_bias)
    B_, S_, D_ = x.shape
    depth = int(np.asarray(w_in).shape[0])
    bc = lambda v, shape: np.ascontiguousarray(np.broadcast_to(v, shape))
    tidx = toeplitz_idx()
    com = dict(
        w_in=A_(w_in), gam=np.stack([_pp(norm_mix[i]) for i in range(depth)]), ident=np.eye(128, dtype=f32),
        G=np.ascontiguousarray(np.stack([rel_bias[tidx, h, c] for h in range(4) for c in range(2)])),
        b31=bc(np.stack([rel_bias[31, h, c] for h in range(4) for c in range(2)])[None, :], (128, 8)),
        neg=neg_mask(),
        lamv=np.stack([bc(np.stack([A_(lam_q1[i]), A_(lam_k1[i]), A_(lam_q2[i]), A_(lam_k2[i])])[None], (128, 4, 64)) for i in range(depth)]),
        sgain=np.stack([bc(A_(attn_subln[i])[None], (128, 128)) for i in range(depth)]),
        woa=A_(w_out_attn), wor=A_(w_out_rwkv), wo=A_(w_out), wg=A_(w_ffn_gate), wu=A_(w_ffn_up), wd=A_(w_ffn_down),
        wple=A_(w_ple), wpg=A_(w_ple_gate),
        gffn=np.stack([_pp(norm_ffn[i]) for i in range(depth)]), gple=np.stack([_pp(norm_ple[i]) for i in range(depth)]),
        gfin=bc(A_(norm_final)[None], (128, D_)),
    )
    cC = consts_C()
    for nm, shp, dt in CCON:
        if nm != "ident":
            com["k_" + nm] = cC[nm]
    blk = (np.arange(128)[:, None] // 64 == np.arange(128)[None, :] // 64)
    cp = {nm: [] for nm, shp, dt in CPAR}
    for i in range(depth):
        row = {nm: [] for nm in cp}
        for hh in range(2):
            sl = np.arange(hh * 256, (hh + 1) * 256)
            idx = np.concatenate([sl, 512 + sl, 1024 + sl, np.arange(1536, 1792)])
            rkf = A_(rwkv_rk[i]).reshape(-1)[sl]
            rkb = np.stack([np.where(blk, rkf[hp * 128:(hp + 1) * 128][:, None], f32(0)) for hp in range(2)], 1).astype(f32)
            d = dict(mu=_pp(A_(rwkv_mu[i])[idx]), w2c=A_(rwkv_w2[i])[:, sl], w0c=A_(rwkv_w0[i])[None, sl],
                     a2c=A_(rwkv_a2[i])[:, sl], a0c=_pp(A_(rwkv_a0[i])[sl]), g2c=A_(rwkv_g2[i])[:, sl],
                     kkw=_pp(A_(rwkv_kk[i])[sl]), ka=_pp(A_(rwkv_ka[i])[sl]), rkb=rkb,
                     lnw=_pp(A_(rwkv_lnx_w[i])[sl]), lnb=_pp(A_(rwkv_lnx_b[i])[sl]))
            for nm in row:
                row[nm].append(np.ascontiguousarray(d[nm]))
        for nm in cp:
            cp[nm].append(np.stack(row[nm]))
    for nm in cp:
        com["c_" + nm] = np.ascontiguousarray(np.stack(cp[nm]))
    nc = build_fused(S_, depth)
    in_maps = []
    for b in range(B_):
        d = dict(com)
        d["x"] = np.ascontiguousarray(x[b])
        d["p"] = np.ascontiguousarray(p[:, b])
        in_maps.append(d)
    res = run_bass_kernel_spmd(nc, in_maps, core_ids=list(range(B_)))
    return np.stack([np.asarray(res.results[b]["out"], f32) for b in range(B_)])
```

```python
import math
import numpy as np
import concourse.bass as bass
import concourse.mybir as mybir
from concourse.bass_utils import run_bass_kernel_spmd

F32 = mybir.dt.float32
BF16 = mybir.dt.bfloat16
AF = mybir.ActivationFunctionType
ALU = mybir.AluOpType
AX = mybir.AxisListType

EPOCH = 30000
DMA_RING = 8
DMA_EPOCH = 700


class KB:
    def __init__(self, nc):
        self.nc = nc
        self.eng = {"pe": nc.tensor, "act": nc.scalar, "dve": nc.vector,
                    "pool": nc.gpsimd, "sp": nc.sync}
        self.sem = {}
        self.cnt = {}
        self.nsem = 0
        for e in ("pe", "act", "dve", "pool"):
            self._new_sem(e)
        self.seen = {e: {} for e in self.eng}
        self.semobj = {}
        self.lastw = {}
        self.readers = {}
        self.dq = {}
        for q in ("sp", "pool", "act"):
            self.dq[q] = {"n": 0, "sems": None, "vals": None, "tok": [None] * DMA_RING}
        self.ninst = {e: 0 for e in self.eng}
        self.colltoks = []

    def _alloc_sem(self, name):
        self.nsem += 1
        s = self.nc.alloc_semaphore(f"{name}_{self.nsem}")
        return s

    def _new_sem(self, e):
        self.sem[e] = self._alloc_sem("s" + e)
        self.cnt[e] = 0

    def _wait(self, e, tok):
        if tok is None:
            return
        s, v, pe = tok
        sid = id(s)
        if self.seen[e].get(sid, 0) >= v:
            return
        self.eng[e].wait_ge(s, v)
        self.seen[e][sid] = v

    def _deps(self, e, reads, writes):
        toks = []
        for k in list(reads) + list(writes):
            t = self.lastw.get(k)
            if t is not None:
                toks.append(t)
        for k in writes:
            toks.extend(self.readers.get(k, ()))
        for t in toks:
            if e == "pe" and t[2] == "pe":
                continue
            self._wait(e, t)

    def _record(self, tok, reads, writes):
        for k in writes:
            self.lastw[k] = tok
            self.readers[k] = []
        for k in reads:
            self.readers.setdefault(k, []).append(tok)

    def emit(self, e, fn, reads=(), writes=()):
        self._deps(e, reads, writes)
        if self.cnt[e] >= EPOCH:
            self._new_sem(e)
        ins = fn()
        self.cnt[e] += 1
        ins.then_inc(self.sem[e], 1)
        tok = (self.sem[e], self.cnt[e], e)
        self._record(tok, reads, writes)
        self.ninst[e] += 1
        return tok

    def dma(self, q, out, in_, reads=(), writes=(), **kw):
        d = self.dq[q]
        n = d["n"]
        slot = n % DMA_RING
        if n % (DMA_RING * DMA_EPOCH) == 0:
            d["sems"] = [self._alloc_sem("d" + q) for _ in range(DMA_RING)]
            d["vals"] = [0] * DMA_RING
            for t in d["tok"]:
                self._wait(q, t)
        self._wait(q, d["tok"][slot])
        self._deps(q, reads, writes)
        ins = self.eng[q].dma_start(out=out, in_=in_, **kw)
        d["vals"][slot] += 16
        ins.then_inc(d["sems"][slot], 16)
        tok = (d["sems"][slot], d["vals"][slot], "dma_" + q)
        d["tok"][slot] = tok
        d["n"] = n + 1
        self._record(tok, reads, writes)
        self.ninst[q] += 1
        return tok

    def finish(self, q="sp"):
        for qq, d in self.dq.items():
            for t in d["tok"]:
                self._wait(q, t)

    def mm(self, out, lhsT, rhs, start, stop, reads, writes, **kw):
        return self.emit("pe", lambda: self.nc.tensor.matmul(out, lhsT, rhs, start=start, stop=stop, **kw),
                         reads, writes)

    def tr(self, out, in_, ident, reads, writes):
        return self.emit("pe", lambda: self.nc.tensor.transpose(out, in_, ident), reads, writes)

    def act(self, out, in_, func, reads, writes, **kw):
        return self.emit("act", lambda: self.nc.scalar.activation(out, in_, func, **kw), reads, writes)

    def tt(self, e, out, a, b, op, reads, writes):
        return self.emit(e, lambda: self.eng[e].tensor_tensor(out, a, b, op), reads, writes)

    def ts(self, e, out, a, s1, s2, op0, op1, reads, writes, **kw):
        return self.emit(e, lambda: self.eng[e].tensor_scalar(out, a, s1, s2, op0, op1, **kw), reads, writes)

    def ts1(self, e, out, a, s1, op, reads, writes):
        return self.emit(e, lambda: self.eng[e].tensor_single_scalar(out, a, s1, op), reads, writes)

    def cp(self, e, out, a, reads, writes):
        if e == "act":
            return self.emit(e, lambda: self.nc.scalar.copy(out, a), reads, writes)
        return self.emit(e, lambda: self.eng[e].tensor_copy(out, a), reads, writes)

    def memset(self, e, ap, val, writes):
        return self.emit(e, lambda: self.eng[e].memset(ap, val), (), writes)


import contextlib


class Alloc:
    def __init__(self, nc):
        self.nc = nc
        self.st = contextlib.ExitStack()
    def __enter__(self):
        self.st.__enter__()
        return self
    def __exit__(self, *a):
        return self.st.__exit__(*a)
    def sb(self, name, shape, dt):
        return self.st.enter_context(self.nc.sbuf_tensor(name, list(shape), dt))
    def ps(self, name, shape, dt):
        return self.st.enter_context(self.nc.psum_tensor(name, list(shape), dt))


def kb_barrier(self):
    toks = []
    for e in ("pe", "act", "dve", "pool"):
        if self.cnt[e] > 0:
            toks.append((self.sem[e], self.cnt[e], e))
    for q, d in self.dq.items():
        for t in d["tok"]:
            if t is not None:
                toks.append(t)
    toks.extend(self.colltoks)
    self.colltoks = []
    for e in ("pe", "act", "dve", "pool", "sp"):
        for t in toks:
            self._wait(e, t)
    self.lastw.clear()
    self.readers.clear()

KB.barrier = kb_barrier


def kb_coll(self, kind, src, dst, groups, reads=(), writes=()):
    q = "pool"
    self._deps(q, reads, writes)
    sem = self._alloc_sem("cc")
    ins = self.nc.gpsimd.collective_compute(kind, ALU.bypass, replica_groups=groups, ins=[src], outs=[dst])
    ins.then_inc(sem, 1)
    tok = (sem, 1, "coll")
    self._record(tok, reads, writes)
    self.colltoks.append(tok)
    return tok

KB.coll = kb_coll

D = 1024
INC = 5376
TB = 512

def build_A(NT):
    nc = bass.Bass("TRN2", target_bir_lowering=False)
    x = nc.dram_tensor("x", [NT, D], F32, kind="ExternalInput").ap()
    w = nc.dram_tensor("w_in", [D, INC], F32, kind="ExternalInput").ap()
    gam = nc.dram_tensor("gam", [128, 8], F32, kind="ExternalInput").ap()
    ident = nc.dram_tensor("ident", [128, 128], F32, kind="ExternalInput").ap()
    qT = nc.dram_tensor("qT", [512, NT], BF16, kind="ExternalOutput").ap()
    kT = nc.dram_tensor("kT", [512, NT], BF16, kind="ExternalOutput").ap()
    vo = nc.dram_tensor("v", [NT, 512], BF16, kind="ExternalOutput").ap()
    zrT = nc.dram_tensor("zrT", [1792, NT], F32, kind="ExternalOutput").ap()
    gT = nc.dram_tensor("gT", [2048, NT], BF16, kind="ExternalOutput").ap()
    k = KB(nc)
    with Alloc(nc) as al:
        emit_A(nc, k, al, NT, x, w, gam, ident, qT, kT, vo, zrT, gT)
        k.finish("sp")
    return nc


def emit_A(nc, k, al, NT, x, w, gam, ident, qT, kT, vo, zrT, gT, pfx="A", vsplit=False, ocb=None):
    nblk = NT // TB
    wb = al.sb(pfx + "wb", [128, 8, INC], BF16)
    wst = [al.sb(pfx + f"wst{i}", [128, 8, 512], F32) for i in range(2)]
    gt = al.sb(pfx + "gt", [128, 8], F32)
    idf = al.sb(pfx + "idf", [128, 128], F32)
    idb = al.sb(pfx + "idb", [128, 128], BF16)
    xt = [al.sb(pfx + f"xt{i}", [128, 4, D], F32) for i in range(2)]
    junk = al.sb(pfx + "junk", [128, D], BF16)
    ss = al.sb(pfx + "ss", [128, 4], F32)
    rstd = al.sb(pfx + "rstd", [128, 4], F32)
    xn = [al.sb(pfx + f"xn{i}", [128, D], BF16) for i in range(2)]
    hT = al.sb(pfx + "hT", [128, 8, TB], BF16)
    NOB = 4
    ob16 = [al.sb(pfx + f"ob16_{i}", [128, 512], BF16) for i in range(NOB)]
    ob32 = [al.sb(pfx + f"ob32_{i}", [128, 512], F32) for i in range(NOB)]
    pT = [al.ps(pfx + f"pT{i}", [128, 8, 128], BF16) for i in range(2)]
    NPZ = 4
    pz = [al.ps(pfx + f"pz{i}", [128, 512], F32) for i in range(NPZ)]

    k.dma("sp", gt[:], gam, writes=["gt"])
    k.dma("sp", idf[:], ident, writes=["idf"])
    k.cp("dve", idb[:], idf[:], ["idf"], ["idb"])
    wv = w.rearrange("(kc p) n -> p kc n", p=128)
    xv = x.rearrange("(b s p) d -> b p s d", p=128, s=4)

    def load_x(b):
        k.dma("sp", xt[b % 2][:], xv[b], writes=[f"xt{b%2}"])

    load_x(0)
    ncc = INC // 512 + (1 if INC % 512 else 0)
    for c in range(ncc):
        c0 = c * 512
        cw = min(512, INC - c0)
        st = wst[c % 2]
        k.dma("sp", st[:, :, :cw], wv[:, :, c0:c0 + cw], writes=[f"wst{c%2}"])
        for kc in range(8):
            e = "dve" if kc % 2 == 0 else "pool"
            k.ts1(e, wb[:, kc, c0:c0 + cw], st[:, kc, :cw], gt[:, kc:kc + 1], ALU.mult,
                  [f"wst{c%2}", "gt"], ["wb"])

    evn = [0]
    for b in range(nblk):
        if b + 1 < nblk:
            load_x(b + 1)
        xb = xt[b % 2]
        xk = f"xt{b%2}"
        for s in range(4):
            k.act(junk[:], xb[:, s, :], AF.Square, [xk], ["junk", "ss"], accum_out=ss[:, s:s + 1])
        k.ts("dve", rstd[:], ss[:], 1.0 / D, 1e-6, ALU.mult, ALU.add, ["ss"], ["rstd"])
        k.act(rstd[:], rstd[:], AF.Sqrt, ["rstd"], ["rstd"])
        k.emit("dve", lambda: nc.vector.reciprocal(rstd[:], rstd[:]), ["rstd"], ["rstd"])
        for s in range(4):
            xnk = f"xn{s%2}"
            k.ts1("dve", xn[s % 2][:], xb[:, s, :], rstd[:, s:s + 1], ALU.mult, [xk, "rstd"], [xnk])
            pk = f"pT{s%2}"
            for kc in range(8):
                k.tr(pT[s % 2][:, kc, :], xn[s % 2][:, kc * 128:(kc + 1) * 128], idb[:], [xnk, "idb"], [pk])
            k.cp("act", hT[:, :, s * 128:(s + 1) * 128], pT[s % 2][:], [pk], ["hT"])
        t0 = b * TB
        chunks = []
        for cc in range(4):
            chunks.append(("q", cc, cc * 128))
        for cc in range(4):
            chunks.append(("k", cc, 512 + cc * 128))
        for cc in range(14):
            chunks.append(("r", cc, 1536 + cc * 128))
        for cc in range(16):
            chunks.append(("g", cc, 1536 + 1792 + cc * 128))
        for (kind, cc, col) in chunks:
            i = evn[0]; evn[0] += 1
            pzi = pz[i % NPZ]; pk = f"pz{i%NPZ}"
            for kc in range(8):
                k.mm(pzi[:], wb[:, kc, col:col + 128], hT[:, kc, :], kc == 0, kc == 7, ["wb", "hT"], [pk])
            oi = i % NOB
            if kind == "q":
                k.act(ob16[oi][:], pzi[:], AF.Copy, [pk], [f"ob16_{oi}"], scale=0.125)
                if ocb:
                    k.dma("pool", ocb["q"](cc, t0), ob16[oi][:], reads=[f"ob16_{oi}"])
                else:
                    k.dma("pool", qT[cc * 128:(cc + 1) * 128, t0:t0 + TB], ob16[oi][:], reads=[f"ob16_{oi}"])
            elif kind == "k":
                k.cp("dve", ob16[oi][:], pzi[:], [pk], [f"ob16_{oi}"])
                if ocb:
                    k.dma("pool", ocb["k"](cc, t0), ob16[oi][:], reads=[f"ob16_{oi}"])
                else:
                    k.dma("pool", kT[cc * 128:(cc + 1) * 128, t0:t0 + TB], ob16[oi][:], reads=[f"ob16_{oi}"])
            elif kind == "r":
                e = "dve" if cc % 2 == 0 else "act"
                k.cp(e, ob32[oi][:], pzi[:], [pk], [f"ob32_{oi}"])
                if ocb:
                    for hq in range(2):
                        k.dma("pool", ocb["z"](cc, hq, t0), ob32[oi][hq * 64:(hq + 1) * 64, :], reads=[f"ob32_{oi}"])
                else:
                    k.dma("pool", zrT[cc * 128:(cc + 1) * 128, t0:t0 + TB], ob32[oi][:], reads=[f"ob32_{oi}"])
            else:
                k.act(ob16[oi][:], pzi[:], AF.Sigmoid, [pk], [f"ob16_{oi}"])
                k.dma("pool", gT[cc * 128:(cc + 1) * 128, t0:t0 + TB], ob16[oi][:], reads=[f"ob16_{oi}"])
        for s in range(4):
            i = evn[0]; evn[0] += 1
            pzi = pz[i % NPZ]; pk = f"pz{i%NPZ}"
            for kc in range(8):
                k.mm(pzi[:], hT[:, kc, s * 128:(s + 1) * 128], wb[:, kc, 1024:1536], kc == 0, kc == 7,
                     ["wb", "hT"], [pk])
            oi = i % NOB
            k.cp("dve", ob16[oi][:], pzi[:], [pk], [f"ob16_{oi}"])
            if ocb:
                for g_ in range(2):
                    k.dma("pool", ocb["v"](g_, t0 + s * 128), ob16[oi][:, g_ * 256:(g_ + 1) * 256], reads=[f"ob16_{oi}"])
            elif vsplit:
                for g_ in range(2):
                    k.dma("pool", vo[g_, t0 + s * 128:t0 + (s + 1) * 128, :], ob16[oi][:, g_ * 256:(g_ + 1) * 256], reads=[f"ob16_{oi}"])
            else:
                k.dma("pool", vo[t0 + s * 128:t0 + (s + 1) * 128, :], ob16[oi][:], reads=[f"ob16_{oi}"])


NEGM = -30000.0

def t5_bucket_np(n):
    n = np.maximum(n, 0)
    nf = np.maximum(n, 16).astype(np.float32)
    large = 16 + (np.log(nf / np.float32(16)) / np.float32(math.log(128 / 16)) * np.float32(16)).astype(np.int32)
    large = np.minimum(large, 31)
    return np.where(n < 16, n, large)

def toeplitz_idx():
    j = np.arange(128)[:, None]
    i = np.arange(256)[None, :]
    return t5_bucket_np(i - j)

def neg_mask():
    j = np.arange(128)[:, None]
    i = np.arange(256)[None, :]
    return np.where(i >= j, 0.0, NEGM).astype(np.float32)

def build_B(S, lam_init):
    nc = bass.Bass("TRN2", target_bir_lowering=False)
    ins = dict(
        qT=nc.dram_tensor("qT", [256, S], BF16, kind="ExternalInput").ap(),
        kT=nc.dram_tensor("kT", [256, S], BF16, kind="ExternalInput").ap(),
        v=nc.dram_tensor("v", [S, 256], BF16, kind="ExternalInput").ap(),
        G=nc.dram_tensor("G", [4, 128, 256], F32, kind="ExternalInput").ap(),
        b31=nc.dram_tensor("b31", [128, 4], F32, kind="ExternalInput").ap(),
        neg=nc.dram_tensor("neg", [128, 256], F32, kind="ExternalInput").ap(),
        lamv=nc.dram_tensor("lamv", [128, 4, 64], F32, kind="ExternalInput").ap(),
        sgain=nc.dram_tensor("sgain", [128, 128], F32, kind="ExternalInput").ap(),
        ident=nc.dram_tensor("ident", [128, 128], F32, kind="ExternalInput").ap(),
    )
    oaT = nc.dram_tensor("oaT", [256, S], BF16, kind="ExternalOutput").ap()
    dbg = nc.dram_tensor("dbg", [4, 128, 512], F32, kind="ExternalOutput").ap()
    k = KB(nc)
    with Alloc(nc) as al:
        emit_B(nc, k, al, S, lam_init, ins, oaT, dbg=dbg)
        k.finish("sp")
    return nc


def emit_B(nc, k, al, S, lam_init, ins, oaT, pfx="B", dbg=None):
    NJ = S // 128
    NG = S // 512
    VW = 136
    qs = al.sb(pfx + "qs", [128, 2, S], BF16)
    ks = al.sb(pfx + "ks", [128, 2, S], BF16)
    vaug = al.sb(pfx + "vaug", [128, NJ, 2, VW], BF16)
    Gs = al.sb(pfx + "Gs", [128, 4, 256], F32)
    negs = al.sb(pfx + "negs", [128, 256], F32)
    b31s = al.sb(pfx + "b31s", [128, 4], F32)
    Tb = al.sb(pfx + "Tb", [128, 4, 256], BF16)
    lamt = al.sb(pfx + "lamt", [128, 4, 64], F32)
    lprod = al.sb(pfx + "lprod", [128, 2, 64], F32)
    lsum = al.sb(pfx + "lsum", [128, 2], F32)
    lam = al.sb(pfx + "lam", [128, 1], F32)
    sg = al.sb(pfx + "sg", [128, 128], F32)
    idf = al.sb(pfx + "idf", [128, 128], F32)
    idb = al.sb(pfx + "idb", [128, 128], BF16)
    NB = 3
    PT = [al.sb(pfx + f"PT{i}", [128, 512], BF16) for i in range(NB)]
    rr = al.sb(pfx + "rr", [128, 2], F32)
    rl = al.sb(pfx + "rl", [128, 1], F32)
    t1 = al.sb(pfx + "t1", [128, 128], F32)
    ot = al.sb(pfx + "ot", [128, 128], F32)
    junk = al.sb(pfx + "junk", [128, 128], F32)
    ms = al.sb(pfx + "ms", [128, 1], F32)
    onb = al.sb(pfx + "onb", [128, 128], BF16)
    oT = [al.sb(pfx + f"oT{i}", [128, 512], BF16) for i in range(2)]
    ST = [al.ps(pfx + f"ST{i}", [128, 512], F32) for i in range(NB)]
    accb = al.ps(pfx + "accb", [128, 4, 512], F32)
    pTr = al.ps(pfx + "pTr", [128, 128], BF16)

    for hl in range(2):
        w = min(S // 2 if ins.get("qfn") else S, 2048)
        for part in range(S // w):
            qfn = ins.get("qfn") or (lambda r0, r1, a, b_: ins["qT"][r0:r1, a:b_])
            kfn = ins.get("kfn") or (lambda r0, r1, a, b_: ins["kT"][r0:r1, a:b_])
            k.dma("sp", qs[:, hl, part * w:(part + 1) * w], qfn(hl * 128, (hl + 1) * 128, part * w, (part + 1) * w), writes=["qs"])
            k.dma("sp", ks[:, hl, part * w:(part + 1) * w], kfn(hl * 128, (hl + 1) * 128, part * w, (part + 1) * w), writes=["ks"])
    JB = min(8, (S // 2) // 128) if ins.get("vfn") else 8
    for j0 in range(0, NJ, JB):
        j1 = min(NJ, j0 + JB)
        vsrc = ins["vfn"](j0 * 128, j1 * 128) if ins.get("vfn") else ins["v"][j0 * 128:j1 * 128, :]
        vv = vsrc.rearrange("(J p) (h d) -> p J h d", p=128, h=2)
        for h2 in range(2):
            k.dma("sp", vaug[:, j0:j1, h2, 0:128], vv[:, :, h2, :], writes=["vaug"])
    k.memset("pool", vaug[:, :, :, 128:129], 1.0, ["vaug"])
    k.dma("sp", Gs[:], ins["G"].rearrange("a p i -> p a i"), writes=["Gs"])
    k.dma("sp", negs[:], ins["neg"], writes=["negs"])
    k.dma("sp", b31s[:], ins["b31"], writes=["b31s"])
    k.dma("sp", lamt[:], ins["lamv"], writes=["lamt"])
    k.dma("sp", sg[:], ins["sgain"], writes=["sg"])
    k.dma("sp", idf[:], ins["ident"], writes=["idf"])
    k.cp("dve", idb[:], idf[:], ["idf"], ["idb"])
    for hc in range(4):
        k.ts1("dve", Gs[:, hc, :], Gs[:, hc, :], b31s[:, hc:hc + 1], ALU.subtract, ["Gs", "b31s"], ["Gs"])
        k.tt("dve", Tb[:, hc, :], Gs[:, hc, :], negs[:], ALU.add, ["Gs", "negs"], ["Tb"])
    k.tt("dve", lprod[:, 0, :], lamt[:, 0, :], lamt[:, 1, :], ALU.mult, ["lamt"], ["lprod"])
    k.tt("dve", lprod[:, 1, :], lamt[:, 2, :], lamt[:, 3, :], ALU.mult, ["lamt"], ["lprod"])
    k.emit("dve", lambda: nc.vector.reduce_sum(lsum[:], lprod[:], AX.X), ["lprod"], ["lsum"])
    k.act(lsum[:], lsum[:], AF.Exp, ["lsum"], ["lsum"])
    k.tt("dve", lam[:], lsum[:, 0:1], lsum[:, 1:2], ALU.subtract, ["lsum"], ["lam"])
    k.ts1("dve", lam[:], lam[:], float(lam_init), ALU.add, ["lam"], ["lam"])
    k.ts1("dve", sg[:], sg[:], float(1.0 - lam_init), ALU.mult, ["sg"], ["sg"])

    def acc(c, I):
        return accb[:, I, c * 256:c * 256 + 129]

    for hl in range(2):
        steps = [(g, J, c) for g in range(NG) for J in range(4 * g + 4) for c in range(2)]
        N = len(steps)

        def qk(n):
            g, J, c = steps[n]
            buf = n % NB
            r = J - 4 * g
            c0 = max(0, r) * 128
            a = 128 * r
            has_t = r >= -1
            k.mm(ST[buf][:, c0:512], ks[c * 64:(c + 1) * 64, hl, J * 128:(J + 1) * 128],
                 qs[c * 64:(c + 1) * 64, hl, g * 512 + c0:(g + 1) * 512], True, not has_t,
                 ["ks", "qs"], [f"ST{buf}"])
            if has_t:
                tc0 = max(a, 0); tc1 = min(a + 256, 512)
                i0 = tc0 - a
                k.mm(ST[buf][:, tc0:tc1], idb[:], Tb[:, hl * 2 + c, i0:i0 + (tc1 - tc0)], False, True,
                     ["idb", "Tb"], [f"ST{buf}"])

        def ex_pv(n):
            g, J, c = steps[n]
            buf = n % NB
            r = J - 4 * g
            c0 = max(0, r) * 128
            k.act(PT[buf][:, c0:512], ST[buf][:, c0:512], AF.Exp, [f"ST{buf}"], [f"PT{buf}"])
            for I in range(4):
                Ia = 4 * g + I
                if Ia >= J:
                    k.mm(acc(c, I), PT[buf][:, I * 128:(I + 1) * 128], vaug[:, J, hl, 0:129], (J == 0 and c == 0), J == Ia,
                         [f"PT{buf}", "vaug"], [f"acc{c}_{I}"], skip_group_check=True)

        def fin(g):
            ob = oT[g % 2]; obk = f"oT{g%2}"
            for I in range(4):
                ak = [f"acc0_{I}", f"acc1_{I}"]
                k.emit("dve", lambda: nc.vector.reciprocal(rr[:, 0:1], accb[:, I, 128:129]), [ak[0]], ["rr"])
                k.emit("dve", lambda: nc.vector.reciprocal(rr[:, 1:2], accb[:, I, 256 + 128:256 + 129]), [ak[1]], ["rr"])
                k.tt("dve", rl[:], rr[:, 1:2], lam[:], ALU.mult, ["rr", "lam"], ["rl"])
                k.ts1("dve", t1[:], accb[:, I, 256:256 + 128], rl[:, 0:1], ALU.mult, [ak[1], "rl"], ["t1"])
                k.emit("dve", lambda: nc.vector.scalar_tensor_tensor(ot[:], accb[:, I, 0:128], rr[:, 0:1], t1[:], ALU.mult, ALU.subtract),
                       [ak[0], "rr", "t1"], ["ot"])
                k.act(junk[:], ot[:], AF.Square, ["ot"], ["junk", "ms"], accum_out=ms[:])
                k.ts("dve", ms[:], ms[:], 1.0 / 128, 1e-5, ALU.mult, ALU.add, ["ms"], ["ms"])
                k.act(ms[:], ms[:], AF.Sqrt, ["ms"], ["ms"])
                k.emit("dve", lambda: nc.vector.reciprocal(ms[:], ms[:]), ["ms"], ["ms"])
                k.emit("dve", lambda: nc.vector.scalar_tensor_tensor(onb[:], ot[:], ms[:, 0:1], sg[:], ALU.mult, ALU.mult),
                       ["ot", "ms", "sg"], ["onb"])
                if dbg is not None and hl == 0 and g == 0 and I == 0:
                    k.cp("dve", dt[:, 3, :], accb[:, 0, :], ["acc0_0", "acc1_0"], ["dbgt"])
                    k.cp("dve", dt[:, 2, 0:128], ot[:], ["ot"], ["dbgt"])
                    k.cp("dve", dt[:, 2, 128:256], onb[:], ["onb"], ["dbgt"])
                    k.dma("pool", dbg.rearrange("a p n -> p a n"), dt[:], reads=["dbgt"])
                k.tr(pTr[:], onb[:], idb[:], ["onb", "idb"], ["pTr"])
                k.cp("act", ob[:, I * 128:(I + 1) * 128], pTr[:], ["pTr"], [obk])
            ofn = ins.get("ofn") or (lambda r0, r1, a, b_: oaT[r0:r1, a:b_])
            k.dma("pool", ofn(hl * 128, (hl + 1) * 128, g * 512, (g + 1) * 512), ob[:], reads=[obk])

        LA = 2
        if dbg is not None and hl == 0:
            dt = al.sb(pfx + "dbgt", [128, 4, 512], F32)
            k.memset("pool", dt[:], 0.0, ["dbgt"])
            qk(0)
            k.cp("dve", dt[:, 0, :], ST[0][:], ["ST0"], ["dbgt"])
            k.act(dt[:, 1, :], ST[0][:], AF.Exp, ["ST0"], ["dbgt"])
            k.cp("dve", dt[:, 2, 0:256], Tb[:, 0, :], ["Tb"], ["dbgt"])
            k.cp("dve", dt[:, 2, 256:257], lam[:], ["lam"], ["dbgt"])
        for n in range(min(LA, N)):
            qk(n)
        for n in range(N):
            if n + LA < N:
                qk(n + LA)
            ex_pv(n)
            g, J, c = steps[n]
            if J == 4 * g + 3 and c == 1:
                fin(g)


CDEC = math.exp(-0.5)
GN_EPS = 64e-5

def consts_C():
    s = np.arange(128)[:, None]; t = np.arange(128)[None, :]
    c = dict(
        ident=np.eye(128, dtype=np.float32),
        triI=np.where(s <= t, -CDEC, 0.0).astype(np.float32),
        triE=np.where(s < t, -CDEC, 0.0).astype(np.float32),
        triR=np.where(s > t, -CDEC, 0.0).astype(np.float32),
        mstrict=(t > s).astype(np.float32),
        mask3=np.concatenate([(t >= s), (t > s), (t >= s)], 1).astype(np.float32),
        mlow=(s > t).astype(np.float32),
        bones=(s // 64 == t // 64).astype(np.float32),
    )
    return c

C_IN = [("zc", [1024, None], F32), ("mu", [128, 8], F32), ("w2c", [64, 256], F32), ("w0c", [1, 256], F32),
        ("a2c", [64, 256], F32), ("a0c", [128, 2], F32), ("g2c", [128, 256], F32), ("kkw", [128, 2], F32),
        ("ka", [128, 2], F32), ("rkb", [128, 2, 128], F32), ("lnw", [128, 2], F32), ("lnb", [128, 2], F32),
        ("ident", [128, 128], F32), ("triI", [128, 128], F32), ("triE", [128, 128], F32), ("triR", [128, 128], F32),
        ("mstrict", [128, 128], F32), ("mask3", [128, 384], F32), ("mlow", [128, 128], F32), ("bones", [128, 128], F32)]

def build_C(S):
    nc = bass.Bass("TRN2", target_bir_lowering=False)
    ins = {}
    for nm, shp, dt in C_IN:
        shp = [S if x is None else x for x in shp]
        ins[nm] = nc.dram_tensor(nm, shp, dt, kind="ExternalInput").ap()
    yT = nc.dram_tensor("yT", [256, S], BF16, kind="ExternalOutput").ap()
    k = KB(nc)
    with Alloc(nc) as al:
        emit_C(nc, k, al, S, ins, yT)
        k.finish("sp")
    return nc


def emit_C(nc, k, al, S, ins, yT, pfx="C", zrows=None):
    if zrows is None:
        zrows = [j * 128 for j in range(8)]
    NBLK = S // 512
    A = lambda nm, shp, dt=F32: al.sb(pfx + nm, shp, dt)
    P = lambda nm, shp, dt=F32: al.ps(pfx + nm, shp, dt)
    cs = {}
    for nm, shp, dt in C_IN[1:]:
        cs[nm] = A("c_" + nm, shp)
        k.dma("sp", cs[nm][:], ins[nm], writes=["c_" + nm])
    wa = A("wa", [128, 256])
    k.dma("sp", wa[0:64, :], ins["w2c"], writes=["wa"])
    k.dma("sp", wa[64:128, :], ins["a2c"], writes=["wa"])
    ones1 = A("ones1", [1, 128])
    k.memset("pool", ones1[:], 1.0, ["ones1"])
    omk = A("omk", [128, 2])
    k.ts("dve", omk[:], cs["ka"][:], -1.0, 1.0, ALU.mult, ALU.add, ["c_ka"], ["omk"])
    ident = cs["ident"]

    zin = [A(f"zin{i}", [128, 513]) for i in range(3)]
    dtmp = [A(f"dtmp{i}", [128, 512]) for i in range(2)]
    zs = A("zs", [128, 8, 512])
    tw = A("tw", [64, 512])
    sgx = A("sgx", [128, 512])
    aT = A("aT", [128, 2, 512])
    kkT = A("kkT", [128, 2, 512])
    kpT = A("kpT", [128, 2, 512])
    kkaT = A("kkaT", [128, 2, 512])
    bonT = A("bonT", [128, 2, 512])
    gTs = A("gTs", [128, 2, 512])
    tmpA = A("tmpA", [128, 512])
    tmpB = A("tmpB", [128, 512])
    sig = A("sig", [128, 256])
    Gi = A("Gi", [128, 2, 128]); iG = A("iG", [128, 2, 128]); Ge = A("Ge", [128, 2, 128])
    Gr = A("Gr", [128, 256])
    ARt = A("ARt", [128, 2, 2, 128])
    KtT = A("KtT", [128, 2, 128]); BtT = A("BtT", [128, 2, 128])
    Vtok = A("Vtok", [128, 256]); Khat = A("Khat", [128, 256]); Bhat = A("Bhat", [128, 256])
    ATs = A("ATs", [128, 4, 3, 128])
    PTs = [A(f"PTs{i}", [128, 4, 2, 128]) for i in range(2)]
    Qs = [A(f"Qs{i}", [128, 4, 128]) for i in range(2)]
    Hst = A("Hst", [128, 2, 64])
    X0 = A("X0", [128, 256]); Up = A("Up", [128, 256])
    Ysb = A("Ysb", [128, 4, 64]); Ysq = A("Ysq", [128, 4, 64])
    st1 = A("st1", [128, 4]); st2 = A("st2", [128, 4]); mean = A("mean", [128, 4]); rstd = A("rstd", [128, 4])
    msq = A("msq", [128, 4])
    Yn = A("Yn", [128, 256])
    fin = A("fin", [128, 2, 128])
    outb = [A(f"outb{i}", [128, 2, 512], BF16) for i in range(2)]
    SA = P("SA", [128, 4, 256]); SB = P("SB", [128, 4, 128])
    ATp = P("ATp", [128, 512]); Q0p = P("Q0p", [128, 4, 128])
    M = [P(f"M{i}", [128, 512]) for i in range(3)]
    mi = [0]
    def misc():
        i = mi[0] % 3; mi[0] += 1
        return M[i], f"M{i}"

    k.memset("pool", Hst[:], 0.0, ["Hst"])
    k.memset("pool", zin[0][:, 0:1], 0.0, ["zin0"])

    zi = [0]
    for b in range(NBLK):
        t0 = b * 512
        for j in range(8):
            zb = zin[zi[0] % 3]; zk = f"zin{zi[0]%3}"; zi[0] += 1
            zsplit = ins.get("zsplit", 0)
            if ins.get("zfn64"):
                zparts = [(slice(hq * 64, (hq + 1) * 64), (lambda a, b_, hq=hq: ins["zfn64"](zrows[j] + hq * 64, a, b_))) for hq in range(2)]
            else:
                zparts = [(slice(0, 128), (lambda a, b_: ins["zc"][zrows[j]:zrows[j] + 128, a:b_]))]
            if b == 0:
                k.memset("pool", zb[:, 0:1], 0.0, [zk])
            for (psl, zf_) in zparts:
                if b == 0:
                    k.dma("sp", zb[psl, 1:513], zf_(0, 512), writes=[zk])
                elif zsplit and t0 % zsplit == 0:
                    k.dma("sp", zb[psl, 0:1], zf_(t0 - 1, t0), writes=[zk], allow_slow_non_contiguous=True)
                    k.dma("sp", zb[psl, 1:513], zf_(t0, t0 + 512), writes=[zk])
                else:
                    k.dma("sp", zb[psl, 0:513], zf_(t0 - 1, t0 + 512), writes=[zk])
            e = "pool"
            dt_ = dtmp[j % 2]; dk = f"dtmp{j%2}"
            k.tt(e, dt_[:], zb[:, 0:512], zb[:, 1:513], ALU.subtract, [zk], [dk])
            k.emit("dve", lambda: nc.vector.scalar_tensor_tensor(zs[:, j, :], dt_[:], cs["mu"][:, j:j + 1], zb[:, 1:513], ALU.mult, ALU.add),
                   [dk, zk, "c_mu"], [f"zs{j}"])
        k.act(tw[:], zs[0:64, 6, :], AF.Tanh, ["zs6"], ["tw"])
        k.act(sgx[:], zs[:, 7, :], AF.Sigmoid, ["zs7"], ["sgx"])
        for hp in range(2):
            m, mk = misc()
            k.mm(m[:], wa[64:128, hp * 128:(hp + 1) * 128], zs[64:128, 6, :], True, True, ["wa", "zs6"], [mk])
            k.act(aT[:, hp, :], m[:], AF.Sigmoid, [mk, "c_a0c"], [f"aT{hp}"], bias=cs["a0c"][:, hp:hp + 1])
            k.ts1("pool", tmpA[:], zs[:, 2 + hp, :], cs["kkw"][:, hp:hp + 1], ALU.mult, [f"zs{2+hp}", "c_kkw"], ["tmpA"])
            k.tt("pool", tmpB[:], tmpA[:], tmpA[:], ALU.mult, ["tmpA"], ["tmpB"])
            m, mk = misc()
            k.mm(m[:], cs["bones"][:], tmpB[:], True, True, ["c_bones", "tmpB"], [mk])
            k.act(tmpB[:], m[:], AF.Sqrt, [mk], ["tmpB"])
            k.ts1("dve", tmpB[:], tmpB[:], 1e-12, ALU.max, ["tmpB"], ["tmpB"])
            k.emit("dve", lambda: nc.vector.reciprocal(tmpB[:], tmpB[:]), ["tmpB"], ["tmpB"])
            k.tt("dve", kkT[:, hp, :], tmpA[:], tmpB[:], ALU.mult, ["tmpA", "tmpB"], [f"kkT{hp}"])
            k.ts("dve", tmpA[:], aT[:, hp, :], cs["ka"][:, hp:hp + 1], omk[:, hp:hp + 1], ALU.mult, ALU.add,
                 [f"aT{hp}", "c_ka", "omk"], ["tmpA"])
            k.tt("pool", kpT[:, hp, :], tmpA[:], zs[:, 2 + hp, :], ALU.mult, ["tmpA", f"zs{2+hp}"], [f"kpT{hp}"])
            k.tt("pool", kkaT[:, hp, :], kkT[:, hp, :], aT[:, hp, :], ALU.mult, [f"kkT{hp}", f"aT{hp}"], [f"kkaT{hp}"])
            k.tt("pool", tmpA[:], zs[:, hp, :], kpT[:, hp, :], ALU.mult, [f"zs{hp}", f"kpT{hp}"], ["tmpA"])
            m, mk = misc()
            k.mm(m[:], cs["rkb"][:, hp, :], tmpA[:], True, True, ["c_rkb", "tmpA"], [mk])
            k.tt("dve", bonT[:, hp, :], m[:], zs[:, 4 + hp, :], ALU.mult, [mk, f"zs{4+hp}"], [f"bonT{hp}"])
            m, mk = misc()
            k.mm(m[:], cs["g2c"][:, hp * 128:(hp + 1) * 128], sgx[:], True, True, ["c_g2c", "sgx"], [mk])
            k.cp("act", gTs[:, hp, :], m[:], [mk], [f"gTs{hp}"])

        ob = outb[b % 2]; obk = f"outb{b%2}"
        for ch in range(4):
            cs0 = ch * 128
            csl = slice(cs0, cs0 + 128)
            m, mk = misc()
            k.mm(m[:, 0:256], tw[:, csl], wa[0:64, :], True, False, ["tw", "wa"], [mk])
            k.mm(m[:, 0:256], ones1[:], cs["w0c"][:], False, True, ["ones1", "c_w0c"], [mk])
            k.act(sig[:], m[:, 0:256], AF.Sigmoid, [mk], ["sig"])
            m, mk = misc()
            for hp in range(2):
                k.mm(m[:, hp * 128:(hp + 1) * 128], sig[:, hp * 128:(hp + 1) * 128], cs["triI"][:], hp == 0, False, ["sig", "c_triI"], [mk], skip_group_check=True)
            for hp in range(2):
                k.mm(m[:, 256 + hp * 128:256 + (hp + 1) * 128], sig[:, hp * 128:(hp + 1) * 128], cs["triE"][:], False, hp == 1, ["sig", "c_triE"], [mk], skip_group_check=True)
            k.act(Gi[:], m[:, 0:256], AF.Exp, [mk], ["Gi"])
            k.act(iG[:], m[:, 0:256], AF.Exp, [mk], ["iG"], scale=-1.0)
            k.act(Ge[:], m[:, 256:512], AF.Exp, [mk], ["Ge"])
            m, mk = misc()
            k.mm(m[:, 0:256], cs["triR"][:], sig[:], True, True, ["sig", "c_triR"], [mk])
            k.act(Gr[:], m[:, 0:256], AF.Exp, [mk], ["Gr"])
            for hp in range(2):
                k.emit("dve", lambda: nc.vector.scalar_tensor_tensor(ARt[:, hp, 0, :], kkT[:, hp, csl], -1.0, Ge[:, hp, :], ALU.mult, ALU.mult),
                       [f"kkT{hp}", "Ge"], ["ARt"])
                k.tt("pool", ARt[:, hp, 1, :], zs[:, hp, csl], Gi[:, hp, :], ALU.mult, [f"zs{hp}", "Gi"], ["ARt"])
                k.tt("dve", KtT[:, hp, :], kpT[:, hp, csl], iG[:, hp, :], ALU.mult, [f"kpT{hp}", "iG"], ["KtT"])
                k.tt("dve", BtT[:, hp, :], kkaT[:, hp, csl], iG[:, hp, :], ALU.mult, [f"kkaT{hp}", "iG"], ["BtT"])
            m, mk = misc()
            for hp in range(2):
                k.tr(m[:, hp * 128:(hp + 1) * 128], zs[:, 4 + hp, csl], ident[:], [f"zs{4+hp}", "c_ident"], [mk])
            k.cp("act", Vtok[:], m[:, 0:256], [mk], ["Vtok"])
            m, mk = misc()
            for hp in range(2):
                k.tr(m[:, hp * 128:(hp + 1) * 128], kpT[:, hp, csl], ident[:], [f"kpT{hp}", "c_ident"], [mk])
                k.tr(m[:, 256 + hp * 128:256 + (hp + 1) * 128], kkaT[:, hp, csl], ident[:], [f"kkaT{hp}", "c_ident"], [mk])
            k.tt("dve", Khat[:], m[:, 0:256], Gr[:], ALU.mult, [mk, "Gr"], ["Khat"])
            k.tt("dve", Bhat[:], m[:, 256:512], Gr[:], ALU.mult, [mk, "Gr"], ["Bhat"])
            for h in range(4):
                hp, e = h // 2, h % 2
                ps = slice(e * 64, (e + 1) * 64)
                k.mm(ATp[:, 0:256], BtT[ps, hp, :], ARt[ps, hp, :, :], True, False, ["BtT", "ARt"], ["ATp"], skip_group_check=True)
                k.mm(ATp[:, 256:512], KtT[ps, hp, :], ARt[ps, hp, :, :], False, True, ["KtT", "ARt"], ["ATp"], skip_group_check=True)
                k.mm(Q0p[:, h, :], ARt[ps, hp, 0, :], BtT[ps, hp, :], h == 0, h == 3, ["ARt", "BtT"], ["Q0p"], skip_group_check=True)
                k.tt("dve", PTs[0][:, h, 0, :], ATp[:, 0:128], cs["mstrict"][:], ALU.mult, ["ATp", "c_mstrict"], ["PTs0"])
                k.tt("dve", ATs[:, h, :, :], ATp[:, 128:512], cs["mask3"][:], ALU.mult, ["ATp", "c_mask3"], [f"ATs{h}"])
                k.cp("pool", PTs[0][:, h, 1, :], ident[:], ["c_ident"], ["PTs0"])
            for h in range(4):
                k.tt("dve", Qs[0][:, h, :], Q0p[:, h, :], cs["mlow"][:], ALU.mult, ["Q0p", "c_mlow"], ["Qs0"])
            for lv in range(7):
                cur, nxt = lv % 2, (lv + 1) % 2
                for h in range(4):
                    if lv < 6:
                        k.mm(SA[:, h, :], Qs[cur][:, h, :], PTs[cur][:, h, :, :], h % 2 == 0, h % 2 == 1,
                             [f"Qs{cur}", f"PTs{cur}"], ["SA"], skip_group_check=True)
                        k.mm(SB[:, h, :], PTs[cur][:, h, 0, :], Qs[cur][:, h, :], h == 0, h == 3,
                             [f"Qs{cur}", f"PTs{cur}"], ["SB"], skip_group_check=True)
                    else:
                        k.mm(SA[:, h, 128:256], Qs[cur][:, h, :], PTs[cur][:, h, 1, :], h % 2 == 0, h % 2 == 1,
                             [f"Qs{cur}", f"PTs{cur}"], ["SA"], skip_group_check=True)
                if lv < 6:
                    k.cp("act", PTs[nxt][:, :, 0, :], SA[:, :, 0:128], ["SA"], [f"PTs{nxt}"])
                    k.cp("act", Qs[nxt][:], SB[:], ["SB"], [f"Qs{nxt}"])
                k.tt("dve", PTs[nxt][:, :, 1, :], SA[:, :, 128:256], PTs[cur][:, :, 1, :], ALU.add, ["SA", f"PTs{cur}"], [f"PTs{nxt}"])
            TT = PTs[1]; TTk = "PTs1"
            m, mk = misc()
            for h in range(4):
                hp, e = h // 2, h % 2
                ps = slice(e * 64, (e + 1) * 64)
                k.mm(m[:, h * 64:(h + 1) * 64], ARt[ps, hp, 0, :], Hst[ps, hp, :], h == 0, False, ["ARt", "Hst"], [mk], skip_group_check=True)
                k.mm(m[:, h * 64:(h + 1) * 64], ATs[:, h, 1, :], Vtok[:, h * 64:(h + 1) * 64], False, h == 3, [f"ATs{h}", "Vtok"], [mk], skip_group_check=True)
            k.cp("act", X0[:], m[:, 0:256], [mk], ["X0"])
            m, mk = misc()
            for h in range(4):
                k.mm(m[:, h * 64:(h + 1) * 64], TT[:, h, 1, :], X0[:, h * 64:(h + 1) * 64], h == 0, h == 3, [TTk, "X0"], [mk], skip_group_check=True)
            k.cp("act", Up[:], m[:, 0:256], [mk], ["Up"])
            my, myk = misc()
            for h in range(4):
                hp, e = h // 2, h % 2
                ps = slice(e * 64, (e + 1) * 64)
                k.mm(my[:, h * 64:(h + 1) * 64], ARt[ps, hp, 1, :], Hst[ps, hp, :], h == 0, False, ["ARt", "Hst"], [myk], skip_group_check=True)
                k.mm(my[:, h * 64:(h + 1) * 64], ATs[:, h, 0, :], Up[:, h * 64:(h + 1) * 64], False, False, [f"ATs{h}", "Up"], [myk], skip_group_check=True)
                k.mm(my[:, h * 64:(h + 1) * 64], ATs[:, h, 2, :], Vtok[:, h * 64:(h + 1) * 64], False, h == 3, [f"ATs{h}", "Vtok"], [myk], skip_group_check=True)
            m, mk = misc()
            for hp in range(2):
                k.mm(m[:, hp * 128:(hp + 1) * 128], Bhat[:, hp * 128:(hp + 1) * 128], Up[:, hp * 128:(hp + 1) * 128], hp == 0, False, ["Bhat", "Up"], [mk], skip_group_check=True)
                k.mm(m[:, hp * 128:(hp + 1) * 128], Khat[:, hp * 128:(hp + 1) * 128], Vtok[:, hp * 128:(hp + 1) * 128], False, hp == 1, ["Khat", "Vtok"], [mk], skip_group_check=True)
            for h in range(4):
                hp, e = h // 2, h % 2
                ps = slice(e * 64, (e + 1) * 64)
                k.emit("dve", lambda: nc.vector.scalar_tensor_tensor(Hst[ps, hp, :], Hst[ps, hp, :], Gi[ps, hp, 127:128],
                                                                    m[ps, hp * 128 + e * 64:hp * 128 + (e + 1) * 64], ALU.mult, ALU.add),
                       ["Hst", "Gi", mk], ["Hst"])
            k.cp("act", Ysb[:], my[:, 0:256], [myk], ["Ysb"])
            k.emit("dve", lambda: nc.vector.reduce_sum(st1[:], Ysb[:], AX.X), ["Ysb"], ["st1"])
            k.tt("pool", Ysq[:], Ysb[:], Ysb[:], ALU.mult, ["Ysb"], ["Ysq"])
            k.emit("dve", lambda: nc.vector.reduce_sum(st2[:], Ysq[:], AX.X), ["Ysq"], ["st2"])
            k.ts1("dve", mean[:], st1[:], 1.0 / 64, ALU.mult, ["st1"], ["mean"])
            k.tt("dve", msq[:], mean[:], mean[:], ALU.mult, ["mean"], ["msq"])
            k.emit("dve", lambda: nc.vector.scalar_tensor_tensor(rstd[:], st2[:], 1.0 / 64, msq[:], ALU.mult, ALU.subtract), ["st2", "msq"], ["rstd"])
            k.ts1("dve", rstd[:], rstd[:], GN_EPS, ALU.add, ["rstd"], ["rstd"])
            k.act(rstd[:], rstd[:], AF.Sqrt, ["rstd"], ["rstd"])
            k.emit("dve", lambda: nc.vector.reciprocal(rstd[:], rstd[:]), ["rstd"], ["rstd"])
            for h in range(4):
                e_ = "dve" if h % 2 == 0 else "pool"
                k.ts(e_, Yn[:, h * 64:(h + 1) * 64], Ysb[:, h, :], mean[:, h:h + 1], rstd[:, h:h + 1], ALU.subtract, ALU.mult,
                     ["Ysb", "mean", "rstd"], ["Yn"])
            m, mk = misc()
            for hp in range(2):
                k.tr(m[:, hp * 128:(hp + 1) * 128], Yn[:, hp * 128:(hp + 1) * 128], ident[:], ["Yn", "c_ident"], [mk])
            for hp in range(2):
                k.act(fin[:, hp, :], m[:, hp * 128:(hp + 1) * 128], AF.Identity, [mk, "c_lnw", "c_lnb"], ["fin"],
                      bias=cs["lnb"][:, hp:hp + 1], scale=cs["lnw"][:, hp:hp + 1])
                k.tt("pool", fin[:, hp, :], fin[:, hp, :], bonT[:, hp, csl], ALU.add, ["fin", f"bonT{hp}"], ["fin"])
                k.tt("pool", ob[:, hp, csl], fin[:, hp, :], gTs[:, hp, csl], ALU.mult, ["fin", f"gTs{hp}"], [obk])
        for hp in range(2):
            yfn = ins.get("yfn") or (lambda r0, r1, a, b_: yT[r0:r1, a:b_])
            k.dma("pool", yfn(hp * 128, (hp + 1) * 128, t0, t0 + 512), ob[:, hp, :], reads=[obk])


WST = 2048
D = 1024
DFF = 2816
NF = DFF // 128
TB = 512

def rmsnorm_T(nc, k, xb, xk, hT, ss, rstd, junk, xn, pT, idb, eps, pfx):
    for s in range(4):
        k.act(junk[:], xb[:, s, :], AF.Square, [xk], [pfx + "junk", pfx + "ss"], accum_out=ss[:, s:s + 1])
    k.ts("dve", rstd[:], ss[:], 1.0 / D, eps, ALU.mult, ALU.add, [pfx + "ss"], [pfx + "rstd"])
    k.act(rstd[:], rstd[:], AF.Sqrt, [pfx + "rstd"], [pfx + "rstd"])
    k.emit("dve", lambda: nc.vector.reciprocal(rstd[:], rstd[:]), [pfx + "rstd"], [pfx + "rstd"])
    for s in range(4):
        xnk = pfx + f"xn{s%2}"
        k.ts1("dve", xn[s % 2][:], xb[:, s, :], rstd[:, s:s + 1], ALU.mult, [xk, pfx + "rstd"], [xnk])
        pk = pfx + f"pT{s%2}"
        for kc in range(8):
            k.tr(pT[s % 2][:, kc, :], xn[s % 2][:, kc * 128:(kc + 1) * 128], idb[:], [xnk, pfx + "idb"], [pk])
        k.cp("act", hT[:, :, s * 128:(s + 1) * 128], pT[s % 2][:], [pk], [pfx + "hT"])


def load_w_bf16(nc, k, dst, dkey, src, K, N, wst, wstk, scale_ap=None, skey=None):
    wv = src.rearrange("(kc p) n -> p kc n", p=128)
    step = WST // K
    i = 0
    for c0 in range(0, N, step):
        cw = min(step, N - c0)
        st = wst[i % 2]; sk = wstk + str(i % 2); i += 1
        stv = st[:, 0:K * cw].rearrange("p (kc n) -> p kc n", kc=K)
        k.dma("sp", stv, wv[:, :, c0:c0 + cw], writes=[sk])
        for kc in range(K):
            e = "dve" if kc % 2 == 0 else "pool"
            if scale_ap is not None:
                k.ts1(e, dst[:, kc, c0:c0 + cw], stv[:, kc, :], scale_ap[:, kc:kc + 1], ALU.mult, [sk, skey], [dkey])
            else:
                k.cp(e, dst[:, kc, c0:c0 + cw], stv[:, kc, :], [sk], [dkey])


def build_D1(NT):
    nc = bass.Bass("TRN2", target_bir_lowering=False)
    I = lambda nm, shp, dt=F32: nc.dram_tensor(nm, shp, dt, kind="ExternalInput").ap()
    ins = dict(x=I("x", [NT, D]), oaT=I("oaT", [512, NT], BF16), yrT=I("yrT", [512, NT], BF16), gT=I("gT", [2048, NT], BF16),
               woa=I("woa", [512, D]), wor=I("wor", [512, D]), wo=I("wo", [D, D]))
    xo = nc.dram_tensor("xo", [NT, D], F32, kind="ExternalOutput").ap()
    k = KB(nc)
    with Alloc(nc) as al:
        emit_D1(nc, k, al, NT, ins, xo)
        k.finish("sp")
    return nc


def emit_D1(nc, k, al, NT, ins, xo, pfx="E"):
    A = lambda nm, shp, dt=F32: al.sb(pfx + nm, shp, dt)
    P = lambda nm, shp, dt=F32: al.ps(pfx + nm, shp, dt)
    woa = A("woa", [128, 4, D], BF16); wor = A("wor", [128, 4, D], BF16); wo = A("wo", [128, 8, D], BF16)
    wst = [A(f"wst{i}", [128, WST]) for i in range(2)]
    load_w_bf16(nc, k, woa, pfx + "woa", ins["woa"], 4, D, wst, pfx + "wst")
    load_w_bf16(nc, k, wor, pfx + "wor", ins["wor"], 4, D, wst, pfx + "wst")
    load_w_bf16(nc, k, wo, pfx + "wo", ins["wo"], 8, D, wst, pfx + "wst")
    oab = [A(f"oab{i}", [128, 4, TB], BF16) for i in range(2)]
    yrb = [A(f"yrb{i}", [128, 4, TB], BF16) for i in range(2)]
    xb = [A(f"xb{i}", [128, 4, D]) for i in range(2)]
    gab = [A(f"gab{i}", [128, TB], BF16) for i in range(3)]
    grb = [A(f"grb{i}", [128, TB], BF16) for i in range(3)]
    t1 = [A(f"t1_{i}", [128, TB]) for i in range(2)]
    t2 = [A(f"t2_{i}", [128, TB]) for i in range(2)]
    mT = A("mT", [128, 8, TB], BF16)
    pa = [P(f"pa{i}", [128, 512]) for i in range(2)]
    pr = [P(f"pr{i}", [128, 512]) for i in range(2)]
    po = [P(f"po{i}", [128, 512]) for i in range(3)]
    nblk = NT // TB
    oav = ins["oaT"].rearrange("(kc p) t -> p kc t", p=128) if ins.get("oaT") is not None else None
    yrv = ins["yrT"].rearrange("(kc p) t -> p kc t", p=128) if ins.get("yrT") is not None else None
    xv = ins["x"].rearrange("(b s p) d -> b p s d", p=128, s=4)
    xov = xo.rearrange("(b s p) d -> b p s d", p=128, s=4)

    def load_blk(b):
        t0 = b * TB
        if ins.get("oafn"):
            for kc in range(4):
                k.dma("sp", oab[b % 2][:, kc, :], ins["oafn"](kc, t0, t0 + TB), writes=[pfx + f"oab{b%2}"])
                k.dma("sp", yrb[b % 2][:, kc, :], ins["yrfn"](kc, t0, t0 + TB), writes=[pfx + f"yrb{b%2}"])
        else:
            k.dma("sp", oab[b % 2][:], oav[:, :, t0:t0 + TB], writes=[pfx + f"oab{b%2}"])
            k.dma("sp", yrb[b % 2][:], yrv[:, :, t0:t0 + TB], writes=[pfx + f"yrb{b%2}"])
        k.dma("sp", xb[b % 2][:], xv[b], writes=[pfx + f"xb{b%2}"])
    load_blk(0)
    gi = [0]; oi = [0]
    for b in range(nblk):
        t0 = b * TB
        if b + 1 < nblk:
            load_blk(b + 1)
        for oc in range(8):
            i = gi[0]; gi[0] += 1
            ga = gab[i % 3]; gr = grb[i % 3]
            k.dma("sp", ga[:], ins["gT"][oc * 128:(oc + 1) * 128, t0:t0 + TB], writes=[pfx + f"gab{i%3}"])
            k.dma("sp", gr[:], ins["gT"][1024 + oc * 128:1024 + (oc + 1) * 128, t0:t0 + TB], writes=[pfx + f"grb{i%3}"])
            pai = pa[i % 2]; pri = pr[i % 2]
            for kc in range(4):
                k.mm(pai[:], woa[:, kc, oc * 128:(oc + 1) * 128], oab[b % 2][:, kc, :], kc == 0, kc == 3,
                     [pfx + "woa", pfx + f"oab{b%2}"], [pfx + f"pa{i%2}"])
            for kc in range(4):
                k.mm(pri[:], wor[:, kc, oc * 128:(oc + 1) * 128], yrb[b % 2][:, kc, :], kc == 0, kc == 3,
                     [pfx + "wor", pfx + f"yrb{b%2}"], [pfx + f"pr{i%2}"])
            k.tt("dve", t1[i % 2][:], pai[:], ga[:], ALU.mult, [pfx + f"pa{i%2}", pfx + f"gab{i%3}"], [pfx + f"t1_{i%2}"])
            k.tt("dve", t2[i % 2][:], pri[:], gr[:], ALU.mult, [pfx + f"pr{i%2}", pfx + f"grb{i%3}"], [pfx + f"t2_{i%2}"])
            k.tt("pool", mT[:, oc, :], t1[i % 2][:], t2[i % 2][:], ALU.add, [pfx + f"t1_{i%2}", pfx + f"t2_{i%2}"], [pfx + "mT"])
        for s in range(4):
            for n in range(2):
                j = oi[0]; oi[0] += 1
                pj = po[j % 3]
                for kc in range(8):
                    k.mm(pj[:], mT[:, kc, s * 128:(s + 1) * 128], wo[:, kc, n * 512:(n + 1) * 512], kc == 0, kc == 7,
                         [pfx + "mT", pfx + "wo"], [pfx + f"po{j%3}"])
                k.tt("dve", xb[b % 2][:, s, n * 512:(n + 1) * 512], pj[:], xb[b % 2][:, s, n * 512:(n + 1) * 512], ALU.add,
                     [pfx + f"po{j%3}", pfx + f"xb{b%2}"], [pfx + f"xb{b%2}"])
        k.dma("pool", xov[b], xb[b % 2][:], reads=[pfx + f"xb{b%2}"])


def build_D2(NT, last):
    nc = bass.Bass("TRN2", target_bir_lowering=False)
    I = lambda nm, shp, dt=F32: nc.dram_tensor(nm, shp, dt, kind="ExternalInput").ap()
    ins = dict(x=I("x", [NT, D]), p=I("p", [NT, 256]), wg=I("wg", [D, DFF]), wu=I("wu", [D, DFF]), wd=I("wd", [DFF, D]),
               wple=I("wple", [256, D]), wpg=I("wpg", [D, D]), gffn=I("gffn", [128, 8]), gple=I("gple", [128, 8]),
               gfin=I("gfin", [128, D]), ident=I("ident", [128, 128]))
    xo = nc.dram_tensor("xo", [NT, D], F32, kind="ExternalOutput").ap()
    wgS = nc.dram_tensor("wgS", [NF, 128, 1024], BF16).ap()
    wuS = nc.dram_tensor("wuS", [NF, 128, 1024], BF16).ap()
    k = KB(nc)
    with Alloc(nc) as al:
        emit_D2(nc, k, al, NT, last, ins, xo, wgS, wuS)
        k.finish("sp")
    return nc


def emit_D2(nc, k, al, NT, last, ins, xo, wgS, wuS, pfx="F"):
    A = lambda nm, shp, dt=F32: al.sb(pfx + nm, shp, dt)
    P = lambda nm, shp, dt=F32: al.ps(pfx + nm, shp, dt)
    gffn = A("gffn", [128, 8]); gple = A("gple", [128, 8]); idf = A("idf", [128, 128]); idb = A("idb", [128, 128], BF16)
    k.dma("sp", gffn[:], ins["gffn"], writes=[pfx + "gffn"])
    k.dma("sp", gple[:], ins["gple"], writes=[pfx + "gple"])
    k.dma("sp", idf[:], ins["ident"], writes=[pfx + "idf"])
    k.cp("dve", idb[:], idf[:], [pfx + "idf"], [pfx + "idb"])
    if last:
        gfin = A("gfin", [128, D])
        k.dma("sp", gfin[:], ins["gfin"], writes=[pfx + "gfin"])
    wd = A("wd", [128, NF, D], BF16); wple = A("wple", [128, 2, D], BF16); wpg = A("wpg", [128, 8, D], BF16)
    wst = [A(f"wst{i}", [128, WST]) for i in range(2)]
    load_w_bf16(nc, k, wple, pfx + "wple", ins["wple"], 2, D, wst, pfx + "wst")
    load_w_bf16(nc, k, wpg, pfx + "wpg", ins["wpg"], 8, D, wst, pfx + "wst", gple, pfx + "gple")
    wdv = ins["wd"].rearrange("(f p) n -> p f n", p=128)
    i = 0
    for f0 in range(0, NF, 2):
        fw_ = min(2, NF - f0)
        st = wst[i % 2]; sk = pfx + f"wst{i%2}"; i += 1
        stv = st[:, 0:fw_ * D].rearrange("p (f n) -> p f n", f=fw_)
        k.dma("sp", stv, wdv[:, f0:f0 + fw_, :], writes=[sk])
        for f in range(fw_):
            e = "dve" if f % 2 == 0 else "pool"
            k.cp(e, wd[:, f0 + f, :], stv[:, f, :], [sk], [pfx + "wd"])
    wcb = [A(f"wcb{i}", [128, 8, 256], BF16) for i in range(2)]
    ci = 0
    for (src, dstS, nm) in ((ins["wg"], wgS, "wgS"), (ins["wu"], wuS, "wuS")):
        wv = src.rearrange("(kc p) n -> p kc n", p=128)
        for c0 in range(0, DFF, 256):
            cw = min(256, DFF - c0)
            st = wst[i % 2]; sk = pfx + f"wst{i%2}"; i += 1
            stv = st[:, 0:8 * cw].rearrange("p (kc n) -> p kc n", kc=8)
            k.dma("sp", stv, wv[:, :, c0:c0 + cw], writes=[sk])
            cb = wcb[ci % 2]; cbk = pfx + f"wcb{ci%2}"; ci += 1
            for kc in range(8):
                e = "dve" if kc % 2 == 0 else "pool"
                k.ts1(e, cb[:, kc, 0:cw], stv[:, kc, :], gffn[:, kc:kc + 1], ALU.mult, [sk, pfx + "gffn"], [cbk])
            for j in range(cw // 128):
                f = c0 // 128 + j
                k.dma("pool", dstS[f].rearrange("p (kc n) -> p kc n", kc=8), cb[:, :, j * 128:(j + 1) * 128], reads=[cbk], writes=[nm])

    xb = [A(f"xb{i}", [128, 4, D]) for i in range(2)]
    pb = [A(f"pb{i}", [128, 4, 256]) for i in range(2)]
    junk = A("junk", [128, D], BF16)
    ss = A("ss", [128, 4]); rstd = A("rstd", [128, 4])
    xn = [A(f"xn{i}", [128, D], BF16) for i in range(2)]
    hT = A("hT", [128, 8, TB], BF16)
    ppT = A("ppT", [128, 2, TB], BF16)
    aT = A("aT", [128, NF, TB], BF16)
    NWB = 3
    wgb = [A(f"wgb{i}", [128, 8, 128], BF16) for i in range(NWB)]
    wub = [A(f"wub{i}", [128, 8, 128], BF16) for i in range(NWB)]
    sl = [A(f"sl{i}", [128, TB]) for i in range(2)]
    tmp = [A(f"tmp{i}", [128, 512]) for i in range(2)]
    pT = [P(f"pT{i}", [128, 8, 128], BF16) for i in range(2)]
    pg = [P(f"pg{i}", [128, 512]) for i in range(2)]
    pu = [P(f"pu{i}", [128, 512]) for i in range(2)]
    pd = [P(f"pd{i}", [128, 512]) for i in range(2)]
    nblk = NT // TB
    xv = ins["x"].rearrange("(b s p) d -> b p s d", p=128, s=4)
    pv = ins["p"].rearrange("(b s p) d -> b p s d", p=128, s=4)
    xov = xo.rearrange("(b s p) d -> b p s d", p=128, s=4)

    def load_blk(b):
        k.dma("sp", xb[b % 2][:], xv[b], writes=[pfx + f"xb{b%2}"])
        k.dma("sp", pb[b % 2][:], pv[b], writes=[pfx + f"pb{b%2}"])
    load_blk(0)
    wi = [0]; di = [0]
    for b in range(nblk):
        if b + 1 < nblk:
            load_blk(b + 1)
        X = xb[b % 2]; xk = pfx + f"xb{b%2}"
        rmsnorm_T(nc, k, X, xk, hT, ss, rstd, junk, xn, pT, idb, 1e-6, pfx)
        def load_w(f):
            i = f % NWB
            k.dma("sp", wgb[i][:], wgS[f].rearrange("p (kc n) -> p kc n", kc=8), reads=["wgS"], writes=[pfx + f"wgb{i}"])
            k.dma("sp", wub[i][:], wuS[f].rearrange("p (kc n) -> p kc n", kc=8), reads=["wuS"], writes=[pfx + f"wub{i}"])
        load_w(0); load_w(1)
        for f in range(NF):
            if f + 2 < NF:
                load_w(f + 2)
            i = f % NWB
            j = wi[0]; wi[0] += 1
            for kc in range(8):
                k.mm(pg[j % 2][:], wgb[i][:, kc, :], hT[:, kc, :], kc == 0, kc == 7, [pfx + f"wgb{i}", pfx + "hT"], [pfx + f"pg{j%2}"])
            for kc in range(8):
                k.mm(pu[j % 2][:], wub[i][:, kc, :], hT[:, kc, :], kc == 0, kc == 7, [pfx + f"wub{i}", pfx + "hT"], [pfx + f"pu{j%2}"])
            k.act(sl[j % 2][:], pg[j % 2][:], AF.Silu, [pfx + f"pg{j%2}"], [pfx + f"sl{j%2}"])
            k.tt("dve", aT[:, f, :], pu[j % 2][:], sl[j % 2][:], ALU.mult, [pfx + f"pu{j%2}", pfx + f"sl{j%2}"], [pfx + "aT"])
        for s in range(4):
            for n in range(2):
                j = di[0]; di[0] += 1
                for f in range(NF):
                    k.mm(pd[j % 2][:], aT[:, f, s * 128:(s + 1) * 128], wd[:, f, n * 512:(n + 1) * 512], f == 0, f == NF - 1,
                         [pfx + "aT", pfx + "wd"], [pfx + f"pd{j%2}"])
                k.tt("dve", X[:, s, n * 512:(n + 1) * 512], pd[j % 2][:], X[:, s, n * 512:(n + 1) * 512], ALU.add,
                     [pfx + f"pd{j%2}", xk], [xk])
        rmsnorm_T(nc, k, X, xk, hT, ss, rstd, junk, xn, pT, idb, 1e-6, pfx)
        PB = pb[b % 2]; pbk = pfx + f"pb{b%2}"
        for s in range(4):
            k.cp("pool", xn[s % 2][:, 0:256], PB[:, s, :], [pbk], [pfx + f"xn{s%2}"])
            for kc in range(2):
                k.tr(pT[s % 2][:, kc, :], xn[s % 2][:, kc * 128:(kc + 1) * 128], idb[:], [pfx + f"xn{s%2}", pfx + "idb"], [pfx + f"pT{s%2}"])
            k.cp("act", ppT[:, :, s * 128:(s + 1) * 128], pT[s % 2][:, 0:2, :], [pfx + f"pT{s%2}"], [pfx + "ppT"])
        for s in range(4):
            for n in range(2):
                j = di[0]; di[0] += 1
                for kc in range(8):
                    k.mm(pg[j % 2][:], hT[:, kc, s * 128:(s + 1) * 128], wpg[:, kc, n * 512:(n + 1) * 512], kc == 0, kc == 7,
                         [pfx + "hT", pfx + "wpg"], [pfx + f"pg{j%2}"])
                for kc in range(2):
                    k.mm(pu[j % 2][:], ppT[:, kc, s * 128:(s + 1) * 128], wple[:, kc, n * 512:(n + 1) * 512], kc == 0, kc == 1,
                         [pfx + "ppT", pfx + "wple"], [pfx + f"pu{j%2}"])
                k.act(tmp[j % 2][:], pg[j % 2][:], AF.Sigmoid, [pfx + f"pg{j%2}"], [pfx + f"tmp{j%2}"])
                k.tt("dve", tmp[j % 2][:], pu[j % 2][:], tmp[j % 2][:], ALU.mult, [pfx + f"pu{j%2}", pfx + f"tmp{j%2}"], [pfx + f"tmp{j%2}"])
                k.tt("pool", X[:, s, n * 512:(n + 1) * 512], X[:, s, n * 512:(n + 1) * 512], tmp[j % 2][:], ALU.add,
                     [xk, pfx + f"tmp{j%2}"], [xk])
        if last:
            for s in range(4):
                k.act(junk[:], X[:, s, :], AF.Square, [xk], [pfx + "junk", pfx + "ss"], accum_out=ss[:, s:s + 1])
            k.ts("dve", rstd[:], ss[:], 1.0 / D, 1e-6, ALU.mult, ALU.add, [pfx + "ss"], [pfx + "rstd"])
            k.act(rstd[:], rstd[:], AF.Sqrt, [pfx + "rstd"], [pfx + "rstd"])
            k.emit("dve", lambda: nc.vector.reciprocal(rstd[:], rstd[:]), [pfx + "rstd"], [pfx + "rstd"])
            for s in range(4):
                k.emit("dve", lambda: nc.vector.scalar_tensor_tensor(X[:, s, :], X[:, s, :], rstd[:, s:s + 1], gfin[:], ALU.mult, ALU.mult),
                       [xk, pfx + "rstd", pfx + "gfin"], [xk])
        k.dma("pool", xov[b], X[:], reads=[xk])

PAIRS = [[0, 1], [2, 3], [4, 5], [6, 7]]
CPAR = [e for e in C_IN[1:12]]
CCON = [e for e in C_IN[12:]]
DS = bass.DynSlice


def build_fused8(HALF, depth):
    S = 2 * HALF
    nc = bass.Bass("TRN2", target_bir_lowering=False)
    I = lambda nm, shp, dt=F32: nc.dram_tensor(nm, list(shp), dt, kind="ExternalInput").ap()
    Sc = lambda nm, shp, dt: nc.dram_tensor(nm, list(shp), dt).ap()
    x = I("x", [HALF, D]); p = I("p", [depth, HALF, 256])
    w_in = I("w_in", [depth, D, INC]); gam = I("gam", [depth, 128, 8]); ident = I("ident", [128, 128])
    G = I("G", [4, 128, 256]); b31 = I("b31", [128, 4]); neg = I("neg", [128, 256])
    lamv = I("lamv", [depth, 128, 4, 64]); sgain = I("sgain", [depth, 128, 128])
    cpar = {nm: I("c_" + nm, [depth] + list(shp)) for nm, shp, dt in CPAR}
    ccon = {nm: (ident if nm == "ident" else I("k_" + nm, shp)) for nm, shp, dt in CCON}
    woa = I("woa", [depth, 512, D]); wor = I("wor", [depth, 512, D]); wo = I("wo", [depth, D, D])
    wg = I("wg", [depth, D, DFF]); wu = I("wu", [depth, D, DFF]); wd = I("wd", [depth, DFF, D])
    wple = I("wple", [depth, 256, D]); wpg = I("wpg", [depth, D, D])
    gffn = I("gffn", [depth, 128, 8]); gple = I("gple", [depth, 128, 8]); gfin = I("gfin", [128, D])
    out = nc.dram_tensor("out", [HALF, D], F32, kind="ExternalOutput").ap()
    PV = min(2048, HALF)
    NPV = HALF // PV
    gT = Sc("s_gT", [2048, HALF], BF16)
    xmid = Sc("s_xmid", [HALF, D], F32); x1 = Sc("s_x1", [HALF, D], F32)
    wgS = Sc("s_wgS", [NF, 128, 1024], BF16); wuS = Sc("s_wuS", [NF, 128, 1024], BF16)
    k = KB(nc)
    pr_p = nc.gpsimd.partition_id() % 2
    pr_s = nc.sync.partition_id() % 2
    DS1 = lambda v: DS(v, 1)

    def sel(t, v):
        return t[DS1(v)].rearrange("a r c -> (a r) c")

    def exch(tag, P_, F_, rows, cols, dt, static_src=False):
        stg = Sc(f"x_stg_{tag}", [rows, cols], dt)
        rcv = Sc(f"x_rcv_{tag}", [2 * rows, cols], dt)
        rcv3 = rcv.rearrange("(a r) c -> a r c", a=2)
        if static_src:
            k.dma("pool", stg, P_, writes=[f"stg{tag}"])
            k.dma("sp", sel(F_, pr_s), P_)
        else:
            k.dma("pool", stg, sel(P_, 1 - pr_p), writes=[f"stg{tag}"])
            k.dma("sp", sel(F_, pr_s), sel(P_, pr_s))
        k.coll("AllGather", stg, rcv, PAIRS, reads=[f"stg{tag}"], writes=[f"rcv{tag}"])
        k.dma("sp", sel(F_, 1 - pr_s), sel(rcv3, 1 - pr_s), reads=[f"rcv{tag}"])

    for i in range(depth):
        last = (i == depth - 1)
        xin = x if i == 0 else x1
        lam_init = 0.8 - 0.6 * math.exp(-0.3 * i)
        L = f"L{i}"
        QP = [Sc(f"{L}QP{b_}", [2, 128, HALF], BF16) for b_ in range(2)]
        KP = [Sc(f"{L}KP{b_}", [2, 128, HALF], BF16) for b_ in range(2)]
        VP = [Sc(f"{L}VP{b_}", [2, PV, 256], BF16) for b_ in range(NPV)]
        ZP = [[Sc(f"{L}ZP{j}_{q}", [2, 64, HALF], F32) for q in range(4)] for j in range(3)]
        ZL = [Sc(f"{L}ZL{q}", [64, HALF], F32) for q in range(4)]
        QF = [Sc(f"{L}QF{b_}", [2, 128, HALF], BF16) for b_ in range(2)]
        KF = [Sc(f"{L}KF{b_}", [2, 128, HALF], BF16) for b_ in range(2)]
        VF = [Sc(f"{L}VF{b_}", [2, PV, 256], BF16) for b_ in range(NPV)]
        ZF = [Sc(f"{L}ZF{r}", [2, 64, HALF], F32) for r in range(16)]
        OP = [Sc(f"{L}OP{b_}", [2, 128, HALF], BF16) for b_ in range(2)]
        YP = [Sc(f"{L}YP{b_}", [2, 128, HALF], BF16) for b_ in range(2)]
        OAF = [Sc(f"{L}OAF{b_}", [2, 128, HALF], BF16) for b_ in range(2)]
        YF = [Sc(f"{L}YF{b_}", [2, 128, HALF], BF16) for b_ in range(2)]

        def zcb(cc, hq, t0):
            if cc < 12:
                j, g_, rb = cc // 4, (cc % 4) // 2, cc % 2
                return ZP[j][rb * 2 + hq][g_, :, t0:t0 + TB]
            return ZL[(cc - 12) * 2 + hq][:, t0:t0 + TB]
        ocb = dict(q=lambda cc, t0: QP[cc % 2][cc // 2, :, t0:t0 + TB],
                   k=lambda cc, t0: KP[cc % 2][cc // 2, :, t0:t0 + TB],
                   v=lambda g_, t: VP[t // PV][g_, t % PV:t % PV + 128, :],
                   z=zcb)
        with Alloc(nc) as al:
            emit_A(nc, k, al, HALF, xin, w_in[i], gam[i], ident, None, None, None, None, gT, pfx=f"A{i}", ocb=ocb)
        k.barrier()
        for b_ in range(2):
            exch(f"{L}q{b_}", QP[b_], QF[b_], 128, HALF, BF16)
            exch(f"{L}k{b_}", KP[b_], KF[b_], 128, HALF, BF16)
        for b_ in range(NPV):
            exch(f"{L}v{b_}", VP[b_], VF[b_], PV, 256, BF16)
        for j in range(3):
            for q in range(4):
                exch(f"{L}z{j}_{q}", ZP[j][q], ZF[j * 4 + q], 64, HALF, F32)
        for q in range(4):
            exch(f"{L}zl{q}", ZL[q], ZF[12 + q], 64, HALF, F32, static_src=True)
        k.barrier()

        def hm(T_):
            def fn(r0, r1, a, b_):
                h = a // HALF
                assert (b_ - 1) // HALF == h and r1 - r0 == 128
                return T_[r0 // 128][h, :, a - h * HALF:b_ - h * HALF]
            return fn

        def vfn(a, b_):
            h = a // HALF
            tl = a - h * HALF
            assert (b_ - 1) // HALF == h and tl // PV == (tl + (b_ - a) - 1) // PV
            return VF[tl // PV][h, tl % PV:tl % PV + (b_ - a), :]

        def zfn64(r0, a, b_):
            h = a // HALF
            assert (b_ - 1) // HALF == h
            return ZF[r0 // 64][h, :, a - h * HALF:b_ - h * HALF]
        with Alloc(nc) as al:
            insB = dict(qfn=hm(QF), kfn=hm(KF), ofn=hm(OP), vfn=vfn, G=G, b31=b31, neg=neg,
                        lamv=lamv[i], sgain=sgain[i], ident=ident)
            emit_B(nc, k, al, S, lam_init, insB, None, pfx=f"B{i}")
        k.barrier()
        with Alloc(nc) as al:
            insC = {nm: cpar[nm][i] for nm, shp, dt in CPAR}
            insC.update(ccon)
            insC["zfn64"] = zfn64
            insC["zsplit"] = HALF
            insC["yfn"] = hm(YP)
            emit_C(nc, k, al, S, insC, None, pfx=f"C{i}")
        k.barrier()
        for b_ in range(2):
            exch(f"{L}o{b_}", OP[b_], OAF[b_], 128, HALF, BF16)
            exch(f"{L}y{b_}", YP[b_], YF[b_], 128, HALF, BF16)
        k.barrier()
        with Alloc(nc) as al:
            emit_D1(nc, k, al, HALF, dict(x=xin, gT=gT, woa=woa[i], wor=wor[i], wo=wo[i],
                                          oafn=lambda kc, a, b_: OAF[kc % 2][kc // 2, :, a:b_],
                                          yrfn=lambda kc, a, b_: YF[kc % 2][kc // 2, :, a:b_]), xmid, pfx=f"E{i}")
        k.barrier()
        with Alloc(nc) as al:
            emit_D2(nc, k, al, HALF, last, dict(x=xmid, p=p[i], wg=wg[i], wu=wu[i], wd=wd[i], wple=wple[i], wpg=wpg[i],
                                                gffn=gffn[i], gple=gple[i], gfin=gfin, ident=ident),
                    out if last else x1, wgS, wuS, pfx=f"F{i}")
        k.barrier()
    k.finish("sp")
    return nc


def _pp(v):
    return np.ascontiguousarray(np.asarray(v, np.float32).reshape(-1, 128).T)


def kernel(x, p, rel_bias, norm_mix, w_in, lam_q1, lam_k1, lam_q2, lam_k2, attn_subln,
           rwkv_mu, rwkv_w0, rwkv_w2, rwkv_a0, rwkv_a2, rwkv_g2, rwkv_kk, rwkv_ka, rwkv_rk,
           rwkv_lnx_w, rwkv_lnx_b, w_out_attn, w_out_rwkv, w_out, norm_ffn, w_ffn_gate,
           w_ffn_up, w_ffn_down, norm_ple, w_ple, w_ple_gate, norm_final):
    f32 = np.float32
    A_ = lambda a: np.ascontiguousarray(np.asarray(a, f32))
    x = A_(x); p = A_(p); rel_bias = A_(rel_bias)
    B_, S_, D_ = x.shape
    HALF = S_ // 2
    NC = 2 * B_
    depth = int(np.asarray(w_in).shape[0])
    bc = lambda v, shape: np.ascontiguousarray(np.broadcast_to(v, shape))
    tidx = toeplitz_idx()
    com = dict(
        w_in=A_(w_in), gam=np.stack([_pp(norm_mix[i]) for i in range(depth)]), ident=np.eye(128, dtype=f32),
        neg=neg_mask(),
        lamv=np.stack([bc(np.stack([A_(lam_q1[i]), A_(lam_k1[i]), A_(lam_q2[i]), A_(lam_k2[i])])[None], (128, 4, 64)) for i in range(depth)]),
        sgain=np.stack([bc(A_(attn_subln[i])[None], (128, 128)) for i in range(depth)]),
        woa=A_(w_out_attn), wor=A_(w_out_rwkv), wo=A_(w_out), wg=A_(w_ffn_gate), wu=A_(w_ffn_up), wd=A_(w_ffn_down),
        wple=A_(w_ple), wpg=A_(w_ple_gate),
        gffn=np.stack([_pp(norm_ffn[i]) for i in range(depth)]), gple=np.stack([_pp(norm_ple[i]) for i in range(depth)]),
        gfin=bc(A_(norm_final)[None], (128, D_)),
    )
    cC = consts_C()
    for nm, shp, dt in CCON:
        if nm != "ident":
            com["k_" + nm] = cC[nm]
    blk = (np.arange(128)[:, None] // 64 == np.arange(128)[None, :] // 64)
    grp = []
    for hh in range(2):
        d = dict(G=np.ascontiguousarray(np.stack([rel_bias[tidx, 2 * hh + hl, c] for hl in range(2) for c in range(2)])),
                 b31=bc(np.stack([rel_bias[31, 2 * hh + hl, c] for hl in range(2) for c in range(2)])[None, :], (128, 4)))
        sl = np.arange(hh * 256, (hh + 1) * 256)
        idx = np.concatenate([sl, 512 + sl, 1024 + sl, np.arange(1536, 1792)])
        cp = {nm: [] for nm, shp, dt in CPAR}
        for i in range(depth):
            rkf = A_(rwkv_rk[i]).reshape(-1)[sl]
            rkb = np.stack([np.where(blk, rkf[hp * 128:(hp + 1) * 128][:, None], f32(0)) for hp in range(2)], 1).astype(f32)
            e = dict(mu=_pp(A_(rwkv_mu[i])[idx]), w2c=A_(rwkv_w2[i])[:, sl], w0c=A_(rwkv_w0[i])[None, sl],
                     a2c=A_(rwkv_a2[i])[:, sl], a0c=_pp(A_(rwkv_a0[i])[sl]), g2c=A_(rwkv_g2[i])[:, sl],
                     kkw=_pp(A_(rwkv_kk[i])[sl]), ka=_pp(A_(rwkv_ka[i])[sl]), rkb=rkb,
                     lnw=_pp(A_(rwkv_lnx_w[i])[sl]), lnb=_pp(A_(rwkv_lnx_b[i])[sl]))
            for nm in cp:
                cp[nm].append(np.ascontiguousarray(e[nm]))
        for nm in cp:
            d["c_" + nm] = np.ascontiguousarray(np.stack(cp[nm]))
        grp.append(d)
    nc = build_fused8(HALF, depth)
    in_maps = []
    for c in range(NC):
        b, pi = c // 2, c % 2
        d = dict(com)
        d.update(grp[pi])
        d["x"] = np.ascontiguousarray(x[b, pi * HALF:(pi + 1) * HALF])
        d["p"] = np.ascontiguousarray(p[:, b, pi * HALF:(pi + 1) * HALF])
        in_maps.append(d)
    res = run_bass_kernel_spmd(nc, in_maps, core_ids=list(range(NC)))
    out = np.empty((B_, S_, D_), f32)
    for c in range(NC):
        out[c // 2, (c % 2) * HALF:(c % 2 + 1) * HALF] = np.asarray(res.results[c]["out"], f32)
    return out
```

```python
import math
import numpy as np
import concourse.bass as bass
import concourse.mybir as mybir
from concourse.bass_utils import run_bass_kernel_spmd

F32 = mybir.dt.float32
BF16 = mybir.dt.bfloat16
AF = mybir.ActivationFunctionType
ALU = mybir.AluOpType
AX = mybir.AxisListType

EPOCH = 30000
DMA_RING = 8
DMA_EPOCH = 700


class KB:
    def __init__(self, nc):
        self.nc = nc
        self.eng = {"pe": nc.tensor, "act": nc.scalar, "dve": nc.vector,
                    "pool": nc.gpsimd, "sp": nc.sync}
        self.sem = {}
        self.cnt = {}
        self.nsem = 0
        for e in ("pe", "act", "dve", "pool"):
            self._new_sem(e)
        self.seen = {e: {} for e in self.eng}
        self.semobj = {}
        self.lastw = {}
        self.readers = {}
        self.dq = {}
        for q in ("sp", "pool", "act"):
            self.dq[q] = {"n": 0, "sems": None, "vals": None, "tok": [None] * DMA_RING}
        self.ninst = {e: 0 for e in self.eng}
        self.colltoks = []

    def _alloc_sem(self, name):
        self.nsem += 1
        s = self.nc.alloc_semaphore(f"{name}_{self.nsem}")
        return s

    def _new_sem(self, e):
        self.sem[e] = self._alloc_sem("s" + e)
        self.cnt[e] = 0

    def _wait(self, e, tok):
        if tok is None:
            return
        s, v, pe = tok
        sid = id(s)
        if self.seen[e].get(sid, 0) >= v:
            return
        self.eng[e].wait_ge(s, v)
        self.seen[e][sid] = v

    def _deps(self, e, reads, writes):
        toks = []
        for k in list(reads) + list(writes):
            t = self.lastw.get(k)
            if t is not None:
                toks.append(t)
        for k in writes:
            toks.extend(self.readers.get(k, ()))
        for t in toks:
            if e == "pe" and t[2] == "pe":
                continue
            self._wait(e, t)

    def _record(self, tok, reads, writes):
        for k in writes:
            self.lastw[k] = tok
            self.readers[k] = []
        for k in reads:
            self.readers.setdefault(k, []).append(tok)

    def emit(self, e, fn, reads=(), writes=()):
        self._deps(e, reads, writes)
        if self.cnt[e] >= EPOCH:
            self._new_sem(e)
        ins = fn()
        self.cnt[e] += 1
        ins.then_inc(self.sem[e], 1)
        tok = (self.sem[e], self.cnt[e], e)
        self._record(tok, reads, writes)
        self.ninst[e] += 1
        return tok

    def dma(self, q, out, in_, reads=(), writes=(), **kw):
        d = self.dq[q]
        n = d["n"]
        slot = n % DMA_RING
        if n % (DMA_RING * DMA_EPOCH) == 0:
            d["sems"] = [self._alloc_sem("d" + q) for _ in range(DMA_RING)]
            d["vals"] = [0] * DMA_RING
            for t in d["tok"]:
                self._wait(q, t)
        self._wait(q, d["tok"][slot])
        self._deps(q, reads, writes)
        ins = self.eng[q].dma_start(out=out, in_=in_, **kw)
        d["vals"][slot] += 16
        ins.then_inc(d["sems"][slot], 16)
        tok = (d["sems"][slot], d["vals"][slot], "dma_" + q)
        d["tok"][slot] = tok
        d["n"] = n + 1
        self._record(tok, reads, writes)
        self.ninst[q] += 1
        return tok

    def finish(self, q="sp"):
        for qq, d in self.dq.items():
            for t in d["tok"]:
                self._wait(q, t)

    def mm(self, out, lhsT, rhs, start, stop, reads, writes, **kw):
        return self.emit("pe", lambda: self.nc.tensor.matmul(out, lhsT, rhs, start=start, stop=stop, **kw),
                         reads, writes)

    def tr(self, out, in_, ident, reads, writes):
        return self.emit("pe", lambda: self.nc.tensor.transpose(out, in_, ident), reads, writes)

    def act(self, out, in_, func, reads, writes, **kw):
        return self.emit("act", lambda: self.nc.scalar.activation(out, in_, func, **kw), reads, writes)

    def tt(self, e, out, a, b, op, reads, writes):
        return self.emit(e, lambda: self.eng[e].tensor_tensor(out, a, b, op), reads, writes)

    def ts(self, e, out, a, s1, s2, op0, op1, reads, writes, **kw):
        return self.emit(e, lambda: self.eng[e].tensor_scalar(out, a, s1, s2, op0, op1, **kw), reads, writes)

    def ts1(self, e, out, a, s1, op, reads, writes):
        return self.emit(e, lambda: self.eng[e].tensor_single_scalar(out, a, s1, op), reads, writes)

    def cp(self, e, out, a, reads, writes):
        if e == "act":
            return self.emit(e, lambda: self.nc.scalar.copy(out, a), reads, writes)
        return self.emit(e, lambda: self.eng[e].tensor_copy(out, a), reads, writes)

    def memset(self, e, ap, val, writes):
        return self.emit(e, lambda: self.eng[e].memset(ap, val), (), writes)


import contextlib


class Alloc:
    def __init__(self, nc):
        self.nc = nc
        self.st = contextlib.ExitStack()
    def __enter__(self):
        self.st.__enter__()
        return self
    def __exit__(self, *a):
        return self.st.__exit__(*a)
    def sb(self, name, shape, dt):
        return self.st.enter_context(self.nc.sbuf_tensor(name, list(shape), dt))
    def ps(self, name, shape, dt):
        return self.st.enter_context(self.nc.psum_tensor(name, list(shape), dt))


def kb_barrier(self):
    toks = []
    for e in ("pe", "act", "dve", "pool"):
        if self.cnt[e] > 0:
            toks.append((self.sem[e], self.cnt[e], e))
    for q, d in self.dq.items():
        for t in d["tok"]:
            if t is not None:
                toks.append(t)
    toks.extend(self.colltoks)
    self.colltoks = []
    for e in ("pe", "act", "dve", "pool", "sp"):
        for t in toks:
            self._wait(e, t)
    self.lastw.clear()
    self.readers.clear()

KB.barrier = kb_barrier


def kb_coll(self, kind, src, dst, groups, reads=(), writes=()):
    q = "pool"
    self._deps(q, reads, writes)
    sem = self._alloc_sem("cc")
    ins = self.nc.gpsimd.collective_compute(kind, ALU.bypass, replica_groups=groups, ins=[src], outs=[dst])
    ins.then_inc(sem, 1)
    tok = (sem, 1, "coll")
    self._record(tok, reads, writes)
    self.colltoks.append(tok)
    return tok

KB.coll = kb_coll

D = 1024
INC = 5376
TB = 512

def build_A(NT):
    nc = bass.Bass("TRN2", target_bir_lowering=False)
    x = nc.dram_tensor("x", [NT, D], F32, kind="ExternalInput").ap()
    w = nc.dram_tensor("w_in", [D, INC], F32, kind="ExternalInput").ap()
    gam = nc.dram_tensor("gam", [128, 8], F32, kind="ExternalInput").ap()
    ident = nc.dram_tensor("ident", [128, 128], F32, kind="ExternalInput").ap()
    qT = nc.dram_tensor("qT", [512, NT], BF16, kind="ExternalOutput").ap()
    kT = nc.dram_tensor("kT", [512, NT], BF16, kind="ExternalOutput").ap()
    vo = nc.dram_tensor("v", [NT, 512], BF16, kind="ExternalOutput").ap()
    zrT = nc.dram_tensor("zrT", [1792, NT], F32, kind="ExternalOutput").ap()
    gT = nc.dram_tensor("gT", [2048, NT], BF16, kind="ExternalOutput").ap()
    k = KB(nc)
    with Alloc(nc) as al:
        emit_A(nc, k, al, NT, x, w, gam, ident, qT, kT, vo, zrT, gT)
        k.finish("sp")
    return nc


def emit_A(nc, k, al, NT, x, w, gam, ident, qT, kT, vo, zrT, gT, pfx="A", vsplit=False, ocb=None):
    nblk = NT // TB
    wb = al.sb(pfx + "wb", [128, 8, INC], BF16)
    wst = [al.sb(pfx + f"wst{i}", [128, 8, 512], F32) for i in range(2)]
    gt = al.sb(pfx + "gt", [128, 8], F32)
    idf = al.sb(pfx + "idf", [128, 128], F32)
    idb = al.sb(pfx + "idb", [128, 128], BF16)
    xt = [al.sb(pfx + f"xt{i}", [128, 4, D], F32) for i in range(2)]
    junk = al.sb(pfx + "junk", [128, D], BF16)
    ss = al.sb(pfx + "ss", [128, 4], F32)
    rstd = al.sb(pfx + "rstd", [128, 4], F32)
    xn = [al.sb(pfx + f"xn{i}", [128, D], BF16) for i in range(2)]
    hT = al.sb(pfx + "hT", [128, 8, TB], BF16)
    NOB = 4
    ob16 = [al.sb(pfx + f"ob16_{i}", [128, 512], BF16) for i in range(NOB)]
    ob32 = [al.sb(pfx + f"ob32_{i}", [128, 512], F32) for i in range(NOB)]
    pT = [al.ps(pfx + f"pT{i}", [128, 8, 128], BF16) for i in range(2)]
    NPZ = 4
    pz = [al.ps(pfx + f"pz{i}", [128, 512], F32) for i in range(NPZ)]

    k.dma("sp", gt[:], gam, writes=["gt"])
    k.dma("sp", idf[:], ident, writes=["idf"])
    k.cp("dve", idb[:], idf[:], ["idf"], ["idb"])
    wv = w.rearrange("(kc p) n -> p kc n", p=128)
    xv = x.rearrange("(b s p) d -> b p s d", p=128, s=4)

    def load_x(b):
        k.dma("sp", xt[b % 2][:], xv[b], writes=[f"xt{b%2}"])

    load_x(0)
    ncc = INC // 512 + (1 if INC % 512 else 0)
    for c in range(ncc):
        c0 = c * 512
        cw = min(512, INC - c0)
        st = wst[c % 2]
        k.dma("sp", st[:, :, :cw], wv[:, :, c0:c0 + cw], writes=[f"wst{c%2}"])
        for kc in range(8):
            e = "dve" if kc % 2 == 0 else "pool"
            k.ts1(e, wb[:, kc, c0:c0 + cw], st[:, kc, :cw], gt[:, kc:kc + 1], ALU.mult,
                  [f"wst{c%2}", "gt"], ["wb"])

    evn = [0]
    for b in range(nblk):
        if b + 1 < nblk:
            load_x(b + 1)
        xb = xt[b % 2]
        xk = f"xt{b%2}"
        for s in range(4):
            k.act(junk[:], xb[:, s, :], AF.Square, [xk], ["junk", "ss"], accum_out=ss[:, s:s + 1])
        k.ts("dve", rstd[:], ss[:], 1.0 / D, 1e-6, ALU.mult, ALU.add, ["ss"], ["rstd"])
        k.act(rstd[:], rstd[:], AF.Sqrt, ["rstd"], ["rstd"])
        k.emit("dve", lambda: nc.vector.reciprocal(rstd[:], rstd[:]), ["rstd"], ["rstd"])
        for s in range(4):
            xnk = f"xn{s%2}"
            k.ts1("dve", xn[s % 2][:], xb[:, s, :], rstd[:, s:s + 1], ALU.mult, [xk, "rstd"], [xnk])
            pk = f"pT{s%2}"
            for kc in range(8):
                k.tr(pT[s % 2][:, kc, :], xn[s % 2][:, kc * 128:(kc + 1) * 128], idb[:], [xnk, "idb"], [pk])
            k.cp("act", hT[:, :, s * 128:(s + 1) * 128], pT[s % 2][:], [pk], ["hT"])
        t0 = b * TB
        chunks = []
        for cc in range(4):
            chunks.append(("q", cc, cc * 128))
        for cc in range(4):
            chunks.append(("k", cc, 512 + cc * 128))
        for cc in range(14):
            chunks.append(("r", cc, 1536 + cc * 128))
        for cc in range(16):
            chunks.append(("g", cc, 1536 + 1792 + cc * 128))
        for (kind, cc, col) in chunks:
            i = evn[0]; evn[0] += 1
            pzi = pz[i % NPZ]; pk = f"pz{i%NPZ}"
            for kc in range(8):
                k.mm(pzi[:], wb[:, kc, col:col + 128], hT[:, kc, :], kc == 0, kc == 7, ["wb", "hT"], [pk])
            oi = i % NOB
            if kind == "q":
                k.act(ob16[oi][:], pzi[:], AF.Copy, [pk], [f"ob16_{oi}"], scale=0.125)
                if ocb:
                    k.dma("pool", ocb["q"](cc, t0), ob16[oi][:], reads=[f"ob16_{oi}"])
                else:
                    k.dma("pool", qT[cc * 128:(cc + 1) * 128, t0:t0 + TB], ob16[oi][:], reads=[f"ob16_{oi}"])
            elif kind == "k":
                k.cp("dve", ob16[oi][:], pzi[:], [pk], [f"ob16_{oi}"])
                if ocb:
                    k.dma("pool", ocb["k"](cc, t0), ob16[oi][:], reads=[f"ob16_{oi}"])
                else:
                    k.dma("pool", kT[cc * 128:(cc + 1) * 128, t0:t0 + TB], ob16[oi][:], reads=[f"ob16_{oi}"])
            elif kind == "r":
                e = "dve" if cc % 2 == 0 else "act"
                k.cp(e, ob32[oi][:], pzi[:], [pk], [f"ob32_{oi}"])
                if ocb:
                    for hq in range(2):
                        k.dma("pool", ocb["z"](cc, hq, t0), ob32[oi][hq * 64:(hq + 1) * 64, :], reads=[f"ob32_{oi}"])
                else:
                    k.dma("pool", zrT[cc * 128:(cc + 1) * 128, t0:t0 + TB], ob32[oi][:], reads=[f"ob32_{oi}"])
            else:
                k.act(ob16[oi][:], pzi[:], AF.Sigmoid, [pk], [f"ob16_{oi}"])
                k.dma("pool", gT[cc * 128:(cc + 1) * 128, t0:t0 + TB], ob16[oi][:], reads=[f"ob16_{oi}"])
        for s in range(4):
            i = evn[0]; evn[0] += 1
            pzi = pz[i % NPZ]; pk = f"pz{i%NPZ}"
            for kc in range(8):
                k.mm(pzi[:], hT[:, kc, s * 128:(s + 1) * 128], wb[:, kc, 1024:1536], kc == 0, kc == 7,
                     ["wb", "hT"], [pk])
            oi = i % NOB
            k.cp("dve", ob16[oi][:], pzi[:], [pk], [f"ob16_{oi}"])
            if ocb:
                for g_ in range(2):
                    k.dma("pool", ocb["v"](g_, t0 + s * 128), ob16[oi][:, g_ * 256:(g_ + 1) * 256], reads=[f"ob16_{oi}"])
            elif vsplit:
                for g_ in range(2):
                    k.dma("pool", vo[g_, t0 + s * 128:t0 + (s + 1) * 128, :], ob16[oi][:, g_ * 256:(g_ + 1) * 256], reads=[f"ob16_{oi}"])
            else:
                k.dma("pool", vo[t0 + s * 128:t0 + (s + 1) * 128, :], ob16[oi][:], reads=[f"ob16_{oi}"])


NEGM = -30000.0

def t5_bucket_np(n):
    n = np.maximum(n, 0)
    nf = np.maximum(n, 16).astype(np.float32)
    large = 16 + (np.log(nf / np.float32(16)) / np.float32(math.log(128 / 16)) * np.float32(16)).astype(np.int32)
    large = np.minimum(large, 31)
    return np.where(n < 16, n, large)

def toeplitz_idx():
    j = np.arange(128)[:, None]
    i = np.arange(256)[None, :]
    return t5_bucket_np(i - j)

def neg_mask():
    j = np.arange(128)[:, None]
    i = np.arange(256)[None, :]
    return np.where(i >= j, 0.0, NEGM).astype(np.float32)

def build_B(S, lam_init):
    nc = bass.Bass("TRN2", target_bir_lowering=False)
    ins = dict(
        qT=nc.dram_tensor("qT", [256, S], BF16, kind="ExternalInput").ap(),
        kT=nc.dram_tensor("kT", [256, S], BF16, kind="ExternalInput").ap(),
        v=nc.dram_tensor("v", [S, 256], BF16, kind="ExternalInput").ap(),
        G=nc.dram_tensor("G", [4, 128, 256], F32, kind="ExternalInput").ap(),
        b31=nc.dram_tensor("b31", [128, 4], F32, kind="ExternalInput").ap(),
        neg=nc.dram_tensor("neg", [128, 256], F32, kind="ExternalInput").ap(),
        lamv=nc.dram_tensor("lamv", [128, 4, 64], F32, kind="ExternalInput").ap(),
        sgain=nc.dram_tensor("sgain", [128, 128], F32, kind="ExternalInput").ap(),
        ident=nc.dram_tensor("ident", [128, 128], F32, kind="ExternalInput").ap(),
    )
    oaT = nc.dram_tensor("oaT", [256, S], BF16, kind="ExternalOutput").ap()
    dbg = nc.dram_tensor("dbg", [4, 128, 512], F32, kind="ExternalOutput").ap()
    k = KB(nc)
    with Alloc(nc) as al:
        emit_B(nc, k, al, S, lam_init, ins, oaT, dbg=dbg)
        k.finish("sp")
    return nc


def emit_B(nc, k, al, S, lam_init, ins, oaT, pfx="B", dbg=None):
    NJ = S // 128
    NG = S // 512
    VW = 136
    qs = al.sb(pfx + "qs", [128, 2, S], BF16)
    ks = al.sb(pfx + "ks", [128, 2, S], BF16)
    vaug = al.sb(pfx + "vaug", [128, NJ, 2, VW], BF16)
    Gs = al.sb(pfx + "Gs", [128, 4, 256], F32)
    negs = al.sb(pfx + "negs", [128, 256], F32)
    b31s = al.sb(pfx + "b31s", [128, 4], F32)
    Tb = al.sb(pfx + "Tb", [128, 4, 256], BF16)
    lamt = al.sb(pfx + "lamt", [128, 4, 64], F32)
    lprod = al.sb(pfx + "lprod", [128, 2, 64], F32)
    lsum = al.sb(pfx + "lsum", [128, 2], F32)
    lam = al.sb(pfx + "lam", [128, 1], F32)
    sg = al.sb(pfx + "sg", [128, 128], F32)
    idf = al.sb(pfx + "idf", [128, 128], F32)
    idb = al.sb(pfx + "idb", [128, 128], BF16)
    NB = 3
    PT = [al.sb(pfx + f"PT{i}", [128, 512], BF16) for i in range(NB)]
    rr = al.sb(pfx + "rr", [128, 2], F32)
    rl = al.sb(pfx + "rl", [128, 1], F32)
    t1 = al.sb(pfx + "t1", [128, 128], F32)
    ot = al.sb(pfx + "ot", [128, 128], F32)
    junk = al.sb(pfx + "junk", [128, 128], F32)
    ms = al.sb(pfx + "ms", [128, 1], F32)
    onb = al.sb(pfx + "onb", [128, 128], BF16)
    oT = [al.sb(pfx + f"oT{i}", [128, 512], BF16) for i in range(2)]
    ST = [al.ps(pfx + f"ST{i}", [128, 512], F32) for i in range(NB)]
    accb = al.ps(pfx + "accb", [128, 4, 512], F32)
    pTr = al.ps(pfx + "pTr", [128, 128], BF16)

    for hl in range(2):
        w = min(S // 2 if ins.get("qfn") else S, 2048)
        for part in range(S // w):
            qfn = ins.get("qfn") or (lambda r0, r1, a, b_: ins["qT"][r0:r1, a:b_])
            kfn = ins.get("kfn") or (lambda r0, r1, a, b_: ins["kT"][r0:r1, a:b_])
            k.dma("sp", qs[:, hl, part * w:(part + 1) * w], qfn(hl * 128, (hl + 1) * 128, part * w, (part + 1) * w), writes=["qs"])
            k.dma("sp", ks[:, hl, part * w:(part + 1) * w], kfn(hl * 128, (hl + 1) * 128, part * w, (part + 1) * w), writes=["ks"])
    JB = min(8, (S // 2) // 128) if ins.get("vfn") else 8
    for j0 in range(0, NJ, JB):
        j1 = min(NJ, j0 + JB)
        vsrc = ins["vfn"](j0 * 128, j1 * 128) if ins.get("vfn") else ins["v"][j0 * 128:j1 * 128, :]
        vv = vsrc.rearrange("(J p) (h d) -> p J h d", p=128, h=2)
        for h2 in range(2):
            k.dma("sp", vaug[:, j0:j1, h2, 0:128], vv[:, :, h2, :], writes=["vaug"])
    k.memset("pool", vaug[:, :, :, 128:129], 1.0, ["vaug"])
    k.dma("sp", Gs[:], ins["G"].rearrange("a p i -> p a i"), writes=["Gs"])
    k.dma("sp", negs[:], ins["neg"], writes=["negs"])
    k.dma("sp", b31s[:], ins["b31"], writes=["b31s"])
    k.dma("sp", lamt[:], ins["lamv"], writes=["lamt"])
    k.dma("sp", sg[:], ins["sgain"], writes=["sg"])
    k.dma("sp", idf[:], ins["ident"], writes=["idf"])
    k.cp("dve", idb[:], idf[:], ["idf"], ["idb"])
    for hc in range(4):
        k.ts1("dve", Gs[:, hc, :], Gs[:, hc, :], b31s[:, hc:hc + 1], ALU.subtract, ["Gs", "b31s"], ["Gs"])
        k.tt("dve", Tb[:, hc, :], Gs[:, hc, :], negs[:], ALU.add, ["Gs", "negs"], ["Tb"])
    k.tt("dve", lprod[:, 0, :], lamt[:, 0, :], lamt[:, 1, :], ALU.mult, ["lamt"], ["lprod"])
    k.tt("dve", lprod[:, 1, :], lamt[:, 2, :], lamt[:, 3, :], ALU.mult, ["lamt"], ["lprod"])
    k.emit("dve", lambda: nc.vector.reduce_sum(lsum[:], lprod[:], AX.X), ["lprod"], ["lsum"])
    k.act(lsum[:], lsum[:], AF.Exp, ["lsum"], ["lsum"])
    k.tt("dve", lam[:], lsum[:, 0:1], lsum[:, 1:2], ALU.subtract, ["lsum"], ["lam"])
    k.ts1("dve", lam[:], lam[:], float(lam_init), ALU.add, ["lam"], ["lam"])
    k.ts1("dve", sg[:], sg[:], float(1.0 - lam_init), ALU.mult, ["sg"], ["sg"])

    def acc(c, I):
        return accb[:, I, c * 256:c * 256 + 129]

    for hl in range(2):
        steps = [(g, J, c) for g in range(NG) for J in range(4 * g + 4) for c in range(2)]
        N = len(steps)

        def qk(n):
            g, J, c = steps[n]
            buf = n % NB
            r = J - 4 * g
            c0 = max(0, r) * 128
            a = 128 * r
            has_t = r >= -1
            k.mm(ST[buf][:, c0:512], ks[c * 64:(c + 1) * 64, hl, J * 128:(J + 1) * 128],
                 qs[c * 64:(c + 1) * 64, hl, g * 512 + c0:(g + 1) * 512], True, not has_t,
                 ["ks", "qs"], [f"ST{buf}"])
            if has_t:
                tc0 = max(a, 0); tc1 = min(a + 256, 512)
                i0 = tc0 - a
                k.mm(ST[buf][:, tc0:tc1], idb[:], Tb[:, hl * 2 + c, i0:i0 + (tc1 - tc0)], False, True,
                     ["idb", "Tb"], [f"ST{buf}"])

        def ex_pv(n):
            g, J, c = steps[n]
            buf = n % NB
            r = J - 4 * g
            c0 = max(0, r) * 128
            k.act(PT[buf][:, c0:512], ST[buf][:, c0:512], AF.Exp, [f"ST{buf}"], [f"PT{buf}"])
            for I in range(4):
                Ia = 4 * g + I
                if Ia >= J:
                    first = (J == 0 and c == 0)
                    wk = [f"acc0_{I}", f"acc1_{I}"] if first else [f"acc{c}_{I}"]
                    k.mm(acc(c, I), PT[buf][:, I * 128:(I + 1) * 128], vaug[:, J, hl, 0:129], first, J == Ia,
                         [f"PT{buf}", "vaug"], wk, skip_group_check=True)

        def fin(g):
            ob = oT[g % 2]; obk = f"oT{g%2}"
            for I in range(4):
                ak = [f"acc0_{I}", f"acc1_{I}"]
                k.emit("dve", lambda: nc.vector.reciprocal(rr[:, 0:1], accb[:, I, 128:129]), [ak[0]], ["rr"])
                k.emit("dve", lambda: nc.vector.reciprocal(rr[:, 1:2], accb[:, I, 256 + 128:256 + 129]), [ak[1]], ["rr"])
                k.tt("dve", rl[:], rr[:, 1:2], lam[:], ALU.mult, ["rr", "lam"], ["rl"])
                k.ts1("dve", t1[:], accb[:, I, 256:256 + 128], rl[:, 0:1], ALU.mult, [ak[1], "rl"], ["t1"])
                k.emit("dve", lambda: nc.vector.scalar_tensor_tensor(ot[:], accb[:, I, 0:128], rr[:, 0:1], t1[:], ALU.mult, ALU.subtract),
                       [ak[0], "rr", "t1"], ["ot"])
                k.act(junk[:], ot[:], AF.Square, ["ot"], ["junk", "ms"], accum_out=ms[:])
                k.ts("dve", ms[:], ms[:], 1.0 / 128, 1e-5, ALU.mult, ALU.add, ["ms"], ["ms"])
                k.act(ms[:], ms[:], AF.Sqrt, ["ms"], ["ms"])
                k.emit("dve", lambda: nc.vector.reciprocal(ms[:], ms[:]), ["ms"], ["ms"])
                k.emit("dve", lambda: nc.vector.scalar_tensor_tensor(onb[:], ot[:], ms[:, 0:1], sg[:], ALU.mult, ALU.mult),
                       ["ot", "ms", "sg"], ["onb"])
                if dbg is not None and hl == 0 and g == 0 and I == 0:
                    k.cp("dve", dt[:, 3, :], accb[:, 0, :], ["acc0_0", "acc1_0"], ["dbgt"])
                    k.cp("dve", dt[:, 2, 0:128], ot[:], ["ot"], ["dbgt"])
                    k.cp("dve", dt[:, 2, 128:256], onb[:], ["onb"], ["dbgt"])
                    k.dma("pool", dbg.rearrange("a p n -> p a n"), dt[:], reads=["dbgt"])
                k.tr(pTr[:], onb[:], idb[:], ["onb", "idb"], ["pTr"])
                k.cp("act", ob[:, I * 128:(I + 1) * 128], pTr[:], ["pTr"], [obk])
            ofn = ins.get("ofn") or (lambda r0, r1, a, b_: oaT[r0:r1, a:b_])
            k.dma("pool", ofn(hl * 128, (hl + 1) * 128, g * 512, (g + 1) * 512), ob[:], reads=[obk])

        LA = 2
        if dbg is not None and hl == 0:
            dt = al.sb(pfx + "dbgt", [128, 4, 512], F32)
            k.memset("pool", dt[:], 0.0, ["dbgt"])
            qk(0)
            k.cp("dve", dt[:, 0, :], ST[0][:], ["ST0"], ["dbgt"])
            k.act(dt[:, 1, :], ST[0][:], AF.Exp, ["ST0"], ["dbgt"])
            k.cp("dve", dt[:, 2, 0:256], Tb[:, 0, :], ["Tb"], ["dbgt"])
            k.cp("dve", dt[:, 2, 256:257], lam[:], ["lam"], ["dbgt"])
        for n in range(min(LA, N)):
            qk(n)
        for n in range(N):
            if n + LA < N:
                qk(n + LA)
            ex_pv(n)
            g, J, c = steps[n]
            if J == 4 * g + 3 and c == 1:
                fin(g)


CDEC = math.exp(-0.5)
GN_EPS = 64e-5

def consts_C():
    s = np.arange(128)[:, None]; t = np.arange(128)[None, :]
    c = dict(
        ident=np.eye(128, dtype=np.float32),
        triI=np.where(s <= t, -CDEC, 0.0).astype(np.float32),
        triE=np.where(s < t, -CDEC, 0.0).astype(np.float32),
        triR=np.where(s > t, -CDEC, 0.0).astype(np.float32),
        mstrict=(t > s).astype(np.float32),
        mask3=np.concatenate([(t >= s), (t > s), (t >= s)], 1).astype(np.float32),
        mlow=(s > t).astype(np.float32),
        bones=(s // 64 == t // 64).astype(np.float32),
    )
    return c

C_IN = [("zc", [1024, None], F32), ("mu", [128, 8], F32), ("w2c", [64, 256], F32), ("w0c", [1, 256], F32),
        ("a2c", [64, 256], F32), ("a0c", [128, 2], F32), ("g2c", [128, 256], F32), ("kkw", [128, 2], F32),
        ("ka", [128, 2], F32), ("rkb", [128, 2, 128], F32), ("lnw", [128, 2], F32), ("lnb", [128, 2], F32),
        ("ident", [128, 128], F32), ("triI", [128, 128], F32), ("triE", [128, 128], F32), ("triR", [128, 128], F32),
        ("mstrict", [128, 128], F32), ("mask3", [128, 384], F32), ("mlow", [128, 128], F32), ("bones", [128, 128], F32)]

def build_C(S):
    nc = bass.Bass("TRN2", target_bir_lowering=False)
    ins = {}
    for nm, shp, dt in C_IN:
        shp = [S if x is None else x for x in shp]
        ins[nm] = nc.dram_tensor(nm, shp, dt, kind="ExternalInput").ap()
    yT = nc.dram_tensor("yT", [256, S], BF16, kind="ExternalOutput").ap()
    k = KB(nc)
    with Alloc(nc) as al:
        emit_C(nc, k, al, S, ins, yT)
        k.finish("sp")
    return nc


def emit_C(nc, k, al, S, ins, yT, pfx="C", zrows=None):
    if zrows is None:
        zrows = [j * 128 for j in range(8)]
    NBLK = S // 512
    A = lambda nm, shp, dt=F32: al.sb(pfx + nm, shp, dt)
    P = lambda nm, shp, dt=F32: al.ps(pfx + nm, shp, dt)
    cs = {}
    for nm, shp, dt in C_IN[1:]:
        cs[nm] = A("c_" + nm, shp)
        k.dma("sp", cs[nm][:], ins[nm], writes=["c_" + nm])
    wa = A("wa", [128, 256])
    k.dma("sp", wa[0:64, :], ins["w2c"], writes=["wa"])
    k.dma("sp", wa[64:128, :], ins["a2c"], writes=["wa"])
    ones1 = A("ones1", [1, 128])
    k.memset("pool", ones1[:], 1.0, ["ones1"])
    omk = A("omk", [128, 2])
    k.ts("dve", omk[:], cs["ka"][:], -1.0, 1.0, ALU.mult, ALU.add, ["c_ka"], ["omk"])
    ident = cs["ident"]

    zin = [A(f"zin{i}", [128, 513]) for i in range(3)]
    dtmp = [A(f"dtmp{i}", [128, 512]) for i in range(2)]
    zs = A("zs", [128, 8, 512])
    tw = A("tw", [64, 512])
    sgx = A("sgx", [128, 512])
    aT = A("aT", [128, 2, 512])
    kkT = A("kkT", [128, 2, 512])
    kpT = A("kpT", [128, 2, 512])
    kkaT = A("kkaT", [128, 2, 512])
    bonT = A("bonT", [128, 2, 512])
    gTs = A("gTs", [128, 2, 512])
    tmpA = A("tmpA", [128, 512])
    tmpB = A("tmpB", [128, 512])
    sig = A("sig", [128, 256])
    Gi = A("Gi", [128, 2, 128]); iG = A("iG", [128, 2, 128]); Ge = A("Ge", [128, 2, 128])
    Gr = A("Gr", [128, 256])
    ARt = A("ARt", [128, 2, 2, 128])
    KtT = A("KtT", [128, 2, 128]); BtT = A("BtT", [128, 2, 128])
    Vtok = A("Vtok", [128, 256]); Khat = A("Khat", [128, 256]); Bhat = A("Bhat", [128, 256])
    ATs = A("ATs", [128, 4, 3, 128])
    PTs = [A(f"PTs{i}", [128, 4, 2, 128]) for i in range(2)]
    Qs = [A(f"Qs{i}", [128, 4, 128]) for i in range(2)]
    Hst = A("Hst", [128, 2, 64])
    X0 = A("X0", [128, 256]); Up = A("Up", [128, 256])
    Ysb = A("Ysb", [128, 4, 64]); Ysq = A("Ysq", [128, 4, 64])
    st1 = A("st1", [128, 4]); st2 = A("st2", [128, 4]); mean = A("mean", [128, 4]); rstd = A("rstd", [128, 4])
    msq = A("msq", [128, 4])
    Yn = A("Yn", [128, 256])
    fin = A("fin", [128, 2, 128])
    outb = [A(f"outb{i}", [128, 2, 512], BF16) for i in range(2)]
    SA = P("SA", [128, 4, 256]); SB = P("SB", [128, 4, 128])
    ATp = P("ATp", [128, 512]); Q0p = P("Q0p", [128, 4, 128])
    M = [P(f"M{i}", [128, 512]) for i in range(3)]
    mi = [0]
    def misc():
        i = mi[0] % 3; mi[0] += 1
        return M[i], f"M{i}"

    k.memset("pool", Hst[:], 0.0, ["Hst"])
    k.memset("pool", zin[0][:, 0:1], 0.0, ["zin0"])

    zi = [0]
    for b in range(NBLK):
        t0 = b * 512
        for j in range(8):
            zb = zin[zi[0] % 3]; zk = f"zin{zi[0]%3}"; zi[0] += 1
            zsplit = ins.get("zsplit", 0)
            if ins.get("zfn64"):
                zparts = [(slice(hq * 64, (hq + 1) * 64), (lambda a, b_, hq=hq: ins["zfn64"](zrows[j] + hq * 64, a, b_))) for hq in range(2)]
            else:
                zparts = [(slice(0, 128), (lambda a, b_: ins["zc"][zrows[j]:zrows[j] + 128, a:b_]))]
            if b == 0:
                k.memset("pool", zb[:, 0:1], 0.0, [zk])
            for (psl, zf_) in zparts:
                if b == 0:
                    k.dma("sp", zb[psl, 1:513], zf_(0, 512), writes=[zk])
                elif zsplit and t0 % zsplit == 0:
                    k.dma("sp", zb[psl, 0:1], zf_(t0 - 1, t0), writes=[zk], allow_slow_non_contiguous=True)
                    k.dma("sp", zb[psl, 1:513], zf_(t0, t0 + 512), writes=[zk])
                else:
                    k.dma("sp", zb[psl, 0:513], zf_(t0 - 1, t0 + 512), writes=[zk])
            e = "pool"
            dt_ = dtmp[j % 2]; dk = f"dtmp{j%2}"
            k.tt(e, dt_[:], zb[:, 0:512], zb[:, 1:513], ALU.subtract, [zk], [dk])
            k.emit("dve", lambda: nc.vector.scalar_tensor_tensor(zs[:, j, :], dt_[:], cs["mu"][:, j:j + 1], zb[:, 1:513], ALU.mult, ALU.add),
                   [dk, zk, "c_mu"], [f"zs{j}"])
        k.act(tw[:], zs[0:64, 6, :], AF.Tanh, ["zs6"], ["tw"])
        k.act(sgx[:], zs[:, 7, :], AF.Sigmoid, ["zs7"], ["sgx"])
        for hp in range(2):
            m, mk = misc()
            k.mm(m[:], wa[64:128, hp * 128:(hp + 1) * 128], zs[64:128, 6, :], True, True, ["wa", "zs6"], [mk])
            k.act(aT[:, hp, :], m[:], AF.Sigmoid, [mk, "c_a0c"], [f"aT{hp}"], bias=cs["a0c"][:, hp:hp + 1])
            k.ts1("pool", tmpA[:], zs[:, 2 + hp, :], cs["kkw"][:, hp:hp + 1], ALU.mult, [f"zs{2+hp}", "c_kkw"], ["tmpA"])
            k.tt("pool", tmpB[:], tmpA[:], tmpA[:], ALU.mult, ["tmpA"], ["tmpB"])
            m, mk = misc()
            k.mm(m[:], cs["bones"][:], tmpB[:], True, True, ["c_bones", "tmpB"], [mk])
            k.act(tmpB[:], m[:], AF.Sqrt, [mk], ["tmpB"])
            k.ts1("dve", tmpB[:], tmpB[:], 1e-12, ALU.max, ["tmpB"], ["tmpB"])
            k.emit("dve", lambda: nc.vector.reciprocal(tmpB[:], tmpB[:]), ["tmpB"], ["tmpB"])
            k.tt("dve", kkT[:, hp, :], tmpA[:], tmpB[:], ALU.mult, ["tmpA", "tmpB"], [f"kkT{hp}"])
            k.ts("dve", tmpA[:], aT[:, hp, :], cs["ka"][:, hp:hp + 1], omk[:, hp:hp + 1], ALU.mult, ALU.add,
                 [f"aT{hp}", "c_ka", "omk"], ["tmpA"])
            k.tt("pool", kpT[:, hp, :], tmpA[:], zs[:, 2 + hp, :], ALU.mult, ["tmpA", f"zs{2+hp}"], [f"kpT{hp}"])
            k.tt("pool", kkaT[:, hp, :], kkT[:, hp, :], aT[:, hp, :], ALU.mult, [f"kkT{hp}", f"aT{hp}"], [f"kkaT{hp}"])
            k.tt("pool", tmpA[:], zs[:, hp, :], kpT[:, hp, :], ALU.mult, [f"zs{hp}", f"kpT{hp}"], ["tmpA"])
            m, mk = misc()
            k.mm(m[:], cs["rkb"][:, hp, :], tmpA[:], True, True, ["c_rkb", "tmpA"], [mk])
            k.tt("dve", bonT[:, hp, :], m[:], zs[:, 4 + hp, :], ALU.mult, [mk, f"zs{4+hp}"], [f"bonT{hp}"])
            m, mk = misc()
            k.mm(m[:], cs["g2c"][:, hp * 128:(hp + 1) * 128], sgx[:], True, True, ["c_g2c", "sgx"], [mk])
            k.cp("act", gTs[:, hp, :], m[:], [mk], [f"gTs{hp}"])

        ob = outb[b % 2]; obk = f"outb{b%2}"
        for ch in range(4):
            cs0 = ch * 128
            csl = slice(cs0, cs0 + 128)
            m, mk = misc()
            k.mm(m[:, 0:256], tw[:, csl], wa[0:64, :], True, False, ["tw", "wa"], [mk])
            k.mm(m[:, 0:256], ones1[:], cs["w0c"][:], False, True, ["ones1", "c_w0c"], [mk])
            k.act(sig[:], m[:, 0:256], AF.Sigmoid, [mk], ["sig"])
            m, mk = misc()
            for hp in range(2):
                k.mm(m[:, hp * 128:(hp + 1) * 128], sig[:, hp * 128:(hp + 1) * 128], cs["triI"][:], hp == 0, False, ["sig", "c_triI"], [mk], skip_group_check=True)
            for hp in range(2):
                k.mm(m[:, 256 + hp * 128:256 + (hp + 1) * 128], sig[:, hp * 128:(hp + 1) * 128], cs["triE"][:], False, hp == 1, ["sig", "c_triE"], [mk], skip_group_check=True)
            k.act(Gi[:], m[:, 0:256], AF.Exp, [mk], ["Gi"])
            k.act(iG[:], m[:, 0:256], AF.Exp, [mk], ["iG"], scale=-1.0)
            k.act(Ge[:], m[:, 256:512], AF.Exp, [mk], ["Ge"])
            m, mk = misc()
            k.mm(m[:, 0:256], cs["triR"][:], sig[:], True, True, ["sig", "c_triR"], [mk])
            k.act(Gr[:], m[:, 0:256], AF.Exp, [mk], ["Gr"])
            for hp in range(2):
                k.emit("dve", lambda: nc.vector.scalar_tensor_tensor(ARt[:, hp, 0, :], kkT[:, hp, csl], -1.0, Ge[:, hp, :], ALU.mult, ALU.mult),
                       [f"kkT{hp}", "Ge"], ["ARt"])
                k.tt("pool", ARt[:, hp, 1, :], zs[:, hp, csl], Gi[:, hp, :], ALU.mult, [f"zs{hp}", "Gi"], ["ARt"])
                k.tt("dve", KtT[:, hp, :], kpT[:, hp, csl], iG[:, hp, :], ALU.mult, [f"kpT{hp}", "iG"], ["KtT"])
                k.tt("dve", BtT[:, hp, :], kkaT[:, hp, csl], iG[:, hp, :], ALU.mult, [f"kkaT{hp}", "iG"], ["BtT"])
            m, mk = misc()
            for hp in range(2):
                k.tr(m[:, hp * 128:(hp + 1) * 128], zs[:, 4 + hp, csl], ident[:], [f"zs{4+hp}", "c_ident"], [mk])
            k.cp("act", Vtok[:], m[:, 0:256], [mk], ["Vtok"])
            m, mk = misc()
            for hp in range(2):
                k.tr(m[:, hp * 128:(hp + 1) * 128], kpT[:, hp, csl], ident[:], [f"kpT{hp}", "c_ident"], [mk])
                k.tr(m[:, 256 + hp * 128:256 + (hp + 1) * 128], kkaT[:, hp, csl], ident[:], [f"kkaT{hp}", "c_ident"], [mk])
            k.tt("dve", Khat[:], m[:, 0:256], Gr[:], ALU.mult, [mk, "Gr"], ["Khat"])
            k.tt("dve", Bhat[:], m[:, 256:512], Gr[:], ALU.mult, [mk, "Gr"], ["Bhat"])
            for h in range(4):
                hp, e = h // 2, h % 2
                ps = slice(e * 64, (e + 1) * 64)
                k.mm(ATp[:, 0:256], BtT[ps, hp, :], ARt[ps, hp, :, :], True, False, ["BtT", "ARt"], ["ATp"], skip_group_check=True)
                k.mm(ATp[:, 256:512], KtT[ps, hp, :], ARt[ps, hp, :, :], False, True, ["KtT", "ARt"], ["ATp"], skip_group_check=True)
                k.mm(Q0p[:, h, :], ARt[ps, hp, 0, :], BtT[ps, hp, :], h == 0, h == 3, ["ARt", "BtT"], ["Q0p"], skip_group_check=True)
                k.tt("dve", PTs[0][:, h, 0, :], ATp[:, 0:128], cs["mstrict"][:], ALU.mult, ["ATp", "c_mstrict"], ["PTs0"])
                k.tt("dve", ATs[:, h, :, :], ATp[:, 128:512], cs["mask3"][:], ALU.mult, ["ATp", "c_mask3"], [f"ATs{h}"])
                k.cp("pool", PTs[0][:, h, 1, :], ident[:], ["c_ident"], ["PTs0"])
            for h in range(4):
                k.tt("dve", Qs[0][:, h, :], Q0p[:, h, :], cs["mlow"][:], ALU.mult, ["Q0p", "c_mlow"], ["Qs0"])
            for lv in range(7):
                cur, nxt = lv % 2, (lv + 1) % 2
                for h in range(4):
                    if lv < 6:
                        k.mm(SA[:, h, :], Qs[cur][:, h, :], PTs[cur][:, h, :, :], h % 2 == 0, h % 2 == 1,
                             [f"Qs{cur}", f"PTs{cur}"], ["SA"], skip_group_check=True)
                        k.mm(SB[:, h, :], PTs[cur][:, h, 0, :], Qs[cur][:, h, :], h == 0, h == 3,
                             [f"Qs{cur}", f"PTs{cur}"], ["SB"], skip_group_check=True)
                    else:
                        k.mm(SA[:, h, 128:256], Qs[cur][:, h, :], PTs[cur][:, h, 1, :], h % 2 == 0, h % 2 == 1,
                             [f"Qs{cur}", f"PTs{cur}"], ["SA"], skip_group_check=True)
                if lv < 6:
                    k.cp("act", PTs[nxt][:, :, 0, :], SA[:, :, 0:128], ["SA"], [f"PTs{nxt}"])
                    k.cp("act", Qs[nxt][:], SB[:], ["SB"], [f"Qs{nxt}"])
                k.tt("dve", PTs[nxt][:, :, 1, :], SA[:, :, 128:256], PTs[cur][:, :, 1, :], ALU.add, ["SA", f"PTs{cur}"], [f"PTs{nxt}"])
            TT = PTs[1]; TTk = "PTs1"
            m, mk = misc()
            for h in range(4):
                hp, e = h // 2, h % 2
                ps = slice(e * 64, (e + 1) * 64)
                k.mm(m[:, h * 64:(h + 1) * 64], ARt[ps, hp, 0, :], Hst[ps, hp, :], h == 0, False, ["ARt", "Hst"], [mk], skip_group_check=True)
                k.mm(m[:, h * 64:(h + 1) * 64], ATs[:, h, 1, :], Vtok[:, h * 64:(h + 1) * 64], False, h == 3, [f"ATs{h}", "Vtok"], [mk], skip_group_check=True)
            k.cp("act", X0[:], m[:, 0:256], [mk], ["X0"])
            m, mk = misc()
            for h in range(4):
                k.mm(m[:, h * 64:(h + 1) * 64], TT[:, h, 1, :], X0[:, h * 64:(h + 1) * 64], h == 0, h == 3, [TTk, "X0"], [mk], skip_group_check=True)
            k.cp("act", Up[:], m[:, 0:256], [mk], ["Up"])
            my, myk = misc()
            for h in range(4):
                hp, e = h // 2, h % 2
                ps = slice(e * 64, (e + 1) * 64)
                k.mm(my[:, h * 64:(h + 1) * 64], ARt[ps, hp, 1, :], Hst[ps, hp, :], h == 0, False, ["ARt", "Hst"], [myk], skip_group_check=True)
                k.mm(my[:, h * 64:(h + 1) * 64], ATs[:, h, 0, :], Up[:, h * 64:(h + 1) * 64], False, False, [f"ATs{h}", "Up"], [myk], skip_group_check=True)
                k.mm(my[:, h * 64:(h + 1) * 64], ATs[:, h, 2, :], Vtok[:, h * 64:(h + 1) * 64], False, h == 3, [f"ATs{h}", "Vtok"], [myk], skip_group_check=True)
            m, mk = misc()
            for hp in range(2):
                k.mm(m[:, hp * 128:(hp + 1) * 128], Bhat[:, hp * 128:(hp + 1) * 128], Up[:, hp * 128:(hp + 1) * 128], hp == 0, False, ["Bhat", "Up"], [mk], skip_group_check=True)
                k.mm(m[:, hp * 128:(hp + 1) * 128], Khat[:, hp * 128:(hp + 1) * 128], Vtok[:, hp * 128:(hp + 1) * 128], False, hp == 1, ["Khat", "Vtok"], [mk], skip_group_check=True)
            for h in range(4):
                hp, e = h // 2, h % 2
                ps = slice(e * 64, (e + 1) * 64)
                k.emit("dve", lambda: nc.vector.scalar_tensor_tensor(Hst[ps, hp, :], Hst[ps, hp, :], Gi[ps, hp, 127:128],
                                                                    m[ps, hp * 128 + e * 64:hp * 128 + (e + 1) * 64], ALU.mult, ALU.add),
                       ["Hst", "Gi", mk], ["Hst"])
            k.cp("act", Ysb[:], my[:, 0:256], [myk], ["Ysb"])
            k.emit("dve", lambda: nc.vector.reduce_sum(st1[:], Ysb[:], AX.X), ["Ysb"], ["st1"])
            k.tt("pool", Ysq[:], Ysb[:], Ysb[:], ALU.mult, ["Ysb"], ["Ysq"])
            k.emit("dve", lambda: nc.vector.reduce_sum(st2[:], Ysq[:], AX.X), ["Ysq"], ["st2"])
            k.ts1("dve", mean[:], st1[:], 1.0 / 64, ALU.mult, ["st1"], ["mean"])
            k.tt("dve", msq[:], mean[:], mean[:], ALU.mult, ["mean"], ["msq"])
            k.emit("dve", lambda: nc.vector.scalar_tensor_tensor(rstd[:], st2[:], 1.0 / 64, msq[:], ALU.mult, ALU.subtract), ["st2", "msq"], ["rstd"])
            k.ts1("dve", rstd[:], rstd[:], GN_EPS, ALU.add, ["rstd"], ["rstd"])
            k.act(rstd[:], rstd[:], AF.Sqrt, ["rstd"], ["rstd"])
            k.emit("dve", lambda: nc.vector.reciprocal(rstd[:], rstd[:]), ["rstd"], ["rstd"])
            for h in range(4):
                e_ = "dve" if h % 2 == 0 else "pool"
                k.ts(e_, Yn[:, h * 64:(h + 1) * 64], Ysb[:, h, :], mean[:, h:h + 1], rstd[:, h:h + 1], ALU.subtract, ALU.mult,
                     ["Ysb", "mean", "rstd"], ["Yn"])
            m, mk = misc()
            for hp in range(2):
                k.tr(m[:, hp * 128:(hp + 1) * 128], Yn[:, hp * 128:(hp + 1) * 128], ident[:], ["Yn", "c_ident"], [mk])
            for hp in range(2):
                k.act(fin[:, hp, :], m[:, hp * 128:(hp + 1) * 128], AF.Identity, [mk, "c_lnw", "c_lnb"], ["fin"],
                      bias=cs["lnb"][:, hp:hp + 1], scale=cs["lnw"][:, hp:hp + 1])
                k.tt("pool", fin[:, hp, :], fin[:, hp, :], bonT[:, hp, csl], ALU.add, ["fin", f"bonT{hp}"], ["fin"])
                k.tt("pool", ob[:, hp, csl], fin[:, hp, :], gTs[:, hp, csl], ALU.mult, ["fin", f"gTs{hp}"], [obk])
        for hp in range(2):
            yfn = ins.get("yfn") or (lambda r0, r1, a, b_: yT[r0:r1, a:b_])
            k.dma("pool", yfn(hp * 128, (hp + 1) * 128, t0, t0 + 512), ob[:, hp, :], reads=[obk])


WST = 2048
D = 1024
DFF = 2816
NF = DFF // 128
TB = 512

def rmsnorm_T(nc, k, xb, xk, hT, ss, rstd, junk, xn, pT, idb, eps, pfx):
    for s in range(4):
        k.act(junk[:], xb[:, s, :], AF.Square, [xk], [pfx + "junk", pfx + "ss"], accum_out=ss[:, s:s + 1])
    k.ts("dve", rstd[:], ss[:], 1.0 / D, eps, ALU.mult, ALU.add, [pfx + "ss"], [pfx + "rstd"])
    k.act(rstd[:], rstd[:], AF.Sqrt, [pfx + "rstd"], [pfx + "rstd"])
    k.emit("dve", lambda: nc.vector.reciprocal(rstd[:], rstd[:]), [pfx + "rstd"], [pfx + "rstd"])
    for s in range(4):
        xnk = pfx + f"xn{s%2}"
        k.ts1("dve", xn[s % 2][:], xb[:, s, :], rstd[:, s:s + 1], ALU.mult, [xk, pfx + "rstd"], [xnk])
        pk = pfx + f"pT{s%2}"
        for kc in range(8):
            k.tr(pT[s % 2][:, kc, :], xn[s % 2][:, kc * 128:(kc + 1) * 128], idb[:], [xnk, pfx + "idb"], [pk])
        k.cp("act", hT[:, :, s * 128:(s + 1) * 128], pT[s % 2][:], [pk], [pfx + "hT"])


def load_w_bf16(nc, k, dst, dkey, src, K, N, wst, wstk, scale_ap=None, skey=None):
    wv = src.rearrange("(kc p) n -> p kc n", p=128)
    step = WST // K
    i = 0
    for c0 in range(0, N, step):
        cw = min(step, N - c0)
        st = wst[i % 2]; sk = wstk + str(i % 2); i += 1
        stv = st[:, 0:K * cw].rearrange("p (kc n) -> p kc n", kc=K)
        k.dma("sp", stv, wv[:, :, c0:c0 + cw], writes=[sk])
        for kc in range(K):
            e = "dve" if kc % 2 == 0 else "pool"
            if scale_ap is not None:
                k.ts1(e, dst[:, kc, c0:c0 + cw], stv[:, kc, :], scale_ap[:, kc:kc + 1], ALU.mult, [sk, skey], [dkey])
            else:
                k.cp(e, dst[:, kc, c0:c0 + cw], stv[:, kc, :], [sk], [dkey])


def build_D1(NT):
    nc = bass.Bass("TRN2", target_bir_lowering=False)
    I = lambda nm, shp, dt=F32: nc.dram_tensor(nm, shp, dt, kind="ExternalInput").ap()
    ins = dict(x=I("x", [NT, D]), oaT=I("oaT", [512, NT], BF16), yrT=I("yrT", [512, NT], BF16), gT=I("gT", [2048, NT], BF16),
               woa=I("woa", [512, D]), wor=I("wor", [512, D]), wo=I("wo", [D, D]))
    xo = nc.dram_tensor("xo", [NT, D], F32, kind="ExternalOutput").ap()
    k = KB(nc)
    with Alloc(nc) as al:
        emit_D1(nc, k, al, NT, ins, xo)
        k.finish("sp")
    return nc


def emit_D1(nc, k, al, NT, ins, xo, pfx="E"):
    A = lambda nm, shp, dt=F32: al.sb(pfx + nm, shp, dt)
    P = lambda nm, shp, dt=F32: al.ps(pfx + nm, shp, dt)
    woa = A("woa", [128, 4, D], BF16); wor = A("wor", [128, 4, D], BF16); wo = A("wo", [128, 8, D], BF16)
    wst = [A(f"wst{i}", [128, WST]) for i in range(2)]
    load_w_bf16(nc, k, woa, pfx + "woa", ins["woa"], 4, D, wst, pfx + "wst")
    load_w_bf16(nc, k, wor, pfx + "wor", ins["wor"], 4, D, wst, pfx + "wst")
    load_w_bf16(nc, k, wo, pfx + "wo", ins["wo"], 8, D, wst, pfx + "wst")
    oab = [A(f"oab{i}", [128, 4, TB], BF16) for i in range(2)]
    yrb = [A(f"yrb{i}", [128, 4, TB], BF16) for i in range(2)]
    xb = [A(f"xb{i}", [128, 4, D]) for i in range(2)]
    gab = [A(f"gab{i}", [128, TB], BF16) for i in range(3)]
    grb = [A(f"grb{i}", [128, TB], BF16) for i in range(3)]
    t1 = [A(f"t1_{i}", [128, TB]) for i in range(2)]
    t2 = [A(f"t2_{i}", [128, TB]) for i in range(2)]
    mT = A("mT", [128, 8, TB], BF16)
    pa = [P(f"pa{i}", [128, 512]) for i in range(2)]
    pr = [P(f"pr{i}", [128, 512]) for i in range(2)]
    po = [P(f"po{i}", [128, 512]) for i in range(3)]
    nblk = NT // TB
    oav = ins["oaT"].rearrange("(kc p) t -> p kc t", p=128) if ins.get("oaT") is not None else None
    yrv = ins["yrT"].rearrange("(kc p) t -> p kc t", p=128) if ins.get("yrT") is not None else None
    xv = ins["x"].rearrange("(b s p) d -> b p s d", p=128, s=4)
    xov = xo.rearrange("(b s p) d -> b p s d", p=128, s=4)

    def load_blk(b):
        t0 = b * TB
        if ins.get("oafn"):
            for kc in range(4):
                k.dma("sp", oab[b % 2][:, kc, :], ins["oafn"](kc, t0, t0 + TB), writes=[pfx + f"oab{b%2}"])
                k.dma("sp", yrb[b % 2][:, kc, :], ins["yrfn"](kc, t0, t0 + TB), writes=[pfx + f"yrb{b%2}"])
        else:
            k.dma("sp", oab[b % 2][:], oav[:, :, t0:t0 + TB], writes=[pfx + f"oab{b%2}"])
            k.dma("sp", yrb[b % 2][:], yrv[:, :, t0:t0 + TB], writes=[pfx + f"yrb{b%2}"])
        k.dma("sp", xb[b % 2][:], xv[b], writes=[pfx + f"xb{b%2}"])
    load_blk(0)
    gi = [0]; oi = [0]
    for b in range(nblk):
        t0 = b * TB
        if b + 1 < nblk:
            load_blk(b + 1)
        for oc in range(8):
            i = gi[0]; gi[0] += 1
            ga = gab[i % 3]; gr = grb[i % 3]
            k.dma("sp", ga[:], ins["gT"][oc * 128:(oc + 1) * 128, t0:t0 + TB], writes=[pfx + f"gab{i%3}"])
            k.dma("sp", gr[:], ins["gT"][1024 + oc * 128:1024 + (oc + 1) * 128, t0:t0 + TB], writes=[pfx + f"grb{i%3}"])
            pai = pa[i % 2]; pri = pr[i % 2]
            for kc in range(4):
                k.mm(pai[:], woa[:, kc, oc * 128:(oc + 1) * 128], oab[b % 2][:, kc, :], kc == 0, kc == 3,
                     [pfx + "woa", pfx + f"oab{b%2}"], [pfx + f"pa{i%2}"])
            for kc in range(4):
                k.mm(pri[:], wor[:, kc, oc * 128:(oc + 1) * 128], yrb[b % 2][:, kc, :], kc == 0, kc == 3,
                     [pfx + "wor", pfx + f"yrb{b%2}"], [pfx + f"pr{i%2}"])
            k.tt("dve", t1[i % 2][:], pai[:], ga[:], ALU.mult, [pfx + f"pa{i%2}", pfx + f"gab{i%3}"], [pfx + f"t1_{i%2}"])
            k.tt("dve", t2[i % 2][:], pri[:], gr[:], ALU.mult, [pfx + f"pr{i%2}", pfx + f"grb{i%3}"], [pfx + f"t2_{i%2}"])
            k.tt("pool", mT[:, oc, :], t1[i % 2][:], t2[i % 2][:], ALU.add, [pfx + f"t1_{i%2}", pfx + f"t2_{i%2}"], [pfx + "mT"])
        for s in range(4):
            for n in range(2):
                j = oi[0]; oi[0] += 1
                pj = po[j % 3]
                for kc in range(8):
                    k.mm(pj[:], mT[:, kc, s * 128:(s + 1) * 128], wo[:, kc, n * 512:(n + 1) * 512], kc == 0, kc == 7,
                         [pfx + "mT", pfx + "wo"], [pfx + f"po{j%3}"])
                k.tt("dve", xb[b % 2][:, s, n * 512:(n + 1) * 512], pj[:], xb[b % 2][:, s, n * 512:(n + 1) * 512], ALU.add,
                     [pfx + f"po{j%3}", pfx + f"xb{b%2}"], [pfx + f"xb{b%2}"])
        k.dma("pool", xov[b], xb[b % 2][:], reads=[pfx + f"xb{b%2}"])


def build_D2(NT, last):
    nc = bass.Bass("TRN2", target_bir_lowering=False)
    I = lambda nm, shp, dt=F32: nc.dram_tensor(nm, shp, dt, kind="ExternalInput").ap()
    ins = dict(x=I("x", [NT, D]), p=I("p", [NT, 256]), wg=I("wg", [D, DFF]), wu=I("wu", [D, DFF]), wd=I("wd", [DFF, D]),
               wple=I("wple", [256, D]), wpg=I("wpg", [D, D]), gffn=I("gffn", [128, 8]), gple=I("gple", [128, 8]),
               gfin=I("gfin", [128, D]), ident=I("ident", [128, 128]))
    xo = nc.dram_tensor("xo", [NT, D], F32, kind="ExternalOutput").ap()
    wgS = nc.dram_tensor("wgS", [NF, 128, 1024], BF16).ap()
    wuS = nc.dram_tensor("wuS", [NF, 128, 1024], BF16).ap()
    k = KB(nc)
    with Alloc(nc) as al:
        emit_D2(nc, k, al, NT, last, ins, xo, wgS, wuS)
        k.finish("sp")
    return nc


def emit_D2(nc, k, al, NT, last, ins, xo, wgS, wuS, pfx="F"):
    A = lambda nm, shp, dt=F32: al.sb(pfx + nm, shp, dt)
    P = lambda nm, shp, dt=F32: al.ps(pfx + nm, shp, dt)
    gffn = A("gffn", [128, 8]); gple = A("gple", [128, 8]); idf = A("idf", [128, 128]); idb = A("idb", [128, 128], BF16)
    k.dma("sp", gffn[:], ins["gffn"], writes=[pfx + "gffn"])
    k.dma("sp", gple[:], ins["gple"], writes=[pfx + "gple"])
    k.dma("sp", idf[:], ins["ident"], writes=[pfx + "idf"])
    k.cp("dve", idb[:], idf[:], [pfx + "idf"], [pfx + "idb"])
    if last:
        gfin = A("gfin", [128, D])
        k.dma("sp", gfin[:], ins["gfin"], writes=[pfx + "gfin"])
    wd = A("wd", [128, NF, D], BF16); wple = A("wple", [128, 2, D], BF16); wpg = A("wpg", [128, 8, D], BF16)
    wst = [A(f"wst{i}", [128, WST]) for i in range(2)]
    load_w_bf16(nc, k, wple, pfx + "wple", ins["wple"], 2, D, wst, pfx + "wst")
    load_w_bf16(nc, k, wpg, pfx + "wpg", ins["wpg"], 8, D, wst, pfx + "wst", gple, pfx + "gple")
    wdv = ins["wd"].rearrange("(f p) n -> p f n", p=128)
    i = 0
    for f0 in range(0, NF, 2):
        fw_ = min(2, NF - f0)
        st = wst[i % 2]; sk = pfx + f"wst{i%2}"; i += 1
        stv = st[:, 0:fw_ * D].rearrange("p (f n) -> p f n", f=fw_)
        k.dma("sp", stv, wdv[:, f0:f0 + fw_, :], writes=[sk])
        for f in range(fw_):
            e = "dve" if f % 2 == 0 else "pool"
            k.cp(e, wd[:, f0 + f, :], stv[:, f, :], [sk], [pfx + "wd"])
    wcb = [A(f"wcb{i}", [128, 8, 256], BF16) for i in range(2)]
    ci = 0
    for (src, dstS, nm) in ((ins["wg"], wgS, "wgS"), (ins["wu"], wuS, "wuS")):
        wv = src.rearrange("(kc p) n -> p kc n", p=128)
        for c0 in range(0, DFF, 256):
            cw = min(256, DFF - c0)
            st = wst[i % 2]; sk = pfx + f"wst{i%2}"; i += 1
            stv = st[:, 0:8 * cw].rearrange("p (kc n) -> p kc n", kc=8)
            k.dma("sp", stv, wv[:, :, c0:c0 + cw], writes=[sk])
            cb = wcb[ci % 2]; cbk = pfx + f"wcb{ci%2}"; ci += 1
            for kc in range(8):
                e = "dve" if kc % 2 == 0 else "pool"
                k.ts1(e, cb[:, kc, 0:cw], stv[:, kc, :], gffn[:, kc:kc + 1], ALU.mult, [sk, pfx + "gffn"], [cbk])
            for j in range(cw // 128):
                f = c0 // 128 + j
                k.dma("pool", dstS[f].rearrange("p (kc n) -> p kc n", kc=8), cb[:, :, j * 128:(j + 1) * 128], reads=[cbk], writes=[nm])

    xb = [A(f"xb{i}", [128, 4, D]) for i in range(2)]
    pb = [A(f"pb{i}", [128, 4, 256]) for i in range(2)]
    junk = A("junk", [128, D], BF16)
    ss = A("ss", [128, 4]); rstd = A("rstd", [128, 4])
    xn = [A(f"xn{i}", [128, D], BF16) for i in range(2)]
    hT = A("hT", [128, 8, TB], BF16)
    ppT = A("ppT", [128, 2, TB], BF16)
    aT = A("aT", [128, NF, TB], BF16)
    NWB = 3
    wgb = [A(f"wgb{i}", [128, 8, 128], BF16) for i in range(NWB)]
    wub = [A(f"wub{i}", [128, 8, 128], BF16) for i in range(NWB)]
    sl = [A(f"sl{i}", [128, TB]) for i in range(2)]
    tmp = [A(f"tmp{i}", [128, 512]) for i in range(2)]
    pT = [P(f"pT{i}", [128, 8, 128], BF16) for i in range(2)]
    pg = [P(f"pg{i}", [128, 512]) for i in range(2)]
    pu = [P(f"pu{i}", [128, 512]) for i in range(2)]
    pd = [P(f"pd{i}", [128, 512]) for i in range(2)]
    nblk = NT // TB
    xv = ins["x"].rearrange("(b s p) d -> b p s d", p=128, s=4)
    pv = ins["p"].rearrange("(b s p) d -> b p s d", p=128, s=4)
    xov = xo.rearrange("(b s p) d -> b p s d", p=128, s=4)

    def load_blk(b):
        k.dma("sp", xb[b % 2][:], xv[b], writes=[pfx + f"xb{b%2}"])
        k.dma("sp", pb[b % 2][:], pv[b], writes=[pfx + f"pb{b%2}"])
    load_blk(0)
    wi = [0]; di = [0]
    for b in range(nblk):
        if b + 1 < nblk:
            load_blk(b + 1)
        X = xb[b % 2]; xk = pfx + f"xb{b%2}"
        rmsnorm_T(nc, k, X, xk, hT, ss, rstd, junk, xn, pT, idb, 1e-6, pfx)
        def load_w(f):
            i = f % NWB
            k.dma("sp", wgb[i][:], wgS[f].rearrange("p (kc n) -> p kc n", kc=8), reads=["wgS"], writes=[pfx + f"wgb{i}"])
            k.dma("sp", wub[i][:], wuS[f].rearrange("p (kc n) -> p kc n", kc=8), reads=["wuS"], writes=[pfx + f"wub{i}"])
        load_w(0); load_w(1)
        for f in range(NF):
            if f + 2 < NF:
                load_w(f + 2)
            i = f % NWB
            j = wi[0]; wi[0] += 1
            for kc in range(8):
                k.mm(pg[j % 2][:], wgb[i][:, kc, :], hT[:, kc, :], kc == 0, kc == 7, [pfx + f"wgb{i}", pfx + "hT"], [pfx + f"pg{j%2}"])
            for kc in range(8):
                k.mm(pu[j % 2][:], wub[i][:, kc, :], hT[:, kc, :], kc == 0, kc == 7, [pfx + f"wub{i}", pfx + "hT"], [pfx + f"pu{j%2}"])
            k.act(sl[j % 2][:], pg[j % 2][:], AF.Silu, [pfx + f"pg{j%2}"], [pfx + f"sl{j%2}"])
            k.tt("dve", aT[:, f, :], pu[j % 2][:], sl[j % 2][:], ALU.mult, [pfx + f"pu{j%2}", pfx + f"sl{j%2}"], [pfx + "aT"])
        for s in range(4):
            for n in range(2):
                j = di[0]; di[0] += 1
                for f in range(NF):
                    k.mm(pd[j % 2][:], aT[:, f, s * 128:(s + 1) * 128], wd[:, f, n * 512:(n + 1) * 512], f == 0, f == NF - 1,
                         [pfx + "aT", pfx + "wd"], [pfx + f"pd{j%2}"])
                k.tt("dve", X[:, s, n * 512:(n + 1) * 512], pd[j % 2][:], X[:, s, n * 512:(n + 1) * 512], ALU.add,
                     [pfx + f"pd{j%2}", xk], [xk])
        rmsnorm_T(nc, k, X, xk, hT, ss, rstd, junk, xn, pT, idb, 1e-6, pfx)
        PB = pb[b % 2]; pbk = pfx + f"pb{b%2}"
        for s in range(4):
            k.cp("pool", xn[s % 2][:, 0:256], PB[:, s, :], [pbk], [pfx + f"xn{s%2}"])
            for kc in range(2):
                k.tr(pT[s % 2][:, kc, :], xn[s % 2][:, kc * 128:(kc + 1) * 128], idb[:], [pfx + f"xn{s%2}", pfx + "idb"], [pfx + f"pT{s%2}"])
            k.cp("act", ppT[:, :, s * 128:(s + 1) * 128], pT[s % 2][:, 0:2, :], [pfx + f"pT{s%2}"], [pfx + "ppT"])
        for s in range(4):
            for n in range(2):
                j = di[0]; di[0] += 1
                for kc in range(8):
                    k.mm(pg[j % 2][:], hT[:, kc, s * 128:(s + 1) * 128], wpg[:, kc, n * 512:(n + 1) * 512], kc == 0, kc == 7,
                         [pfx + "hT", pfx + "wpg"], [pfx + f"pg{j%2}"])
                for kc in range(2):
                    k.mm(pu[j % 2][:], ppT[:, kc, s * 128:(s + 1) * 128], wple[:, kc, n * 512:(n + 1) * 512], kc == 0, kc == 1,
                         [pfx + "ppT", pfx + "wple"], [pfx + f"pu{j%2}"])
                k.act(tmp[j % 2][:], pg[j % 2][:], AF.Sigmoid, [pfx + f"pg{j%2}"], [pfx + f"tmp{j%2}"])
                k.tt("dve", tmp[j % 2][:], pu[j % 2][:], tmp[j % 2][:], ALU.mult, [pfx + f"pu{j%2}", pfx + f"tmp{j%2}"], [pfx + f"tmp{j%2}"])
                k.tt("pool", X[:, s, n * 512:(n + 1) * 512], X[:, s, n * 512:(n + 1) * 512], tmp[j % 2][:], ALU.add,
                     [xk, pfx + f"tmp{j%2}"], [xk])
        if last:
            for s in range(4):
                k.act(junk[:], X[:, s, :], AF.Square, [xk], [pfx + "junk", pfx + "ss"], accum_out=ss[:, s:s + 1])
            k.ts("dve", rstd[:], ss[:], 1.0 / D, 1e-6, ALU.mult, ALU.add, [pfx + "ss"], [pfx + "rstd"])
            k.act(rstd[:], rstd[:], AF.Sqrt, [pfx + "rstd"], [pfx + "rstd"])
            k.emit("dve", lambda: nc.vector.reciprocal(rstd[:], rstd[:]), [pfx + "rstd"], [pfx + "rstd"])
            for s in range(4):
                k.emit("dve", lambda: nc.vector.scalar_tensor_tensor(X[:, s, :], X[:, s, :], rstd[:, s:s + 1], gfin[:], ALU.mult, ALU.mult),
                       [xk, pfx + "rstd", pfx + "gfin"], [xk])
        k.dma("pool", xov[b], X[:], reads=[xk])

PAIRS = [[0, 1], [2, 3], [4, 5], [6, 7]]
CPAR = [e for e in C_IN[1:12]]
CCON = [e for e in C_IN[12:]]
DS = bass.DynSlice


def build_fused8(HALF, depth):
    S = 2 * HALF
    nc = bass.Bass("TRN2", target_bir_lowering=False)
    I = lambda nm, shp, dt=F32: nc.dram_tensor(nm, list(shp), dt, kind="ExternalInput").ap()
    Sc = lambda nm, shp, dt: nc.dram_tensor(nm, list(shp), dt).ap()
    x = I("x", [HALF, D]); p = I("p", [depth, HALF, 256])
    w_in = I("w_in", [depth, D, INC]); gam = I("gam", [depth, 128, 8]); ident = I("ident", [128, 128])
    G = I("G", [4, 128, 256]); b31 = I("b31", [128, 4]); neg = I("neg", [128, 256])
    lamv = I("lamv", [depth, 128, 4, 64]); sgain = I("sgain", [depth, 128, 128])
    cpar = {nm: I("c_" + nm, [depth] + list(shp)) for nm, shp, dt in CPAR}
    ccon = {nm: (ident if nm == "ident" else I("k_" + nm, shp)) for nm, shp, dt in CCON}
    woa = I("woa", [depth, 512, D]); wor = I("wor", [depth, 512, D]); wo = I("wo", [depth, D, D])
    wg = I("wg", [depth, D, DFF]); wu = I("wu", [depth, D, DFF]); wd = I("wd", [depth, DFF, D])
    wple = I("wple", [depth, 256, D]); wpg = I("wpg", [depth, D, D])
    gffn = I("gffn", [depth, 128, 8]); gple = I("gple", [depth, 128, 8]); gfin = I("gfin", [128, D])
    out = nc.dram_tensor("out", [HALF, D], F32, kind="ExternalOutput").ap()
    PV = min(2048, HALF)
    NPV = HALF // PV
    gT = Sc("s_gT", [2048, HALF], BF16)
    xmid = Sc("s_xmid", [HALF, D], F32); x1 = Sc("s_x1", [HALF, D], F32)
    wgS = Sc("s_wgS", [NF, 128, 1024], BF16); wuS = Sc("s_wuS", [NF, 128, 1024], BF16)
    k = KB(nc)
    pr_p = nc.gpsimd.partition_id() % 2
    pr_s = nc.sync.partition_id() % 2
    DS1 = lambda v: DS(v, 1)

    def sel(t, v):
        return t[DS1(v)].rearrange("a r c -> (a r) c")

    def exch(tag, P_, F_, rows, cols, dt, static_src=False):
        stg = Sc(f"x_stg_{tag}", [rows, cols], dt)
        rcv = Sc(f"x_rcv_{tag}", [2 * rows, cols], dt)
        rcv3 = rcv.rearrange("(a r) c -> a r c", a=2)
        if static_src:
            k.dma("pool", stg, P_, writes=[f"stg{tag}"])
            k.dma("sp", sel(F_, pr_s), P_)
        else:
            k.dma("pool", stg, sel(P_, 1 - pr_p), writes=[f"stg{tag}"])
            k.dma("sp", sel(F_, pr_s), sel(P_, pr_s))
        k.coll("AllGather", stg, rcv, PAIRS, reads=[f"stg{tag}"], writes=[f"rcv{tag}"])
        k.dma("sp", sel(F_, 1 - pr_s), sel(rcv3, 1 - pr_s), reads=[f"rcv{tag}"])

    for i in range(depth):
        last = (i == depth - 1)
        xin = x if i == 0 else x1
        lam_init = 0.8 - 0.6 * math.exp(-0.3 * i)
        L = f"L{i}"
        QP = [Sc(f"{L}QP{b_}", [2, 128, HALF], BF16) for b_ in range(2)]
        KP = [Sc(f"{L}KP{b_}", [2, 128, HALF], BF16) for b_ in range(2)]
        VP = [Sc(f"{L}VP{b_}", [2, PV, 256], BF16) for b_ in range(NPV)]
        ZP = [[Sc(f"{L}ZP{j}_{q}", [2, 64, HALF], F32) for q in range(4)] for j in range(3)]
        ZL = [Sc(f"{L}ZL{q}", [64, HALF], F32) for q in range(4)]
        QF = [Sc(f"{L}QF{b_}", [2, 128, HALF], BF16) for b_ in range(2)]
        KF = [Sc(f"{L}KF{b_}", [2, 128, HALF], BF16) for b_ in range(2)]
        VF = [Sc(f"{L}VF{b_}", [2, PV, 256], BF16) for b_ in range(NPV)]
        ZF = [Sc(f"{L}ZF{r}", [2, 64, HALF], F32) for r in range(16)]
        OP = [Sc(f"{L}OP{b_}", [2, 128, HALF], BF16) for b_ in range(2)]
        YP = [Sc(f"{L}YP{b_}", [2, 128, HALF], BF16) for b_ in range(2)]
        OAF = [Sc(f"{L}OAF{b_}", [2, 128, HALF], BF16) for b_ in range(2)]
        YF = [Sc(f"{L}YF{b_}", [2, 128, HALF], BF16) for b_ in range(2)]

        def zcb(cc, hq, t0):
            if cc < 12:
                j, g_, rb = cc // 4, (cc % 4) // 2, cc % 2
                return ZP[j][rb * 2 + hq][g_, :, t0:t0 + TB]
            return ZL[(cc - 12) * 2 + hq][:, t0:t0 + TB]
        ocb = dict(q=lambda cc, t0: QP[cc % 2][cc // 2, :, t0:t0 + TB],
                   k=lambda cc, t0: KP[cc % 2][cc // 2, :, t0:t0 + TB],
                   v=lambda g_, t: VP[t // PV][g_, t % PV:t % PV + 128, :],
                   z=zcb)
        with Alloc(nc) as al:
            emit_A(nc, k, al, HALF, xin, w_in[i], gam[i], ident, None, None, None, None, gT, pfx=f"A{i}", ocb=ocb)
        k.barrier()
        for b_ in range(2):
            exch(f"{L}q{b_}", QP[b_], QF[b_], 128, HALF, BF16)
            exch(f"{L}k{b_}", KP[b_], KF[b_], 128, HALF, BF16)
        for b_ in range(NPV):
            exch(f"{L}v{b_}", VP[b_], VF[b_], PV, 256, BF16)
        for j in range(3):
            for q in range(4):
                exch(f"{L}z{j}_{q}", ZP[j][q], ZF[j * 4 + q], 64, HALF, F32)
        for q in range(4):
            exch(f"{L}zl{q}", ZL[q], ZF[12 + q], 64, HALF, F32, static_src=True)
        k.barrier()

        def hm(T_):
            def fn(r0, r1, a, b_):
                h = a // HALF
                assert (b_ - 1) // HALF == h and r1 - r0 == 128
                return T_[r0 // 128][h, :, a - h * HALF:b_ - h * HALF]
            return fn

        def vfn(a, b_):
            h = a // HALF
            tl = a - h * HALF
            assert (b_ - 1) // HALF == h and tl // PV == (tl + (b_ - a) - 1) // PV
            return VF[tl // PV][h, tl % PV:tl % PV + (b_ - a), :]

        def zfn64(r0, a, b_):
            h = a // HALF
            assert (b_ - 1) // HALF == h
            return ZF[r0 // 64][h, :, a - h * HALF:b_ - h * HALF]
        with Alloc(nc) as al:
            insB = dict(qfn=hm(QF), kfn=hm(KF), ofn=hm(OP), vfn=vfn, G=G, b31=b31, neg=neg,
                        lamv=lamv[i], sgain=sgain[i], ident=ident)
            emit_B(nc, k, al, S, lam_init, insB, None, pfx=f"B{i}")
        k.barrier()
        with Alloc(nc) as al:
            insC = {nm: cpar[nm][i] for nm, shp, dt in CPAR}
            insC.update(ccon)
            insC["zfn64"] = zfn64
            insC["zsplit"] = HALF
            insC["yfn"] = hm(YP)
            emit_C(nc, k, al, S, insC, None, pfx=f"C{i}")
        k.barrier()
        for b_ in range(2):
            exch(f"{L}o{b_}", OP[b_], OAF[b_], 128, HALF, BF16)
            exch(f"{L}y{b_}", YP[b_], YF[b_], 128, HALF, BF16)
        k.barrier()
        with Alloc(nc) as al:
            emit_D1(nc, k, al, HALF, dict(x=xin, gT=gT, woa=woa[i], wor=wor[i], wo=wo[i],
                                          oafn=lambda kc, a, b_: OAF[kc % 2][kc // 2, :, a:b_],
                                          yrfn=lambda kc, a, b_: YF[kc % 2][kc // 2, :, a:b_]), xmid, pfx=f"E{i}")
        k.barrier()
        with Alloc(nc) as al:
            emit_D2(nc, k, al, HALF, last, dict(x=xmid, p=p[i], wg=wg[i], wu=wu[i], wd=wd[i], wple=wple[i], wpg=wpg[i],
                                                gffn=gffn[i], gple=gple[i], gfin=gfin, ident=ident),
                    out if last else x1, wgS, wuS, pfx=f"F{i}")
        k.barrier()
    k.finish("sp")
    return nc


def _pp(v):
    return np.ascontiguousarray(np.asarray(v, np.float32).reshape(-1, 128).T)


def kernel(x, p, rel_bias, norm_mix, w_in, lam_q1, lam_k1, lam_q2, lam_k2, attn_subln,
           rwkv_mu, rwkv_w0, rwkv_w2, rwkv_a0, rwkv_a2, rwkv_g2, rwkv_kk, rwkv_ka, rwkv_rk,
           rwkv_lnx_w, rwkv_lnx_b, w_out_attn, w_out_rwkv, w_out, norm_ffn, w_ffn_gate,
           w_ffn_up, w_ffn_down, norm_ple, w_ple, w_ple_gate, norm_final):
    f32 = np.float32
    A_ = lambda a: np.ascontiguousarray(np.asarray(a, f32))
    x = A_(x); p = A_(p); rel_bias = A_(rel_bias)
    B_, S_, D_ = x.shape
    HALF = S_ // 2
    NC = 2 * B_
    depth = int(np.asarray(w_in).shape[0])
    bc = lambda v, shape: np.ascontiguousarray(np.broadcast_to(v, shape))
    tidx = toeplitz_idx()
    com = dict(
        w_in=A_(w_in), gam=np.stack([_pp(norm_mix[i]) for i in range(depth)]), ident=np.eye(128, dtype=f32),
        neg=neg_mask(),
        lamv=np.stack([bc(np.stack([A_(lam_q1[i]), A_(lam_k1[i]), A_(lam_q2[i]), A_(lam_k2[i])])[None], (128, 4, 64)) for i in range(depth)]),
        sgain=np.stack([bc(A_(attn_subln[i])[None], (128, 128)) for i in range(depth)]),
        woa=A_(w_out_attn), wor=A_(w_out_rwkv), wo=A_(w_out), wg=A_(w_ffn_gate), wu=A_(w_ffn_up), wd=A_(w_ffn_down),
        wple=A_(w_ple), wpg=A_(w_ple_gate),
        gffn=np.stack([_pp(norm_ffn[i]) for i in range(depth)]), gple=np.stack([_pp(norm_ple[i]) for i in range(depth)]),
        gfin=bc(A_(norm_final)[None], (128, D_)),
    )
    cC = consts_C()
    for nm, shp, dt in CCON:
        if nm != "ident":
            com["k_" + nm] = cC[nm]
    blk = (np.arange(128)[:, None] // 64 == np.arange(128)[None, :] // 64)
    grp = []
    for hh in range(2):
        d = dict(G=np.ascontiguousarray(np.stack([rel_bias[tidx, 2 * hh + hl, c] for hl in range(2) for c in range(2)])),
                 b31=bc(np.stack([rel_bias[31, 2 * hh + hl, c] for hl in range(2) for c in range(2)])[None, :], (128, 4)))
        sl = np.arange(hh * 256, (hh + 1) * 256)
        idx = np.concatenate([sl, 512 + sl, 1024 + sl, np.arange(1536, 1792)])
        cp = {nm: [] for nm, shp, dt in CPAR}
        for i in range(depth):
            rkf = A_(rwkv_rk[i]).reshape(-1)[sl]
            rkb = np.stack([np.where(blk, rkf[hp * 128:(hp + 1) * 128][:, None], f32(0)) for hp in range(2)], 1).astype(f32)
            e = dict(mu=_pp(A_(rwkv_mu[i])[idx]), w2c=A_(rwkv_w2[i])[:, sl], w0c=A_(rwkv_w0[i])[None, sl],
                     a2c=A_(rwkv_a2[i])[:, sl], a0c=_pp(A_(rwkv_a0[i])[sl]), g2c=A_(rwkv_g2[i])[:, sl],
                     kkw=_pp(A_(rwkv_kk[i])[sl]), ka=_pp(A_(rwkv_ka[i])[sl]), rkb=rkb,
                     lnw=_pp(A_(rwkv_lnx_w[i])[sl]), lnb=_pp(A_(rwkv_lnx_b[i])[sl]))
            for nm in cp:
                cp[nm].append(np.ascontiguousarray(e[nm]))
        for nm in cp:
            d["c_" + nm] = np.ascontiguousarray(np.stack(cp[nm]))
        grp.append(d)
    nc = build_fused8(HALF, depth)
    in_maps = []
    for c in range(NC):
        b, pi = c // 2, c % 2
        d = dict(com)
        d.update(grp[pi])
        d["x"] = np.ascontiguousarray(x[b, pi * HALF:(pi + 1) * HALF])
        d["p"] = np.ascontiguousarray(p[:, b, pi * HALF:(pi + 1) * HALF])
        in_maps.append(d)
    res = run_bass_kernel_spmd(nc, in_maps, core_ids=list(range(NC)))
    out = np.empty((B_, S_, D_), f32)
    for c in range(NC):
        out[c // 2, (c % 2) * HALF:(c % 2 + 1) * HALF] = np.asarray(res.results[c]["out"], f32)
    return out
```

```python
import math
import numpy as np
import concourse.bass as bass
import concourse.mybir as mybir
from concourse.bass_utils import run_bass_kernel_spmd

F32 = mybir.dt.float32
BF16 = mybir.dt.bfloat16
AF = mybir.ActivationFunctionType
ALU = mybir.AluOpType
AX = mybir.AxisListType

EPOCH = 30000
DMA_RING = 8
DMA_EPOCH = 700


class KB:
    def __init__(self, nc):
        self.nc = nc
        self.eng = {"pe": nc.tensor, "act": nc.scalar, "dve": nc.vector,
                    "pool": nc.gpsimd, "sp": nc.sync}
        self.sem = {}
        self.cnt = {}
        self.nsem = 0
        for e in ("pe", "act", "dve", "pool"):
            self._new_sem(e)
        self.seen = {e: {} for e in self.eng}
        self.semobj = {}
        self.lastw = {}
        self.readers = {}
        self.dq = {}
        for q in ("sp", "pool", "act"):
            self.dq[q] = {"n": 0, "sems": None, "vals": None, "tok": [None] * DMA_RING}
        self.ninst = {e: 0 for e in self.eng}
        self.colltoks = []

    def _alloc_sem(self, name):
        self.nsem += 1
        s = self.nc.alloc_semaphore(f"{name}_{self.nsem}")
        return s

    def _new_sem(self, e):
        self.sem[e] = self._alloc_sem("s" + e)
        self.cnt[e] = 0

    def _wait(self, e, tok):
        if tok is None:
            return
        s, v, pe = tok
        sid = id(s)
        if self.seen[e].get(sid, 0) >= v:
            return
        self.eng[e].wait_ge(s, v)
        self.seen[e][sid] = v

    def _deps(self, e, reads, writes):
        toks = []
        for k in list(reads) + list(writes):
            t = self.lastw.get(k)
            if t is not None:
                toks.append(t)
        for k in writes:
            toks.extend(self.readers.get(k, ()))
        for t in toks:
            if e == "pe" and t[2] == "pe":
                continue
            self._wait(e, t)

    def _record(self, tok, reads, writes):
        for k in writes:
            self.lastw[k] = tok
            self.readers[k] = []
        for k in reads:
            self.readers.setdefault(k, []).append(tok)

    def emit(self, e, fn, reads=(), writes=()):
        self._deps(e, reads, writes)
        if self.cnt[e] >= EPOCH:
            self._new_sem(e)
        ins = fn()
        self.cnt[e] += 1
        ins.then_inc(self.sem[e], 1)
        tok = (self.sem[e], self.cnt[e], e)
        self._record(tok, reads, writes)
        self.ninst[e] += 1
        return tok

    def dma(self, q, out, in_, reads=(), writes=(), **kw):
        d = self.dq[q]
        n = d["n"]
        slot = n % DMA_RING
        if n % (DMA_RING * DMA_EPOCH) == 0:
            d["sems"] = [self._alloc_sem("d" + q) for _ in range(DMA_RING)]
            d["vals"] = [0] * DMA_RING
            for t in d["tok"]:
                self._wait(q, t)
        self._wait(q, d["tok"][slot])
        self._deps(q, reads, writes)
        ins = self.eng[q].dma_start(out=out, in_=in_, **kw)
        d["vals"][slot] += 16
        ins.then_inc(d["sems"][slot], 16)
        tok = (d["sems"][slot], d["vals"][slot], "dma_" + q)
        d["tok"][slot] = tok
        d["n"] = n + 1
        self._record(tok, reads, writes)
        self.ninst[q] += 1
        return tok

    def finish(self, q="sp"):
        for qq, d in self.dq.items():
            for t in d["tok"]:
                self._wait(q, t)

    def mm(self, out, lhsT, rhs, start, stop, reads, writes, **kw):
        return self.emit("pe", lambda: self.nc.tensor.matmul(out, lhsT, rhs, start=start, stop=stop, **kw),
                         reads, writes)

    def tr(self, out, in_, ident, reads, writes):
        return self.emit("pe", lambda: self.nc.tensor.transpose(out, in_, ident), reads, writes)

    def act(self, out, in_, func, reads, writes, **kw):
        return self.emit("act", lambda: self.nc.scalar.activation(out, in_, func, **kw), reads, writes)

    def tt(self, e, out, a, b, op, reads, writes):
        return self.emit(e, lambda: self.eng[e].tensor_tensor(out, a, b, op), reads, writes)

    def ts(self, e, out, a, s1, s2, op0, op1, reads, writes, **kw):
        return self.emit(e, lambda: self.eng[e].tensor_scalar(out, a, s1, s2, op0, op1, **kw), reads, writes)

    def ts1(self, e, out, a, s1, op, reads, writes):
        return self.emit(e, lambda: self.eng[e].tensor_single_scalar(out, a, s1, op), reads, writes)

    def cp(self, e, out, a, reads, writes):
        if e == "act":
            return self.emit(e, lambda: self.nc.scalar.copy(out, a), reads, writes)
        return self.emit(e, lambda: self.eng[e].tensor_copy(out, a), reads, writes)

    def memset(self, e, ap, val, writes):
        return self.emit(e, lambda: self.eng[e].memset(ap, val), (), writes)


import contextlib


class Alloc:
    def __init__(self, nc):
        self.nc = nc
        self.st = contextlib.ExitStack()
    def __enter__(self):
        self.st.__enter__()
        return self
    def __exit__(self, *a):
        return self.st.__exit__(*a)
    def sb(self, name, shape, dt):
        return self.st.enter_context(self.nc.sbuf_tensor(name, list(shape), dt))
    def ps(self, name, shape, dt):
        return self.st.enter_context(self.nc.psum_tensor(name, list(shape), dt))


def kb_barrier(self):
    toks = []
    for e in ("pe", "act", "dve", "pool"):
        if self.cnt[e] > 0:
            toks.append((self.sem[e], self.cnt[e], e))
    for q, d in self.dq.items():
        for t in d["tok"]:
            if t is not None:
                toks.append(t)
    toks.extend(self.colltoks)
    self.colltoks = []
    for e in ("pe", "act", "dve", "pool", "sp"):
        for t in toks:
            self._wait(e, t)
    self.lastw.clear()
    self.readers.clear()

KB.barrier = kb_barrier


def kb_coll(self, kind, src, dst, groups, reads=(), writes=()):
    q = "pool"
    self._deps(q, reads, writes)
    sem = self._alloc_sem("cc")
    ins = self.nc.gpsimd.collective_compute(kind, ALU.bypass, replica_groups=groups, ins=[src], outs=[dst])
    ins.then_inc(sem, 1)
    tok = (sem, 1, "coll")
    self._record(tok, reads, writes)
    self.colltoks.append(tok)
    return tok

KB.coll = kb_coll

D = 1024
INC = 5376
TB = 512

def build_A(NT):
    nc = bass.Bass("TRN2", target_bir_lowering=False)
    x = nc.dram_tensor("x", [NT, D], F32, kind="ExternalInput").ap()
    w = nc.dram_tensor("w_in", [D, INC], F32, kind="ExternalInput").ap()
    gam = nc.dram_tensor("gam", [128, 8], F32, kind="ExternalInput").ap()
    ident = nc.dram_tensor("ident", [128, 128], F32, kind="ExternalInput").ap()
    qT = nc.dram_tensor("qT", [512, NT], BF16, kind="ExternalOutput").ap()
    kT = nc.dram_tensor("kT", [512, NT], BF16, kind="ExternalOutput").ap()
    vo = nc.dram_tensor("v", [NT, 512], BF16, kind="ExternalOutput").ap()
    zrT = nc.dram_tensor("zrT", [1792, NT], F32, kind="ExternalOutput").ap()
    gT = nc.dram_tensor("gT", [2048, NT], BF16, kind="ExternalOutput").ap()
    k = KB(nc)
    with Alloc(nc) as al:
        emit_A(nc, k, al, NT, x, w, gam, ident, qT, kT, vo, zrT, gT)
        k.finish("sp")
    return nc


def emit_A(nc, k, al, NT, x, w, gam, ident, qT, kT, vo, zrT, gT, pfx="A", vsplit=False, ocb=None):
    nblk = NT // TB
    wb = al.sb(pfx + "wb", [128, 8, INC], BF16)
    wst = [al.sb(pfx + f"wst{i}", [128, 8, 512], F32) for i in range(2)]
    gt = al.sb(pfx + "gt", [128, 8], F32)
    idf = al.sb(pfx + "idf", [128, 128], F32)
    idb = al.sb(pfx + "idb", [128, 128], BF16)
    xt = [al.sb(pfx + f"xt{i}", [128, 4, D], F32) for i in range(2)]
    junk = al.sb(pfx + "junk", [128, D], BF16)
    ss = al.sb(pfx + "ss", [128, 4], F32)
    rstd = al.sb(pfx + "rstd", [128, 4], F32)
    xn = [al.sb(pfx + f"xn{i}", [128, D], BF16) for i in range(2)]
    hT = al.sb(pfx + "hT", [128, 8, TB], BF16)
    NOB = 4
    ob16 = [al.sb(pfx + f"ob16_{i}", [128, 512], BF16) for i in range(NOB)]
    ob32 = [al.sb(pfx + f"ob32_{i}", [128, 512], F32) for i in range(NOB)]
    pT = [al.ps(pfx + f"pT{i}", [128, 8, 128], BF16) for i in range(2)]
    NPZ = 4
    pz = [al.ps(pfx + f"pz{i}", [128, 512], F32) for i in range(NPZ)]

    k.dma("sp", gt[:], gam, writes=["gt"])
    k.dma("sp", idf[:], ident, writes=["idf"])
    k.cp("dve", idb[:], idf[:], ["idf"], ["idb"])
    wv = w.rearrange("(kc p) n -> p kc n", p=128)
    xv = x.rearrange("(b s p) d -> b p s d", p=128, s=4)

    def load_x(b):
        k.dma("sp", xt[b % 2][:], xv[b], writes=[f"xt{b%2}"])

    load_x(0)
    ncc = INC // 512 + (1 if INC % 512 else 0)
    for c in range(ncc):
        c0 = c * 512
        cw = min(512, INC - c0)
        st = wst[c % 2]
        k.dma("sp", st[:, :, :cw], wv[:, :, c0:c0 + cw], writes=[f"wst{c%2}"])
        for kc in range(8):
            e = "dve" if kc % 2 == 0 else "pool"
            k.ts1(e, wb[:, kc, c0:c0 + cw], st[:, kc, :cw], gt[:, kc:kc + 1], ALU.mult,
                  [f"wst{c%2}", "gt"], ["wb"])

    evn = [0]
    for b in range(nblk):
        if b + 1 < nblk:
            load_x(b + 1)
        xb = xt[b % 2]
        xk = f"xt{b%2}"
        for s in range(4):
            k.act(junk[:], xb[:, s, :], AF.Square, [xk], ["junk", "ss"], accum_out=ss[:, s:s + 1])
        k.ts("dve", rstd[:], ss[:], 1.0 / D, 1e-6, ALU.mult, ALU.add, ["ss"], ["rstd"])
        k.act(rstd[:], rstd[:], AF.Sqrt, ["rstd"], ["rstd"])
        k.emit("dve", lambda: nc.vector.reciprocal(rstd[:], rstd[:]), ["rstd"], ["rstd"])
        for s in range(4):
            xnk = f"xn{s%2}"
            k.ts1("dve", xn[s % 2][:], xb[:, s, :], rstd[:, s:s + 1], ALU.mult, [xk, "rstd"], [xnk])
            pk = f"pT{s%2}"
            for kc in range(8):
                k.tr(pT[s % 2][:, kc, :], xn[s % 2][:, kc * 128:(kc + 1) * 128], idb[:], [xnk, "idb"], [pk])
            k.cp("act", hT[:, :, s * 128:(s + 1) * 128], pT[s % 2][:], [pk], ["hT"])
        t0 = b * TB
        chunks = []
        for cc in range(4):
            chunks.append(("q", cc, cc * 128))
        for cc in range(4):
            chunks.append(("k", cc, 512 + cc * 128))
        for cc in range(14):
            chunks.append(("r", cc, 1536 + cc * 128))
        for cc in range(16):
            chunks.append(("g", cc, 1536 + 1792 + cc * 128))
        for (kind, cc, col) in chunks:
            i = evn[0]; evn[0] += 1
            pzi = pz[i % NPZ]; pk = f"pz{i%NPZ}"
            for kc in range(8):
                k.mm(pzi[:], wb[:, kc, col:col + 128], hT[:, kc, :], kc == 0, kc == 7, ["wb", "hT"], [pk])
            oi = i % NOB
            if kind == "q":
                k.act(ob16[oi][:], pzi[:], AF.Copy, [pk], [f"ob16_{oi}"], scale=0.125)
                if ocb:
                    k.dma("pool", ocb["q"](cc, t0), ob16[oi][:], reads=[f"ob16_{oi}"])
                else:
                    k.dma("pool", qT[cc * 128:(cc + 1) * 128, t0:t0 + TB], ob16[oi][:], reads=[f"ob16_{oi}"])
            elif kind == "k":
                k.cp("dve", ob16[oi][:], pzi[:], [pk], [f"ob16_{oi}"])
                if ocb:
                    k.dma("pool", ocb["k"](cc, t0), ob16[oi][:], reads=[f"ob16_{oi}"])
                else:
                    k.dma("pool", kT[cc * 128:(cc + 1) * 128, t0:t0 + TB], ob16[oi][:], reads=[f"ob16_{oi}"])
            elif kind == "r":
                e = "dve" if cc % 2 == 0 else "act"
                k.cp(e, ob32[oi][:], pzi[:], [pk], [f"ob32_{oi}"])
                if ocb:
                    for hq in range(2):
                        k.dma("pool", ocb["z"](cc, hq, t0), ob32[oi][hq * 64:(hq + 1) * 64, :], reads=[f"ob32_{oi}"])
                else:
                    k.dma("pool", zrT[cc * 128:(cc + 1) * 128, t0:t0 + TB], ob32[oi][:], reads=[f"ob32_{oi}"])
            else:
                k.act(ob16[oi][:], pzi[:], AF.Sigmoid, [pk], [f"ob16_{oi}"])
                k.dma("pool", gT[cc * 128:(cc + 1) * 128, t0:t0 + TB], ob16[oi][:], reads=[f"ob16_{oi}"])
        for s in range(4):
            i = evn[0]; evn[0] += 1
            pzi = pz[i % NPZ]; pk = f"pz{i%NPZ}"
            for kc in range(8):
                k.mm(pzi[:], hT[:, kc, s * 128:(s + 1) * 128], wb[:, kc, 1024:1536], kc == 0, kc == 7,
                     ["wb", "hT"], [pk])
            oi = i % NOB
            k.cp("dve", ob16[oi][:], pzi[:], [pk], [f"ob16_{oi}"])
            if ocb:
                for g_ in range(2):
                    k.dma("pool", ocb["v"](g_, t0 + s * 128), ob16[oi][:, g_ * 256:(g_ + 1) * 256], reads=[f"ob16_{oi}"])
            elif vsplit:
                for g_ in range(2):
                    k.dma("pool", vo[g_, t0 + s * 128:t0 + (s + 1) * 128, :], ob16[oi][:, g_ * 256:(g_ + 1) * 256], reads=[f"ob16_{oi}"])
            else:
                k.dma("pool", vo[t0 + s * 128:t0 + (s + 1) * 128, :], ob16[oi][:], reads=[f"ob16_{oi}"])


NEGM = -30000.0

def t5_bucket_np(n):
    n = np.maximum(n, 0)
    nf = np.maximum(n, 16).astype(np.float32)
    large = 16 + (np.log(nf / np.float32(16)) / np.float32(math.log(128 / 16)) * np.float32(16)).astype(np.int32)
    large = np.minimum(large, 31)
    return np.where(n < 16, n, large)

def toeplitz_idx():
    j = np.arange(128)[:, None]
    i = np.arange(256)[None, :]
    return t5_bucket_np(i - j)

def neg_mask():
    j = np.arange(128)[:, None]
    i = np.arange(256)[None, :]
    return np.where(i >= j, 0.0, NEGM).astype(np.float32)

def build_B(S, lam_init):
    nc = bass.Bass("TRN2", target_bir_lowering=False)
    ins = dict(
        qT=nc.dram_tensor("qT", [256, S], BF16, kind="ExternalInput").ap(),
        kT=nc.dram_tensor("kT", [256, S], BF16, kind="ExternalInput").ap(),
        v=nc.dram_tensor("v", [S, 256], BF16, kind="ExternalInput").ap(),
        G=nc.dram_tensor("G", [4, 128, 256], F32, kind="ExternalInput").ap(),
        b31=nc.dram_tensor("b31", [128, 4], F32, kind="ExternalInput").ap(),
        neg=nc.dram_tensor("neg", [128, 256], F32, kind="ExternalInput").ap(),
        lamv=nc.dram_tensor("lamv", [128, 4, 64], F32, kind="ExternalInput").ap(),
        sgain=nc.dram_tensor("sgain", [128, 128], F32, kind="ExternalInput").ap(),
        ident=nc.dram_tensor("ident", [128, 128], F32, kind="ExternalInput").ap(),
    )
    oaT = nc.dram_tensor("oaT", [256, S], BF16, kind="ExternalOutput").ap()
    dbg = nc.dram_tensor("dbg", [4, 128, 512], F32, kind="ExternalOutput").ap()
    k = KB(nc)
    with Alloc(nc) as al:
        emit_B(nc, k, al, S, lam_init, ins, oaT, dbg=dbg)
        k.finish("sp")
    return nc


def emit_B(nc, k, al, S, lam_init, ins, oaT, pfx="B", dbg=None):
    NJ = S // 128
    NG = S // 512
    VW = 136
    qs = al.sb(pfx + "qs", [128, 2, S], BF16)
    ks = al.sb(pfx + "ks", [128, 2, S], BF16)
    vaug = al.sb(pfx + "vaug", [128, NJ, 2, VW], BF16)
    Gs = al.sb(pfx + "Gs", [128, 4, 256], F32)
    negs = al.sb(pfx + "negs", [128, 256], F32)
    b31s = al.sb(pfx + "b31s", [128, 4], F32)
    Tb = al.sb(pfx + "Tb", [128, 4, 256], BF16)
    lamt = al.sb(pfx + "lamt", [128, 4, 64], F32)
    lprod = al.sb(pfx + "lprod", [128, 2, 64], F32)
    lsum = al.sb(pfx + "lsum", [128, 2], F32)
    lam = al.sb(pfx + "lam", [128, 1], F32)
    sg = al.sb(pfx + "sg", [128, 128], F32)
    idf = al.sb(pfx + "idf", [128, 128], F32)
    idb = al.sb(pfx + "idb", [128, 128], BF16)
    NB = 3
    PT = [al.sb(pfx + f"PT{i}", [128, 512], BF16) for i in range(NB)]
    rr = al.sb(pfx + "rr", [128, 2], F32)
    rl = al.sb(pfx + "rl", [128, 1], F32)
    t1 = al.sb(pfx + "t1", [128, 128], F32)
    ot = al.sb(pfx + "ot", [128, 128], F32)
    junk = al.sb(pfx + "junk", [128, 128], F32)
    ms = al.sb(pfx + "ms", [128, 1], F32)
    onb = al.sb(pfx + "onb", [128, 128], BF16)
    oT = [al.sb(pfx + f"oT{i}", [128, 512], BF16) for i in range(2)]
    ST = [al.ps(pfx + f"ST{i}", [128, 512], F32) for i in range(NB)]
    accb = al.ps(pfx + "accb", [128, 4, 512], F32)
    pTr = al.ps(pfx + "pTr", [128, 128], BF16)

    for hl in range(2):
        w = min(S // 2 if ins.get("qfn") else S, 2048)
        for part in range(S // w):
            qfn = ins.get("qfn") or (lambda r0, r1, a, b_: ins["qT"][r0:r1, a:b_])
            kfn = ins.get("kfn") or (lambda r0, r1, a, b_: ins["kT"][r0:r1, a:b_])
            k.dma("sp", qs[:, hl, part * w:(part + 1) * w], qfn(hl * 128, (hl + 1) * 128, part * w, (part + 1) * w), writes=["qs"])
            k.dma("sp", ks[:, hl, part * w:(part + 1) * w], kfn(hl * 128, (hl + 1) * 128, part * w, (part + 1) * w), writes=["ks"])
    JB = min(8, (S // 2) // 128) if ins.get("vfn") else 8
    for j0 in range(0, NJ, JB):
        j1 = min(NJ, j0 + JB)
        vsrc = ins["vfn"](j0 * 128, j1 * 128) if ins.get("vfn") else ins["v"][j0 * 128:j1 * 128, :]
        vv = vsrc.rearrange("(J p) (h d) -> p J h d", p=128, h=2)
        for h2 in range(2):
            k.dma("sp", vaug[:, j0:j1, h2, 0:128], vv[:, :, h2, :], writes=["vaug"])
    k.memset("pool", vaug[:, :, :, 128:129], 1.0, ["vaug"])
    k.dma("sp", Gs[:], ins["G"].rearrange("a p i -> p a i"), writes=["Gs"])
    k.dma("sp", negs[:], ins["neg"], writes=["negs"])
    k.dma("sp", b31s[:], ins["b31"], writes=["b31s"])
    k.dma("sp", lamt[:], ins["lamv"], writes=["lamt"])
    k.dma("sp", sg[:], ins["sgain"], writes=["sg"])
    k.dma("sp", idf[:], ins["ident"], writes=["idf"])
    k.cp("dve", idb[:], idf[:], ["idf"], ["idb"])
    for hc in range(4):
        k.ts1("dve", Gs[:, hc, :], Gs[:, hc, :], b31s[:, hc:hc + 1], ALU.subtract, ["Gs", "b31s"], ["Gs"])
        k.tt("dve", Tb[:, hc, :], Gs[:, hc, :], negs[:], ALU.add, ["Gs", "negs"], ["Tb"])
    k.tt("dve", lprod[:, 0, :], lamt[:, 0, :], lamt[:, 1, :], ALU.mult, ["lamt"], ["lprod"])
    k.tt("dve", lprod[:, 1, :], lamt[:, 2, :], lamt[:, 3, :], ALU.mult, ["lamt"], ["lprod"])
    k.emit("dve", lambda: nc.vector.reduce_sum(lsum[:], lprod[:], AX.X), ["lprod"], ["lsum"])
    k.act(lsum[:], lsum[:], AF.Exp, ["lsum"], ["lsum"])
    k.tt("dve", lam[:], lsum[:, 0:1], lsum[:, 1:2], ALU.subtract, ["lsum"], ["lam"])
    k.ts1("dve", lam[:], lam[:], float(lam_init), ALU.add, ["lam"], ["lam"])
    k.ts1("dve", sg[:], sg[:], float(1.0 - lam_init), ALU.mult, ["sg"], ["sg"])

    def acc(c, I):
        return accb[:, I, c * 256:c * 256 + 129]

    for hl in range(2):
        steps = [(g, J, c) for g in range(NG) for J in range(4 * g + 4) for c in range(2)]
        N = len(steps)

        def qk(n):
            g, J, c = steps[n]
            buf = n % NB
            r = J - 4 * g
            c0 = max(0, r) * 128
            a = 128 * r
            has_t = r >= -1
            k.mm(ST[buf][:, c0:512], ks[c * 64:(c + 1) * 64, hl, J * 128:(J + 1) * 128],
                 qs[c * 64:(c + 1) * 64, hl, g * 512 + c0:(g + 1) * 512], True, not has_t,
                 ["ks", "qs"], [f"ST{buf}"])
            if has_t:
                tc0 = max(a, 0); tc1 = min(a + 256, 512)
                i0 = tc0 - a
                k.mm(ST[buf][:, tc0:tc1], idb[:], Tb[:, hl * 2 + c, i0:i0 + (tc1 - tc0)], False, True,
                     ["idb", "Tb"], [f"ST{buf}"])

        def ex_pv(n):
            g, J, c = steps[n]
            buf = n % NB
            r = J - 4 * g
            c0 = max(0, r) * 128
            k.act(PT[buf][:, c0:512], ST[buf][:, c0:512], AF.Exp, [f"ST{buf}"], [f"PT{buf}"])
            for I in range(4):
                Ia = 4 * g + I
                if Ia >= J:
                    first = (J == 0 and c == 0)
                    wk = [f"acc0_{I}", f"acc1_{I}"] if first else [f"acc{c}_{I}"]
                    k.mm(acc(c, I), PT[buf][:, I * 128:(I + 1) * 128], vaug[:, J, hl, 0:129], first, J == Ia,
                         [f"PT{buf}", "vaug"], wk, skip_group_check=True)

        def fin(g):
            ob = oT[g % 2]; obk = f"oT{g%2}"
            for I in range(4):
                ak = [f"acc0_{I}", f"acc1_{I}"]
                k.emit("dve", lambda: nc.vector.reciprocal(rr[:, 0:1], accb[:, I, 128:129]), [ak[0]], ["rr"])
                k.emit("dve", lambda: nc.vector.reciprocal(rr[:, 1:2], accb[:, I, 256 + 128:256 + 129]), [ak[1]], ["rr"])
                k.tt("dve", rl[:], rr[:, 1:2], lam[:], ALU.mult, ["rr", "lam"], ["rl"])
                k.ts1("dve", t1[:], accb[:, I, 256:256 + 128], rl[:, 0:1], ALU.mult, [ak[1], "rl"], ["t1"])
                k.emit("dve", lambda: nc.vector.scalar_tensor_tensor(ot[:], accb[:, I, 0:128], rr[:, 0:1], t1[:], ALU.mult, ALU.subtract),
                       [ak[0], "rr", "t1"], ["ot"])
                k.act(junk[:], ot[:], AF.Square, ["ot"], ["junk", "ms"], accum_out=ms[:])
                k.ts("dve", ms[:], ms[:], 1.0 / 128, 1e-5, ALU.mult, ALU.add, ["ms"], ["ms"])
                k.act(ms[:], ms[:], AF.Sqrt, ["ms"], ["ms"])
                k.emit("dve", lambda: nc.vector.reciprocal(ms[:], ms[:]), ["ms"], ["ms"])
                k.emit("dve", lambda: nc.vector.scalar_tensor_tensor(onb[:], ot[:], ms[:, 0:1], sg[:], ALU.mult, ALU.mult),
                       ["ot", "ms", "sg"], ["onb"])
                if dbg is not None and hl == 0 and g == 0 and I == 0:
                    k.cp("dve", dt[:, 3, :], accb[:, 0, :], ["acc0_0", "acc1_0"], ["dbgt"])
                    k.cp("dve", dt[:, 2, 0:128], ot[:], ["ot"], ["dbgt"])
                    k.cp("dve", dt[:, 2, 128:256], onb[:], ["onb"], ["dbgt"])
                    k.dma("pool", dbg.rearrange("a p n -> p a n"), dt[:], reads=["dbgt"])
                k.tr(pTr[:], onb[:], idb[:], ["onb", "idb"], ["pTr"])
                k.cp("act", ob[:, I * 128:(I + 1) * 128], pTr[:], ["pTr"], [obk])
            ofn = ins.get("ofn") or (lambda r0, r1, a, b_: oaT[r0:r1, a:b_])
            k.dma("pool", ofn(hl * 128, (hl + 1) * 128, g * 512, (g + 1) * 512), ob[:], reads=[obk])

        LA = 2
        if dbg is not None and hl == 0:
            dt = al.sb(pfx + "dbgt", [128, 4, 512], F32)
            k.memset("pool", dt[:], 0.0, ["dbgt"])
            qk(0)
            k.cp("dve", dt[:, 0, :], ST[0][:], ["ST0"], ["dbgt"])
            k.act(dt[:, 1, :], ST[0][:], AF.Exp, ["ST0"], ["dbgt"])
            k.cp("dve", dt[:, 2, 0:256], Tb[:, 0, :], ["Tb"], ["dbgt"])
            k.cp("dve", dt[:, 2, 256:257], lam[:], ["lam"], ["dbgt"])
        for n in range(min(LA, N)):
            qk(n)
        for n in range(N):
            if n + LA < N:
                qk(n + LA)
            ex_pv(n)
            g, J, c = steps[n]
            if J == 4 * g + 3 and c == 1:
                fin(g)


CDEC = math.exp(-0.5)
GN_EPS = 64e-5

def consts_C():
    s = np.arange(128)[:, None]; t = np.arange(128)[None, :]
    c = dict(
        ident=np.eye(128, dtype=np.float32),
        triI=np.where(s <= t, -CDEC, 0.0).astype(np.float32),
        triE=np.where(s < t, -CDEC, 0.0).astype(np.float32),
        triR=np.where(s > t, -CDEC, 0.0).astype(np.float32),
        mstrict=(t > s).astype(np.float32),
        mask3=np.concatenate([(t >= s), (t > s), (t >= s)], 1).astype(np.float32),
        mlow=(s > t).astype(np.float32),
        bones=(s // 64 == t // 64).astype(np.float32),
    )
    return c

C_IN = [("zc", [1024, None], F32), ("mu", [128, 8], F32), ("w2c", [64, 256], F32), ("w0c", [1, 256], F32),
        ("a2c", [64, 256], F32), ("a0c", [128, 2], F32), ("g2c", [128, 256], F32), ("kkw", [128, 2], F32),
        ("ka", [128, 2], F32), ("rkb", [128, 2, 128], F32), ("lnw", [128, 2], F32), ("lnb", [128, 2], F32),
        ("ident", [128, 128], F32), ("triI", [128, 128], F32), ("triE", [128, 128], F32), ("triR", [128, 128], F32),
        ("mstrict", [128, 128], F32), ("mask3", [128, 384], F32), ("mlow", [128, 128], F32), ("bones", [128, 128], F32)]

def build_C(S):
    nc = bass.Bass("TRN2", target_bir_lowering=False)
    ins = {}
    for nm, shp, dt in C_IN:
        shp = [S if x is None else x for x in shp]
        ins[nm] = nc.dram_tensor(nm, shp, dt, kind="ExternalInput").ap()
    yT = nc.dram_tensor("yT", [256, S], BF16, kind="ExternalOutput").ap()
    k = KB(nc)
    with Alloc(nc) as al:
        emit_C(nc, k, al, S, ins, yT)
        k.finish("sp")
    return nc


def emit_C(nc, k, al, S, ins, yT, pfx="C", zrows=None):
    if zrows is None:
        zrows = [j * 128 for j in range(8)]
    NBLK = S // 512
    A = lambda nm, shp, dt=F32: al.sb(pfx + nm, shp, dt)
    P = lambda nm, shp, dt=F32: al.ps(pfx + nm, shp, dt)
    cs = {}
    for nm, shp, dt in C_IN[1:]:
        cs[nm] = A("c_" + nm, shp)
        k.dma("sp", cs[nm][:], ins[nm], writes=["c_" + nm])
    wa = A("wa", [128, 256])
    k.dma("sp", wa[0:64, :], ins["w2c"], writes=["wa"])
    k.dma("sp", wa[64:128, :], ins["a2c"], writes=["wa"])
    ones1 = A("ones1", [1, 128])
    k.memset("pool", ones1[:], 1.0, ["ones1"])
    omk = A("omk", [128, 2])
    k.ts("dve", omk[:], cs["ka"][:], -1.0, 1.0, ALU.mult, ALU.add, ["c_ka"], ["omk"])
    ident = cs["ident"]

    zin = [A(f"zin{i}", [128, 513]) for i in range(3)]
    dtmp = [A(f"dtmp{i}", [128, 512]) for i in range(2)]
    zs = A("zs", [128, 8, 512])
    tw = A("tw", [64, 512])
    sgx = A("sgx", [128, 512])
    aT = A("aT", [128, 2, 512])
    kkT = A("kkT", [128, 2, 512])
    kpT = A("kpT", [128, 2, 512])
    kkaT = A("kkaT", [128, 2, 512])
    bonT = A("bonT", [128, 2, 512])
    gTs = A("gTs", [128, 2, 512])
    tmpA = A("tmpA", [128, 512])
    tmpB = A("tmpB", [128, 512])
    sig = A("sig", [128, 256])
    Gi = A("Gi", [128, 2, 128]); iG = A("iG", [128, 2, 128]); Ge = A("Ge", [128, 2, 128])
    Gr = A("Gr", [128, 256])
    ARt = A("ARt", [128, 2, 2, 128])
    KtT = A("KtT", [128, 2, 128]); BtT = A("BtT", [128, 2, 128])
    Vtok = A("Vtok", [128, 256]); Khat = A("Khat", [128, 256]); Bhat = A("Bhat", [128, 256])
    ATs = A("ATs", [128, 4, 3, 128])
    PTs = [A(f"PTs{i}", [128, 4, 2, 128]) for i in range(2)]
    Qs = [A(f"Qs{i}", [128, 4, 128]) for i in range(2)]
    Hst = A("Hst", [128, 2, 64])
    X0 = A("X0", [128, 256]); Up = A("Up", [128, 256])
    Ysb = A("Ysb", [128, 4, 64]); Ysq = A("Ysq", [128, 4, 64])
    st1 = A("st1", [128, 4]); st2 = A("st2", [128, 4]); mean = A("mean", [128, 4]); rstd = A("rstd", [128, 4])
    msq = A("msq", [128, 4])
    Yn = A("Yn", [128, 256])
    fin = A("fin", [128, 2, 128])
    outb = [A(f"outb{i}", [128, 2, 512], BF16) for i in range(2)]
    SA = P("SA", [128, 4, 256]); SB = P("SB", [128, 4, 128])
    ATp = P("ATp", [128, 512]); Q0p = P("Q0p", [128, 4, 128])
    M = [P(f"M{i}", [128, 512]) for i in range(3)]
    mi = [0]
    def misc():
        i = mi[0] % 3; mi[0] += 1
        return M[i], f"M{i}"

    k.memset("pool", Hst[:], 0.0, ["Hst"])
    k.memset("pool", zin[0][:, 0:1], 0.0, ["zin0"])

    zi = [0]
    for b in range(NBLK):
        t0 = b * 512
        for j in range(8):
            zb = zin[zi[0] % 3]; zk = f"zin{zi[0]%3}"; zi[0] += 1
            zsplit = ins.get("zsplit", 0)
            if ins.get("zfn64"):
                zparts = [(slice(hq * 64, (hq + 1) * 64), (lambda a, b_, hq=hq: ins["zfn64"](zrows[j] + hq * 64, a, b_))) for hq in range(2)]
            else:
                zparts = [(slice(0, 128), (lambda a, b_: ins["zc"][zrows[j]:zrows[j] + 128, a:b_]))]
            if b == 0:
                k.memset("pool", zb[:, 0:1], 0.0, [zk])
            for (psl, zf_) in zparts:
                if b == 0:
                    k.dma("sp", zb[psl, 1:513], zf_(0, 512), writes=[zk])
                elif zsplit and t0 % zsplit == 0:
                    k.dma("sp", zb[psl, 0:1], zf_(t0 - 1, t0), writes=[zk], allow_slow_non_contiguous=True)
                    k.dma("sp", zb[psl, 1:513], zf_(t0, t0 + 512), writes=[zk])
                else:
                    k.dma("sp", zb[psl, 0:513], zf_(t0 - 1, t0 + 512), writes=[zk])
            e = "pool"
            dt_ = dtmp[j % 2]; dk = f"dtmp{j%2}"
            k.tt(e, dt_[:], zb[:, 0:512], zb[:, 1:513], ALU.subtract, [zk], [dk])
            k.emit("dve", lambda: nc.vector.scalar_tensor_tensor(zs[:, j, :], dt_[:], cs["mu"][:, j:j + 1], zb[:, 1:513], ALU.mult, ALU.add),
                   [dk, zk, "c_mu"], [f"zs{j}"])
        k.act(tw[:], zs[0:64, 6, :], AF.Tanh, ["zs6"], ["tw"])
        k.act(sgx[:], zs[:, 7, :], AF.Sigmoid, ["zs7"], ["sgx"])
        for hp in range(2):
            m, mk = misc()
            k.mm(m[:], wa[64:128, hp * 128:(hp + 1) * 128], zs[64:128, 6, :], True, True, ["wa", "zs6"], [mk])
            k.act(aT[:, hp, :], m[:], AF.Sigmoid, [mk, "c_a0c"], [f"aT{hp}"], bias=cs["a0c"][:, hp:hp + 1])
            k.ts1("pool", tmpA[:], zs[:, 2 + hp, :], cs["kkw"][:, hp:hp + 1], ALU.mult, [f"zs{2+hp}", "c_kkw"], ["tmpA"])
            k.tt("pool", tmpB[:], tmpA[:], tmpA[:], ALU.mult, ["tmpA"], ["tmpB"])
            m, mk = misc()
            k.mm(m[:], cs["bones"][:], tmpB[:], True, True, ["c_bones", "tmpB"], [mk])
            k.act(tmpB[:], m[:], AF.Sqrt, [mk], ["tmpB"])
            k.ts1("dve", tmpB[:], tmpB[:], 1e-12, ALU.max, ["tmpB"], ["tmpB"])
            k.emit("dve", lambda: nc.vector.reciprocal(tmpB[:], tmpB[:]), ["tmpB"], ["tmpB"])
            k.tt("dve", kkT[:, hp, :], tmpA[:], tmpB[:], ALU.mult, ["tmpA", "tmpB"], [f"kkT{hp}"])
            k.ts("dve", tmpA[:], aT[:, hp, :], cs["ka"][:, hp:hp + 1], omk[:, hp:hp + 1], ALU.mult, ALU.add,
                 [f"aT{hp}", "c_ka", "omk"], ["tmpA"])
            k.tt("pool", kpT[:, hp, :], tmpA[:], zs[:, 2 + hp, :], ALU.mult, ["tmpA", f"zs{2+hp}"], [f"kpT{hp}"])
            k.tt("pool", kkaT[:, hp, :], kkT[:, hp, :], aT[:, hp, :], ALU.mult, [f"kkT{hp}", f"aT{hp}"], [f"kkaT{hp}"])
            k.tt("pool", tmpA[:], zs[:, hp, :], kpT[:, hp, :], ALU.mult, [f"zs{hp}", f"kpT{hp}"], ["tmpA"])
            m, mk = misc()
            k.mm(m[:], cs["rkb"][:, hp, :], tmpA[:], True, True, ["c_rkb", "tmpA"], [mk])
            k.tt("dve", bonT[:, hp, :], m[:], zs[:, 4 + hp, :], ALU.mult, [mk, f"zs{4+hp}"], [f"bonT{hp}"])
            m, mk = misc()
            k.mm(m[:], cs["g2c"][:, hp * 128:(hp + 1) * 128], sgx[:], True, True, ["c_g2c", "sgx"], [mk])
            k.cp("act", gTs[:, hp, :], m[:], [mk], [f"gTs{hp}"])

        ob = outb[b % 2]; obk = f"outb{b%2}"
        for ch in range(4):
            cs0 = ch * 128
            csl = slice(cs0, cs0 + 128)
            m, mk = misc()
            k.mm(m[:, 0:256], tw[:, csl], wa[0:64, :], True, False, ["tw", "wa"], [mk])
            k.mm(m[:, 0:256], ones1[:], cs["w0c"][:], False, True, ["ones1", "c_w0c"], [mk])
            k.act(sig[:], m[:, 0:256], AF.Sigmoid, [mk], ["sig"])
            m, mk = misc()
            for hp in range(2):
                k.mm(m[:, hp * 128:(hp + 1) * 128], sig[:, hp * 128:(hp + 1) * 128], cs["triI"][:], hp == 0, False, ["sig", "c_triI"], [mk], skip_group_check=True)
            for hp in range(2):
                k.mm(m[:, 256 + hp * 128:256 + (hp + 1) * 128], sig[:, hp * 128:(hp + 1) * 128], cs["triE"][:], False, hp == 1, ["sig", "c_triE"], [mk], skip_group_check=True)
            k.act(Gi[:], m[:, 0:256], AF.Exp, [mk], ["Gi"])
            k.act(iG[:], m[:, 0:256], AF.Exp, [mk], ["iG"], scale=-1.0)
            k.act(Ge[:], m[:, 256:512], AF.Exp, [mk], ["Ge"])
            m, mk = misc()
            k.mm(m[:, 0:256], cs["triR"][:], sig[:], True, True, ["sig", "c_triR"], [mk])
            k.act(Gr[:], m[:, 0:256], AF.Exp, [mk], ["Gr"])
            for hp in range(2):
                k.emit("dve", lambda: nc.vector.scalar_tensor_tensor(ARt[:, hp, 0, :], kkT[:, hp, csl], -1.0, Ge[:, hp, :], ALU.mult, ALU.mult),
                       [f"kkT{hp}", "Ge"], ["ARt"])
                k.tt("pool", ARt[:, hp, 1, :], zs[:, hp, csl], Gi[:, hp, :], ALU.mult, [f"zs{hp}", "Gi"], ["ARt"])
                k.tt("dve", KtT[:, hp, :], kpT[:, hp, csl], iG[:, hp, :], ALU.mult, [f"kpT{hp}", "iG"], ["KtT"])
                k.tt("dve", BtT[:, hp, :], kkaT[:, hp, csl], iG[:, hp, :], ALU.mult, [f"kkaT{hp}", "iG"], ["BtT"])
            m, mk = misc()
            for hp in range(2):
                k.tr(m[:, hp * 128:(hp + 1) * 128], zs[:, 4 + hp, csl], ident[:], [f"zs{4+hp}", "c_ident"], [mk])
            k.cp("act", Vtok[:], m[:, 0:256], [mk], ["Vtok"])
            m, mk = misc()
            for hp in range(2):
                k.tr(m[:, hp * 128:(hp + 1) * 128], kpT[:, hp, csl], ident[:], [f"kpT{hp}", "c_ident"], [mk])
                k.tr(m[:, 256 + hp * 128:256 + (hp + 1) * 128], kkaT[:, hp, csl], ident[:], [f"kkaT{hp}", "c_ident"], [mk])
            k.tt("dve", Khat[:], m[:, 0:256], Gr[:], ALU.mult, [mk, "Gr"], ["Khat"])
            k.tt("dve", Bhat[:], m[:, 256:512], Gr[:], ALU.mult, [mk, "Gr"], ["Bhat"])
            for h in range(4):
                hp, e = h // 2, h % 2
                ps = slice(e * 64, (e + 1) * 64)
                k.mm(ATp[:, 0:256], BtT[ps, hp, :], ARt[ps, hp, :, :], True, False, ["BtT", "ARt"], ["ATp"], skip_group_check=True)
                k.mm(ATp[:, 256:512], KtT[ps, hp, :], ARt[ps, hp, :, :], False, True, ["KtT", "ARt"], ["ATp"], skip_group_check=True)
                k.mm(Q0p[:, h, :], ARt[ps, hp, 0, :], BtT[ps, hp, :], h == 0, h == 3, ["ARt", "BtT"], ["Q0p"], skip_group_check=True)
                k.tt("dve", PTs[0][:, h, 0, :], ATp[:, 0:128], cs["mstrict"][:], ALU.mult, ["ATp", "c_mstrict"], [f"PTs0_{h // 2}"])
                k.tt("dve", ATs[:, h, :, :], ATp[:, 128:512], cs["mask3"][:], ALU.mult, ["ATp", "c_mask3"], [f"ATs{h}"])
                k.cp("pool", PTs[0][:, h, 1, :], ident[:], ["c_ident"], [f"PTs0_{h // 2}"])
            for h in range(4):
                k.tt("dve", Qs[0][:, h, :], Q0p[:, h, :], cs["mlow"][:], ALU.mult, ["Q0p", "c_mlow"], [f"Qs0_{h // 2}"])
            for lv in range(7):
                cur, nxt = lv % 2, (lv + 1) % 2
                for pa in range(2):
                    sbt = SB if pa == 0 else Q0p
                    sbk = "SB" if pa == 0 else "Q0p"
                    rk = [f"Qs{cur}_{pa}", f"PTs{cur}_{pa}"]
                    for h in (2 * pa, 2 * pa + 1):
                        if lv < 6:
                            k.mm(SA[:, h, :], Qs[cur][:, h, :], PTs[cur][:, h, :, :], h % 2 == 0, h % 2 == 1,
                                 rk, [f"SA{pa}"], skip_group_check=True)
                        else:
                            k.mm(SA[:, h, 128:256], Qs[cur][:, h, :], PTs[cur][:, h, 1, :], h % 2 == 0, h % 2 == 1,
                                 rk, [f"SA{pa}"], skip_group_check=True)
                    if lv < 6:
                        for h in (2 * pa, 2 * pa + 1):
                            k.mm(sbt[:, h % 2, :], PTs[cur][:, h, 0, :], Qs[cur][:, h, :], h % 2 == 0, h % 2 == 1,
                                 rk, [sbk], skip_group_check=True)
                for pa in range(2):
                    sbt = SB if pa == 0 else Q0p
                    sbk = "SB" if pa == 0 else "Q0p"
                    hs = slice(2 * pa, 2 * pa + 2)
                    if lv < 6:
                        k.cp("act", PTs[nxt][:, hs, 0, :], SA[:, hs, 0:128], [f"SA{pa}"], [f"PTs{nxt}_{pa}"])
                        k.cp("act", Qs[nxt][:, hs, :], sbt[:, 0:2, :], [sbk], [f"Qs{nxt}_{pa}"])
                    k.tt("dve", PTs[nxt][:, hs, 1, :], SA[:, hs, 128:256], PTs[cur][:, hs, 1, :], ALU.add,
                         [f"SA{pa}", f"PTs{cur}_{pa}"], [f"PTs{nxt}_{pa}"])
            TT = PTs[1]; TTk = "PTs1"
            m, mk = misc()
            for h in range(4):
                hp, e = h // 2, h % 2
                ps = slice(e * 64, (e + 1) * 64)
                k.mm(m[:, h * 64:(h + 1) * 64], ARt[ps, hp, 0, :], Hst[ps, hp, :], h == 0, False, ["ARt", "Hst"], [mk], skip_group_check=True)
                k.mm(m[:, h * 64:(h + 1) * 64], ATs[:, h, 1, :], Vtok[:, h * 64:(h + 1) * 64], False, h == 3, [f"ATs{h}", "Vtok"], [mk], skip_group_check=True)
            k.cp("act", X0[:], m[:, 0:256], [mk], ["X0"])
            m, mk = misc()
            for h in range(4):
                k.mm(m[:, h * 64:(h + 1) * 64], TT[:, h, 1, :], X0[:, h * 64:(h + 1) * 64], h == 0, h == 3, [f"PTs1_{h // 2}", "X0"], [mk], skip_group_check=True)
            k.cp("act", Up[:], m[:, 0:256], [mk], ["Up"])
            my, myk = misc()
            for h in range(4):
                hp, e = h // 2, h % 2
                ps = slice(e * 64, (e + 1) * 64)
                k.mm(my[:, h * 64:(h + 1) * 64], ARt[ps, hp, 1, :], Hst[ps, hp, :], h == 0, False, ["ARt", "Hst"], [myk], skip_group_check=True)
                k.mm(my[:, h * 64:(h + 1) * 64], ATs[:, h, 0, :], Up[:, h * 64:(h + 1) * 64], False, False, [f"ATs{h}", "Up"], [myk], skip_group_check=True)
                k.mm(my[:, h * 64:(h + 1) * 64], ATs[:, h, 2, :], Vtok[:, h * 64:(h + 1) * 64], False, h == 3, [f"ATs{h}", "Vtok"], [myk], skip_group_check=True)
            m, mk = misc()
            for hp in range(2):
                k.mm(m[:, hp * 128:(hp + 1) * 128], Bhat[:, hp * 128:(hp + 1) * 128], Up[:, hp * 128:(hp + 1) * 128], hp == 0, False, ["Bhat", "Up"], [mk], skip_group_check=True)
                k.mm(m[:, hp * 128:(hp + 1) * 128], Khat[:, hp * 128:(hp + 1) * 128], Vtok[:, hp * 128:(hp + 1) * 128], False, hp == 1, ["Khat", "Vtok"], [mk], skip_group_check=True)
            for h in range(4):
                hp, e = h // 2, h % 2
                ps = slice(e * 64, (e + 1) * 64)
                k.emit("dve", lambda: nc.vector.scalar_tensor_tensor(Hst[ps, hp, :], Hst[ps, hp, :], Gi[ps, hp, 127:128],
                                                                    m[ps, hp * 128 + e * 64:hp * 128 + (e + 1) * 64], ALU.mult, ALU.add),
                       ["Hst", "Gi", mk], ["Hst"])
            k.cp("act", Ysb[:], my[:, 0:256], [myk], ["Ysb"])
            k.emit("dve", lambda: nc.vector.reduce_sum(st1[:], Ysb[:], AX.X), ["Ysb"], ["st1"])
            k.tt("pool", Ysq[:], Ysb[:], Ysb[:], ALU.mult, ["Ysb"], ["Ysq"])
            k.emit("dve", lambda: nc.vector.reduce_sum(st2[:], Ysq[:], AX.X), ["Ysq"], ["st2"])
            k.ts1("dve", mean[:], st1[:], 1.0 / 64, ALU.mult, ["st1"], ["mean"])
            k.tt("dve", msq[:], mean[:], mean[:], ALU.mult, ["mean"], ["msq"])
            k.emit("dve", lambda: nc.vector.scalar_tensor_tensor(rstd[:], st2[:], 1.0 / 64, msq[:], ALU.mult, ALU.subtract), ["st2", "msq"], ["rstd"])
            k.ts1("dve", rstd[:], rstd[:], GN_EPS, ALU.add, ["rstd"], ["rstd"])
            k.act(rstd[:], rstd[:], AF.Sqrt, ["rstd"], ["rstd"])
            k.emit("dve", lambda: nc.vector.reciprocal(rstd[:], rstd[:]), ["rstd"], ["rstd"])
            for h in range(4):
                e_ = "dve" if h % 2 == 0 else "pool"
                k.ts(e_, Yn[:, h * 64:(h + 1) * 64], Ysb[:, h, :], mean[:, h:h + 1], rstd[:, h:h + 1], ALU.subtract, ALU.mult,
                     ["Ysb", "mean", "rstd"], ["Yn"])
            m, mk = misc()
            for hp in range(2):
                k.tr(m[:, hp * 128:(hp + 1) * 128], Yn[:, hp * 128:(hp + 1) * 128], ident[:], ["Yn", "c_ident"], [mk])
            for hp in range(2):
                k.act(fin[:, hp, :], m[:, hp * 128:(hp + 1) * 128], AF.Identity, [mk, "c_lnw", "c_lnb"], ["fin"],
                      bias=cs["lnb"][:, hp:hp + 1], scale=cs["lnw"][:, hp:hp + 1])
                k.tt("pool", fin[:, hp, :], fin[:, hp, :], bonT[:, hp, csl], ALU.add, ["fin", f"bonT{hp}"], ["fin"])
                k.tt("pool", ob[:, hp, csl], fin[:, hp, :], gTs[:, hp, csl], ALU.mult, ["fin", f"gTs{hp}"], [obk])
        for hp in range(2):
            yfn = ins.get("yfn") or (lambda r0, r1, a, b_: yT[r0:r1, a:b_])
            k.dma("pool", yfn(hp * 128, (hp + 1) * 128, t0, t0 + 512), ob[:, hp, :], reads=[obk])


WST = 2048
D = 1024
DFF = 2816
NF = DFF // 128
TB = 512

def rmsnorm_T(nc, k, xb, xk, hT, ss, rstd, junk, xn, pT, idb, eps, pfx):
    for s in range(4):
        k.act(junk[:], xb[:, s, :], AF.Square, [xk], [pfx + "junk", pfx + "ss"], accum_out=ss[:, s:s + 1])
    k.ts("dve", rstd[:], ss[:], 1.0 / D, eps, ALU.mult, ALU.add, [pfx + "ss"], [pfx + "rstd"])
    k.act(rstd[:], rstd[:], AF.Sqrt, [pfx + "rstd"], [pfx + "rstd"])
    k.emit("dve", lambda: nc.vector.reciprocal(rstd[:], rstd[:]), [pfx + "rstd"], [pfx + "rstd"])
    for s in range(4):
        xnk = pfx + f"xn{s%2}"
        k.ts1("dve", xn[s % 2][:], xb[:, s, :], rstd[:, s:s + 1], ALU.mult, [xk, pfx + "rstd"], [xnk])
        pk = pfx + f"pT{s%2}"
        for kc in range(8):
            k.tr(pT[s % 2][:, kc, :], xn[s % 2][:, kc * 128:(kc + 1) * 128], idb[:], [xnk, pfx + "idb"], [pk])
        k.cp("act", hT[:, :, s * 128:(s + 1) * 128], pT[s % 2][:], [pk], [pfx + "hT"])


def load_w_bf16(nc, k, dst, dkey, src, K, N, wst, wstk, scale_ap=None, skey=None):
    wv = src.rearrange("(kc p) n -> p kc n", p=128)
    step = WST // K
    i = 0
    for c0 in range(0, N, step):
        cw = min(step, N - c0)
        st = wst[i % 2]; sk = wstk + str(i % 2); i += 1
        stv = st[:, 0:K * cw].rearrange("p (kc n) -> p kc n", kc=K)
        k.dma("sp", stv, wv[:, :, c0:c0 + cw], writes=[sk])
        for kc in range(K):
            e = "dve" if kc % 2 == 0 else "pool"
            if scale_ap is not None:
                k.ts1(e, dst[:, kc, c0:c0 + cw], stv[:, kc, :], scale_ap[:, kc:kc + 1], ALU.mult, [sk, skey], [dkey])
            else:
                k.cp(e, dst[:, kc, c0:c0 + cw], stv[:, kc, :], [sk], [dkey])


def build_D1(NT):
    nc = bass.Bass("TRN2", target_bir_lowering=False)
    I = lambda nm, shp, dt=F32: nc.dram_tensor(nm, shp, dt, kind="ExternalInput").ap()
    ins = dict(x=I("x", [NT, D]), oaT=I("oaT", [512, NT], BF16), yrT=I("yrT", [512, NT], BF16), gT=I("gT", [2048, NT], BF16),
               woa=I("woa", [512, D]), wor=I("wor", [512, D]), wo=I("wo", [D, D]))
    xo = nc.dram_tensor("xo", [NT, D], F32, kind="ExternalOutput").ap()
    k = KB(nc)
    with Alloc(nc) as al:
        emit_D1(nc, k, al, NT, ins, xo)
        k.finish("sp")
    return nc


def emit_D1(nc, k, al, NT, ins, xo, pfx="E"):
    A = lambda nm, shp, dt=F32: al.sb(pfx + nm, shp, dt)
    P = lambda nm, shp, dt=F32: al.ps(pfx + nm, shp, dt)
    woa = A("woa", [128, 4, D], BF16); wor = A("wor", [128, 4, D], BF16); wo = A("wo", [128, 8, D], BF16)
    wst = [A(f"wst{i}", [128, WST]) for i in range(2)]
    load_w_bf16(nc, k, woa, pfx + "woa", ins["woa"], 4, D, wst, pfx + "wst")
    load_w_bf16(nc, k, wor, pfx + "wor", ins["wor"], 4, D, wst, pfx + "wst")
    load_w_bf16(nc, k, wo, pfx + "wo", ins["wo"], 8, D, wst, pfx + "wst")
    oab = [A(f"oab{i}", [128, 4, TB], BF16) for i in range(2)]
    yrb = [A(f"yrb{i}", [128, 4, TB], BF16) for i in range(2)]
    xb = [A(f"xb{i}", [128, 4, D]) for i in range(2)]
    gab = [A(f"gab{i}", [128, TB], BF16) for i in range(3)]
    grb = [A(f"grb{i}", [128, TB], BF16) for i in range(3)]
    t1 = [A(f"t1_{i}", [128, TB]) for i in range(2)]
    t2 = [A(f"t2_{i}", [128, TB]) for i in range(2)]
    mT = A("mT", [128, 8, TB], BF16)
    pa = [P(f"pa{i}", [128, 512]) for i in range(2)]
    pr = [P(f"pr{i}", [128, 512]) for i in range(2)]
    po = [P(f"po{i}", [128, 512]) for i in range(3)]
    nblk = NT // TB
    oav = ins["oaT"].rearrange("(kc p) t -> p kc t", p=128) if ins.get("oaT") is not None else None
    yrv = ins["yrT"].rearrange("(kc p) t -> p kc t", p=128) if ins.get("yrT") is not None else None
    xv = ins["x"].rearrange("(b s p) d -> b p s d", p=128, s=4)
    xov = xo.rearrange("(b s p) d -> b p s d", p=128, s=4)

    def load_blk(b):
        t0 = b * TB
        if ins.get("oafn"):
            for kc in range(4):
                k.dma("sp", oab[b % 2][:, kc, :], ins["oafn"](kc, t0, t0 + TB), writes=[pfx + f"oab{b%2}"])
                k.dma("sp", yrb[b % 2][:, kc, :], ins["yrfn"](kc, t0, t0 + TB), writes=[pfx + f"yrb{b%2}"])
        else:
            k.dma("sp", oab[b % 2][:], oav[:, :, t0:t0 + TB], writes=[pfx + f"oab{b%2}"])
            k.dma("sp", yrb[b % 2][:], yrv[:, :, t0:t0 + TB], writes=[pfx + f"yrb{b%2}"])
        k.dma("sp", xb[b % 2][:], xv[b], writes=[pfx + f"xb{b%2}"])
    load_blk(0)
    gi = [0]; oi = [0]
    for b in range(nblk):
        t0 = b * TB
        if b + 1 < nblk:
            load_blk(b + 1)
        for oc in range(8):
            i = gi[0]; gi[0] += 1
            ga = gab[i % 3]; gr = grb[i % 3]
            k.dma("sp", ga[:], ins["gT"][oc * 128:(oc + 1) * 128, t0:t0 + TB], writes=[pfx + f"gab{i%3}"])
            k.dma("sp", gr[:], ins["gT"][1024 + oc * 128:1024 + (oc + 1) * 128, t0:t0 + TB], writes=[pfx + f"grb{i%3}"])
            pai = pa[i % 2]; pri = pr[i % 2]
            for kc in range(4):
                k.mm(pai[:], woa[:, kc, oc * 128:(oc + 1) * 128], oab[b % 2][:, kc, :], kc == 0, kc == 3,
                     [pfx + "woa", pfx + f"oab{b%2}"], [pfx + f"pa{i%2}"])
            for kc in range(4):
                k.mm(pri[:], wor[:, kc, oc * 128:(oc + 1) * 128], yrb[b % 2][:, kc, :], kc == 0, kc == 3,
                     [pfx + "wor", pfx + f"yrb{b%2}"], [pfx + f"pr{i%2}"])
            k.tt("dve", t1[i % 2][:], pai[:], ga[:], ALU.mult, [pfx + f"pa{i%2}", pfx + f"gab{i%3}"], [pfx + f"t1_{i%2}"])
            k.tt("dve", t2[i % 2][:], pri[:], gr[:], ALU.mult, [pfx + f"pr{i%2}", pfx + f"grb{i%3}"], [pfx + f"t2_{i%2}"])
            k.tt("pool", mT[:, oc, :], t1[i % 2][:], t2[i % 2][:], ALU.add, [pfx + f"t1_{i%2}", pfx + f"t2_{i%2}"], [pfx + "mT"])
        for s in range(4):
            for n in range(2):
                j = oi[0]; oi[0] += 1
                pj = po[j % 3]
                for kc in range(8):
                    k.mm(pj[:], mT[:, kc, s * 128:(s + 1) * 128], wo[:, kc, n * 512:(n + 1) * 512], kc == 0, kc == 7,
                         [pfx + "mT", pfx + "wo"], [pfx + f"po{j%3}"])
                k.tt("dve", xb[b % 2][:, s, n * 512:(n + 1) * 512], pj[:], xb[b % 2][:, s, n * 512:(n + 1) * 512], ALU.add,
                     [pfx + f"po{j%3}", pfx + f"xb{b%2}"], [pfx + f"xb{b%2}"])
        k.dma("pool", xov[b], xb[b % 2][:], reads=[pfx + f"xb{b%2}"])


def build_D2(NT, last):
    nc = bass.Bass("TRN2", target_bir_lowering=False)
    I = lambda nm, shp, dt=F32: nc.dram_tensor(nm, shp, dt, kind="ExternalInput").ap()
    ins = dict(x=I("x", [NT, D]), p=I("p", [NT, 256]), wg=I("wg", [D, DFF]), wu=I("wu", [D, DFF]), wd=I("wd", [DFF, D]),
               wple=I("wple", [256, D]), wpg=I("wpg", [D, D]), gffn=I("gffn", [128, 8]), gple=I("gple", [128, 8]),
               gfin=I("gfin", [128, D]), ident=I("ident", [128, 128]))
    xo = nc.dram_tensor("xo", [NT, D], F32, kind="ExternalOutput").ap()
    wgS = nc.dram_tensor("wgS", [NF, 128, 1024], BF16).ap()
    wuS = nc.dram_tensor("wuS", [NF, 128, 1024], BF16).ap()
    k = KB(nc)
    with Alloc(nc) as al:
        emit_D2(nc, k, al, NT, last, ins, xo, wgS, wuS)
        k.finish("sp")
    return nc


def emit_D2(nc, k, al, NT, last, ins, xo, wgS, wuS, pfx="F"):
    A = lambda nm, shp, dt=F32: al.sb(pfx + nm, shp, dt)
    P = lambda nm, shp, dt=F32: al.ps(pfx + nm, shp, dt)
    gffn = A("gffn", [128, 8]); gple = A("gple", [128, 8]); idf = A("idf", [128, 128]); idb = A("idb", [128, 128], BF16)
    k.dma("sp", gffn[:], ins["gffn"], writes=[pfx + "gffn"])
    k.dma("sp", gple[:], ins["gple"], writes=[pfx + "gple"])
    k.dma("sp", idf[:], ins["ident"], writes=[pfx + "idf"])
    k.cp("dve", idb[:], idf[:], [pfx + "idf"], [pfx + "idb"])
    if last:
        gfin = A("gfin", [128, D])
        k.dma("sp", gfin[:], ins["gfin"], writes=[pfx + "gfin"])
    wd = A("wd", [128, NF, D], BF16); wple = A("wple", [128, 2, D], BF16); wpg = A("wpg", [128, 8, D], BF16)
    wst = [A(f"wst{i}", [128, WST]) for i in range(2)]
    load_w_bf16(nc, k, wple, pfx + "wple", ins["wple"], 2, D, wst, pfx + "wst")
    load_w_bf16(nc, k, wpg, pfx + "wpg", ins["wpg"], 8, D, wst, pfx + "wst", gple, pfx + "gple")
    wdv = ins["wd"].rearrange("(f p) n -> p f n", p=128)
    i = 0
    for f0 in range(0, NF, 2):
        fw_ = min(2, NF - f0)
        st = wst[i % 2]; sk = pfx + f"wst{i%2}"; i += 1
        stv = st[:, 0:fw_ * D].rearrange("p (f n) -> p f n", f=fw_)
        k.dma("sp", stv, wdv[:, f0:f0 + fw_, :], writes=[sk])
        for f in range(fw_):
            e = "dve" if f % 2 == 0 else "pool"
            k.cp(e, wd[:, f0 + f, :], stv[:, f, :], [sk], [pfx + "wd"])
    wcb = [A(f"wcb{i}", [128, 8, 256], BF16) for i in range(2)]
    ci = 0
    for (src, dstS, nm) in ((ins["wg"], wgS, "wgS"), (ins["wu"], wuS, "wuS")):
        wv = src.rearrange("(kc p) n -> p kc n", p=128)
        for c0 in range(0, DFF, 256):
            cw = min(256, DFF - c0)
            st = wst[i % 2]; sk = pfx + f"wst{i%2}"; i += 1
            stv = st[:, 0:8 * cw].rearrange("p (kc n) -> p kc n", kc=8)
            k.dma("sp", stv, wv[:, :, c0:c0 + cw], writes=[sk])
            cb = wcb[ci % 2]; cbk = pfx + f"wcb{ci%2}"; ci += 1
            for kc in range(8):
                e = "dve" if kc % 2 == 0 else "pool"
                k.ts1(e, cb[:, kc, 0:cw], stv[:, kc, :], gffn[:, kc:kc + 1], ALU.mult, [sk, pfx + "gffn"], [cbk])
            for j in range(cw // 128):
                f = c0 // 128 + j
                k.dma("pool", dstS[f].rearrange("p (kc n) -> p kc n", kc=8), cb[:, :, j * 128:(j + 1) * 128], reads=[cbk], writes=[nm])

    xb = [A(f"xb{i}", [128, 4, D]) for i in range(2)]
    pb = [A(f"pb{i}", [128, 4, 256]) for i in range(2)]
    junk = A("junk", [128, D], BF16)
    ss = A("ss", [128, 4]); rstd = A("rstd", [128, 4])
    xn = [A(f"xn{i}", [128, D], BF16) for i in range(2)]
    hT = A("hT", [128, 8, TB], BF16)
    ppT = A("ppT", [128, 2, TB], BF16)
    aT = A("aT", [128, NF, TB], BF16)
    NWB = 3
    wgb = [A(f"wgb{i}", [128, 8, 128], BF16) for i in range(NWB)]
    wub = [A(f"wub{i}", [128, 8, 128], BF16) for i in range(NWB)]
    sl = [A(f"sl{i}", [128, TB]) for i in range(2)]
    tmp = [A(f"tmp{i}", [128, 512]) for i in range(2)]
    pT = [P(f"pT{i}", [128, 8, 128], BF16) for i in range(2)]
    pg = [P(f"pg{i}", [128, 512]) for i in range(2)]
    pu = [P(f"pu{i}", [128, 512]) for i in range(2)]
    pd = [P(f"pd{i}", [128, 512]) for i in range(2)]
    nblk = NT // TB
    xv = ins["x"].rearrange("(b s p) d -> b p s d", p=128, s=4)
    pv = ins["p"].rearrange("(b s p) d -> b p s d", p=128, s=4)
    xov = xo.rearrange("(b s p) d -> b p s d", p=128, s=4)

    def load_blk(b):
        k.dma("sp", xb[b % 2][:], xv[b], writes=[pfx + f"xb{b%2}"])
        k.dma("sp", pb[b % 2][:], pv[b], writes=[pfx + f"pb{b%2}"])
    load_blk(0)
    wi = [0]; di = [0]
    for b in range(nblk):
        if b + 1 < nblk:
            load_blk(b + 1)
        X = xb[b % 2]; xk = pfx + f"xb{b%2}"
        rmsnorm_T(nc, k, X, xk, hT, ss, rstd, junk, xn, pT, idb, 1e-6, pfx)
        def load_w(f):
            i = f % NWB
            k.dma("sp", wgb[i][:], wgS[f].rearrange("p (kc n) -> p kc n", kc=8), reads=["wgS"], writes=[pfx + f"wgb{i}"])
            k.dma("sp", wub[i][:], wuS[f].rearrange("p (kc n) -> p kc n", kc=8), reads=["wuS"], writes=[pfx + f"wub{i}"])
        load_w(0); load_w(1)
        for f in range(NF):
            if f + 2 < NF:
                load_w(f + 2)
            i = f % NWB
            j = wi[0]; wi[0] += 1
            for kc in range(8):
                k.mm(pg[j % 2][:], wgb[i][:, kc, :], hT[:, kc, :], kc == 0, kc == 7, [pfx + f"wgb{i}", pfx + "hT"], [pfx + f"pg{j%2}"])
            for kc in range(8):
                k.mm(pu[j % 2][:], wub[i][:, kc, :], hT[:, kc, :], kc == 0, kc == 7, [pfx + f"wub{i}", pfx + "hT"], [pfx + f"pu{j%2}"])
            k.act(sl[j % 2][:], pg[j % 2][:], AF.Silu, [pfx + f"pg{j%2}"], [pfx + f"sl{j%2}"])
            k.tt("dve", aT[:, f, :], pu[j % 2][:], sl[j % 2][:], ALU.mult, [pfx + f"pu{j%2}", pfx + f"sl{j%2}"], [pfx + "aT"])
        for s in range(4):
            for n in range(2):
                j = di[0]; di[0] += 1
                for f in range(NF):
                    k.mm(pd[j % 2][:], aT[:, f, s * 128:(s + 1) * 128], wd[:, f, n * 512:(n + 1) * 512], f == 0, f == NF - 1,
                         [pfx + "aT", pfx + "wd"], [pfx + f"pd{j%2}"])
                k.tt("dve", X[:, s, n * 512:(n + 1) * 512], pd[j % 2][:], X[:, s, n * 512:(n + 1) * 512], ALU.add,
                     [pfx + f"pd{j%2}", xk], [xk])
        rmsnorm_T(nc, k, X, xk, hT, ss, rstd, junk, xn, pT, idb, 1e-6, pfx)
        PB = pb[b % 2]; pbk = pfx + f"pb{b%2}"
        for s in range(4):
            k.cp("pool", xn[s % 2][:, 0:256], PB[:, s, :], [pbk], [pfx + f"xn{s%2}"])
            for kc in range(2):
                k.tr(pT[s % 2][:, kc, :], xn[s % 2][:, kc * 128:(kc + 1) * 128], idb[:], [pfx + f"xn{s%2}", pfx + "idb"], [pfx + f"pT{s%2}"])
            k.cp("act", ppT[:, :, s * 128:(s + 1) * 128], pT[s % 2][:, 0:2, :], [pfx + f"pT{s%2}"], [pfx + "ppT"])
        for s in range(4):
            for n in range(2):
                j = di[0]; di[0] += 1
                for kc in range(8):
                    k.mm(pg[j % 2][:], hT[:, kc, s * 128:(s + 1) * 128], wpg[:, kc, n * 512:(n + 1) * 512], kc == 0, kc == 7,
                         [pfx + "hT", pfx + "wpg"], [pfx + f"pg{j%2}"])
                for kc in range(2):
                    k.mm(pu[j % 2][:], ppT[:, kc, s * 128:(s + 1) * 128], wple[:, kc, n * 512:(n + 1) * 512], kc == 0, kc == 1,
                         [pfx + "ppT", pfx + "wple"], [pfx + f"pu{j%2}"])
                k.act(tmp[j % 2][:], pg[j % 2][:], AF.Sigmoid, [pfx + f"pg{j%2}"], [pfx + f"tmp{j%2}"])
                k.tt("dve", tmp[j % 2][:], pu[j % 2][:], tmp[j % 2][:], ALU.mult, [pfx + f"pu{j%2}", pfx + f"tmp{j%2}"], [pfx + f"tmp{j%2}"])
                k.tt("pool", X[:, s, n * 512:(n + 1) * 512], X[:, s, n * 512:(n + 1) * 512], tmp[j % 2][:], ALU.add,
                     [xk, pfx + f"tmp{j%2}"], [xk])
        if last:
            for s in range(4):
                k.act(junk[:], X[:, s, :], AF.Square, [xk], [pfx + "junk", pfx + "ss"], accum_out=ss[:, s:s + 1])
            k.ts("dve", rstd[:], ss[:], 1.0 / D, 1e-6, ALU.mult, ALU.add, [pfx + "ss"], [pfx + "rstd"])
            k.act(rstd[:], rstd[:], AF.Sqrt, [pfx + "rstd"], [pfx + "rstd"])
            k.emit("dve", lambda: nc.vector.reciprocal(rstd[:], rstd[:]), [pfx + "rstd"], [pfx + "rstd"])
            for s in range(4):
                k.emit("dve", lambda: nc.vector.scalar_tensor_tensor(X[:, s, :], X[:, s, :], rstd[:, s:s + 1], gfin[:], ALU.mult, ALU.mult),
                       [xk, pfx + "rstd", pfx + "gfin"], [xk])
        k.dma("pool", xov[b], X[:], reads=[xk])

PAIRS = [[0, 1], [2, 3], [4, 5], [6, 7]]
CPAR = [e for e in C_IN[1:12]]
CCON = [e for e in C_IN[12:]]
DS = bass.DynSlice


def build_fused8(HALF, depth):
    S = 2 * HALF
    nc = bass.Bass("TRN2", target_bir_lowering=False)
    I = lambda nm, shp, dt=F32: nc.dram_tensor(nm, list(shp), dt, kind="ExternalInput").ap()
    Sc = lambda nm, shp, dt: nc.dram_tensor(nm, list(shp), dt).ap()
    x = I("x", [HALF, D]); p = I("p", [depth, HALF, 256])
    w_in = I("w_in", [depth, D, INC]); gam = I("gam", [depth, 128, 8]); ident = I("ident", [128, 128])
    G = I("G", [4, 128, 256]); b31 = I("b31", [128, 4]); neg = I("neg", [128, 256])
    lamv = I("lamv", [depth, 128, 4, 64]); sgain = I("sgain", [depth, 128, 128])
    cpar = {nm: I("c_" + nm, [depth] + list(shp)) for nm, shp, dt in CPAR}
    ccon = {nm: (ident if nm == "ident" else I("k_" + nm, shp)) for nm, shp, dt in CCON}
    woa = I("woa", [depth, 512, D]); wor = I("wor", [depth, 512, D]); wo = I("wo", [depth, D, D])
    wg = I("wg", [depth, D, DFF]); wu = I("wu", [depth, D, DFF]); wd = I("wd", [depth, DFF, D])
    wple = I("wple", [depth, 256, D]); wpg = I("wpg", [depth, D, D])
    gffn = I("gffn", [depth, 128, 8]); gple = I("gple", [depth, 128, 8]); gfin = I("gfin", [128, D])
    out = nc.dram_tensor("out", [HALF, D], F32, kind="ExternalOutput").ap()
    PV = min(2048, HALF)
    NPV = HALF // PV
    gT = Sc("s_gT", [2048, HALF], BF16)
    xmid = Sc("s_xmid", [HALF, D], F32); x1 = Sc("s_x1", [HALF, D], F32)
    wgS = Sc("s_wgS", [NF, 128, 1024], BF16); wuS = Sc("s_wuS", [NF, 128, 1024], BF16)
    k = KB(nc)
    pr_p = nc.gpsimd.partition_id() % 2
    pr_s = nc.sync.partition_id() % 2
    DS1 = lambda v: DS(v, 1)

    def sel(t, v):
        return t[DS1(v)].rearrange("a r c -> (a r) c")

    def exch(tag, P_, F_, rows, cols, dt, static_src=False):
        stg = Sc(f"x_stg_{tag}", [rows, cols], dt)
        rcv = Sc(f"x_rcv_{tag}", [2 * rows, cols], dt)
        rcv3 = rcv.rearrange("(a r) c -> a r c", a=2)
        if static_src:
            k.dma("pool", stg, P_, writes=[f"stg{tag}"])
            k.dma("sp", sel(F_, pr_s), P_)
        else:
            k.dma("pool", stg, sel(P_, 1 - pr_p), writes=[f"stg{tag}"])
            k.dma("sp", sel(F_, pr_s), sel(P_, pr_s))
        k.coll("AllGather", stg, rcv, PAIRS, reads=[f"stg{tag}"], writes=[f"rcv{tag}"])
        k.dma("sp", sel(F_, 1 - pr_s), sel(rcv3, 1 - pr_s), reads=[f"rcv{tag}"])

    for i in range(depth):
        last = (i == depth - 1)
        xin = x if i == 0 else x1
        lam_init = 0.8 - 0.6 * math.exp(-0.3 * i)
        L = f"L{i}"
        QP = [Sc(f"{L}QP{b_}", [2, 128, HALF], BF16) for b_ in range(2)]
        KP = [Sc(f"{L}KP{b_}", [2, 128, HALF], BF16) for b_ in range(2)]
        VP = [Sc(f"{L}VP{b_}", [2, PV, 256], BF16) for b_ in range(NPV)]
        ZP = [[Sc(f"{L}ZP{j}_{q}", [2, 64, HALF], F32) for q in range(4)] for j in range(3)]
        ZL = [Sc(f"{L}ZL{q}", [64, HALF], F32) for q in range(4)]
        QF = [Sc(f"{L}QF{b_}", [2, 128, HALF], BF16) for b_ in range(2)]
        KF = [Sc(f"{L}KF{b_}", [2, 128, HALF], BF16) for b_ in range(2)]
        VF = [Sc(f"{L}VF{b_}", [2, PV, 256], BF16) for b_ in range(NPV)]
        ZF = [Sc(f"{L}ZF{r}", [2, 64, HALF], F32) for r in range(16)]
        OP = [Sc(f"{L}OP{b_}", [2, 128, HALF], BF16) for b_ in range(2)]
        YP = [Sc(f"{L}YP{b_}", [2, 128, HALF], BF16) for b_ in range(2)]
        OAF = [Sc(f"{L}OAF{b_}", [2, 128, HALF], BF16) for b_ in range(2)]
        YF = [Sc(f"{L}YF{b_}", [2, 128, HALF], BF16) for b_ in range(2)]

        def zcb(cc, hq, t0):
            if cc < 12:
                j, g_, rb = cc // 4, (cc % 4) // 2, cc % 2
                return ZP[j][rb * 2 + hq][g_, :, t0:t0 + TB]
            return ZL[(cc - 12) * 2 + hq][:, t0:t0 + TB]
        ocb = dict(q=lambda cc, t0: QP[cc % 2][cc // 2, :, t0:t0 + TB],
                   k=lambda cc, t0: KP[cc % 2][cc // 2, :, t0:t0 + TB],
                   v=lambda g_, t: VP[t // PV][g_, t % PV:t % PV + 128, :],
                   z=zcb)
        with Alloc(nc) as al:
            emit_A(nc, k, al, HALF, xin, w_in[i], gam[i], ident, None, None, None, None, gT, pfx=f"A{i}", ocb=ocb)
        k.barrier()
        for b_ in range(2):
            exch(f"{L}q{b_}", QP[b_], QF[b_], 128, HALF, BF16)
            exch(f"{L}k{b_}", KP[b_], KF[b_], 128, HALF, BF16)
        for b_ in range(NPV):
            exch(f"{L}v{b_}", VP[b_], VF[b_], PV, 256, BF16)
        for j in range(3):
            for q in range(4):
                exch(f"{L}z{j}_{q}", ZP[j][q], ZF[j * 4 + q], 64, HALF, F32)
        for q in range(4):
            exch(f"{L}zl{q}", ZL[q], ZF[12 + q], 64, HALF, F32, static_src=True)
        k.barrier()

        def hm(T_):
            def fn(r0, r1, a, b_):
                h = a // HALF
                assert (b_ - 1) // HALF == h and r1 - r0 == 128
                return T_[r0 // 128][h, :, a - h * HALF:b_ - h * HALF]
            return fn

        def vfn(a, b_):
            h = a // HALF
            tl = a - h * HALF
            assert (b_ - 1) // HALF == h and tl // PV == (tl + (b_ - a) - 1) // PV
            return VF[tl // PV][h, tl % PV:tl % PV + (b_ - a), :]

        def zfn64(r0, a, b_):
            h = a // HALF
            assert (b_ - 1) // HALF == h
            return ZF[r0 // 64][h, :, a - h * HALF:b_ - h * HALF]
        with Alloc(nc) as al:
            insB = dict(qfn=hm(QF), kfn=hm(KF), ofn=hm(OP), vfn=vfn, G=G, b31=b31, neg=neg,
                        lamv=lamv[i], sgain=sgain[i], ident=ident)
            emit_B(nc, k, al, S, lam_init, insB, None, pfx=f"B{i}")
        k.barrier()
        with Alloc(nc) as al:
            insC = {nm: cpar[nm][i] for nm, shp, dt in CPAR}
            insC.update(ccon)
            insC["zfn64"] = zfn64
            insC["zsplit"] = HALF
            insC["yfn"] = hm(YP)
            emit_C(nc, k, al, S, insC, None, pfx=f"C{i}")
        k.barrier()
        for b_ in range(2):
            exch(f"{L}o{b_}", OP[b_], OAF[b_], 128, HALF, BF16)
            exch(f"{L}y{b_}", YP[b_], YF[b_], 128, HALF, BF16)
        k.barrier()
        with Alloc(nc) as al:
            emit_D1(nc, k, al, HALF, dict(x=xin, gT=gT, woa=woa[i], wor=wor[i], wo=wo[i],
                                          oafn=lambda kc, a, b_: OAF[kc % 2][kc // 2, :, a:b_],
                                          yrfn=lambda kc, a, b_: YF[kc % 2][kc // 2, :, a:b_]), xmid, pfx=f"E{i}")
        k.barrier()
        with Alloc(nc) as al:
            emit_D2(nc, k, al, HALF, last, dict(x=xmid, p=p[i], wg=wg[i], wu=wu[i], wd=wd[i], wple=wple[i], wpg=wpg[i],
                                                gffn=gffn[i], gple=gple[i], gfin=gfin, ident=ident),
                    out if last else x1, wgS, wuS, pfx=f"F{i}")
        k.barrier()
    k.finish("sp")
    return nc


def _pp(v):
    return np.ascontiguousarray(np.asarray(v, np.float32).reshape(-1, 128).T)


def kernel(x, p, rel_bias, norm_mix, w_in, lam_q1, lam_k1, lam_q2, lam_k2, attn_subln,
           rwkv_mu, rwkv_w0, rwkv_w2, rwkv_a0, rwkv_a2, rwkv_g2, rwkv_kk, rwkv_ka, rwkv_rk,
           rwkv_lnx_w, rwkv_lnx_b, w_out_attn, w_out_rwkv, w_out, norm_ffn, w_ffn_gate,
           w_ffn_up, w_ffn_down, norm_ple, w_ple, w_ple_gate, norm_final):
    f32 = np.float32
    A_ = lambda a: np.ascontiguousarray(np.asarray(a, f32))
    x = A_(x); p = A_(p); rel_bias = A_(rel_bias)
    B_, S_, D_ = x.shape
    HALF = S_ // 2
    NC = 2 * B_
    depth = int(np.asarray(w_in).shape[0])
    bc = lambda v, shape: np.ascontiguousarray(np.broadcast_to(v, shape))
    tidx = toeplitz_idx()
    com = dict(
        w_in=A_(w_in), gam=np.stack([_pp(norm_mix[i]) for i in range(depth)]), ident=np.eye(128, dtype=f32),
        neg=neg_mask(),
        lamv=np.stack([bc(np.stack([A_(lam_q1[i]), A_(lam_k1[i]), A_(lam_q2[i]), A_(lam_k2[i])])[None], (128, 4, 64)) for i in range(depth)]),
        sgain=np.stack([bc(A_(attn_subln[i])[None], (128, 128)) for i in range(depth)]),
        woa=A_(w_out_attn), wor=A_(w_out_rwkv), wo=A_(w_out), wg=A_(w_ffn_gate), wu=A_(w_ffn_up), wd=A_(w_ffn_down),
        wple=A_(w_ple), wpg=A_(w_ple_gate),
        gffn=np.stack([_pp(norm_ffn[i]) for i in range(depth)]), gple=np.stack([_pp(norm_ple[i]) for i in range(depth)]),
        gfin=bc(A_(norm_final)[None], (128, D_)),
    )
    cC = consts_C()
    for nm, shp, dt in CCON:
        if nm != "ident":
            com["k_" + nm] = cC[nm]
    blk = (np.arange(128)[:, None] // 64 == np.arange(128)[None, :] // 64)
    grp = []
    for hh in range(2):
        d = dict(G=np.ascontiguousarray(np.stack([rel_bias[tidx, 2 * hh + hl, c] for hl in range(2) for c in range(2)])),
                 b31=bc(np.stack([rel_bias[31, 2 * hh + hl, c] for hl in range(2) for c in range(2)])[None, :], (128, 4)))
        sl = np.arange(hh * 256, (hh + 1) * 256)
        idx = np.concatenate([sl, 512 + sl, 1024 + sl, np.arange(1536, 1792)])
        cp = {nm: [] for nm, shp, dt in CPAR}
        for i in range(depth):
            rkf = A_(rwkv_rk[i]).reshape(-1)[sl]
            rkb = np.stack([np.where(blk, rkf[hp * 128:(hp + 1) * 128][:, None], f32(0)) for hp in range(2)], 1).astype(f32)
            e = dict(mu=_pp(A_(rwkv_mu[i])[idx]), w2c=A_(rwkv_w2[i])[:, sl], w0c=A_(rwkv_w0[i])[None, sl],
                     a2c=A_(rwkv_a2[i])[:, sl], a0c=_pp(A_(rwkv_a0[i])[sl]), g2c=A_(rwkv_g2[i])[:, sl],
                     kkw=_pp(A_(rwkv_kk[i])[sl]), ka=_pp(A_(rwkv_ka[i])[sl]), rkb=rkb,
                     lnw=_pp(A_(rwkv_lnx_w[i])[sl]), lnb=_pp(A_(rwkv_lnx_b[i])[sl]))
            for nm in cp:
                cp[nm].append(np.ascontiguousarray(e[nm]))
        for nm in cp:
            d["c_" + nm] = np.ascontiguousarray(np.stack(cp[nm]))
        grp.append(d)
    nc = build_fused8(HALF, depth)
    in_maps = []
    for c in range(NC):
        b, pi = c // 2, c % 2
        d = dict(com)
        d.update(grp[pi])
        d["x"] = np.ascontiguousarray(x[b, pi * HALF:(pi + 1) * HALF])
        d["p"] = np.ascontiguousarray(p[:, b, pi * HALF:(pi + 1) * HALF])
        in_maps.append(d)
    res = run_bass_kernel_spmd(nc, in_maps, core_ids=list(range(NC)))
    out = np.empty((B_, S_, D_), f32)
    for c in range(NC):
        out[c // 2, (c % 2) * HALF:(c % 2 + 1) * HALF] = np.asarray(res.results[c]["out"], f32)
    return out
```

```python
import math
import numpy as np
import concourse.bass as bass
import concourse.mybir as mybir
from concourse.bass_utils import run_bass_kernel_spmd

F32 = mybir.dt.float32
BF16 = mybir.dt.bfloat16
AF = mybir.ActivationFunctionType
ALU = mybir.AluOpType
AX = mybir.AxisListType

EPOCH = 30000
DMA_RING = 8
DMA_EPOCH = 700


class KB:
    def __init__(self, nc):
        self.nc = nc
        self.eng = {"pe": nc.tensor, "act": nc.scalar, "dve": nc.vector,
                    "pool": nc.gpsimd, "sp": nc.sync}
        self.sem = {}
        self.cnt = {}
        self.nsem = 0
        for e in ("pe", "act", "dve", "pool"):
            self._new_sem(e)
        self.seen = {e: {} for e in self.eng}
        self.semobj = {}
        self.lastw = {}
        self.readers = {}
        self.dq = {}
        for q in ("sp", "pool", "act"):
            self.dq[q] = {"n": 0, "sems": None, "vals": None, "tok": [None] * DMA_RING}
        self.ninst = {e: 0 for e in self.eng}
        self.colltoks = []

    def _alloc_sem(self, name):
        self.nsem += 1
        s = self.nc.alloc_semaphore(f"{name}_{self.nsem}")
        return s

    def _new_sem(self, e):
        self.sem[e] = self._alloc_sem("s" + e)
        self.cnt[e] = 0

    def _wait(self, e, tok):
        if tok is None:
            return
        s, v, pe = tok
        sid = id(s)
        if self.seen[e].get(sid, 0) >= v:
            return
        self.eng[e].wait_ge(s, v)
        self.seen[e][sid] = v

    def _deps(self, e, reads, writes):
        toks = []
        for k in list(reads) + list(writes):
            t = self.lastw.get(k)
            if t is not None:
                toks.append(t)
        for k in writes:
            toks.extend(self.readers.get(k, ()))
        for t in toks:
            if e == "pe" and t[2] == "pe":
                continue
            self._wait(e, t)

    def _record(self, tok, reads, writes):
        for k in writes:
            self.lastw[k] = tok
            self.readers[k] = []
        for k in reads:
            self.readers.setdefault(k, []).append(tok)

    def emit(self, e, fn, reads=(), writes=()):
        self._deps(e, reads, writes)
        if self.cnt[e] >= EPOCH:
            self._new_sem(e)
        ins = fn()
        self.cnt[e] += 1
        ins.then_inc(self.sem[e], 1)
        tok = (self.sem[e], self.cnt[e], e)
        self._record(tok, reads, writes)
        self.ninst[e] += 1
        return tok

    def dma(self, q, out, in_, reads=(), writes=(), **kw):
        d = self.dq[q]
        n = d["n"]
        slot = n % DMA_RING
        if n % (DMA_RING * DMA_EPOCH) == 0:
            d["sems"] = [self._alloc_sem("d" + q) for _ in range(DMA_RING)]
            d["vals"] = [0] * DMA_RING
            for t in d["tok"]:
                self._wait(q, t)
        self._wait(q, d["tok"][slot])
        self._deps(q, reads, writes)
        ins = self.eng[q].dma_start(out=out, in_=in_, **kw)
        d["vals"][slot] += 16
        ins.then_inc(d["sems"][slot], 16)
        tok = (d["sems"][slot], d["vals"][slot], "dma_" + q)
        d["tok"][slot] = tok
        d["n"] = n + 1
        self._record(tok, reads, writes)
        self.ninst[q] += 1
        return tok

    def finish(self, q="sp"):
        for qq, d in self.dq.items():
            for t in d["tok"]:
                self._wait(q, t)

    def mm(self, out, lhsT, rhs, start, stop, reads, writes, **kw):
        return self.emit("pe", lambda: self.nc.tensor.matmul(out, lhsT, rhs, start=start, stop=stop, **kw),
                         reads, writes)

    def tr(self, out, in_, ident, reads, writes):
        return self.emit("pe", lambda: self.nc.tensor.transpose(out, in_, ident), reads, writes)

    def act(self, out, in_, func, reads, writes, **kw):
        return self.emit("act", lambda: self.nc.scalar.activation(out, in_, func, **kw), reads, writes)

    def tt(self, e, out, a, b, op, reads, writes):
        return self.emit(e, lambda: self.eng[e].tensor_tensor(out, a, b, op), reads, writes)

    def ts(self, e, out, a, s1, s2, op0, op1, reads, writes, **kw):
        return self.emit(e, lambda: self.eng[e].tensor_scalar(out, a, s1, s2, op0, op1, **kw), reads, writes)

    def ts1(self, e, out, a, s1, op, reads, writes):
        return self.emit(e, lambda: self.eng[e].tensor_single_scalar(out, a, s1, op), reads, writes)

    def cp(self, e, out, a, reads, writes):
        if e == "act":
            return self.emit(e, lambda: self.nc.scalar.copy(out, a), reads, writes)
        return self.emit(e, lambda: self.eng[e].tensor_copy(out, a), reads, writes)

    def memset(self, e, ap, val, writes):
        return self.emit(e, lambda: self.eng[e].memset(ap, val), (), writes)


import contextlib


class Alloc:
    def __init__(self, nc):
        self.nc = nc
        self.st = contextlib.ExitStack()
    def __enter__(self):
        self.st.__enter__()
        return self
    def __exit__(self, *a):
        return self.st.__exit__(*a)
    def sb(self, name, shape, dt):
        return self.st.enter_context(self.nc.sbuf_tensor(name, list(shape), dt))
    def ps(self, name, shape, dt):
        return self.st.enter_context(self.nc.psum_tensor(name, list(shape), dt))


def kb_barrier(self):
    toks = []
    for e in ("pe", "act", "dve", "pool"):
        if self.cnt[e] > 0:
            toks.append((self.sem[e], self.cnt[e], e))
    for q, d in self.dq.items():
        for t in d["tok"]:
            if t is not None:
                toks.append(t)
    toks.extend(self.colltoks)
    self.colltoks = []
    for e in ("pe", "act", "dve", "pool", "sp"):
        for t in toks:
            self._wait(e, t)
    self.lastw.clear()
    self.readers.clear()

KB.barrier = kb_barrier


def kb_coll(self, kind, src, dst, groups, reads=(), writes=()):
    q = "pool"
    self._deps(q, reads, writes)
    sem = self._alloc_sem("cc")
    ins = self.nc.gpsimd.collective_compute(kind, ALU.bypass, replica_groups=groups, ins=[src], outs=[dst])
    ins.then_inc(sem, 1)
    tok = (sem, 1, "coll")
    self._record(tok, reads, writes)
    self.colltoks.append(tok)
    return tok

KB.coll = kb_coll

D = 1024
INC = 5376
TB = 512

def build_A(NT):
    nc = bass.Bass("TRN2", target_bir_lowering=False)
    x = nc.dram_tensor("x", [NT, D], F32, kind="ExternalInput").ap()
    w = nc.dram_tensor("w_in", [D, INC], F32, kind="ExternalInput").ap()
    gam = nc.dram_tensor("gam", [128, 8], F32, kind="ExternalInput").ap()
    ident = nc.dram_tensor("ident", [128, 128], F32, kind="ExternalInput").ap()
    qT = nc.dram_tensor("qT", [512, NT], BF16, kind="ExternalOutput").ap()
    kT = nc.dram_tensor("kT", [512, NT], BF16, kind="ExternalOutput").ap()
    vo = nc.dram_tensor("v", [NT, 512], BF16, kind="ExternalOutput").ap()
    zrT = nc.dram_tensor("zrT", [1792, NT], F32, kind="ExternalOutput").ap()
    gT = nc.dram_tensor("gT", [2048, NT], BF16, kind="ExternalOutput").ap()
    k = KB(nc)
    with Alloc(nc) as al:
        emit_A(nc, k, al, NT, x, w, gam, ident, qT, kT, vo, zrT, gT)
        k.finish("sp")
    return nc


def emit_A(nc, k, al, NT, x, w, gam, ident, qT, kT, vo, zrT, gT, pfx="A", vsplit=False, ocb=None):
    nblk = NT // TB
    wb = al.sb(pfx + "wb", [128, 8, INC], BF16)
    wst = [al.sb(pfx + f"wst{i}", [128, 8, 512], F32) for i in range(2)]
    gt = al.sb(pfx + "gt", [128, 8], F32)
    idf = al.sb(pfx + "idf", [128, 128], F32)
    idb = al.sb(pfx + "idb", [128, 128], BF16)
    xt = [al.sb(pfx + f"xt{i}", [128, 4, D], F32) for i in range(2)]
    junk = al.sb(pfx + "junk", [128, D], BF16)
    ss = al.sb(pfx + "ss", [128, 4], F32)
    rstd = al.sb(pfx + "rstd", [128, 4], F32)
    xn = [al.sb(pfx + f"xn{i}", [128, D], BF16) for i in range(2)]
    hT = al.sb(pfx + "hT", [128, 8, TB], BF16)
    NOB = 4
    ob16 = [al.sb(pfx + f"ob16_{i}", [128, 512], BF16) for i in range(NOB)]
    ob32 = [al.sb(pfx + f"ob32_{i}", [128, 512], F32) for i in range(NOB)]
    pT = [al.ps(pfx + f"pT{i}", [128, 8, 128], BF16) for i in range(2)]
    NPZ = 4
    pz = [al.ps(pfx + f"pz{i}", [128, 512], F32) for i in range(NPZ)]

    k.dma("sp", gt[:], gam, writes=["gt"])
    k.dma("sp", idf[:], ident, writes=["idf"])
    k.cp("dve", idb[:], idf[:], ["idf"], ["idb"])
    wv = w.rearrange("(kc p) n -> p kc n", p=128)
    xv = x.rearrange("(b s p) d -> b p s d", p=128, s=4)

    def load_x(b):
        k.dma("sp", xt[b % 2][:], xv[b], writes=[f"xt{b%2}"])

    load_x(0)
    ncc = INC // 512 + (1 if INC % 512 else 0)
    for c in range(ncc):
        c0 = c * 512
        cw = min(512, INC - c0)
        st = wst[c % 2]
        k.dma("sp", st[:, :, :cw], wv[:, :, c0:c0 + cw], writes=[f"wst{c%2}"])
        for kc in range(8):
            e = "dve" if kc % 2 == 0 else "pool"
            k.ts1(e, wb[:, kc, c0:c0 + cw], st[:, kc, :cw], gt[:, kc:kc + 1], ALU.mult,
                  [f"wst{c%2}", "gt"], ["wb"])

    evn = [0]
    for b in range(nblk):
        if b + 1 < nblk:
            load_x(b + 1)
        xb = xt[b % 2]
        xk = f"xt{b%2}"
        for s in range(4):
            k.act(junk[:], xb[:, s, :], AF.Square, [xk], ["junk", "ss"], accum_out=ss[:, s:s + 1])
        k.ts("dve", rstd[:], ss[:], 1.0 / D, 1e-6, ALU.mult, ALU.add, ["ss"], ["rstd"])
        k.act(rstd[:], rstd[:], AF.Sqrt, ["rstd"], ["rstd"])
        k.emit("dve", lambda: nc.vector.reciprocal(rstd[:], rstd[:]), ["rstd"], ["rstd"])
        for s in range(4):
            xnk = f"xn{s%2}"
            k.ts1("dve", xn[s % 2][:], xb[:, s, :], rstd[:, s:s + 1], ALU.mult, [xk, "rstd"], [xnk])
            pk = f"pT{s%2}"
            for kc in range(8):
                k.tr(pT[s % 2][:, kc, :], xn[s % 2][:, kc * 128:(kc + 1) * 128], idb[:], [xnk, "idb"], [pk])
            k.cp("act", hT[:, :, s * 128:(s + 1) * 128], pT[s % 2][:], [pk], ["hT"])
        t0 = b * TB
        chunks = []
        for cc in range(4):
            chunks.append(("q", cc, cc * 128))
        for cc in range(4):
            chunks.append(("k", cc, 512 + cc * 128))
        for cc in range(14):
            chunks.append(("r", cc, 1536 + cc * 128))
        for cc in range(16):
            chunks.append(("g", cc, 1536 + 1792 + cc * 128))
        for (kind, cc, col) in chunks:
            i = evn[0]; evn[0] += 1
            pzi = pz[i % NPZ]; pk = f"pz{i%NPZ}"
            for kc in range(8):
                k.mm(pzi[:], wb[:, kc, col:col + 128], hT[:, kc, :], kc == 0, kc == 7, ["wb", "hT"], [pk])
            oi = i % NOB
            if kind == "q":
                k.act(ob16[oi][:], pzi[:], AF.Copy, [pk], [f"ob16_{oi}"], scale=0.125)
                if ocb:
                    k.dma("pool", ocb["q"](cc, t0), ob16[oi][:], reads=[f"ob16_{oi}"])
                else:
                    k.dma("pool", qT[cc * 128:(cc + 1) * 128, t0:t0 + TB], ob16[oi][:], reads=[f"ob16_{oi}"])
            elif kind == "k":
                k.cp("dve", ob16[oi][:], pzi[:], [pk], [f"ob16_{oi}"])
                if ocb:
                    k.dma("pool", ocb["k"](cc, t0), ob16[oi][:], reads=[f"ob16_{oi}"])
                else:
                    k.dma("pool", kT[cc * 128:(cc + 1) * 128, t0:t0 + TB], ob16[oi][:], reads=[f"ob16_{oi}"])
            elif kind == "r":
                e = "dve" if cc % 2 == 0 else "act"
                k.cp(e, ob32[oi][:], pzi[:], [pk], [f"ob32_{oi}"])
                if ocb:
                    for hq in range(2):
                        k.dma("pool", ocb["z"](cc, hq, t0), ob32[oi][hq * 64:(hq + 1) * 64, :], reads=[f"ob32_{oi}"])
                else:
                    k.dma("pool", zrT[cc * 128:(cc + 1) * 128, t0:t0 + TB], ob32[oi][:], reads=[f"ob32_{oi}"])
            else:
                k.act(ob16[oi][:], pzi[:], AF.Sigmoid, [pk], [f"ob16_{oi}"])
                k.dma("pool", gT[cc * 128:(cc + 1) * 128, t0:t0 + TB], ob16[oi][:], reads=[f"ob16_{oi}"])
        for s in range(4):
            i = evn[0]; evn[0] += 1
            pzi = pz[i % NPZ]; pk = f"pz{i%NPZ}"
            for kc in range(8):
                k.mm(pzi[:], hT[:, kc, s * 128:(s + 1) * 128], wb[:, kc, 1024:1536], kc == 0, kc == 7,
                     ["wb", "hT"], [pk])
            oi = i % NOB
            k.cp("dve", ob16[oi][:], pzi[:], [pk], [f"ob16_{oi}"])
            if ocb:
                for g_ in range(2):
                    k.dma("pool", ocb["v"](g_, t0 + s * 128), ob16[oi][:, g_ * 256:(g_ + 1) * 256], reads=[f"ob16_{oi}"])
            elif vsplit:
                for g_ in range(2):
                    k.dma("pool", vo[g_, t0 + s * 128:t0 + (s + 1) * 128, :], ob16[oi][:, g_ * 256:(g_ + 1) * 256], reads=[f"ob16_{oi}"])
            else:
                k.dma("pool", vo[t0 + s * 128:t0 + (s + 1) * 128, :], ob16[oi][:], reads=[f"ob16_{oi}"])


NEGM = -30000.0

def t5_bucket_np(n):
    n = np.maximum(n, 0)
    nf = np.maximum(n, 16).astype(np.float32)
    large = 16 + (np.log(nf / np.float32(16)) / np.float32(math.log(128 / 16)) * np.float32(16)).astype(np.int32)
    large = np.minimum(large, 31)
    return np.where(n < 16, n, large)

def toeplitz_idx():
    j = np.arange(128)[:, None]
    i = np.arange(256)[None, :]
    return t5_bucket_np(i - j)

def neg_mask():
    j = np.arange(128)[:, None]
    i = np.arange(256)[None, :]
    return np.where(i >= j, 0.0, NEGM).astype(np.float32)

def build_B(S, lam_init):
    nc = bass.Bass("TRN2", target_bir_lowering=False)
    ins = dict(
        qT=nc.dram_tensor("qT", [256, S], BF16, kind="ExternalInput").ap(),
        kT=nc.dram_tensor("kT", [256, S], BF16, kind="ExternalInput").ap(),
        v=nc.dram_tensor("v", [S, 256], BF16, kind="ExternalInput").ap(),
        G=nc.dram_tensor("G", [4, 128, 256], F32, kind="ExternalInput").ap(),
        b31=nc.dram_tensor("b31", [128, 4], F32, kind="ExternalInput").ap(),
        neg=nc.dram_tensor("neg", [128, 256], F32, kind="ExternalInput").ap(),
        lamv=nc.dram_tensor("lamv", [128, 4, 64], F32, kind="ExternalInput").ap(),
        sgain=nc.dram_tensor("sgain", [128, 128], F32, kind="ExternalInput").ap(),
        ident=nc.dram_tensor("ident", [128, 128], F32, kind="ExternalInput").ap(),
    )
    oaT = nc.dram_tensor("oaT", [256, S], BF16, kind="ExternalOutput").ap()
    dbg = nc.dram_tensor("dbg", [4, 128, 512], F32, kind="ExternalOutput").ap()
    k = KB(nc)
    with Alloc(nc) as al:
        emit_B(nc, k, al, S, lam_init, ins, oaT, dbg=dbg)
        k.finish("sp")
    return nc


def emit_B(nc, k, al, S, lam_init, ins, oaT, pfx="B", dbg=None):
    NJ = S // 128
    NG = S // 512
    VW = 136
    qs = al.sb(pfx + "qs", [128, 2, S], BF16)
    ks = al.sb(pfx + "ks", [128, 2, S], BF16)
    vaug = al.sb(pfx + "vaug", [128, NJ, 2, VW], BF16)
    Gs = al.sb(pfx + "Gs", [128, 4, 256], F32)
    negs = al.sb(pfx + "negs", [128, 256], F32)
    b31s = al.sb(pfx + "b31s", [128, 4], F32)
    Tb = al.sb(pfx + "Tb", [128, 4, 256], BF16)
    lamt = al.sb(pfx + "lamt", [128, 4, 64], F32)
    lprod = al.sb(pfx + "lprod", [128, 2, 64], F32)
    lsum = al.sb(pfx + "lsum", [128, 2], F32)
    lam = al.sb(pfx + "lam", [128, 1], F32)
    sg = al.sb(pfx + "sg", [128, 128], F32)
    idf = al.sb(pfx + "idf", [128, 128], F32)
    idb = al.sb(pfx + "idb", [128, 128], BF16)
    NB = 3
    PT = [al.sb(pfx + f"PT{i}", [128, 512], BF16) for i in range(NB)]
    rr = al.sb(pfx + "rr", [128, 2], F32)
    rl = al.sb(pfx + "rl", [128, 1], F32)
    t1 = al.sb(pfx + "t1", [128, 128], F32)
    ot = al.sb(pfx + "ot", [128, 128], F32)
    junk = al.sb(pfx + "junk", [128, 128], F32)
    ms = al.sb(pfx + "ms", [128, 1], F32)
    onb = al.sb(pfx + "onb", [128, 128], BF16)
    oT = [al.sb(pfx + f"oT{i}", [128, 512], BF16) for i in range(2)]
    ST = [al.ps(pfx + f"ST{i}", [128, 512], F32) for i in range(NB)]
    accb = al.ps(pfx + "accb", [128, 4, 512], F32)
    pTr = al.ps(pfx + "pTr", [128, 128], BF16)

    for hl in range(2):
        w = min(S // 2 if ins.get("qfn") else S, 2048)
        for part in range(S // w):
            qfn = ins.get("qfn") or (lambda r0, r1, a, b_: ins["qT"][r0:r1, a:b_])
            kfn = ins.get("kfn") or (lambda r0, r1, a, b_: ins["kT"][r0:r1, a:b_])
            k.dma("sp", qs[:, hl, part * w:(part + 1) * w], qfn(hl * 128, (hl + 1) * 128, part * w, (part + 1) * w), writes=["qs"])
            k.dma("sp", ks[:, hl, part * w:(part + 1) * w], kfn(hl * 128, (hl + 1) * 128, part * w, (part + 1) * w), writes=["ks"])
    JB = min(8, (S // 2) // 128) if ins.get("vfn") else 8
    for j0 in range(0, NJ, JB):
        j1 = min(NJ, j0 + JB)
        vsrc = ins["vfn"](j0 * 128, j1 * 128) if ins.get("vfn") else ins["v"][j0 * 128:j1 * 128, :]
        vv = vsrc.rearrange("(J p) (h d) -> p J h d", p=128, h=2)
        for h2 in range(2):
            k.dma("sp", vaug[:, j0:j1, h2, 0:128], vv[:, :, h2, :], writes=["vaug"])
    k.memset("pool", vaug[:, :, :, 128:129], 1.0, ["vaug"])
    k.dma("sp", Gs[:], ins["G"].rearrange("a p i -> p a i"), writes=["Gs"])
    k.dma("sp", negs[:], ins["neg"], writes=["negs"])
    k.dma("sp", b31s[:], ins["b31"], writes=["b31s"])
    k.dma("sp", lamt[:], ins["lamv"], writes=["lamt"])
    k.dma("sp", sg[:], ins["sgain"], writes=["sg"])
    k.dma("sp", idf[:], ins["ident"], writes=["idf"])
    k.cp("dve", idb[:], idf[:], ["idf"], ["idb"])
    for hc in range(4):
        k.ts1("dve", Gs[:, hc, :], Gs[:, hc, :], b31s[:, hc:hc + 1], ALU.subtract, ["Gs", "b31s"], ["Gs"])
        k.tt("dve", Tb[:, hc, :], Gs[:, hc, :], negs[:], ALU.add, ["Gs", "negs"], ["Tb"])
    k.tt("dve", lprod[:, 0, :], lamt[:, 0, :], lamt[:, 1, :], ALU.mult, ["lamt"], ["lprod"])
    k.tt("dve", lprod[:, 1, :], lamt[:, 2, :], lamt[:, 3, :], ALU.mult, ["lamt"], ["lprod"])
    k.emit("dve", lambda: nc.vector.reduce_sum(lsum[:], lprod[:], AX.X), ["lprod"], ["lsum"])
    k.act(lsum[:], lsum[:], AF.Exp, ["lsum"], ["lsum"])
    k.tt("dve", lam[:], lsum[:, 0:1], lsum[:, 1:2], ALU.subtract, ["lsum"], ["lam"])
    k.ts1("dve", lam[:], lam[:], float(lam_init), ALU.add, ["lam"], ["lam"])
    k.ts1("dve", sg[:], sg[:], float(1.0 - lam_init), ALU.mult, ["sg"], ["sg"])

    def acc(c, I):
        return accb[:, I, c * 256:c * 256 + 129]

    for hl in range(2):
        steps = [(g, J, c) for g in range(NG) for J in range(4 * g + 4) for c in range(2)]
        N = len(steps)

        def qk(n):
            g, J, c = steps[n]
            buf = n % NB
            r = J - 4 * g
            c0 = max(0, r) * 128
            a = 128 * r
            has_t = r >= -1
            k.mm(ST[buf][:, c0:512], ks[c * 64:(c + 1) * 64, hl, J * 128:(J + 1) * 128],
                 qs[c * 64:(c + 1) * 64, hl, g * 512 + c0:(g + 1) * 512], True, not has_t,
                 ["ks", "qs"], [f"ST{buf}"])
            if has_t:
                tc0 = max(a, 0); tc1 = min(a + 256, 512)
                i0 = tc0 - a
                k.mm(ST[buf][:, tc0:tc1], idb[:], Tb[:, hl * 2 + c, i0:i0 + (tc1 - tc0)], False, True,
                     ["idb", "Tb"], [f"ST{buf}"])

        def ex_pv(n):
            g, J, c = steps[n]
            buf = n % NB
            r = J - 4 * g
            c0 = max(0, r) * 128
            k.act(PT[buf][:, c0:512], ST[buf][:, c0:512], AF.Exp, [f"ST{buf}"], [f"PT{buf}"])
            for I in range(4):
                Ia = 4 * g + I
                if Ia >= J:
                    first = (J == 0 and c == 0)
                    wk = [f"acc0_{I}", f"acc1_{I}"] if first else [f"acc{c}_{I}"]
                    k.mm(acc(c, I), PT[buf][:, I * 128:(I + 1) * 128], vaug[:, J, hl, 0:129], first, J == Ia,
                         [f"PT{buf}", "vaug"], wk, skip_group_check=True)

        def fin(g):
            ob = oT[g % 2]; obk = f"oT{g%2}"
            for I in range(4):
                ak = [f"acc0_{I}", f"acc1_{I}"]
                k.emit("dve", lambda: nc.vector.reciprocal(rr[:, 0:1], accb[:, I, 128:129]), [ak[0]], ["rr"])
                k.emit("dve", lambda: nc.vector.reciprocal(rr[:, 1:2], accb[:, I, 256 + 128:256 + 129]), [ak[1]], ["rr"])
                k.tt("dve", rl[:], rr[:, 1:2], lam[:], ALU.mult, ["rr", "lam"], ["rl"])
                k.ts1("dve", t1[:], accb[:, I, 256:256 + 128], rl[:, 0:1], ALU.mult, [ak[1], "rl"], ["t1"])
                k.emit("dve", lambda: nc.vector.scalar_tensor_tensor(ot[:], accb[:, I, 0:128], rr[:, 0:1], t1[:], ALU.mult, ALU.subtract),
                       [ak[0], "rr", "t1"], ["ot"])
                k.act(junk[:], ot[:], AF.Square, ["ot"], ["junk", "ms"], accum_out=ms[:])
                k.ts("dve", ms[:], ms[:], 1.0 / 128, 1e-5, ALU.mult, ALU.add, ["ms"], ["ms"])
                k.act(ms[:], ms[:], AF.Sqrt, ["ms"], ["ms"])
                k.emit("dve", lambda: nc.vector.reciprocal(ms[:], ms[:]), ["ms"], ["ms"])
                k.emit("dve", lambda: nc.vector.scalar_tensor_tensor(onb[:], ot[:], ms[:, 0:1], sg[:], ALU.mult, ALU.mult),
                       ["ot", "ms", "sg"], ["onb"])
                if dbg is not None and hl == 0 and g == 0 and I == 0:
                    k.cp("dve", dt[:, 3, :], accb[:, 0, :], ["acc0_0", "acc1_0"], ["dbgt"])
                    k.cp("dve", dt[:, 2, 0:128], ot[:], ["ot"], ["dbgt"])
                    k.cp("dve", dt[:, 2, 128:256], onb[:], ["onb"], ["dbgt"])
                    k.dma("pool", dbg.rearrange("a p n -> p a n"), dt[:], reads=["dbgt"])
                k.tr(pTr[:], onb[:], idb[:], ["onb", "idb"], ["pTr"])
                k.cp("act", ob[:, I * 128:(I + 1) * 128], pTr[:], ["pTr"], [obk])
            ofn = ins.get("ofn") or (lambda r0, r1, a, b_: oaT[r0:r1, a:b_])
            k.dma("pool", ofn(hl * 128, (hl + 1) * 128, g * 512, (g + 1) * 512), ob[:], reads=[obk])

        LA = 2
        if dbg is not None and hl == 0:
            dt = al.sb(pfx + "dbgt", [128, 4, 512], F32)
            k.memset("pool", dt[:], 0.0, ["dbgt"])
            qk(0)
            k.cp("dve", dt[:, 0, :], ST[0][:], ["ST0"], ["dbgt"])
            k.act(dt[:, 1, :], ST[0][:], AF.Exp, ["ST0"], ["dbgt"])
            k.cp("dve", dt[:, 2, 0:256], Tb[:, 0, :], ["Tb"], ["dbgt"])
            k.cp("dve", dt[:, 2, 256:257], lam[:], ["lam"], ["dbgt"])
        for n in range(min(LA, N)):
            qk(n)
        for n in range(N):
            if n + LA < N:
                qk(n + LA)
            ex_pv(n)
            g, J, c = steps[n]
            if J == 4 * g + 3 and c == 1:
                fin(g)


CDEC = math.exp(-0.5)
GN_EPS = 64e-5

def consts_C():
    s = np.arange(128)[:, None]; t = np.arange(128)[None, :]
    c = dict(
        ident=np.eye(128, dtype=np.float32),
        triI=np.where(s <= t, -CDEC, 0.0).astype(np.float32),
        triE=np.where(s < t, -CDEC, 0.0).astype(np.float32),
        triR=np.where(s > t, -CDEC, 0.0).astype(np.float32),
        mstrict=(t > s).astype(np.float32),
        mask3=np.concatenate([(t >= s), (t > s), (t >= s)], 1).astype(np.float32),
        mlow=(s > t).astype(np.float32),
        bones=(s // 64 == t // 64).astype(np.float32),
    )
    return c

C_IN = [("zc", [1024, None], F32), ("mu", [128, 8], F32), ("w2c", [64, 256], F32), ("w0c", [1, 256], F32),
        ("a2c", [64, 256], F32), ("a0c", [128, 2], F32), ("g2c", [128, 256], F32), ("kkw", [128, 2], F32),
        ("ka", [128, 2], F32), ("rkb", [128, 2, 128], F32), ("lnw", [128, 2], F32), ("lnb", [128, 2], F32),
        ("ident", [128, 128], F32), ("triI", [128, 128], F32), ("triE", [128, 128], F32), ("triR", [128, 128], F32),
        ("mstrict", [128, 128], F32), ("mask3", [128, 384], F32), ("mlow", [128, 128], F32), ("bones", [128, 128], F32)]

def build_C(S):
    nc = bass.Bass("TRN2", target_bir_lowering=False)
    ins = {}
    for nm, shp, dt in C_IN:
        shp = [S if x is None else x for x in shp]
        ins[nm] = nc.dram_tensor(nm, shp, dt, kind="ExternalInput").ap()
    yT = nc.dram_tensor("yT", [256, S], BF16, kind="ExternalOutput").ap()
    k = KB(nc)
    with Alloc(nc) as al:
        emit_C(nc, k, al, S, ins, yT)
        k.finish("sp")
    return nc


def emit_C(nc, k, al, S, ins, yT, pfx="C", zrows=None):
    if zrows is None:
        zrows = [j * 128 for j in range(8)]
    NBLK = S // 512
    A = lambda nm, shp, dt=F32: al.sb(pfx + nm, shp, dt)
    P = lambda nm, shp, dt=F32: al.ps(pfx + nm, shp, dt)
    cs = {}
    for nm, shp, dt in C_IN[1:]:
        cs[nm] = A("c_" + nm, shp)
        k.dma("sp", cs[nm][:], ins[nm], writes=["c_" + nm])
    wa = A("wa", [128, 256])
    k.dma("sp", wa[0:64, :], ins["w2c"], writes=["wa"])
    k.dma("sp", wa[64:128, :], ins["a2c"], writes=["wa"])
    ones1 = A("ones1", [1, 128])
    k.memset("pool", ones1[:], 1.0, ["ones1"])
    omk = A("omk", [128, 2])
    k.ts("dve", omk[:], cs["ka"][:], -1.0, 1.0, ALU.mult, ALU.add, ["c_ka"], ["omk"])
    ident = cs["ident"]

    zin = [A(f"zin{i}", [128, 513]) for i in range(3)]
    dtmp = [A(f"dtmp{i}", [128, 512]) for i in range(2)]
    zs = A("zs", [128, 8, 512])
    tw = A("tw", [64, 512])
    sgx = A("sgx", [128, 512])
    aT = A("aT", [128, 2, 512])
    kkT = A("kkT", [128, 2, 512])
    kpT = A("kpT", [128, 2, 512])
    kkaT = A("kkaT", [128, 2, 512])
    bonT = A("bonT", [128, 2, 512])
    gTs = A("gTs", [128, 2, 512])
    tmpA = A("tmpA", [128, 512])
    tmpB = A("tmpB", [128, 512])
    sig = A("sig", [128, 256])
    Gi = [A(f"Gi{i_}", [128, 2, 128]) for i_ in range(2)]; iG = A("iG", [128, 2, 128]); Ge = A("Ge", [128, 2, 128])
    Gr = A("Gr", [128, 256])
    ARt = [A(f"ARt{i_}", [128, 2, 2, 128]) for i_ in range(2)]
    KtT = A("KtT", [128, 2, 128]); BtT = A("BtT", [128, 2, 128])
    Vtok = [A(f"Vtok{i_}", [128, 256]) for i_ in range(2)]; Khat = [A(f"Khat{i_}", [128, 256]) for i_ in range(2)]
    Bhat = [A(f"Bhat{i_}", [128, 256]) for i_ in range(2)]
    ATs = [A(f"ATs{i_}", [128, 4, 3, 128]) for i_ in range(2)]
    PTs = [A(f"PTs{i}", [128, 4, 2, 128]) for i in range(2)]
    Qs = [A(f"Qs{i}", [128, 4, 128]) for i in range(2)]
    Hst = A("Hst", [128, 2, 64])
    X0 = A("X0", [128, 256]); Up = A("Up", [128, 256])
    Ysb = A("Ysb", [128, 4, 64]); Ysq = A("Ysq", [128, 4, 64])
    st1 = A("st1", [128, 4]); st2 = A("st2", [128, 4]); mean = A("mean", [128, 4]); rstd = A("rstd", [128, 4])
    msq = A("msq", [128, 4])
    Yn = A("Yn", [128, 256])
    fin = A("fin", [128, 2, 128])
    outb = [A(f"outb{i}", [128, 2, 512], BF16) for i in range(2)]
    SA = P("SA", [128, 4, 256]); SB = P("SB", [128, 4, 128])
    ATp = P("ATp", [128, 512]); Q0p = P("Q0p", [128, 4, 128])
    M = [P(f"M{i}", [128, 512]) for i in range(3)]
    mi = [0]
    def misc():
        i = mi[0] % 3; mi[0] += 1
        return M[i], f"M{i}"

    k.memset("pool", Hst[:], 0.0, ["Hst"])
    k.memset("pool", zin[0][:, 0:1], 0.0, ["zin0"])

    zi = [0]
    for b in range(NBLK):
        t0 = b * 512
        for j in range(8):
            zb = zin[zi[0] % 3]; zk = f"zin{zi[0]%3}"; zi[0] += 1
            zsplit = ins.get("zsplit", 0)
            if ins.get("zfn64"):
                zparts = [(slice(hq * 64, (hq + 1) * 64), (lambda a, b_, hq=hq: ins["zfn64"](zrows[j] + hq * 64, a, b_))) for hq in range(2)]
            else:
                zparts = [(slice(0, 128), (lambda a, b_: ins["zc"][zrows[j]:zrows[j] + 128, a:b_]))]
            if b == 0:
                k.memset("pool", zb[:, 0:1], 0.0, [zk])
            for (psl, zf_) in zparts:
                if b == 0:
                    k.dma("sp", zb[psl, 1:513], zf_(0, 512), writes=[zk])
                elif zsplit and t0 % zsplit == 0:
                    k.dma("sp", zb[psl, 0:1], zf_(t0 - 1, t0), writes=[zk], allow_slow_non_contiguous=True)
                    k.dma("sp", zb[psl, 1:513], zf_(t0, t0 + 512), writes=[zk])
                else:
                    k.dma("sp", zb[psl, 0:513], zf_(t0 - 1, t0 + 512), writes=[zk])
            e = "pool"
            dt_ = dtmp[j % 2]; dk = f"dtmp{j%2}"
            k.tt(e, dt_[:], zb[:, 0:512], zb[:, 1:513], ALU.subtract, [zk], [dk])
            k.emit("dve", lambda: nc.vector.scalar_tensor_tensor(zs[:, j, :], dt_[:], cs["mu"][:, j:j + 1], zb[:, 1:513], ALU.mult, ALU.add),
                   [dk, zk, "c_mu"], [f"zs{j}"])
        k.act(tw[:], zs[0:64, 6, :], AF.Tanh, ["zs6"], ["tw"])
        k.act(sgx[:], zs[:, 7, :], AF.Sigmoid, ["zs7"], ["sgx"])
        for hp in range(2):
            m, mk = misc()
            k.mm(m[:], wa[64:128, hp * 128:(hp + 1) * 128], zs[64:128, 6, :], True, True, ["wa", "zs6"], [mk])
            k.act(aT[:, hp, :], m[:], AF.Sigmoid, [mk, "c_a0c"], [f"aT{hp}"], bias=cs["a0c"][:, hp:hp + 1])
            k.ts1("pool", tmpA[:], zs[:, 2 + hp, :], cs["kkw"][:, hp:hp + 1], ALU.mult, [f"zs{2+hp}", "c_kkw"], ["tmpA"])
            k.tt("pool", tmpB[:], tmpA[:], tmpA[:], ALU.mult, ["tmpA"], ["tmpB"])
            m, mk = misc()
            k.mm(m[:], cs["bones"][:], tmpB[:], True, True, ["c_bones", "tmpB"], [mk])
            k.act(tmpB[:], m[:], AF.Sqrt, [mk], ["tmpB"])
            k.ts1("dve", tmpB[:], tmpB[:], 1e-12, ALU.max, ["tmpB"], ["tmpB"])
            k.emit("dve", lambda: nc.vector.reciprocal(tmpB[:], tmpB[:]), ["tmpB"], ["tmpB"])
            k.tt("dve", kkT[:, hp, :], tmpA[:], tmpB[:], ALU.mult, ["tmpA", "tmpB"], [f"kkT{hp}"])
            k.ts("dve", tmpA[:], aT[:, hp, :], cs["ka"][:, hp:hp + 1], omk[:, hp:hp + 1], ALU.mult, ALU.add,
                 [f"aT{hp}", "c_ka", "omk"], ["tmpA"])
            k.tt("pool", kpT[:, hp, :], tmpA[:], zs[:, 2 + hp, :], ALU.mult, ["tmpA", f"zs{2+hp}"], [f"kpT{hp}"])
            k.tt("pool", kkaT[:, hp, :], kkT[:, hp, :], aT[:, hp, :], ALU.mult, [f"kkT{hp}", f"aT{hp}"], [f"kkaT{hp}"])
            k.tt("pool", tmpA[:], zs[:, hp, :], kpT[:, hp, :], ALU.mult, [f"zs{hp}", f"kpT{hp}"], ["tmpA"])
            m, mk = misc()
            k.mm(m[:], cs["rkb"][:, hp, :], tmpA[:], True, True, ["c_rkb", "tmpA"], [mk])
            k.tt("dve", bonT[:, hp, :], m[:], zs[:, 4 + hp, :], ALU.mult, [mk, f"zs{4+hp}"], [f"bonT{hp}"])
            m, mk = misc()
            k.mm(m[:], cs["g2c"][:, hp * 128:(hp + 1) * 128], sgx[:], True, True, ["c_g2c", "sgx"], [mk])
            k.cp("act", gTs[:, hp, :], m[:], [mk], [f"gTs{hp}"])

        ob = outb[b % 2]; obk = f"outb{b%2}"
        def prep(ch, par):
            csl = slice(ch * 128, ch * 128 + 128)
            Gi_, ARt_, Vtok_, Khat_, Bhat_, ATs_ = Gi[par], ARt[par], Vtok[par], Khat[par], Bhat[par], ATs[par]
            m, mk = misc()
            k.mm(m[:, 0:256], tw[:, csl], wa[0:64, :], True, False, ["tw", "wa"], [mk])
            k.mm(m[:, 0:256], ones1[:], cs["w0c"][:], False, True, ["ones1", "c_w0c"], [mk])
            k.act(sig[:], m[:, 0:256], AF.Sigmoid, [mk], ["sig"])
            m, mk = misc()
            for hp in range(2):
                k.mm(m[:, hp * 128:(hp + 1) * 128], sig[:, hp * 128:(hp + 1) * 128], cs["triI"][:], hp == 0, False, ["sig", "c_triI"], [mk], skip_group_check=True)
            for hp in range(2):
                k.mm(m[:, 256 + hp * 128:256 + (hp + 1) * 128], sig[:, hp * 128:(hp + 1) * 128], cs["triE"][:], False, hp == 1, ["sig", "c_triE"], [mk], skip_group_check=True)
            k.act(Gi_[:], m[:, 0:256], AF.Exp, [mk], [f"Gi{par}"])
            k.act(iG[:], m[:, 0:256], AF.Exp, [mk], ["iG"], scale=-1.0)
            k.act(Ge[:], m[:, 256:512], AF.Exp, [mk], ["Ge"])
            m, mk = misc()
            k.mm(m[:, 0:256], cs["triR"][:], sig[:], True, True, ["sig", "c_triR"], [mk])
            k.act(Gr[:], m[:, 0:256], AF.Exp, [mk], ["Gr"])
            yield
            for hp in range(2):
                k.emit("dve", lambda: nc.vector.scalar_tensor_tensor(ARt_[:, hp, 0, :], kkT[:, hp, csl], -1.0, Ge[:, hp, :], ALU.mult, ALU.mult),
                       [f"kkT{hp}", "Ge"], [f"ARt{par}"])
                k.tt("pool", ARt_[:, hp, 1, :], zs[:, hp, csl], Gi_[:, hp, :], ALU.mult, [f"zs{hp}", f"Gi{par}"], [f"ARt{par}"])
                k.tt("dve", KtT[:, hp, :], kpT[:, hp, csl], iG[:, hp, :], ALU.mult, [f"kpT{hp}", "iG"], ["KtT"])
                k.tt("dve", BtT[:, hp, :], kkaT[:, hp, csl], iG[:, hp, :], ALU.mult, [f"kkaT{hp}", "iG"], ["BtT"])
            yield
            m, mk = misc()
            for hp in range(2):
                k.tr(m[:, hp * 128:(hp + 1) * 128], zs[:, 4 + hp, csl], ident[:], [f"zs{4+hp}", "c_ident"], [mk])
            k.cp("act", Vtok_[:], m[:, 0:256], [mk], [f"Vtok{par}"])
            m, mk = misc()
            for hp in range(2):
                k.tr(m[:, hp * 128:(hp + 1) * 128], kpT[:, hp, csl], ident[:], [f"kpT{hp}", "c_ident"], [mk])
                k.tr(m[:, 256 + hp * 128:256 + (hp + 1) * 128], kkaT[:, hp, csl], ident[:], [f"kkaT{hp}", "c_ident"], [mk])
            k.tt("dve", Khat_[:], m[:, 0:256], Gr[:], ALU.mult, [mk, "Gr"], [f"Khat{par}"])
            k.tt("dve", Bhat_[:], m[:, 256:512], Gr[:], ALU.mult, [mk, "Gr"], [f"Bhat{par}"])
            yield
            for h in range(4):
                hp, e = h // 2, h % 2
                ps = slice(e * 64, (e + 1) * 64)
                k.mm(ATp[:, 0:256], BtT[ps, hp, :], ARt_[ps, hp, :, :], True, False, ["BtT", f"ARt{par}"], ["ATp"], skip_group_check=True)
                k.mm(ATp[:, 256:512], KtT[ps, hp, :], ARt_[ps, hp, :, :], False, True, ["KtT", f"ARt{par}"], ["ATp"], skip_group_check=True)
                k.mm(Q0p[:, h, :], ARt_[ps, hp, 0, :], BtT[ps, hp, :], h == 0, h == 3, [f"ARt{par}", "BtT"], ["Q0p"], skip_group_check=True)
                k.tt("dve", PTs[0][:, h, 0, :], ATp[:, 0:128], cs["mstrict"][:], ALU.mult, ["ATp", "c_mstrict"], [f"PTs0_{h // 2}"])
                k.tt("dve", ATs_[:, h, :, :], ATp[:, 128:512], cs["mask3"][:], ALU.mult, ["ATp", "c_mask3"], [f"ATs{par}_{h}"])
                k.cp("pool", PTs[0][:, h, 1, :], ident[:], ["c_ident"], [f"PTs0_{h // 2}"])
            for h in range(4):
                k.tt("dve", Qs[0][:, h, :], Q0p[:, h, :], cs["mlow"][:], ALU.mult, ["Q0p", "c_mlow"], [f"Qs0_{h // 2}"])
            yield
            for lv in range(7):
                cur, nxt = lv % 2, (lv + 1) % 2
                for pa in range(2):
                    sbt = SB if pa == 0 else Q0p
                    sbk = "SB" if pa == 0 else "Q0p"
                    rk = [f"Qs{cur}_{pa}", f"PTs{cur}_{pa}"]
                    for h in (2 * pa, 2 * pa + 1):
                        if lv < 6:
                            k.mm(SA[:, h, :], Qs[cur][:, h, :], PTs[cur][:, h, :, :], h % 2 == 0, h % 2 == 1,
                                 rk, [f"SA{pa}"], skip_group_check=True)
                        else:
                            k.mm(SA[:, h, 128:256], Qs[cur][:, h, :], PTs[cur][:, h, 1, :], h % 2 == 0, h % 2 == 1,
                                 rk, [f"SA{pa}"], skip_group_check=True)
                    if lv < 6:
                        for h in (2 * pa, 2 * pa + 1):
                            k.mm(sbt[:, h % 2, :], PTs[cur][:, h, 0, :], Qs[cur][:, h, :], h % 2 == 0, h % 2 == 1,
                                 rk, [sbk], skip_group_check=True)
                for pa in range(2):
                    sbt = SB if pa == 0 else Q0p
                    sbk = "SB" if pa == 0 else "Q0p"
                    hs = slice(2 * pa, 2 * pa + 2)
                    if lv < 6:
                        k.cp("act", PTs[nxt][:, hs, 0, :], SA[:, hs, 0:128], [f"SA{pa}"], [f"PTs{nxt}_{pa}"])
                        k.cp("act", Qs[nxt][:, hs, :], sbt[:, 0:2, :], [sbk], [f"Qs{nxt}_{pa}"])
                    k.tt("dve", PTs[nxt][:, hs, 1, :], SA[:, hs, 128:256], PTs[cur][:, hs, 1, :], ALU.add,
                         [f"SA{pa}", f"PTs{cur}_{pa}"], [f"PTs{nxt}_{pa}"])
                yield

        def tail(ch, par):
            csl = slice(ch * 128, ch * 128 + 128)
            Gi_, ARt_, Vtok_, Khat_, Bhat_, ATs_ = Gi[par], ARt[par], Vtok[par], Khat[par], Bhat[par], ATs[par]
            TT = PTs[1]
            m, mk = misc()
            for h in range(4):
                hp, e = h // 2, h % 2
                ps = slice(e * 64, (e + 1) * 64)
                k.mm(m[:, h * 64:(h + 1) * 64], ARt_[ps, hp, 0, :], Hst[ps, hp, :], h == 0, False, [f"ARt{par}", "Hst"], [mk], skip_group_check=True)
                k.mm(m[:, h * 64:(h + 1) * 64], ATs_[:, h, 1, :], Vtok_[:, h * 64:(h + 1) * 64], False, h == 3, [f"ATs{par}_{h}", f"Vtok{par}"], [mk], skip_group_check=True)
            k.cp("act", X0[:], m[:, 0:256], [mk], ["X0"])
            yield
            m, mk = misc()
            for h in range(4):
                k.mm(m[:, h * 64:(h + 1) * 64], TT[:, h, 1, :], X0[:, h * 64:(h + 1) * 64], h == 0, h == 3, [f"PTs1_{h // 2}", "X0"], [mk], skip_group_check=True)
            k.cp("act", Up[:], m[:, 0:256], [mk], ["Up"])
            yield
            my, myk = misc()
            for h in range(4):
                hp, e = h // 2, h % 2
                ps = slice(e * 64, (e + 1) * 64)
                k.mm(my[:, h * 64:(h + 1) * 64], ARt_[ps, hp, 1, :], Hst[ps, hp, :], h == 0, False, [f"ARt{par}", "Hst"], [myk], skip_group_check=True)
                k.mm(my[:, h * 64:(h + 1) * 64], ATs_[:, h, 0, :], Up[:, h * 64:(h + 1) * 64], False, False, [f"ATs{par}_{h}", "Up"], [myk], skip_group_check=True)
                k.mm(my[:, h * 64:(h + 1) * 64], ATs_[:, h, 2, :], Vtok_[:, h * 64:(h + 1) * 64], False, h == 3, [f"ATs{par}_{h}", f"Vtok{par}"], [myk], skip_group_check=True)
            k.cp("act", Ysb[:], my[:, 0:256], [myk], ["Ysb"])
            yield
            m, mk = misc()
            for hp in range(2):
                k.mm(m[:, hp * 128:(hp + 1) * 128], Bhat_[:, hp * 128:(hp + 1) * 128], Up[:, hp * 128:(hp + 1) * 128], hp == 0, False, [f"Bhat{par}", "Up"], [mk], skip_group_check=True)
                k.mm(m[:, hp * 128:(hp + 1) * 128], Khat_[:, hp * 128:(hp + 1) * 128], Vtok_[:, hp * 128:(hp + 1) * 128], False, hp == 1, [f"Khat{par}", f"Vtok{par}"], [mk], skip_group_check=True)
            for h in range(4):
                hp, e = h // 2, h % 2
                ps = slice(e * 64, (e + 1) * 64)
                k.emit("dve", lambda: nc.vector.scalar_tensor_tensor(Hst[ps, hp, :], Hst[ps, hp, :], Gi_[ps, hp, 127:128],
                                                                    m[ps, hp * 128 + e * 64:hp * 128 + (e + 1) * 64], ALU.mult, ALU.add),
                       ["Hst", f"Gi{par}", mk], ["Hst"])
            yield
            k.emit("dve", lambda: nc.vector.reduce_sum(st1[:], Ysb[:], AX.X), ["Ysb"], ["st1"])
            k.tt("pool", Ysq[:], Ysb[:], Ysb[:], ALU.mult, ["Ysb"], ["Ysq"])
            k.emit("dve", lambda: nc.vector.reduce_sum(st2[:], Ysq[:], AX.X), ["Ysq"], ["st2"])
            k.ts1("dve", mean[:], st1[:], 1.0 / 64, ALU.mult, ["st1"], ["mean"])
            k.tt("dve", msq[:], mean[:], mean[:], ALU.mult, ["mean"], ["msq"])
            k.emit("dve", lambda: nc.vector.scalar_tensor_tensor(rstd[:], st2[:], 1.0 / 64, msq[:], ALU.mult, ALU.subtract), ["st2", "msq"], ["rstd"])
            k.ts1("dve", rstd[:], rstd[:], GN_EPS, ALU.add, ["rstd"], ["rstd"])
            k.act(rstd[:], rstd[:], AF.Sqrt, ["rstd"], ["rstd"])
            k.emit("dve", lambda: nc.vector.reciprocal(rstd[:], rstd[:]), ["rstd"], ["rstd"])
            for h in range(4):
                e_ = "dve" if h % 2 == 0 else "pool"
                k.ts(e_, Yn[:, h * 64:(h + 1) * 64], Ysb[:, h, :], mean[:, h:h + 1], rstd[:, h:h + 1], ALU.subtract, ALU.mult,
                     ["Ysb", "mean", "rstd"], ["Yn"])
            yield
            m, mk = misc()
            for hp in range(2):
                k.tr(m[:, hp * 128:(hp + 1) * 128], Yn[:, hp * 128:(hp + 1) * 128], ident[:], ["Yn", "c_ident"], [mk])
            for hp in range(2):
                k.act(fin[:, hp, :], m[:, hp * 128:(hp + 1) * 128], AF.Identity, [mk, "c_lnw", "c_lnb"], ["fin"],
                      bias=cs["lnb"][:, hp:hp + 1], scale=cs["lnw"][:, hp:hp + 1])
                k.tt("pool", fin[:, hp, :], fin[:, hp, :], bonT[:, hp, csl], ALU.add, ["fin", f"bonT{hp}"], ["fin"])
                k.tt("pool", ob[:, hp, csl], fin[:, hp, :], gTs[:, hp, csl], ALU.mult, ["fin", f"gTs{hp}"], [obk])
            yield

        def run(*gens):
            gens = list(gens)
            while gens:
                for g_ in list(gens):
                    try:
                        next(g_)
                    except StopIteration:
                        gens.remove(g_)

        run(prep(0, 0))
        for ch in range(4):
            if ch < 3:
                run(prep(ch + 1, (ch + 1) % 2), tail(ch, ch % 2))
            else:
                run(tail(ch, ch % 2))
        for hp in range(2):
            yfn = ins.get("yfn") or (lambda r0, r1, a, b_: yT[r0:r1, a:b_])
            k.dma("pool", yfn(hp * 128, (hp + 1) * 128, t0, t0 + 512), ob[:, hp, :], reads=[obk])


WST = 2048
D = 1024
DFF = 2816
NF = DFF // 128
TB = 512

def rmsnorm_T(nc, k, xb, xk, hT, ss, rstd, junk, xn, pT, idb, eps, pfx):
    for s in range(4):
        k.act(junk[:], xb[:, s, :], AF.Square, [xk], [pfx + "junk", pfx + "ss"], accum_out=ss[:, s:s + 1])
    k.ts("dve", rstd[:], ss[:], 1.0 / D, eps, ALU.mult, ALU.add, [pfx + "ss"], [pfx + "rstd"])
    k.act(rstd[:], rstd[:], AF.Sqrt, [pfx + "rstd"], [pfx + "rstd"])
    k.emit("dve", lambda: nc.vector.reciprocal(rstd[:], rstd[:]), [pfx + "rstd"], [pfx + "rstd"])
    for s in range(4):
        xnk = pfx + f"xn{s%2}"
        k.ts1("dve", xn[s % 2][:], xb[:, s, :], rstd[:, s:s + 1], ALU.mult, [xk, pfx + "rstd"], [xnk])
        pk = pfx + f"pT{s%2}"
        for kc in range(8):
            k.tr(pT[s % 2][:, kc, :], xn[s % 2][:, kc * 128:(kc + 1) * 128], idb[:], [xnk, pfx + "idb"], [pk])
        k.cp("act", hT[:, :, s * 128:(s + 1) * 128], pT[s % 2][:], [pk], [pfx + "hT"])


def load_w_bf16(nc, k, dst, dkey, src, K, N, wst, wstk, scale_ap=None, skey=None):
    wv = src.rearrange("(kc p) n -> p kc n", p=128)
    step = WST // K
    i = 0
    for c0 in range(0, N, step):
        cw = min(step, N - c0)
        st = wst[i % 2]; sk = wstk + str(i % 2); i += 1
        stv = st[:, 0:K * cw].rearrange("p (kc n) -> p kc n", kc=K)
        k.dma("sp", stv, wv[:, :, c0:c0 + cw], writes=[sk])
        for kc in range(K):
            e = "dve" if kc % 2 == 0 else "pool"
            if scale_ap is not None:
                k.ts1(e, dst[:, kc, c0:c0 + cw], stv[:, kc, :], scale_ap[:, kc:kc + 1], ALU.mult, [sk, skey], [dkey])
            else:
                k.cp(e, dst[:, kc, c0:c0 + cw], stv[:, kc, :], [sk], [dkey])


def build_D1(NT):
    nc = bass.Bass("TRN2", target_bir_lowering=False)
    I = lambda nm, shp, dt=F32: nc.dram_tensor(nm, shp, dt, kind="ExternalInput").ap()
    ins = dict(x=I("x", [NT, D]), oaT=I("oaT", [512, NT], BF16), yrT=I("yrT", [512, NT], BF16), gT=I("gT", [2048, NT], BF16),
               woa=I("woa", [512, D]), wor=I("wor", [512, D]), wo=I("wo", [D, D]))
    xo = nc.dram_tensor("xo", [NT, D], F32, kind="ExternalOutput").ap()
    k = KB(nc)
    with Alloc(nc) as al:
        emit_D1(nc, k, al, NT, ins, xo)
        k.finish("sp")
    return nc


def emit_D1(nc, k, al, NT, ins, xo, pfx="E"):
    A = lambda nm, shp, dt=F32: al.sb(pfx + nm, shp, dt)
    P = lambda nm, shp, dt=F32: al.ps(pfx + nm, shp, dt)
    woa = A("woa", [128, 4, D], BF16); wor = A("wor", [128, 4, D], BF16); wo = A("wo", [128, 8, D], BF16)
    wst = [A(f"wst{i}", [128, WST]) for i in range(2)]
    load_w_bf16(nc, k, woa, pfx + "woa", ins["woa"], 4, D, wst, pfx + "wst")
    load_w_bf16(nc, k, wor, pfx + "wor", ins["wor"], 4, D, wst, pfx + "wst")
    load_w_bf16(nc, k, wo, pfx + "wo", ins["wo"], 8, D, wst, pfx + "wst")
    oab = [A(f"oab{i}", [128, 4, TB], BF16) for i in range(2)]
    yrb = [A(f"yrb{i}", [128, 4, TB], BF16) for i in range(2)]
    xb = [A(f"xb{i}", [128, 4, D]) for i in range(2)]
    gab = [A(f"gab{i}", [128, TB], BF16) for i in range(3)]
    grb = [A(f"grb{i}", [128, TB], BF16) for i in range(3)]
    t1 = [A(f"t1_{i}", [128, TB]) for i in range(2)]
    t2 = [A(f"t2_{i}", [128, TB]) for i in range(2)]
    mT = A("mT", [128, 8, TB], BF16)
    pa = [P(f"pa{i}", [128, 512]) for i in range(2)]
    pr = [P(f"pr{i}", [128, 512]) for i in range(2)]
    po = [P(f"po{i}", [128, 512]) for i in range(3)]
    nblk = NT // TB
    oav = ins["oaT"].rearrange("(kc p) t -> p kc t", p=128) if ins.get("oaT") is not None else None
    yrv = ins["yrT"].rearrange("(kc p) t -> p kc t", p=128) if ins.get("yrT") is not None else None
    xv = ins["x"].rearrange("(b s p) d -> b p s d", p=128, s=4)
    xov = xo.rearrange("(b s p) d -> b p s d", p=128, s=4)

    def load_blk(b):
        t0 = b * TB
        if ins.get("oafn"):
            for kc in range(4):
                k.dma("sp", oab[b % 2][:, kc, :], ins["oafn"](kc, t0, t0 + TB), writes=[pfx + f"oab{b%2}"])
                k.dma("sp", yrb[b % 2][:, kc, :], ins["yrfn"](kc, t0, t0 + TB), writes=[pfx + f"yrb{b%2}"])
        else:
            k.dma("sp", oab[b % 2][:], oav[:, :, t0:t0 + TB], writes=[pfx + f"oab{b%2}"])
            k.dma("sp", yrb[b % 2][:], yrv[:, :, t0:t0 + TB], writes=[pfx + f"yrb{b%2}"])
        k.dma("sp", xb[b % 2][:], xv[b], writes=[pfx + f"xb{b%2}"])
    load_blk(0)
    gi = [0]; oi = [0]
    for b in range(nblk):
        t0 = b * TB
        if b + 1 < nblk:
            load_blk(b + 1)
        for oc in range(8):
            i = gi[0]; gi[0] += 1
            ga = gab[i % 3]; gr = grb[i % 3]
            k.dma("sp", ga[:], ins["gT"][oc * 128:(oc + 1) * 128, t0:t0 + TB], writes=[pfx + f"gab{i%3}"])
            k.dma("sp", gr[:], ins["gT"][1024 + oc * 128:1024 + (oc + 1) * 128, t0:t0 + TB], writes=[pfx + f"grb{i%3}"])
            pai = pa[i % 2]; pri = pr[i % 2]
            for kc in range(4):
                k.mm(pai[:], woa[:, kc, oc * 128:(oc + 1) * 128], oab[b % 2][:, kc, :], kc == 0, kc == 3,
                     [pfx + "woa", pfx + f"oab{b%2}"], [pfx + f"pa{i%2}"])
            for kc in range(4):
                k.mm(pri[:], wor[:, kc, oc * 128:(oc + 1) * 128], yrb[b % 2][:, kc, :], kc == 0, kc == 3,
                     [pfx + "wor", pfx + f"yrb{b%2}"], [pfx + f"pr{i%2}"])
            k.tt("dve", t1[i % 2][:], pai[:], ga[:], ALU.mult, [pfx + f"pa{i%2}", pfx + f"gab{i%3}"], [pfx + f"t1_{i%2}"])
            k.tt("dve", t2[i % 2][:], pri[:], gr[:], ALU.mult, [pfx + f"pr{i%2}", pfx + f"grb{i%3}"], [pfx + f"t2_{i%2}"])
            k.tt("pool", mT[:, oc, :], t1[i % 2][:], t2[i % 2][:], ALU.add, [pfx + f"t1_{i%2}", pfx + f"t2_{i%2}"], [pfx + "mT"])
        for s in range(4):
            for n in range(2):
                j = oi[0]; oi[0] += 1
                pj = po[j % 3]
                for kc in range(8):
                    k.mm(pj[:], mT[:, kc, s * 128:(s + 1) * 128], wo[:, kc, n * 512:(n + 1) * 512], kc == 0, kc == 7,
                         [pfx + "mT", pfx + "wo"], [pfx + f"po{j%3}"])
                k.tt("dve", xb[b % 2][:, s, n * 512:(n + 1) * 512], pj[:], xb[b % 2][:, s, n * 512:(n + 1) * 512], ALU.add,
                     [pfx + f"po{j%3}", pfx + f"xb{b%2}"], [pfx + f"xb{b%2}"])
        k.dma("pool", xov[b], xb[b % 2][:], reads=[pfx + f"xb{b%2}"])


def build_D2(NT, last):
    nc = bass.Bass("TRN2", target_bir_lowering=False)
    I = lambda nm, shp, dt=F32: nc.dram_tensor(nm, shp, dt, kind="ExternalInput").ap()
    ins = dict(x=I("x", [NT, D]), p=I("p", [NT, 256]), wg=I("wg", [D, DFF]), wu=I("wu", [D, DFF]), wd=I("wd", [DFF, D]),
               wple=I("wple", [256, D]), wpg=I("wpg", [D, D]), gffn=I("gffn", [128, 8]), gple=I("gple", [128, 8]),
               gfin=I("gfin", [128, D]), ident=I("ident", [128, 128]))
    xo = nc.dram_tensor("xo", [NT, D], F32, kind="ExternalOutput").ap()
    wgS = nc.dram_tensor("wgS", [NF, 128, 1024], BF16).ap()
    wuS = nc.dram_tensor("wuS", [NF, 128, 1024], BF16).ap()
    k = KB(nc)
    with Alloc(nc) as al:
        emit_D2(nc, k, al, NT, last, ins, xo, wgS, wuS)
        k.finish("sp")
    return nc


def emit_D2(nc, k, al, NT, last, ins, xo, wgS, wuS, pfx="F"):
    A = lambda nm, shp, dt=F32: al.sb(pfx + nm, shp, dt)
    P = lambda nm, shp, dt=F32: al.ps(pfx + nm, shp, dt)
    gffn = A("gffn", [128, 8]); gple = A("gple", [128, 8]); idf = A("idf", [128, 128]); idb = A("idb", [128, 128], BF16)
    k.dma("sp", gffn[:], ins["gffn"], writes=[pfx + "gffn"])
    k.dma("sp", gple[:], ins["gple"], writes=[pfx + "gple"])
    k.dma("sp", idf[:], ins["ident"], writes=[pfx + "idf"])
    k.cp("dve", idb[:], idf[:], [pfx + "idf"], [pfx + "idb"])
    if last:
        gfin = A("gfin", [128, D])
        k.dma("sp", gfin[:], ins["gfin"], writes=[pfx + "gfin"])
    wd = A("wd", [128, NF, D], BF16); wple = A("wple", [128, 2, D], BF16); wpg = A("wpg", [128, 8, D], BF16)
    wst = [A(f"wst{i}", [128, WST]) for i in range(2)]
    load_w_bf16(nc, k, wple, pfx + "wple", ins["wple"], 2, D, wst, pfx + "wst")
    load_w_bf16(nc, k, wpg, pfx + "wpg", ins["wpg"], 8, D, wst, pfx + "wst", gple, pfx + "gple")
    wdv = ins["wd"].rearrange("(f p) n -> p f n", p=128)
    i = 0
    for f0 in range(0, NF, 2):
        fw_ = min(2, NF - f0)
        st = wst[i % 2]; sk = pfx + f"wst{i%2}"; i += 1
        stv = st[:, 0:fw_ * D].rearrange("p (f n) -> p f n", f=fw_)
        k.dma("sp", stv, wdv[:, f0:f0 + fw_, :], writes=[sk])
        for f in range(fw_):
            e = "dve" if f % 2 == 0 else "pool"
            k.cp(e, wd[:, f0 + f, :], stv[:, f, :], [sk], [pfx + "wd"])
    wcb = [A(f"wcb{i}", [128, 8, 256], BF16) for i in range(2)]
    ci = 0
    for (src, dstS, nm) in ((ins["wg"], wgS, "wgS"), (ins["wu"], wuS, "wuS")):
        wv = src.rearrange("(kc p) n -> p kc n", p=128)
        for c0 in range(0, DFF, 256):
            cw = min(256, DFF - c0)
            st = wst[i % 2]; sk = pfx + f"wst{i%2}"; i += 1
            stv = st[:, 0:8 * cw].rearrange("p (kc n) -> p kc n", kc=8)
            k.dma("sp", stv, wv[:, :, c0:c0 + cw], writes=[sk])
            cb = wcb[ci % 2]; cbk = pfx + f"wcb{ci%2}"; ci += 1
            for kc in range(8):
                e = "dve" if kc % 2 == 0 else "pool"
                k.ts1(e, cb[:, kc, 0:cw], stv[:, kc, :], gffn[:, kc:kc + 1], ALU.mult, [sk, pfx + "gffn"], [cbk])
            for j in range(cw // 128):
                f = c0 // 128 + j
                k.dma("pool", dstS[f].rearrange("p (kc n) -> p kc n", kc=8), cb[:, :, j * 128:(j + 1) * 128], reads=[cbk], writes=[nm])

    xb = [A(f"xb{i}", [128, 4, D]) for i in range(2)]
    pb = [A(f"pb{i}", [128, 4, 256]) for i in range(2)]
    junk = A("junk", [128, D], BF16)
    ss = A("ss", [128, 4]); rstd = A("rstd", [128, 4])
    xn = [A(f"xn{i}", [128, D], BF16) for i in range(2)]
    hT = A("hT", [128, 8, TB], BF16)
    ppT = A("ppT", [128, 2, TB], BF16)
    aT = A("aT", [128, NF, TB], BF16)
    NWB = 3
    wgb = [A(f"wgb{i}", [128, 8, 128], BF16) for i in range(NWB)]
    wub = [A(f"wub{i}", [128, 8, 128], BF16) for i in range(NWB)]
    sl = [A(f"sl{i}", [128, TB]) for i in range(2)]
    tmp = [A(f"tmp{i}", [128, 512]) for i in range(2)]
    pT = [P(f"pT{i}", [128, 8, 128], BF16) for i in range(2)]
    pg = [P(f"pg{i}", [128, 512]) for i in range(2)]
    pu = [P(f"pu{i}", [128, 512]) for i in range(2)]
    pd = [P(f"pd{i}", [128, 512]) for i in range(2)]
    nblk = NT // TB
    xv = ins["x"].rearrange("(b s p) d -> b p s d", p=128, s=4)
    pv = ins["p"].rearrange("(b s p) d -> b p s d", p=128, s=4)
    xov = xo.rearrange("(b s p) d -> b p s d", p=128, s=4)

    def load_blk(b):
        k.dma("sp", xb[b % 2][:], xv[b], writes=[pfx + f"xb{b%2}"])
        k.dma("sp", pb[b % 2][:], pv[b], writes=[pfx + f"pb{b%2}"])
    load_blk(0)
    wi = [0]; di = [0]
    for b in range(nblk):
        if b + 1 < nblk:
            load_blk(b + 1)
        X = xb[b % 2]; xk = pfx + f"xb{b%2}"
        rmsnorm_T(nc, k, X, xk, hT, ss, rstd, junk, xn, pT, idb, 1e-6, pfx)
        def load_w(f):
            i = f % NWB
            k.dma("sp", wgb[i][:], wgS[f].rearrange("p (kc n) -> p kc n", kc=8), reads=["wgS"], writes=[pfx + f"wgb{i}"])
            k.dma("sp", wub[i][:], wuS[f].rearrange("p (kc n) -> p kc n", kc=8), reads=["wuS"], writes=[pfx + f"wub{i}"])
        load_w(0); load_w(1)
        for f in range(NF):
            if f + 2 < NF:
                load_w(f + 2)
            i = f % NWB
            j = wi[0]; wi[0] += 1
            for kc in range(8):
                k.mm(pg[j % 2][:], wgb[i][:, kc, :], hT[:, kc, :], kc == 0, kc == 7, [pfx + f"wgb{i}", pfx + "hT"], [pfx + f"pg{j%2}"])
            for kc in range(8):
                k.mm(pu[j % 2][:], wub[i][:, kc, :], hT[:, kc, :], kc == 0, kc == 7, [pfx + f"wub{i}", pfx + "hT"], [pfx + f"pu{j%2}"])
            k.act(sl[j % 2][:], pg[j % 2][:], AF.Silu, [pfx + f"pg{j%2}"], [pfx + f"sl{j%2}"])
            k.tt("dve", aT[:, f, :], pu[j % 2][:], sl[j % 2][:], ALU.mult, [pfx + f"pu{j%2}", pfx + f"sl{j%2}"], [pfx + "aT"])
        for s in range(4):
            for n in range(2):
                j = di[0]; di[0] += 1
                for f in range(NF):
                    k.mm(pd[j % 2][:], aT[:, f, s * 128:(s + 1) * 128], wd[:, f, n * 512:(n + 1) * 512], f == 0, f == NF - 1,
                         [pfx + "aT", pfx + "wd"], [pfx + f"pd{j%2}"])
                k.tt("dve", X[:, s, n * 512:(n + 1) * 512], pd[j % 2][:], X[:, s, n * 512:(n + 1) * 512], ALU.add,
                     [pfx + f"pd{j%2}", xk], [xk])
        rmsnorm_T(nc, k, X, xk, hT, ss, rstd, junk, xn, pT, idb, 1e-6, pfx)
        PB = pb[b % 2]; pbk = pfx + f"pb{b%2}"
        for s in range(4):
            k.cp("pool", xn[s % 2][:, 0:256], PB[:, s, :], [pbk], [pfx + f"xn{s%2}"])
            for kc in range(2):
                k.tr(pT[s % 2][:, kc, :], xn[s % 2][:, kc * 128:(kc + 1) * 128], idb[:], [pfx + f"xn{s%2}", pfx + "idb"], [pfx + f"pT{s%2}"])
            k.cp("act", ppT[:, :, s * 128:(s + 1) * 128], pT[s % 2][:, 0:2, :], [pfx + f"pT{s%2}"], [pfx + "ppT"])
        for s in range(4):
            for n in range(2):
                j = di[0]; di[0] += 1
                for kc in range(8):
                    k.mm(pg[j % 2][:], hT[:, kc, s * 128:(s + 1) * 128], wpg[:, kc, n * 512:(n + 1) * 512], kc == 0, kc == 7,
                         [pfx + "hT", pfx + "wpg"], [pfx + f"pg{j%2}"])
                for kc in range(2):
                    k.mm(pu[j % 2][:], ppT[:, kc, s * 128:(s + 1) * 128], wple[:, kc, n * 512:(n + 1) * 512], kc == 0, kc == 1,
                         [pfx + "ppT", pfx + "wple"], [pfx + f"pu{j%2}"])
                k.act(tmp[j % 2][:], pg[j % 2][:], AF.Sigmoid, [pfx + f"pg{j%2}"], [pfx + f"tmp{j%2}"])
                k.tt("dve", tmp[j % 2][:], pu[j % 2][:], tmp[j % 2][:], ALU.mult, [pfx + f"pu{j%2}", pfx + f"tmp{j%2}"], [pfx + f"tmp{j%2}"])
                k.tt("pool", X[:, s, n * 512:(n + 1) * 512], X[:, s, n * 512:(n + 1) * 512], tmp[j % 2][:], ALU.add,
                     [xk, pfx + f"tmp{j%2}"], [xk])
        if last:
            for s in range(4):
                k.act(junk[:], X[:, s, :], AF.Square, [xk], [pfx + "junk", pfx + "ss"], accum_out=ss[:, s:s + 1])
            k.ts("dve", rstd[:], ss[:], 1.0 / D, 1e-6, ALU.mult, ALU.add, [pfx + "ss"], [pfx + "rstd"])
            k.act(rstd[:], rstd[:], AF.Sqrt, [pfx + "rstd"], [pfx + "rstd"])
            k.emit("dve", lambda: nc.vector.reciprocal(rstd[:], rstd[:]), [pfx + "rstd"], [pfx + "rstd"])
            for s in range(4):
                k.emit("dve", lambda: nc.vector.scalar_tensor_tensor(X[:, s, :], X[:, s, :], rstd[:, s:s + 1], gfin[:], ALU.mult, ALU.mult),
                       [xk, pfx + "rstd", pfx + "gfin"], [xk])
        k.dma("pool", xov[b], X[:], reads=[xk])

PAIRS = [[0, 1], [2, 3], [4, 5], [6, 7]]
CPAR = [e for e in C_IN[1:12]]
CCON = [e for e in C_IN[12:]]
DS = bass.DynSlice


def build_fused8(HALF, depth):
    S = 2 * HALF
    nc = bass.Bass("TRN2", target_bir_lowering=False)
    I = lambda nm, shp, dt=F32: nc.dram_tensor(nm, list(shp), dt, kind="ExternalInput").ap()
    Sc = lambda nm, shp, dt: nc.dram_tensor(nm, list(shp), dt).ap()
    x = I("x", [HALF, D]); p = I("p", [depth, HALF, 256])
    w_in = I("w_in", [depth, D, INC]); gam = I("gam", [depth, 128, 8]); ident = I("ident", [128, 128])
    G = I("G", [4, 128, 256]); b31 = I("b31", [128, 4]); neg = I("neg", [128, 256])
    lamv = I("lamv", [depth, 128, 4, 64]); sgain = I("sgain", [depth, 128, 128])
    cpar = {nm: I("c_" + nm, [depth] + list(shp)) for nm, shp, dt in CPAR}
    ccon = {nm: (ident if nm == "ident" else I("k_" + nm, shp)) for nm, shp, dt in CCON}
    woa = I("woa", [depth, 512, D]); wor = I("wor", [depth, 512, D]); wo = I("wo", [depth, D, D])
    wg = I("wg", [depth, D, DFF]); wu = I("wu", [depth, D, DFF]); wd = I("wd", [depth, DFF, D])
    wple = I("wple", [depth, 256, D]); wpg = I("wpg", [depth, D, D])
    gffn = I("gffn", [depth, 128, 8]); gple = I("gple", [depth, 128, 8]); gfin = I("gfin", [128, D])
    out = nc.dram_tensor("out", [HALF, D], F32, kind="ExternalOutput").ap()
    PV = min(2048, HALF)
    NPV = HALF // PV
    gT = Sc("s_gT", [2048, HALF], BF16)
    xmid = Sc("s_xmid", [HALF, D], F32); x1 = Sc("s_x1", [HALF, D], F32)
    wgS = Sc("s_wgS", [NF, 128, 1024], BF16); wuS = Sc("s_wuS", [NF, 128, 1024], BF16)
    k = KB(nc)
    pr_p = nc.gpsimd.partition_id() % 2
    pr_s = nc.sync.partition_id() % 2
    DS1 = lambda v: DS(v, 1)

    def sel(t, v):
        return t[DS1(v)].rearrange("a r c -> (a r) c")

    def exch(tag, P_, F_, rows, cols, dt, static_src=False):
        stg = Sc(f"x_stg_{tag}", [rows, cols], dt)
        rcv = Sc(f"x_rcv_{tag}", [2 * rows, cols], dt)
        rcv3 = rcv.rearrange("(a r) c -> a r c", a=2)
        if static_src:
            k.dma("pool", stg, P_, writes=[f"stg{tag}"])
            k.dma("sp", sel(F_, pr_s), P_)
        else:
            k.dma("pool", stg, sel(P_, 1 - pr_p), writes=[f"stg{tag}"])
            k.dma("sp", sel(F_, pr_s), sel(P_, pr_s))
        k.coll("AllGather", stg, rcv, PAIRS, reads=[f"stg{tag}"], writes=[f"rcv{tag}"])
        k.dma("sp", sel(F_, 1 - pr_s), sel(rcv3, 1 - pr_s), reads=[f"rcv{tag}"])

    for i in range(depth):
        last = (i == depth - 1)
        xin = x if i == 0 else x1
        lam_init = 0.8 - 0.6 * math.exp(-0.3 * i)
        L = f"L{i}"
        QP = [Sc(f"{L}QP{b_}", [2, 128, HALF], BF16) for b_ in range(2)]
        KP = [Sc(f"{L}KP{b_}", [2, 128, HALF], BF16) for b_ in range(2)]
        VP = [Sc(f"{L}VP{b_}", [2, PV, 256], BF16) for b_ in range(NPV)]
        ZP = [[Sc(f"{L}ZP{j}_{q}", [2, 64, HALF], F32) for q in range(4)] for j in range(3)]
        ZL = [Sc(f"{L}ZL{q}", [64, HALF], F32) for q in range(4)]
        QF = [Sc(f"{L}QF{b_}", [2, 128, HALF], BF16) for b_ in range(2)]
        KF = [Sc(f"{L}KF{b_}", [2, 128, HALF], BF16) for b_ in range(2)]
        VF = [Sc(f"{L}VF{b_}", [2, PV, 256], BF16) for b_ in range(NPV)]
        ZF = [Sc(f"{L}ZF{r}", [2, 64, HALF], F32) for r in range(16)]
        OP = [Sc(f"{L}OP{b_}", [2, 128, HALF], BF16) for b_ in range(2)]
        YP = [Sc(f"{L}YP{b_}", [2, 128, HALF], BF16) for b_ in range(2)]
        OAF = [Sc(f"{L}OAF{b_}", [2, 128, HALF], BF16) for b_ in range(2)]
        YF = [Sc(f"{L}YF{b_}", [2, 128, HALF], BF16) for b_ in range(2)]

        def zcb(cc, hq, t0):
            if cc < 12:
                j, g_, rb = cc // 4, (cc % 4) // 2, cc % 2
                return ZP[j][rb * 2 + hq][g_, :, t0:t0 + TB]
            return ZL[(cc - 12) * 2 + hq][:, t0:t0 + TB]
        ocb = dict(q=lambda cc, t0: QP[cc % 2][cc // 2, :, t0:t0 + TB],
                   k=lambda cc, t0: KP[cc % 2][cc // 2, :, t0:t0 + TB],
                   v=lambda g_, t: VP[t // PV][g_, t % PV:t % PV + 128, :],
                   z=zcb)
        with Alloc(nc) as al:
            emit_A(nc, k, al, HALF, xin, w_in[i], gam[i], ident, None, None, None, None, gT, pfx=f"A{i}", ocb=ocb)
        k.barrier()
        for b_ in range(2):
            exch(f"{L}q{b_}", QP[b_], QF[b_], 128, HALF, BF16)
            exch(f"{L}k{b_}", KP[b_], KF[b_], 128, HALF, BF16)
        for b_ in range(NPV):
            exch(f"{L}v{b_}", VP[b_], VF[b_], PV, 256, BF16)
        for j in range(3):
            for q in range(4):
                exch(f"{L}z{j}_{q}", ZP[j][q], ZF[j * 4 + q], 64, HALF, F32)
        for q in range(4):
            exch(f"{L}zl{q}", ZL[q], ZF[12 + q], 64, HALF, F32, static_src=True)
        k.barrier()

        def hm(T_):
            def fn(r0, r1, a, b_):
                h = a // HALF
                assert (b_ - 1) // HALF == h and r1 - r0 == 128
                return T_[r0 // 128][h, :, a - h * HALF:b_ - h * HALF]
            return fn

        def vfn(a, b_):
            h = a // HALF
            tl = a - h * HALF
            assert (b_ - 1) // HALF == h and tl // PV == (tl + (b_ - a) - 1) // PV
            return VF[tl // PV][h, tl % PV:tl % PV + (b_ - a), :]

        def zfn64(r0, a, b_):
            h = a // HALF
            assert (b_ - 1) // HALF == h
            return ZF[r0 // 64][h, :, a - h * HALF:b_ - h * HALF]
        with Alloc(nc) as al:
            insB = dict(qfn=hm(QF), kfn=hm(KF), ofn=hm(OP), vfn=vfn, G=G, b31=b31, neg=neg,
                        lamv=lamv[i], sgain=sgain[i], ident=ident)
            emit_B(nc, k, al, S, lam_init, insB, None, pfx=f"B{i}")
        k.barrier()
        with Alloc(nc) as al:
            insC = {nm: cpar[nm][i] for nm, shp, dt in CPAR}
            insC.update(ccon)
            insC["zfn64"] = zfn64
            insC["zsplit"] = HALF
            insC["yfn"] = hm(YP)
            emit_C(nc, k, al, S, insC, None, pfx=f"C{i}")
        k.barrier()
        for b_ in range(2):
            exch(f"{L}o{b_}", OP[b_], OAF[b_], 128, HALF, BF16)
            exch(f"{L}y{b_}", YP[b_], YF[b_], 128, HALF, BF16)
        k.barrier()
        with Alloc(nc) as al:
            emit_D1(nc, k, al, HALF, dict(x=xin, gT=gT, woa=woa[i], wor=wor[i], wo=wo[i],
                                          oafn=lambda kc, a, b_: OAF[kc % 2][kc // 2, :, a:b_],
                                          yrfn=lambda kc, a, b_: YF[kc % 2][kc // 2, :, a:b_]), xmid, pfx=f"E{i}")
        k.barrier()
        with Alloc(nc) as al:
            emit_D2(nc, k, al, HALF, last, dict(x=xmid, p=p[i], wg=wg[i], wu=wu[i], wd=wd[i], wple=wple[i], wpg=wpg[i],
                                                gffn=gffn[i], gple=gple[i], gfin=gfin, ident=ident),
                    out if last else x1, wgS, wuS, pfx=f"F{i}")
        k.barrier()
    k.finish("sp")
    return nc


def _pp(v):
    return np.ascontiguousarray(np.asarray(v, np.float32).reshape(-1, 128).T)


def kernel(x, p, rel_bias, norm_mix, w_in, lam_q1, lam_k1, lam_q2, lam_k2, attn_subln,
           rwkv_mu, rwkv_w0, rwkv_w2, rwkv_a0, rwkv_a2, rwkv_g2, rwkv_kk, rwkv_ka, rwkv_rk,
           rwkv_lnx_w, rwkv_lnx_b, w_out_attn, w_out_rwkv, w_out, norm_ffn, w_ffn_gate,
           w_ffn_up, w_ffn_down, norm_ple, w_ple, w_ple_gate, norm_final):
    f32 = np.float32
    A_ = lambda a: np.ascontiguousarray(np.asarray(a, f32))
    x = A_(x); p = A_(p); rel_bias = A_(rel_bias)
    B_, S_, D_ = x.shape
    HALF = S_ // 2
    NC = 2 * B_
    depth = int(np.asarray(w_in).shape[0])
    bc = lambda v, shape: np.ascontiguousarray(np.broadcast_to(v, shape))
    tidx = toeplitz_idx()
    com = dict(
        w_in=A_(w_in), gam=np.stack([_pp(norm_mix[i]) for i in range(depth)]), ident=np.eye(128, dtype=f32),
        neg=neg_mask(),
        lamv=np.stack([bc(np.stack([A_(lam_q1[i]), A_(lam_k1[i]), A_(lam_q2[i]), A_(lam_k2[i])])[None], (128, 4, 64)) for i in range(depth)]),
        sgain=np.stack([bc(A_(attn_subln[i])[None], (128, 128)) for i in range(depth)]),
        woa=A_(w_out_attn), wor=A_(w_out_rwkv), wo=A_(w_out), wg=A_(w_ffn_gate), wu=A_(w_ffn_up), wd=A_(w_ffn_down),
        wple=A_(w_ple), wpg=A_(w_ple_gate),
        gffn=np.stack([_pp(norm_ffn[i]) for i in range(depth)]), gple=np.stack([_pp(norm_ple[i]) for i in range(depth)]),
        gfin=bc(A_(norm_final)[None], (128, D_)),
    )
    cC = consts_C()
    for nm, shp, dt in CCON:
        if nm != "ident":
            com["k_" + nm] = cC[nm]
    blk = (np.arange(128)[:, None] // 64 == np.arange(128)[None, :] // 64)
    grp = []
    for hh in range(2):
        d = dict(G=np.ascontiguousarray(np.stack([rel_bias[tidx, 2 * hh + hl, c] for hl in range(2) for c in range(2)])),
                 b31=bc(np.stack([rel_bias[31, 2 * hh + hl, c] for hl in range(2) for c in range(2)])[None, :], (128, 4)))
        sl = np.arange(hh * 256, (hh + 1) * 256)
        idx = np.concatenate([sl, 512 + sl, 1024 + sl, np.arange(1536, 1792)])
        cp = {nm: [] for nm, shp, dt in CPAR}
        for i in range(depth):
            rkf = A_(rwkv_rk[i]).reshape(-1)[sl]
            rkb = np.stack([np.where(blk, rkf[hp * 128:(hp + 1) * 128][:, None], f32(0)) for hp in range(2)], 1).astype(f32)
            e = dict(mu=_pp(A_(rwkv_mu[i])[idx]), w2c=A_(rwkv_w2[i])[:, sl], w0c=A_(rwkv_w0[i])[None, sl],
                     a2c=A_(rwkv_a2[i])[:, sl], a0c=_pp(A_(rwkv_a0[i])[sl]), g2c=A_(rwkv_g2[i])[:, sl],
                     kkw=_pp(A_(rwkv_kk[i])[sl]), ka=_pp(A_(rwkv_ka[i])[sl]), rkb=rkb,
                     lnw=_pp(A_(rwkv_lnx_w[i])[sl]), lnb=_pp(A_(rwkv_lnx_b[i])[sl]))
            for nm in cp:
                cp[nm].append(np.ascontiguousarray(e[nm]))
        for nm in cp:
            d["c_" + nm] = np.ascontiguousarray(np.stack(cp[nm]))
        grp.append(d)
    nc = build_fused8(HALF, depth)
    in_maps = []
    for c in range(NC):
        b, pi = c // 2, c % 2
        d = dict(com)
        d.update(grp[pi])
        d["x"] = np.ascontiguousarray(x[b, pi * HALF:(pi + 1) * HALF])
        d["p"] = np.ascontiguousarray(p[:, b, pi * HALF:(pi + 1) * HALF])
        in_maps.append(d)
    res = run_bass_kernel_spmd(nc, in_maps, core_ids=list(range(NC)))
    out = np.empty((B_, S_, D_), f32)
    for c in range(NC):
        out[c // 2, (c % 2) * HALF:(c % 2 + 1) * HALF] = np.asarray(res.results[c]["out"], f32)
    return out
```

```python
import math
import numpy as np
import concourse.bass as bass
import concourse.mybir as mybir
from concourse.bass_utils import run_bass_kernel_spmd

F32 = mybir.dt.float32
BF16 = mybir.dt.bfloat16
AF = mybir.ActivationFunctionType
ALU = mybir.AluOpType
AX = mybir.AxisListType

EPOCH = 30000
DMA_RING = 8
DMA_EPOCH = 700


class KB:
    def __init__(self, nc):
        self.nc = nc
        self.eng = {"pe": nc.tensor, "act": nc.scalar, "dve": nc.vector,
                    "pool": nc.gpsimd, "sp": nc.sync}
        self.sem = {}
        self.cnt = {}
        self.nsem = 0
        for e in ("pe", "act", "dve", "pool"):
            self._new_sem(e)
        self.seen = {e: {} for e in self.eng}
        self.semobj = {}
        self.lastw = {}
        self.readers = {}
        self.dq = {}
        for q in ("sp", "pool", "act"):
            self.dq[q] = {"n": 0, "sems": None, "vals": None, "tok": [None] * DMA_RING}
        self.ninst = {e: 0 for e in self.eng}
        self.colltoks = []

    def _alloc_sem(self, name):
        self.nsem += 1
        s = self.nc.alloc_semaphore(f"{name}_{self.nsem}")
        return s

    def _new_sem(self, e):
        self.sem[e] = self._alloc_sem("s" + e)
        self.cnt[e] = 0

    def _wait(self, e, tok):
        if tok is None:
            return
        s, v, pe = tok
        sid = id(s)
        if self.seen[e].get(sid, 0) >= v:
            return
        self.eng[e].wait_ge(s, v)
        self.seen[e][sid] = v

    def _deps(self, e, reads, writes):
        toks = []
        for k in list(reads) + list(writes):
            t = self.lastw.get(k)
            if t is not None:
                toks.append(t)
        for k in writes:
            toks.extend(self.readers.get(k, ()))
        for t in toks:
            if e == "pe" and t[2] == "pe":
                continue
            self._wait(e, t)

    def _record(self, tok, reads, writes):
        for k in writes:
            self.lastw[k] = tok
            self.readers[k] = []
        for k in reads:
            self.readers.setdefault(k, []).append(tok)

    def emit(self, e, fn, reads=(), writes=()):
        self._deps(e, reads, writes)
        if self.cnt[e] >= EPOCH:
            self._new_sem(e)
        ins = fn()
        self.cnt[e] += 1
        ins.then_inc(self.sem[e], 1)
        tok = (self.sem[e], self.cnt[e], e)
        self._record(tok, reads, writes)
        self.ninst[e] += 1
        return tok

    def dma(self, q, out, in_, reads=(), writes=(), **kw):
        d = self.dq[q]
        n = d["n"]
        slot = n % DMA_RING
        if n % (DMA_RING * DMA_EPOCH) == 0:
            d["sems"] = [self._alloc_sem("d" + q) for _ in range(DMA_RING)]
            d["vals"] = [0] * DMA_RING
            for t in d["tok"]:
                self._wait(q, t)
        self._wait(q, d["tok"][slot])
        self._deps(q, reads, writes)
        ins = self.eng[q].dma_start(out=out, in_=in_, **kw)
        d["vals"][slot] += 16
        ins.then_inc(d["sems"][slot], 16)
        tok = (d["sems"][slot], d["vals"][slot], "dma_" + q)
        d["tok"][slot] = tok
        d["n"] = n + 1
        self._record(tok, reads, writes)
        self.ninst[q] += 1
        return tok

    def finish(self, q="sp"):
        for qq, d in self.dq.items():
            for t in d["tok"]:
                self._wait(q, t)

    def mm(self, out, lhsT, rhs, start, stop, reads, writes, **kw):
        return self.emit("pe", lambda: self.nc.tensor.matmul(out, lhsT, rhs, start=start, stop=stop, **kw),
                         reads, writes)

    def tr(self, out, in_, ident, reads, writes):
        return self.emit("pe", lambda: self.nc.tensor.transpose(out, in_, ident), reads, writes)

    def act(self, out, in_, func, reads, writes, **kw):
        return self.emit("act", lambda: self.nc.scalar.activation(out, in_, func, **kw), reads, writes)

    def tt(self, e, out, a, b, op, reads, writes):
        return self.emit(e, lambda: self.eng[e].tensor_tensor(out, a, b, op), reads, writes)

    def ts(self, e, out, a, s1, s2, op0, op1, reads, writes, **kw):
        return self.emit(e, lambda: self.eng[e].tensor_scalar(out, a, s1, s2, op0, op1, **kw), reads, writes)

    def ts1(self, e, out, a, s1, op, reads, writes):
        return self.emit(e, lambda: self.eng[e].tensor_single_scalar(out, a, s1, op), reads, writes)

    def cp(self, e, out, a, reads, writes):
        if e == "act":
            return self.emit(e, lambda: self.nc.scalar.copy(out, a), reads, writes)
        return self.emit(e, lambda: self.eng[e].tensor_copy(out, a), reads, writes)

    def memset(self, e, ap, val, writes):
        return self.emit(e, lambda: self.eng[e].memset(ap, val), (), writes)


import contextlib


class Alloc:
    def __init__(self, nc):
        self.nc = nc
        self.st = contextlib.ExitStack()
    def __enter__(self):
        self.st.__enter__()
        return self
    def __exit__(self, *a):
        return self.st.__exit__(*a)
    def sb(self, name, shape, dt):
        return self.st.enter_context(self.nc.sbuf_tensor(name, list(shape), dt))
    def ps(self, name, shape, dt):
        return self.st.enter_context(self.nc.psum_tensor(name, list(shape), dt))


def kb_barrier(self):
    toks = []
    for e in ("pe", "act", "dve", "pool"):
        if self.cnt[e] > 0:
            toks.append((self.sem[e], self.cnt[e], e))
    for q, d in self.dq.items():
        for t in d["tok"]:
            if t is not None:
                toks.append(t)
    toks.extend(self.colltoks)
    self.colltoks = []
    for e in ("pe", "act", "dve", "pool", "sp"):
        for t in toks:
            self._wait(e, t)
    self.lastw.clear()
    self.readers.clear()

KB.barrier = kb_barrier


def kb_coll(self, kind, src, dst, groups, reads=(), writes=()):
    q = "pool"
    self._deps(q, reads, writes)
    sem = self._alloc_sem("cc")
    ins = self.nc.gpsimd.collective_compute(kind, ALU.bypass, replica_groups=groups, ins=[src], outs=[dst])
    ins.then_inc(sem, 1)
    tok = (sem, 1, "coll")
    self._record(tok, reads, writes)
    self.colltoks.append(tok)
    return tok

KB.coll = kb_coll

D = 1024
INC = 5376
TB = 512

def build_A(NT):
    nc = bass.Bass("TRN2", target_bir_lowering=False)
    x = nc.dram_tensor("x", [NT, D], F32, kind="ExternalInput").ap()
    w = nc.dram_tensor("w_in", [D, INC], F32, kind="ExternalInput").ap()
    gam = nc.dram_tensor("gam", [128, 8], F32, kind="ExternalInput").ap()
    ident = nc.dram_tensor("ident", [128, 128], F32, kind="ExternalInput").ap()
    qT = nc.dram_tensor("qT", [512, NT], BF16, kind="ExternalOutput").ap()
    kT = nc.dram_tensor("kT", [512, NT], BF16, kind="ExternalOutput").ap()
    vo = nc.dram_tensor("v", [NT, 512], BF16, kind="ExternalOutput").ap()
    zrT = nc.dram_tensor("zrT", [1792, NT], F32, kind="ExternalOutput").ap()
    gT = nc.dram_tensor("gT", [2048, NT], BF16, kind="ExternalOutput").ap()
    k = KB(nc)
    with Alloc(nc) as al:
        emit_A(nc, k, al, NT, x, w, gam, ident, qT, kT, vo, zrT, gT)
        k.finish("sp")
    return nc


def emit_A(nc, k, al, NT, x, w, gam, ident, qT, kT, vo, zrT, gT, pfx="A", vsplit=False, ocb=None):
    nblk = NT // TB
    wb = al.sb(pfx + "wb", [128, 8, INC], BF16)
    wst = [al.sb(pfx + f"wst{i}", [128, 8, 512], F32) for i in range(2)]
    gt = al.sb(pfx + "gt", [128, 8], F32)
    idf = al.sb(pfx + "idf", [128, 128], F32)
    idb = al.sb(pfx + "idb", [128, 128], BF16)
    xt = [al.sb(pfx + f"xt{i}", [128, 4, D], F32) for i in range(2)]
    junk = al.sb(pfx + "junk", [128, D], BF16)
    ss = al.sb(pfx + "ss", [128, 4], F32)
    rstd = al.sb(pfx + "rstd", [128, 4], F32)
    xn = [al.sb(pfx + f"xn{i}", [128, D], BF16) for i in range(2)]
    hT = al.sb(pfx + "hT", [128, 8, TB], BF16)
    NOB = 4
    ob16 = [al.sb(pfx + f"ob16_{i}", [128, 512], BF16) for i in range(NOB)]
    ob32 = [al.sb(pfx + f"ob32_{i}", [128, 512], F32) for i in range(NOB)]
    pT = [al.ps(pfx + f"pT{i}", [128, 8, 128], BF16) for i in range(2)]
    NPZ = 4
    pz = [al.ps(pfx + f"pz{i}", [128, 512], F32) for i in range(NPZ)]

    k.dma("sp", gt[:], gam, writes=["gt"])
    k.dma("sp", idf[:], ident, writes=["idf"])
    k.cp("dve", idb[:], idf[:], ["idf"], ["idb"])
    wv = w.rearrange("(kc p) n -> p kc n", p=128)
    xv = x.rearrange("(b s p) d -> b p s d", p=128, s=4)

    def load_x(b):
        k.dma("sp", xt[b % 2][:], xv[b], writes=[f"xt{b%2}"])

    load_x(0)
    ncc = INC // 512 + (1 if INC % 512 else 0)
    for c in range(ncc):
        c0 = c * 512
        cw = min(512, INC - c0)
        st = wst[c % 2]
        k.dma("sp", st[:, :, :cw], wv[:, :, c0:c0 + cw], writes=[f"wst{c%2}"])
        for kc in range(8):
            e = "dve" if kc % 2 == 0 else "pool"
            k.ts1(e, wb[:, kc, c0:c0 + cw], st[:, kc, :cw], gt[:, kc:kc + 1], ALU.mult,
                  [f"wst{c%2}", "gt"], ["wb"])

    evn = [0]
    for b in range(nblk):
        if b + 1 < nblk:
            load_x(b + 1)
        xb = xt[b % 2]
        xk = f"xt{b%2}"
        for s in range(4):
            k.act(junk[:], xb[:, s, :], AF.Square, [xk], ["junk", "ss"], accum_out=ss[:, s:s + 1])
        k.ts("dve", rstd[:], ss[:], 1.0 / D, 1e-6, ALU.mult, ALU.add, ["ss"], ["rstd"])
        k.act(rstd[:], rstd[:], AF.Sqrt, ["rstd"], ["rstd"])
        k.emit("dve", lambda: nc.vector.reciprocal(rstd[:], rstd[:]), ["rstd"], ["rstd"])
        for s in range(4):
            xnk = f"xn{s%2}"
            k.ts1("dve", xn[s % 2][:], xb[:, s, :], rstd[:, s:s + 1], ALU.mult, [xk, "rstd"], [xnk])
            pk = f"pT{s%2}"
            for kc in range(8):
                k.tr(pT[s % 2][:, kc, :], xn[s % 2][:, kc * 128:(kc + 1) * 128], idb[:], [xnk, "idb"], [pk])
            k.cp("act", hT[:, :, s * 128:(s + 1) * 128], pT[s % 2][:], [pk], ["hT"])
        t0 = b * TB
        chunks = []
        for cc in range(4):
            chunks.append(("q", cc, cc * 128))
        for cc in range(4):
            chunks.append(("k", cc, 512 + cc * 128))
        for cc in range(14):
            chunks.append(("r", cc, 1536 + cc * 128))
        for cc in range(16):
            chunks.append(("g", cc, 1536 + 1792 + cc * 128))
        for (kind, cc, col) in chunks:
            i = evn[0]; evn[0] += 1
            pzi = pz[i % NPZ]; pk = f"pz{i%NPZ}"
            for kc in range(8):
                k.mm(pzi[:], wb[:, kc, col:col + 128], hT[:, kc, :], kc == 0, kc == 7, ["wb", "hT"], [pk])
            oi = i % NOB
            if kind == "q":
                k.act(ob16[oi][:], pzi[:], AF.Copy, [pk], [f"ob16_{oi}"], scale=0.125)
                if ocb:
                    k.dma("pool", ocb["q"](cc, t0), ob16[oi][:], reads=[f"ob16_{oi}"])
                else:
                    k.dma("pool", qT[cc * 128:(cc + 1) * 128, t0:t0 + TB], ob16[oi][:], reads=[f"ob16_{oi}"])
            elif kind == "k":
                k.cp("dve", ob16[oi][:], pzi[:], [pk], [f"ob16_{oi}"])
                if ocb:
                    k.dma("pool", ocb["k"](cc, t0), ob16[oi][:], reads=[f"ob16_{oi}"])
                else:
                    k.dma("pool", kT[cc * 128:(cc + 1) * 128, t0:t0 + TB], ob16[oi][:], reads=[f"ob16_{oi}"])
            elif kind == "r":
                e = "dve" if cc % 2 == 0 else "act"
                k.cp(e, ob32[oi][:], pzi[:], [pk], [f"ob32_{oi}"])
                if ocb:
                    for hq in range(2):
                        k.dma("pool", ocb["z"](cc, hq, t0), ob32[oi][hq * 64:(hq + 1) * 64, :], reads=[f"ob32_{oi}"])
                else:
                    k.dma("pool", zrT[cc * 128:(cc + 1) * 128, t0:t0 + TB], ob32[oi][:], reads=[f"ob32_{oi}"])
            else:
                k.act(ob16[oi][:], pzi[:], AF.Sigmoid, [pk], [f"ob16_{oi}"])
                k.dma("pool", gT[cc * 128:(cc + 1) * 128, t0:t0 + TB], ob16[oi][:], reads=[f"ob16_{oi}"])
        for s in range(4):
            i = evn[0]; evn[0] += 1
            pzi = pz[i % NPZ]; pk = f"pz{i%NPZ}"
            for kc in range(8):
                k.mm(pzi[:], hT[:, kc, s * 128:(s + 1) * 128], wb[:, kc, 1024:1536], kc == 0, kc == 7,
                     ["wb", "hT"], [pk])
            oi = i % NOB
            k.cp("dve", ob16[oi][:], pzi[:], [pk], [f"ob16_{oi}"])
            if ocb:
                for g_ in range(2):
                    k.dma("pool", ocb["v"](g_, t0 + s * 128), ob16[oi][:, g_ * 256:(g_ + 1) * 256], reads=[f"ob16_{oi}"])
            elif vsplit:
                for g_ in range(2):
                    k.dma("pool", vo[g_, t0 + s * 128:t0 + (s + 1) * 128, :], ob16[oi][:, g_ * 256:(g_ + 1) * 256], reads=[f"ob16_{oi}"])
            else:
                k.dma("pool", vo[t0 + s * 128:t0 + (s + 1) * 128, :], ob16[oi][:], reads=[f"ob16_{oi}"])


NEGM = -30000.0

def t5_bucket_np(n):
    n = np.maximum(n, 0)
    nf = np.maximum(n, 16).astype(np.float32)
    large = 16 + (np.log(nf / np.float32(16)) / np.float32(math.log(128 / 16)) * np.float32(16)).astype(np.int32)
    large = np.minimum(large, 31)
    return np.where(n < 16, n, large)

def toeplitz_idx():
    j = np.arange(128)[:, None]
    i = np.arange(256)[None, :]
    return t5_bucket_np(i - j)

def neg_mask():
    j = np.arange(128)[:, None]
    i = np.arange(256)[None, :]
    return np.where(i >= j, 0.0, NEGM).astype(np.float32)

def build_B(S, lam_init):
    nc = bass.Bass("TRN2", target_bir_lowering=False)
    ins = dict(
        qT=nc.dram_tensor("qT", [256, S], BF16, kind="ExternalInput").ap(),
        kT=nc.dram_tensor("kT", [256, S], BF16, kind="ExternalInput").ap(),
        v=nc.dram_tensor("v", [S, 256], BF16, kind="ExternalInput").ap(),
        G=nc.dram_tensor("G", [4, 128, 256], F32, kind="ExternalInput").ap(),
        b31=nc.dram_tensor("b31", [128, 4], F32, kind="ExternalInput").ap(),
        neg=nc.dram_tensor("neg", [128, 256], F32, kind="ExternalInput").ap(),
        lamv=nc.dram_tensor("lamv", [128, 4, 64], F32, kind="ExternalInput").ap(),
        sgain=nc.dram_tensor("sgain", [128, 128], F32, kind="ExternalInput").ap(),
        ident=nc.dram_tensor("ident", [128, 128], F32, kind="ExternalInput").ap(),
    )
    oaT = nc.dram_tensor("oaT", [256, S], BF16, kind="ExternalOutput").ap()
    dbg = nc.dram_tensor("dbg", [4, 128, 512], F32, kind="ExternalOutput").ap()
    k = KB(nc)
    with Alloc(nc) as al:
        emit_B(nc, k, al, S, lam_init, ins, oaT, dbg=dbg)
        k.finish("sp")
    return nc


def emit_B(nc, k, al, S, lam_init, ins, oaT, pfx="B", dbg=None):
    NJ = S // 128
    NG = S // 512
    VW = 136
    qs = al.sb(pfx + "qs", [128, 2, S], BF16)
    ks = al.sb(pfx + "ks", [128, 2, S], BF16)
    vaug = al.sb(pfx + "vaug", [128, NJ, 2, VW], BF16)
    Gs = al.sb(pfx + "Gs", [128, 4, 256], F32)
    negs = al.sb(pfx + "negs", [128, 256], F32)
    b31s = al.sb(pfx + "b31s", [128, 4], F32)
    Tb = al.sb(pfx + "Tb", [128, 4, 256], BF16)
    lamt = al.sb(pfx + "lamt", [128, 4, 64], F32)
    lprod = al.sb(pfx + "lprod", [128, 2, 64], F32)
    lsum = al.sb(pfx + "lsum", [128, 2], F32)
    lam = al.sb(pfx + "lam", [128, 1], F32)
    sg = al.sb(pfx + "sg", [128, 128], F32)
    idf = al.sb(pfx + "idf", [128, 128], F32)
    idb = al.sb(pfx + "idb", [128, 128], BF16)
    NB = 3
    PT = [al.sb(pfx + f"PT{i}", [128, 512], BF16) for i in range(NB)]
    rr = al.sb(pfx + "rr", [128, 2], F32)
    rl = al.sb(pfx + "rl", [128, 1], F32)
    t1 = al.sb(pfx + "t1", [128, 128], F32)
    ot = al.sb(pfx + "ot", [128, 128], F32)
    junk = al.sb(pfx + "junk", [128, 128], F32)
    ms = al.sb(pfx + "ms", [128, 1], F32)
    onb = al.sb(pfx + "onb", [128, 128], BF16)
    oT = [al.sb(pfx + f"oT{i}", [128, 512], BF16) for i in range(2)]
    ST = [al.ps(pfx + f"ST{i}", [128, 512], F32) for i in range(NB)]
    accb = al.ps(pfx + "accb", [128, 4, 512], F32)
    pTr = al.ps(pfx + "pTr", [128, 128], BF16)

    for hl in range(2):
        w = min(S // 2 if ins.get("qfn") else S, 2048)
        for part in range(S // w):
            qfn = ins.get("qfn") or (lambda r0, r1, a, b_: ins["qT"][r0:r1, a:b_])
            kfn = ins.get("kfn") or (lambda r0, r1, a, b_: ins["kT"][r0:r1, a:b_])
            k.dma("sp", qs[:, hl, part * w:(part + 1) * w], qfn(hl * 128, (hl + 1) * 128, part * w, (part + 1) * w), writes=["qs"])
            k.dma("sp", ks[:, hl, part * w:(part + 1) * w], kfn(hl * 128, (hl + 1) * 128, part * w, (part + 1) * w), writes=["ks"])
    JB = min(8, (S // 2) // 128) if ins.get("vfn") else 8
    for j0 in range(0, NJ, JB):
        j1 = min(NJ, j0 + JB)
        vsrc = ins["vfn"](j0 * 128, j1 * 128) if ins.get("vfn") else ins["v"][j0 * 128:j1 * 128, :]
        vv = vsrc.rearrange("(J p) (h d) -> p J h d", p=128, h=2)
        for h2 in range(2):
            k.dma("sp", vaug[:, j0:j1, h2, 0:128], vv[:, :, h2, :], writes=["vaug"])
    k.memset("pool", vaug[:, :, :, 128:129], 1.0, ["vaug"])
    k.dma("sp", Gs[:], ins["G"].rearrange("a p i -> p a i"), writes=["Gs"])
    k.dma("sp", negs[:], ins["neg"], writes=["negs"])
    k.dma("sp", b31s[:], ins["b31"], writes=["b31s"])
    k.dma("sp", lamt[:], ins["lamv"], writes=["lamt"])
    k.dma("sp", sg[:], ins["sgain"], writes=["sg"])
    k.dma("sp", idf[:], ins["ident"], writes=["idf"])
    k.cp("dve", idb[:], idf[:], ["idf"], ["idb"])
    for hc in range(4):
        k.ts1("dve", Gs[:, hc, :], Gs[:, hc, :], b31s[:, hc:hc + 1], ALU.subtract, ["Gs", "b31s"], ["Gs"])
        k.tt("dve", Tb[:, hc, :], Gs[:, hc, :], negs[:], ALU.add, ["Gs", "negs"], ["Tb"])
    k.tt("dve", lprod[:, 0, :], lamt[:, 0, :], lamt[:, 1, :], ALU.mult, ["lamt"], ["lprod"])
    k.tt("dve", lprod[:, 1, :], lamt[:, 2, :], lamt[:, 3, :], ALU.mult, ["lamt"], ["lprod"])
    k.emit("dve", lambda: nc.vector.reduce_sum(lsum[:], lprod[:], AX.X), ["lprod"], ["lsum"])
    k.act(lsum[:], lsum[:], AF.Exp, ["lsum"], ["lsum"])
    k.tt("dve", lam[:], lsum[:, 0:1], lsum[:, 1:2], ALU.subtract, ["lsum"], ["lam"])
    k.ts1("dve", lam[:], lam[:], float(lam_init), ALU.add, ["lam"], ["lam"])
    k.ts1("dve", sg[:], sg[:], float(1.0 - lam_init), ALU.mult, ["sg"], ["sg"])

    def acc(c, I):
        return accb[:, I, c * 256:c * 256 + 129]

    for hl in range(2):
        steps = [(g, J, c) for g in range(NG) for J in range(4 * g + 4) for c in range(2)]
        N = len(steps)

        def qk(n):
            g, J, c = steps[n]
            buf = n % NB
            r = J - 4 * g
            c0 = max(0, r) * 128
            a = 128 * r
            has_t = r >= -1
            k.mm(ST[buf][:, c0:512], ks[c * 64:(c + 1) * 64, hl, J * 128:(J + 1) * 128],
                 qs[c * 64:(c + 1) * 64, hl, g * 512 + c0:(g + 1) * 512], True, not has_t,
                 ["ks", "qs"], [f"ST{buf}"])
            if has_t:
                tc0 = max(a, 0); tc1 = min(a + 256, 512)
                i0 = tc0 - a
                k.mm(ST[buf][:, tc0:tc1], idb[:], Tb[:, hl * 2 + c, i0:i0 + (tc1 - tc0)], False, True,
                     ["idb", "Tb"], [f"ST{buf}"])

        def ex_pv(n):
            g, J, c = steps[n]
            buf = n % NB
            r = J - 4 * g
            c0 = max(0, r) * 128
            k.act(PT[buf][:, c0:512], ST[buf][:, c0:512], AF.Exp, [f"ST{buf}"], [f"PT{buf}"])
            for I in range(4):
                Ia = 4 * g + I
                if Ia >= J:
                    first = (J == 0 and c == 0)
                    wk = [f"acc0_{I}", f"acc1_{I}"] if first else [f"acc{c}_{I}"]
                    k.mm(acc(c, I), PT[buf][:, I * 128:(I + 1) * 128], vaug[:, J, hl, 0:129], first, J == Ia,
                         [f"PT{buf}", "vaug"], wk, skip_group_check=True)

        def fin(g):
            ob = oT[g % 2]; obk = f"oT{g%2}"
            for I in range(4):
                ak = [f"acc0_{I}", f"acc1_{I}"]
                k.emit("dve", lambda: nc.vector.reciprocal(rr[:, 0:1], accb[:, I, 128:129]), [ak[0]], ["rr"])
                k.emit("dve", lambda: nc.vector.reciprocal(rr[:, 1:2], accb[:, I, 256 + 128:256 + 129]), [ak[1]], ["rr"])
                k.tt("dve", rl[:], rr[:, 1:2], lam[:], ALU.mult, ["rr", "lam"], ["rl"])
                k.ts1("dve", t1[:], accb[:, I, 256:256 + 128], rl[:, 0:1], ALU.mult, [ak[1], "rl"], ["t1"])
                k.emit("dve", lambda: nc.vector.scalar_tensor_tensor(ot[:], accb[:, I, 0:128], rr[:, 0:1], t1[:], ALU.mult, ALU.subtract),
                       [ak[0], "rr", "t1"], ["ot"])
                k.act(junk[:], ot[:], AF.Square, ["ot"], ["junk", "ms"], accum_out=ms[:])
                k.ts("dve", ms[:], ms[:], 1.0 / 128, 1e-5, ALU.mult, ALU.add, ["ms"], ["ms"])
                k.act(ms[:], ms[:], AF.Sqrt, ["ms"], ["ms"])
                k.emit("dve", lambda: nc.vector.reciprocal(ms[:], ms[:]), ["ms"], ["ms"])
                k.emit("dve", lambda: nc.vector.scalar_tensor_tensor(onb[:], ot[:], ms[:, 0:1], sg[:], ALU.mult, ALU.mult),
                       ["ot", "ms", "sg"], ["onb"])
                if dbg is not None and hl == 0 and g == 0 and I == 0:
                    k.cp("dve", dt[:, 3, :], accb[:, 0, :], ["acc0_0", "acc1_0"], ["dbgt"])
                    k.cp("dve", dt[:, 2, 0:128], ot[:], ["ot"], ["dbgt"])
                    k.cp("dve", dt[:, 2, 128:256], onb[:], ["onb"], ["dbgt"])
                    k.dma("pool", dbg.rearrange("a p n -> p a n"), dt[:], reads=["dbgt"])
                k.tr(pTr[:], onb[:], idb[:], ["onb", "idb"], ["pTr"])
                k.cp("act", ob[:, I * 128:(I + 1) * 128], pTr[:], ["pTr"], [obk])
            ofn = ins.get("ofn") or (lambda r0, r1, a, b_: oaT[r0:r1, a:b_])
            k.dma("pool", ofn(hl * 128, (hl + 1) * 128, g * 512, (g + 1) * 512), ob[:], reads=[obk])

        LA = 2
        if dbg is not None and hl == 0:
            dt = al.sb(pfx + "dbgt", [128, 4, 512], F32)
            k.memset("pool", dt[:], 0.0, ["dbgt"])
            qk(0)
            k.cp("dve", dt[:, 0, :], ST[0][:], ["ST0"], ["dbgt"])
            k.act(dt[:, 1, :], ST[0][:], AF.Exp, ["ST0"], ["dbgt"])
            k.cp("dve", dt[:, 2, 0:256], Tb[:, 0, :], ["Tb"], ["dbgt"])
            k.cp("dve", dt[:, 2, 256:257], lam[:], ["lam"], ["dbgt"])
        for n in range(min(LA, N)):
            qk(n)
        for n in range(N):
            if n + LA < N:
                qk(n + LA)
            ex_pv(n)
            g, J, c = steps[n]
            if J == 4 * g + 3 and c == 1:
                fin(g)


CDEC = math.exp(-0.5)
GN_EPS = 64e-5

def consts_C():
    s = np.arange(128)[:, None]; t = np.arange(128)[None, :]
    c = dict(
        ident=np.eye(128, dtype=np.float32),
        triI=np.where(s <= t, -CDEC, 0.0).astype(np.float32),
        triE=np.where(s < t, -CDEC, 0.0).astype(np.float32),
        triR=np.where(s > t, -CDEC, 0.0).astype(np.float32),
        mstrict=(t > s).astype(np.float32),
        mask3=np.concatenate([(t >= s), (t > s), (t >= s)], 1).astype(np.float32),
        mlow=(s > t).astype(np.float32),
        bones=(s // 64 == t // 64).astype(np.float32),
    )
    return c

C_IN = [("zc", [1024, None], F32), ("mu", [128, 8], F32), ("w2c", [64, 256], F32), ("w0c", [1, 256], F32),
        ("a2c", [64, 256], F32), ("a0c", [128, 2], F32), ("g2c", [128, 256], F32), ("kkw", [128, 2], F32),
        ("ka", [128, 2], F32), ("rkb", [128, 2, 128], F32), ("lnw", [128, 2], F32), ("lnb", [128, 2], F32),
        ("ident", [128, 128], F32), ("triI", [128, 128], F32), ("triE", [128, 128], F32), ("triR", [128, 128], F32),
        ("mstrict", [128, 128], F32), ("mask3", [128, 384], F32), ("mlow", [128, 128], F32), ("bones", [128, 128], F32)]

def build_C(S):
    nc = bass.Bass("TRN2", target_bir_lowering=False)
    ins = {}
    for nm, shp, dt in C_IN:
        shp = [S if x is None else x for x in shp]
        ins[nm] = nc.dram_tensor(nm, shp, dt, kind="ExternalInput").ap()
    yT = nc.dram_tensor("yT", [256, S], BF16, kind="ExternalOutput").ap()
    k = KB(nc)
    with Alloc(nc) as al:
        emit_C(nc, k, al, S, ins, yT)
        k.finish("sp")
    return nc


def emit_C(nc, k, al, S, ins, yT, pfx="C", zrows=None):
    if zrows is None:
        zrows = [j * 128 for j in range(8)]
    NBLK = S // 512
    A = lambda nm, shp, dt=F32: al.sb(pfx + nm, shp, dt)
    P = lambda nm, shp, dt=F32: al.ps(pfx + nm, shp, dt)
    cs = {}
    for nm, shp, dt in C_IN[1:]:
        cs[nm] = A("c_" + nm, shp)
        k.dma("sp", cs[nm][:], ins[nm], writes=["c_" + nm])
    wa = A("wa", [128, 256])
    k.dma("sp", wa[0:64, :], ins["w2c"], writes=["wa"])
    k.dma("sp", wa[64:128, :], ins["a2c"], writes=["wa"])
    ones1 = A("ones1", [1, 128])
    k.memset("pool", ones1[:], 1.0, ["ones1"])
    omk = A("omk", [128, 2])
    k.ts("dve", omk[:], cs["ka"][:], -1.0, 1.0, ALU.mult, ALU.add, ["c_ka"], ["omk"])
    ident = cs["ident"]

    zin = [A(f"zin{i}", [128, 513]) for i in range(3)]
    dtmp = [A(f"dtmp{i}", [128, 512]) for i in range(2)]
    zs = A("zs", [128, 8, 512])
    tw = A("tw", [64, 512])
    sgx = A("sgx", [128, 512])
    aT = A("aT", [128, 2, 512])
    kkT = A("kkT", [128, 2, 512])
    kpT = A("kpT", [128, 2, 512])
    kkaT = A("kkaT", [128, 2, 512])
    bonT = A("bonT", [128, 2, 512])
    gTs = A("gTs", [128, 2, 512])
    tmpA = A("tmpA", [128, 512])
    tmpB = A("tmpB", [128, 512])
    sig = A("sig", [128, 256])
    Gi = [A(f"Gi{i_}", [128, 2, 128]) for i_ in range(2)]; iG = A("iG", [128, 2, 128]); Ge = A("Ge", [128, 2, 128])
    Gr = A("Gr", [128, 256])
    ARt = [A(f"ARt{i_}", [128, 2, 2, 128]) for i_ in range(2)]
    KtT = A("KtT", [128, 2, 128]); BtT = A("BtT", [128, 2, 128])
    Vtok = [A(f"Vtok{i_}", [128, 256]) for i_ in range(2)]; Khat = [A(f"Khat{i_}", [128, 256]) for i_ in range(2)]
    Bhat = [A(f"Bhat{i_}", [128, 256]) for i_ in range(2)]
    ATs = [A(f"ATs{i_}", [128, 4, 3, 128]) for i_ in range(2)]
    PTs = [A(f"PTs{i}", [128, 4, 2, 128]) for i in range(2)]
    Qs = [A(f"Qs{i}", [128, 4, 128]) for i in range(2)]
    Hst = A("Hst", [128, 2, 64])
    X0 = A("X0", [128, 256]); Up = A("Up", [128, 256])
    Ysb = A("Ysb", [128, 4, 64]); Ysq = A("Ysq", [128, 4, 64])
    st1 = A("st1", [128, 4]); st2 = A("st2", [128, 4]); mean = A("mean", [128, 4]); rstd = A("rstd", [128, 4])
    msq = A("msq", [128, 4])
    Yn = A("Yn", [128, 256])
    fin = A("fin", [128, 2, 128])
    outb = [A(f"outb{i}", [128, 2, 512], BF16) for i in range(2)]
    SA = P("SA", [128, 4, 256]); SB = P("SB", [128, 4, 128])
    ATp = P("ATp", [128, 512]); Q0p = P("Q0p", [128, 4, 128])
    M = [P(f"M{i}", [128, 512]) for i in range(3)]
    mi = [0]
    def misc():
        i = mi[0] % 3; mi[0] += 1
        return M[i], f"M{i}"

    k.memset("pool", Hst[:], 0.0, ["Hst"])
    k.memset("pool", zin[0][:, 0:1], 0.0, ["zin0"])

    zi = [0]
    for b in range(NBLK):
        t0 = b * 512
        for j in range(8):
            zb = zin[zi[0] % 3]; zk = f"zin{zi[0]%3}"; zi[0] += 1
            zsplit = ins.get("zsplit", 0)
            if ins.get("zfn64"):
                zparts = [(slice(hq * 64, (hq + 1) * 64), (lambda a, b_, hq=hq: ins["zfn64"](zrows[j] + hq * 64, a, b_))) for hq in range(2)]
            else:
                zparts = [(slice(0, 128), (lambda a, b_: ins["zc"][zrows[j]:zrows[j] + 128, a:b_]))]
            if b == 0:
                k.memset("pool", zb[:, 0:1], 0.0, [zk])
            for (psl, zf_) in zparts:
                if b == 0:
                    k.dma("sp", zb[psl, 1:513], zf_(0, 512), writes=[zk])
                elif zsplit and t0 % zsplit == 0:
                    k.dma("sp", zb[psl, 0:1], zf_(t0 - 1, t0), writes=[zk], allow_slow_non_contiguous=True)
                    k.dma("sp", zb[psl, 1:513], zf_(t0, t0 + 512), writes=[zk])
                else:
                    k.dma("sp", zb[psl, 0:513], zf_(t0 - 1, t0 + 512), writes=[zk])
            e = "pool"
            dt_ = dtmp[j % 2]; dk = f"dtmp{j%2}"
            k.tt(e, dt_[:], zb[:, 0:512], zb[:, 1:513], ALU.subtract, [zk], [dk])
            k.emit("dve", lambda: nc.vector.scalar_tensor_tensor(zs[:, j, :], dt_[:], cs["mu"][:, j:j + 1], zb[:, 1:513], ALU.mult, ALU.add),
                   [dk, zk, "c_mu"], [f"zs{j}"])
        k.act(tw[:], zs[0:64, 6, :], AF.Tanh, ["zs6"], ["tw"])
        k.act(sgx[:], zs[:, 7, :], AF.Sigmoid, ["zs7"], ["sgx"])
        for hp in range(2):
            m, mk = misc()
            k.mm(m[:], wa[64:128, hp * 128:(hp + 1) * 128], zs[64:128, 6, :], True, True, ["wa", "zs6"], [mk])
            k.act(aT[:, hp, :], m[:], AF.Sigmoid, [mk, "c_a0c"], [f"aT{hp}"], bias=cs["a0c"][:, hp:hp + 1])
            k.ts1("pool", tmpA[:], zs[:, 2 + hp, :], cs["kkw"][:, hp:hp + 1], ALU.mult, [f"zs{2+hp}", "c_kkw"], ["tmpA"])
            k.tt("pool", tmpB[:], tmpA[:], tmpA[:], ALU.mult, ["tmpA"], ["tmpB"])
            m, mk = misc()
            k.mm(m[:], cs["bones"][:], tmpB[:], True, True, ["c_bones", "tmpB"], [mk])
            k.act(tmpB[:], m[:], AF.Sqrt, [mk], ["tmpB"])
            k.ts1("dve", tmpB[:], tmpB[:], 1e-12, ALU.max, ["tmpB"], ["tmpB"])
            k.emit("dve", lambda: nc.vector.reciprocal(tmpB[:], tmpB[:]), ["tmpB"], ["tmpB"])
            k.tt("dve", kkT[:, hp, :], tmpA[:], tmpB[:], ALU.mult, ["tmpA", "tmpB"], [f"kkT{hp}"])
            k.ts("dve", tmpA[:], aT[:, hp, :], cs["ka"][:, hp:hp + 1], omk[:, hp:hp + 1], ALU.mult, ALU.add,
                 [f"aT{hp}", "c_ka", "omk"], ["tmpA"])
            k.tt("pool", kpT[:, hp, :], tmpA[:], zs[:, 2 + hp, :], ALU.mult, ["tmpA", f"zs{2+hp}"], [f"kpT{hp}"])
            k.tt("pool", kkaT[:, hp, :], kkT[:, hp, :], aT[:, hp, :], ALU.mult, [f"kkT{hp}", f"aT{hp}"], [f"kkaT{hp}"])
            k.tt("pool", tmpA[:], zs[:, hp, :], kpT[:, hp, :], ALU.mult, [f"zs{hp}", f"kpT{hp}"], ["tmpA"])
            m, mk = misc()
            k.mm(m[:], cs["rkb"][:, hp, :], tmpA[:], True, True, ["c_rkb", "tmpA"], [mk])
            k.tt("dve", bonT[:, hp, :], m[:], zs[:, 4 + hp, :], ALU.mult, [mk, f"zs{4+hp}"], [f"bonT{hp}"])
            m, mk = misc()
            k.mm(m[:], cs["g2c"][:, hp * 128:(hp + 1) * 128], sgx[:], True, True, ["c_g2c", "sgx"], [mk])
            k.cp("act", gTs[:, hp, :], m[:], [mk], [f"gTs{hp}"])

        ob = outb[b % 2]; obk = f"outb{b%2}"
        def prep(ch, par):
            csl = slice(ch * 128, ch * 128 + 128)
            Gi_, ARt_, Vtok_, Khat_, Bhat_, ATs_ = Gi[par], ARt[par], Vtok[par], Khat[par], Bhat[par], ATs[par]
            m, mk = misc()
            k.mm(m[:, 0:256], tw[:, csl], wa[0:64, :], True, False, ["tw", "wa"], [mk])
            k.mm(m[:, 0:256], ones1[:], cs["w0c"][:], False, True, ["ones1", "c_w0c"], [mk])
            k.act(sig[:], m[:, 0:256], AF.Sigmoid, [mk], ["sig"])
            m, mk = misc()
            for hp in range(2):
                k.mm(m[:, hp * 128:(hp + 1) * 128], sig[:, hp * 128:(hp + 1) * 128], cs["triI"][:], hp == 0, False, ["sig", "c_triI"], [mk], skip_group_check=True)
            for hp in range(2):
                k.mm(m[:, 256 + hp * 128:256 + (hp + 1) * 128], sig[:, hp * 128:(hp + 1) * 128], cs["triE"][:], False, hp == 1, ["sig", "c_triE"], [mk], skip_group_check=True)
            k.act(Gi_[:], m[:, 0:256], AF.Exp, [mk], [f"Gi{par}"])
            k.act(iG[:], m[:, 0:256], AF.Exp, [mk], ["iG"], scale=-1.0)
            k.act(Ge[:], m[:, 256:512], AF.Exp, [mk], ["Ge"])
            m, mk = misc()
            k.mm(m[:, 0:256], cs["triR"][:], sig[:], True, True, ["sig", "c_triR"], [mk])
            k.act(Gr[:], m[:, 0:256], AF.Exp, [mk], ["Gr"])
            yield
            for hp in range(2):
                k.emit("dve", lambda: nc.vector.scalar_tensor_tensor(ARt_[:, hp, 0, :], kkT[:, hp, csl], -1.0, Ge[:, hp, :], ALU.mult, ALU.mult),
                       [f"kkT{hp}", "Ge"], [f"ARt{par}"])
                k.tt("pool", ARt_[:, hp, 1, :], zs[:, hp, csl], Gi_[:, hp, :], ALU.mult, [f"zs{hp}", f"Gi{par}"], [f"ARt{par}"])
                k.tt("dve", KtT[:, hp, :], kpT[:, hp, csl], iG[:, hp, :], ALU.mult, [f"kpT{hp}", "iG"], ["KtT"])
                k.tt("dve", BtT[:, hp, :], kkaT[:, hp, csl], iG[:, hp, :], ALU.mult, [f"kkaT{hp}", "iG"], ["BtT"])
            yield
            m, mk = misc()
            for hp in range(2):
                k.tr(m[:, hp * 128:(hp + 1) * 128], zs[:, 4 + hp, csl], ident[:], [f"zs{4+hp}", "c_ident"], [mk])
            k.cp("act", Vtok_[:], m[:, 0:256], [mk], [f"Vtok{par}"])
            m, mk = misc()
            for hp in range(2):
                k.tr(m[:, hp * 128:(hp + 1) * 128], kpT[:, hp, csl], ident[:], [f"kpT{hp}", "c_ident"], [mk])
                k.tr(m[:, 256 + hp * 128:256 + (hp + 1) * 128], kkaT[:, hp, csl], ident[:], [f"kkaT{hp}", "c_ident"], [mk])
            k.tt("dve", Khat_[:], m[:, 0:256], Gr[:], ALU.mult, [mk, "Gr"], [f"Khat{par}"])
            k.tt("dve", Bhat_[:], m[:, 256:512], Gr[:], ALU.mult, [mk, "Gr"], [f"Bhat{par}"])
            yield
            for h in range(4):
                hp, e = h // 2, h % 2
                ps = slice(e * 64, (e + 1) * 64)
                atp, atk = (ATp, "ATp") if h % 2 == 0 else misc()
                k.mm(atp[:, 0:256], BtT[ps, hp, :], ARt_[ps, hp, :, :], True, False, ["BtT", f"ARt{par}"], [atk], skip_group_check=True)
                k.mm(atp[:, 256:512], KtT[ps, hp, :], ARt_[ps, hp, :, :], False, True, ["KtT", f"ARt{par}"], [atk], skip_group_check=True)
                k.mm(Q0p[:, h, :], ARt_[ps, hp, 0, :], BtT[ps, hp, :], h == 0, h == 3, [f"ARt{par}", "BtT"], ["Q0p"], skip_group_check=True)
                k.tt("dve", PTs[0][:, h, 0, :], atp[:, 0:128], cs["mstrict"][:], ALU.mult, [atk, "c_mstrict"], [f"PTs0_{h // 2}"])
                k.tt("dve", ATs_[:, h, :, :], atp[:, 128:512], cs["mask3"][:], ALU.mult, [atk, "c_mask3"], [f"ATs{par}_{h}"])
                k.cp("pool", PTs[0][:, h, 1, :], ident[:], ["c_ident"], [f"PTs0_{h // 2}"])
            for h in range(4):
                k.tt("dve", Qs[0][:, h, :], Q0p[:, h, :], cs["mlow"][:], ALU.mult, ["Q0p", "c_mlow"], [f"Qs0_{h // 2}"])
            yield
            for lv in range(7):
                cur, nxt = lv % 2, (lv + 1) % 2
                for pa in range(2):
                    sbt = SB if pa == 0 else Q0p
                    sbk = "SB" if pa == 0 else "Q0p"
                    rk = [f"Qs{cur}_{pa}", f"PTs{cur}_{pa}"]
                    for h in (2 * pa, 2 * pa + 1):
                        if lv < 6:
                            k.mm(SA[:, h, :], Qs[cur][:, h, :], PTs[cur][:, h, :, :], h % 2 == 0, h % 2 == 1,
                                 rk, [f"SA{pa}"], skip_group_check=True)
                        else:
                            k.mm(SA[:, h, 128:256], Qs[cur][:, h, :], PTs[cur][:, h, 1, :], h % 2 == 0, h % 2 == 1,
                                 rk, [f"SA{pa}"], skip_group_check=True)
                    if lv < 6:
                        for h in (2 * pa, 2 * pa + 1):
                            k.mm(sbt[:, h % 2, :], PTs[cur][:, h, 0, :], Qs[cur][:, h, :], h % 2 == 0, h % 2 == 1,
                                 rk, [sbk], skip_group_check=True)
                for pa in range(2):
                    sbt = SB if pa == 0 else Q0p
                    sbk = "SB" if pa == 0 else "Q0p"
                    hs = slice(2 * pa, 2 * pa + 2)
                    if lv < 6:
                        k.cp("act", PTs[nxt][:, hs, 0, :], SA[:, hs, 0:128], [f"SA{pa}"], [f"PTs{nxt}_{pa}"])
                        k.cp("act", Qs[nxt][:, hs, :], sbt[:, 0:2, :], [sbk], [f"Qs{nxt}_{pa}"])
                    k.tt("dve", PTs[nxt][:, hs, 1, :], SA[:, hs, 128:256], PTs[cur][:, hs, 1, :], ALU.add,
                         [f"SA{pa}", f"PTs{cur}_{pa}"], [f"PTs{nxt}_{pa}"])
                yield

        def tail(ch, par):
            csl = slice(ch * 128, ch * 128 + 128)
            Gi_, ARt_, Vtok_, Khat_, Bhat_, ATs_ = Gi[par], ARt[par], Vtok[par], Khat[par], Bhat[par], ATs[par]
            TT = PTs[1]
            m, mk = misc()
            for h in range(4):
                hp, e = h // 2, h % 2
                ps = slice(e * 64, (e + 1) * 64)
                k.mm(m[:, h * 64:(h + 1) * 64], ARt_[ps, hp, 0, :], Hst[ps, hp, :], h == 0, False, [f"ARt{par}", "Hst"], [mk], skip_group_check=True)
                k.mm(m[:, h * 64:(h + 1) * 64], ATs_[:, h, 1, :], Vtok_[:, h * 64:(h + 1) * 64], False, h == 3, [f"ATs{par}_{h}", f"Vtok{par}"], [mk], skip_group_check=True)
            k.cp("act", X0[:], m[:, 0:256], [mk], ["X0"])
            yield
            m, mk = misc()
            for h in range(4):
                k.mm(m[:, h * 64:(h + 1) * 64], TT[:, h, 1, :], X0[:, h * 64:(h + 1) * 64], h == 0, h == 3, [f"PTs1_{h // 2}", "X0"], [mk], skip_group_check=True)
            k.cp("act", Up[:], m[:, 0:256], [mk], ["Up"])
            yield
            my, myk = misc()
            for h in range(4):
                hp, e = h // 2, h % 2
                ps = slice(e * 64, (e + 1) * 64)
                k.mm(my[:, h * 64:(h + 1) * 64], ARt_[ps, hp, 1, :], Hst[ps, hp, :], h == 0, False, [f"ARt{par}", "Hst"], [myk], skip_group_check=True)
                k.mm(my[:, h * 64:(h + 1) * 64], ATs_[:, h, 0, :], Up[:, h * 64:(h + 1) * 64], False, False, [f"ATs{par}_{h}", "Up"], [myk], skip_group_check=True)
                k.mm(my[:, h * 64:(h + 1) * 64], ATs_[:, h, 2, :], Vtok_[:, h * 64:(h + 1) * 64], False, h == 3, [f"ATs{par}_{h}", f"Vtok{par}"], [myk], skip_group_check=True)
            k.cp("act", Ysb[:], my[:, 0:256], [myk], ["Ysb"])
            yield
            m, mk = misc()
            for hp in range(2):
                k.mm(m[:, hp * 128:(hp + 1) * 128], Bhat_[:, hp * 128:(hp + 1) * 128], Up[:, hp * 128:(hp + 1) * 128], hp == 0, False, [f"Bhat{par}", "Up"], [mk], skip_group_check=True)
                k.mm(m[:, hp * 128:(hp + 1) * 128], Khat_[:, hp * 128:(hp + 1) * 128], Vtok_[:, hp * 128:(hp + 1) * 128], False, hp == 1, [f"Khat{par}", f"Vtok{par}"], [mk], skip_group_check=True)
            for h in range(4):
                hp, e = h // 2, h % 2
                ps = slice(e * 64, (e + 1) * 64)
                k.emit("dve", lambda: nc.vector.scalar_tensor_tensor(Hst[ps, hp, :], Hst[ps, hp, :], Gi_[ps, hp, 127:128],
                                                                    m[ps, hp * 128 + e * 64:hp * 128 + (e + 1) * 64], ALU.mult, ALU.add),
                       ["Hst", f"Gi{par}", mk], ["Hst"])
            yield
            k.emit("dve", lambda: nc.vector.reduce_sum(st1[:], Ysb[:], AX.X), ["Ysb"], ["st1"])
            k.tt("pool", Ysq[:], Ysb[:], Ysb[:], ALU.mult, ["Ysb"], ["Ysq"])
            k.emit("dve", lambda: nc.vector.reduce_sum(st2[:], Ysq[:], AX.X), ["Ysq"], ["st2"])
            k.ts1("dve", mean[:], st1[:], 1.0 / 64, ALU.mult, ["st1"], ["mean"])
            k.tt("dve", msq[:], mean[:], mean[:], ALU.mult, ["mean"], ["msq"])
            k.emit("dve", lambda: nc.vector.scalar_tensor_tensor(rstd[:], st2[:], 1.0 / 64, msq[:], ALU.mult, ALU.subtract), ["st2", "msq"], ["rstd"])
            k.ts1("dve", rstd[:], rstd[:], GN_EPS, ALU.add, ["rstd"], ["rstd"])
            k.act(rstd[:], rstd[:], AF.Sqrt, ["rstd"], ["rstd"])
            k.emit("dve", lambda: nc.vector.reciprocal(rstd[:], rstd[:]), ["rstd"], ["rstd"])
            for h in range(4):
                e_ = "dve" if h % 2 == 0 else "pool"
                k.ts(e_, Yn[:, h * 64:(h + 1) * 64], Ysb[:, h, :], mean[:, h:h + 1], rstd[:, h:h + 1], ALU.subtract, ALU.mult,
                     ["Ysb", "mean", "rstd"], ["Yn"])
            yield
            m, mk = misc()
            for hp in range(2):
                k.tr(m[:, hp * 128:(hp + 1) * 128], Yn[:, hp * 128:(hp + 1) * 128], ident[:], ["Yn", "c_ident"], [mk])
            for hp in range(2):
                k.act(fin[:, hp, :], m[:, hp * 128:(hp + 1) * 128], AF.Identity, [mk, "c_lnw", "c_lnb"], ["fin"],
                      bias=cs["lnb"][:, hp:hp + 1], scale=cs["lnw"][:, hp:hp + 1])
                k.tt("pool", fin[:, hp, :], fin[:, hp, :], bonT[:, hp, csl], ALU.add, ["fin", f"bonT{hp}"], ["fin"])
                k.tt("pool", ob[:, hp, csl], fin[:, hp, :], gTs[:, hp, csl], ALU.mult, ["fin", f"gTs{hp}"], [obk])
            yield

        def run(*gens):
            gens = list(gens)
            while gens:
                for g_ in list(gens):
                    try:
                        next(g_)
                    except StopIteration:
                        gens.remove(g_)

        run(prep(0, 0))
        for ch in range(4):
            if ch < 3:
                run(prep(ch + 1, (ch + 1) % 2), tail(ch, ch % 2))
            else:
                run(tail(ch, ch % 2))
        for hp in range(2):
            yfn = ins.get("yfn") or (lambda r0, r1, a, b_: yT[r0:r1, a:b_])
            k.dma("pool", yfn(hp * 128, (hp + 1) * 128, t0, t0 + 512), ob[:, hp, :], reads=[obk])


WST = 2048
D = 1024
DFF = 2816
NF = DFF // 128
TB = 512

def rmsnorm_T(nc, k, xb, xk, hT, ss, rstd, junk, xn, pT, idb, eps, pfx):
    for s in range(4):
        k.act(junk[:], xb[:, s, :], AF.Square, [xk], [pfx + "junk", pfx + "ss"], accum_out=ss[:, s:s + 1])
    k.ts("dve", rstd[:], ss[:], 1.0 / D, eps, ALU.mult, ALU.add, [pfx + "ss"], [pfx + "rstd"])
    k.act(rstd[:], rstd[:], AF.Sqrt, [pfx + "rstd"], [pfx + "rstd"])
    k.emit("dve", lambda: nc.vector.reciprocal(rstd[:], rstd[:]), [pfx + "rstd"], [pfx + "rstd"])
    for s in range(4):
        xnk = pfx + f"xn{s%2}"
        k.ts1("dve", xn[s % 2][:], xb[:, s, :], rstd[:, s:s + 1], ALU.mult, [xk, pfx + "rstd"], [xnk])
        pk = pfx + f"pT{s%2}"
        for kc in range(8):
            k.tr(pT[s % 2][:, kc, :], xn[s % 2][:, kc * 128:(kc + 1) * 128], idb[:], [xnk, pfx + "idb"], [pk])
        k.cp("act", hT[:, :, s * 128:(s + 1) * 128], pT[s % 2][:], [pk], [pfx + "hT"])


def load_w_bf16(nc, k, dst, dkey, src, K, N, wst, wstk, scale_ap=None, skey=None):
    wv = src.rearrange("(kc p) n -> p kc n", p=128)
    step = WST // K
    i = 0
    for c0 in range(0, N, step):
        cw = min(step, N - c0)
        st = wst[i % 2]; sk = wstk + str(i % 2); i += 1
        stv = st[:, 0:K * cw].rearrange("p (kc n) -> p kc n", kc=K)
        k.dma("sp", stv, wv[:, :, c0:c0 + cw], writes=[sk])
        for kc in range(K):
            e = "dve" if kc % 2 == 0 else "pool"
            if scale_ap is not None:
                k.ts1(e, dst[:, kc, c0:c0 + cw], stv[:, kc, :], scale_ap[:, kc:kc + 1], ALU.mult, [sk, skey], [dkey])
            else:
                k.cp(e, dst[:, kc, c0:c0 + cw], stv[:, kc, :], [sk], [dkey])


def build_D1(NT):
    nc = bass.Bass("TRN2", target_bir_lowering=False)
    I = lambda nm, shp, dt=F32: nc.dram_tensor(nm, shp, dt, kind="ExternalInput").ap()
    ins = dict(x=I("x", [NT, D]), oaT=I("oaT", [512, NT], BF16), yrT=I("yrT", [512, NT], BF16), gT=I("gT", [2048, NT], BF16),
               woa=I("woa", [512, D]), wor=I("wor", [512, D]), wo=I("wo", [D, D]))
    xo = nc.dram_tensor("xo", [NT, D], F32, kind="ExternalOutput").ap()
    k = KB(nc)
    with Alloc(nc) as al:
        emit_D1(nc, k, al, NT, ins, xo)
        k.finish("sp")
    return nc


def emit_D1(nc, k, al, NT, ins, xo, pfx="E"):
    A = lambda nm, shp, dt=F32: al.sb(pfx + nm, shp, dt)
    P = lambda nm, shp, dt=F32: al.ps(pfx + nm, shp, dt)
    woa = A("woa", [128, 4, D], BF16); wor = A("wor", [128, 4, D], BF16); wo = A("wo", [128, 8, D], BF16)
    wst = [A(f"wst{i}", [128, WST]) for i in range(2)]
    load_w_bf16(nc, k, woa, pfx + "woa", ins["woa"], 4, D, wst, pfx + "wst")
    load_w_bf16(nc, k, wor, pfx + "wor", ins["wor"], 4, D, wst, pfx + "wst")
    load_w_bf16(nc, k, wo, pfx + "wo", ins["wo"], 8, D, wst, pfx + "wst")
    oab = [A(f"oab{i}", [128, 4, TB], BF16) for i in range(2)]
    yrb = [A(f"yrb{i}", [128, 4, TB], BF16) for i in range(2)]
    xb = [A(f"xb{i}", [128, 4, D]) for i in range(2)]
    gab = [A(f"gab{i}", [128, TB], BF16) for i in range(3)]
    grb = [A(f"grb{i}", [128, TB], BF16) for i in range(3)]
    t1 = [A(f"t1_{i}", [128, TB]) for i in range(2)]
    t2 = [A(f"t2_{i}", [128, TB]) for i in range(2)]
    mT = A("mT", [128, 8, TB], BF16)
    pa = [P(f"pa{i}", [128, 512]) for i in range(2)]
    pr = [P(f"pr{i}", [128, 512]) for i in range(2)]
    po = [P(f"po{i}", [128, 512]) for i in range(3)]
    nblk = NT // TB
    oav = ins["oaT"].rearrange("(kc p) t -> p kc t", p=128) if ins.get("oaT") is not None else None
    yrv = ins["yrT"].rearrange("(kc p) t -> p kc t", p=128) if ins.get("yrT") is not None else None
    xv = ins["x"].rearrange("(b s p) d -> b p s d", p=128, s=4)
    xov = xo.rearrange("(b s p) d -> b p s d", p=128, s=4)

    def load_blk(b):
        t0 = b * TB
        if ins.get("oafn"):
            for kc in range(4):
                k.dma("sp", oab[b % 2][:, kc, :], ins["oafn"](kc, t0, t0 + TB), writes=[pfx + f"oab{b%2}"])
                k.dma("sp", yrb[b % 2][:, kc, :], ins["yrfn"](kc, t0, t0 + TB), writes=[pfx + f"yrb{b%2}"])
        else:
            k.dma("sp", oab[b % 2][:], oav[:, :, t0:t0 + TB], writes=[pfx + f"oab{b%2}"])
            k.dma("sp", yrb[b % 2][:], yrv[:, :, t0:t0 + TB], writes=[pfx + f"yrb{b%2}"])
        k.dma("sp", xb[b % 2][:], xv[b], writes=[pfx + f"xb{b%2}"])
    load_blk(0)
    gi = [0]; oi = [0]
    for b in range(nblk):
        t0 = b * TB
        if b + 1 < nblk:
            load_blk(b + 1)
        for oc in range(8):
            i = gi[0]; gi[0] += 1
            ga = gab[i % 3]; gr = grb[i % 3]
            k.dma("sp", ga[:], ins["gT"][oc * 128:(oc + 1) * 128, t0:t0 + TB], writes=[pfx + f"gab{i%3}"])
            k.dma("sp", gr[:], ins["gT"][1024 + oc * 128:1024 + (oc + 1) * 128, t0:t0 + TB], writes=[pfx + f"grb{i%3}"])
            pai = pa[i % 2]; pri = pr[i % 2]
            for kc in range(4):
                k.mm(pai[:], woa[:, kc, oc * 128:(oc + 1) * 128], oab[b % 2][:, kc, :], kc == 0, kc == 3,
                     [pfx + "woa", pfx + f"oab{b%2}"], [pfx + f"pa{i%2}"])
            for kc in range(4):
                k.mm(pri[:], wor[:, kc, oc * 128:(oc + 1) * 128], yrb[b % 2][:, kc, :], kc == 0, kc == 3,
                     [pfx + "wor", pfx + f"yrb{b%2}"], [pfx + f"pr{i%2}"])
            k.tt("dve", t1[i % 2][:], pai[:], ga[:], ALU.mult, [pfx + f"pa{i%2}", pfx + f"gab{i%3}"], [pfx + f"t1_{i%2}"])
            k.tt("dve", t2[i % 2][:], pri[:], gr[:], ALU.mult, [pfx + f"pr{i%2}", pfx + f"grb{i%3}"], [pfx + f"t2_{i%2}"])
            k.tt("pool", mT[:, oc, :], t1[i % 2][:], t2[i % 2][:], ALU.add, [pfx + f"t1_{i%2}", pfx + f"t2_{i%2}"], [pfx + "mT"])
        for s in range(4):
            for n in range(2):
                j = oi[0]; oi[0] += 1
                pj = po[j % 3]
                for kc in range(8):
                    k.mm(pj[:], mT[:, kc, s * 128:(s + 1) * 128], wo[:, kc, n * 512:(n + 1) * 512], kc == 0, kc == 7,
                         [pfx + "mT", pfx + "wo"], [pfx + f"po{j%3}"])
                k.tt("dve", xb[b % 2][:, s, n * 512:(n + 1) * 512], pj[:], xb[b % 2][:, s, n * 512:(n + 1) * 512], ALU.add,
                     [pfx + f"po{j%3}", pfx + f"xb{b%2}"], [pfx + f"xb{b%2}"])
        k.dma("pool", xov[b], xb[b % 2][:], reads=[pfx + f"xb{b%2}"])


def build_D2(NT, last):
    nc = bass.Bass("TRN2", target_bir_lowering=False)
    I = lambda nm, shp, dt=F32: nc.dram_tensor(nm, shp, dt, kind="ExternalInput").ap()
    ins = dict(x=I("x", [NT, D]), p=I("p", [NT, 256]), wg=I("wg", [D, DFF]), wu=I("wu", [D, DFF]), wd=I("wd", [DFF, D]),
               wple=I("wple", [256, D]), wpg=I("wpg", [D, D]), gffn=I("gffn", [128, 8]), gple=I("gple", [128, 8]),
               gfin=I("gfin", [128, D]), ident=I("ident", [128, 128]))
    xo = nc.dram_tensor("xo", [NT, D], F32, kind="ExternalOutput").ap()
    wgS = nc.dram_tensor("wgS", [NF, 128, 1024], BF16).ap()
    wuS = nc.dram_tensor("wuS", [NF, 128, 1024], BF16).ap()
    k = KB(nc)
    with Alloc(nc) as al:
        emit_D2(nc, k, al, NT, last, ins, xo, wgS, wuS)
        k.finish("sp")
    return nc


def emit_D2(nc, k, al, NT, last, ins, xo, wgS, wuS, pfx="F"):
    A = lambda nm, shp, dt=F32: al.sb(pfx + nm, shp, dt)
    P = lambda nm, shp, dt=F32: al.ps(pfx + nm, shp, dt)
    gffn = A("gffn", [128, 8]); gple = A("gple", [128, 8]); idf = A("idf", [128, 128]); idb = A("idb", [128, 128], BF16)
    k.dma("sp", gffn[:], ins["gffn"], writes=[pfx + "gffn"])
    k.dma("sp", gple[:], ins["gple"], writes=[pfx + "gple"])
    k.dma("sp", idf[:], ins["ident"], writes=[pfx + "idf"])
    k.cp("dve", idb[:], idf[:], [pfx + "idf"], [pfx + "idb"])
    if last:
        gfin = A("gfin", [128, D])
        k.dma("sp", gfin[:], ins["gfin"], writes=[pfx + "gfin"])
    wd = A("wd", [128, NF, D], BF16); wple = A("wple", [128, 2, D], BF16); wpg = A("wpg", [128, 8, D], BF16)
    wst = [A(f"wst{i}", [128, WST]) for i in range(2)]
    load_w_bf16(nc, k, wple, pfx + "wple", ins["wple"], 2, D, wst, pfx + "wst")
    load_w_bf16(nc, k, wpg, pfx + "wpg", ins["wpg"], 8, D, wst, pfx + "wst", gple, pfx + "gple")
    wdv = ins["wd"].rearrange("(f p) n -> p f n", p=128)
    i = 0
    for f0 in range(0, NF, 2):
        fw_ = min(2, NF - f0)
        st = wst[i % 2]; sk = pfx + f"wst{i%2}"; i += 1
        stv = st[:, 0:fw_ * D].rearrange("p (f n) -> p f n", f=fw_)
        k.dma("sp", stv, wdv[:, f0:f0 + fw_, :], writes=[sk])
        for f in range(fw_):
            e = "dve" if f % 2 == 0 else "pool"
            k.cp(e, wd[:, f0 + f, :], stv[:, f, :], [sk], [pfx + "wd"])
    wcb = [A(f"wcb{i}", [128, 8, 256], BF16) for i in range(2)]
    ci = 0
    for (src, dstS, nm) in ((ins["wg"], wgS, "wgS"), (ins["wu"], wuS, "wuS")):
        wv = src.rearrange("(kc p) n -> p kc n", p=128)
        for c0 in range(0, DFF, 256):
            cw = min(256, DFF - c0)
            st = wst[i % 2]; sk = pfx + f"wst{i%2}"; i += 1
            stv = st[:, 0:8 * cw].rearrange("p (kc n) -> p kc n", kc=8)
            k.dma("sp", stv, wv[:, :, c0:c0 + cw], writes=[sk])
            cb = wcb[ci % 2]; cbk = pfx + f"wcb{ci%2}"; ci += 1
            for kc in range(8):
                e = "dve" if kc % 2 == 0 else "pool"
                k.ts1(e, cb[:, kc, 0:cw], stv[:, kc, :], gffn[:, kc:kc + 1], ALU.mult, [sk, pfx + "gffn"], [cbk])
            for j in range(cw // 128):
                f = c0 // 128 + j
                k.dma("pool", dstS[f].rearrange("p (kc n) -> p kc n", kc=8), cb[:, :, j * 128:(j + 1) * 128], reads=[cbk], writes=[nm])

    xb = [A(f"xb{i}", [128, 4, D]) for i in range(2)]
    pb = [A(f"pb{i}", [128, 4, 256]) for i in range(2)]
    junk = A("junk", [128, D], BF16)
    ss = A("ss", [128, 4]); rstd = A("rstd", [128, 4])
    xn = [A(f"xn{i}", [128, D], BF16) for i in range(2)]
    hT = A("hT", [128, 8, TB], BF16)
    ppT = A("ppT", [128, 2, TB], BF16)
    aT = A("aT", [128, NF, TB], BF16)
    NWB = 3
    wgb = [A(f"wgb{i}", [128, 8, 128], BF16) for i in range(NWB)]
    wub = [A(f"wub{i}", [128, 8, 128], BF16) for i in range(NWB)]
    sl = [A(f"sl{i}", [128, TB]) for i in range(2)]
    tmp = [A(f"tmp{i}", [128, 512]) for i in range(2)]
    pT = [P(f"pT{i}", [128, 8, 128], BF16) for i in range(2)]
    pg = [P(f"pg{i}", [128, 512]) for i in range(2)]
    pu = [P(f"pu{i}", [128, 512]) for i in range(2)]
    pd = [P(f"pd{i}", [128, 512]) for i in range(2)]
    nblk = NT // TB
    xv = ins["x"].rearrange("(b s p) d -> b p s d", p=128, s=4)
    pv = ins["p"].rearrange("(b s p) d -> b p s d", p=128, s=4)
    xov = xo.rearrange("(b s p) d -> b p s d", p=128, s=4)

    def load_blk(b):
        k.dma("sp", xb[b % 2][:], xv[b], writes=[pfx + f"xb{b%2}"])
        k.dma("sp", pb[b % 2][:], pv[b], writes=[pfx + f"pb{b%2}"])
    load_blk(0)
    wi = [0]; di = [0]
    for b in range(nblk):
        if b + 1 < nblk:
            load_blk(b + 1)
        X = xb[b % 2]; xk = pfx + f"xb{b%2}"
        rmsnorm_T(nc, k, X, xk, hT, ss, rstd, junk, xn, pT, idb, 1e-6, pfx)
        def load_w(f):
            i = f % NWB
            k.dma("sp", wgb[i][:], wgS[f].rearrange("p (kc n) -> p kc n", kc=8), reads=["wgS"], writes=[pfx + f"wgb{i}"])
            k.dma("sp", wub[i][:], wuS[f].rearrange("p (kc n) -> p kc n", kc=8), reads=["wuS"], writes=[pfx + f"wub{i}"])
        load_w(0); load_w(1)
        for f in range(NF):
            if f + 2 < NF:
                load_w(f + 2)
            i = f % NWB
            j = wi[0]; wi[0] += 1
            for kc in range(8):
                k.mm(pg[j % 2][:], wgb[i][:, kc, :], hT[:, kc, :], kc == 0, kc == 7, [pfx + f"wgb{i}", pfx + "hT"], [pfx + f"pg{j%2}"])
            for kc in range(8):
                k.mm(pu[j % 2][:], wub[i][:, kc, :], hT[:, kc, :], kc == 0, kc == 7, [pfx + f"wub{i}", pfx + "hT"], [pfx + f"pu{j%2}"])
            k.act(sl[j % 2][:], pg[j % 2][:], AF.Silu, [pfx + f"pg{j%2}"], [pfx + f"sl{j%2}"])
            k.tt("dve", aT[:, f, :], pu[j % 2][:], sl[j % 2][:], ALU.mult, [pfx + f"pu{j%2}", pfx + f"sl{j%2}"], [pfx + "aT"])
        for s in range(4):
            for n in range(2):
                j = di[0]; di[0] += 1
                for f in range(NF):
                    k.mm(pd[j % 2][:], aT[:, f, s * 128:(s + 1) * 128], wd[:, f, n * 512:(n + 1) * 512], f == 0, f == NF - 1,
                         [pfx + "aT", pfx + "wd"], [pfx + f"pd{j%2}"])
                k.tt("dve", X[:, s, n * 512:(n + 1) * 512], pd[j % 2][:], X[:, s, n * 512:(n + 1) * 512], ALU.add,
                     [pfx + f"pd{j%2}", xk], [xk])
        rmsnorm_T(nc, k, X, xk, hT, ss, rstd, junk, xn, pT, idb, 1e-6, pfx)
        PB = pb[b % 2]; pbk = pfx + f"pb{b%2}"
        for s in range(4):
            k.cp("pool", xn[s % 2][:, 0:256], PB[:, s, :], [pbk], [pfx + f"xn{s%2}"])
            for kc in range(2):
                k.tr(pT[s % 2][:, kc, :], xn[s % 2][:, kc * 128:(kc + 1) * 128], idb[:], [pfx + f"xn{s%2}", pfx + "idb"], [pfx + f"pT{s%2}"])
            k.cp("act", ppT[:, :, s * 128:(s + 1) * 128], pT[s % 2][:, 0:2, :], [pfx + f"pT{s%2}"], [pfx + "ppT"])
        for s in range(4):
            for n in range(2):
                j = di[0]; di[0] += 1
                for kc in range(8):
                    k.mm(pg[j % 2][:], hT[:, kc, s * 128:(s + 1) * 128], wpg[:, kc, n * 512:(n + 1) * 512], kc == 0, kc == 7,
                         [pfx + "hT", pfx + "wpg"], [pfx + f"pg{j%2}"])
                for kc in range(2):
                    k.mm(pu[j % 2][:], ppT[:, kc, s * 128:(s + 1) * 128], wple[:, kc, n * 512:(n + 1) * 512], kc == 0, kc == 1,
                         [pfx + "ppT", pfx + "wple"], [pfx + f"pu{j%2}"])
                k.act(tmp[j % 2][:], pg[j % 2][:], AF.Sigmoid, [pfx + f"pg{j%2}"], [pfx + f"tmp{j%2}"])
                k.tt("dve", tmp[j % 2][:], pu[j % 2][:], tmp[j % 2][:], ALU.mult, [pfx + f"pu{j%2}", pfx + f"tmp{j%2}"], [pfx + f"tmp{j%2}"])
                k.tt("pool", X[:, s, n * 512:(n + 1) * 512], X[:, s, n * 512:(n + 1) * 512], tmp[j % 2][:], ALU.add,
                     [xk, pfx + f"tmp{j%2}"], [xk])
        if last:
            for s in range(4):
                k.act(junk[:], X[:, s, :], AF.Square, [xk], [pfx + "junk", pfx + "ss"], accum_out=ss[:, s:s + 1])
            k.ts("dve", rstd[:], ss[:], 1.0 / D, 1e-6, ALU.mult, ALU.add, [pfx + "ss"], [pfx + "rstd"])
            k.act(rstd[:], rstd[:], AF.Sqrt, [pfx + "rstd"], [pfx + "rstd"])
            k.emit("dve", lambda: nc.vector.reciprocal(rstd[:], rstd[:]), [pfx + "rstd"], [pfx + "rstd"])
            for s in range(4):
                k.emit("dve", lambda: nc.vector.scalar_tensor_tensor(X[:, s, :], X[:, s, :], rstd[:, s:s + 1], gfin[:], ALU.mult, ALU.mult),
                       [xk, pfx + "rstd", pfx + "gfin"], [xk])
        k.dma("pool", xov[b], X[:], reads=[xk])

PAIRS = [[0, 1], [2, 3], [4, 5], [6, 7]]
CPAR = [e for e in C_IN[1:12]]
CCON = [e for e in C_IN[12:]]
DS = bass.DynSlice


def build_fused8(HALF, depth):
    S = 2 * HALF
    nc = bass.Bass("TRN2", target_bir_lowering=False)
    I = lambda nm, shp, dt=F32: nc.dram_tensor(nm, list(shp), dt, kind="ExternalInput").ap()
    Sc = lambda nm, shp, dt: nc.dram_tensor(nm, list(shp), dt).ap()
    x = I("x", [HALF, D]); p = I("p", [depth, HALF, 256])
    w_in = I("w_in", [depth, D, INC]); gam = I("gam", [depth, 128, 8]); ident = I("ident", [128, 128])
    G = I("G", [4, 128, 256]); b31 = I("b31", [128, 4]); neg = I("neg", [128, 256])
    lamv = I("lamv", [depth, 128, 4, 64]); sgain = I("sgain", [depth, 128, 128])
    cpar = {nm: I("c_" + nm, [depth] + list(shp)) for nm, shp, dt in CPAR}
    ccon = {nm: (ident if nm == "ident" else I("k_" + nm, shp)) for nm, shp, dt in CCON}
    woa = I("woa", [depth, 512, D]); wor = I("wor", [depth, 512, D]); wo = I("wo", [depth, D, D])
    wg = I("wg", [depth, D, DFF]); wu = I("wu", [depth, D, DFF]); wd = I("wd", [depth, DFF, D])
    wple = I("wple", [depth, 256, D]); wpg = I("wpg", [depth, D, D])
    gffn = I("gffn", [depth, 128, 8]); gple = I("gple", [depth, 128, 8]); gfin = I("gfin", [128, D])
    out = nc.dram_tensor("out", [HALF, D], F32, kind="ExternalOutput").ap()
    PV = min(2048, HALF)
    NPV = HALF // PV
    gT = Sc("s_gT", [2048, HALF], BF16)
    xmid = Sc("s_xmid", [HALF, D], F32); x1 = Sc("s_x1", [HALF, D], F32)
    wgS = Sc("s_wgS", [NF, 128, 1024], BF16); wuS = Sc("s_wuS", [NF, 128, 1024], BF16)
    k = KB(nc)
    pr_p = nc.gpsimd.partition_id() % 2
    pr_s = nc.sync.partition_id() % 2
    DS1 = lambda v: DS(v, 1)

    def sel(t, v):
        return t[DS1(v)].rearrange("a r c -> (a r) c")

    def exch(tag, P_, F_, rows, cols, dt, static_src=False):
        stg = Sc(f"x_stg_{tag}", [rows, cols], dt)
        rcv = Sc(f"x_rcv_{tag}", [2 * rows, cols], dt)
        rcv3 = rcv.rearrange("(a r) c -> a r c", a=2)
        if static_src:
            k.dma("pool", stg, P_, writes=[f"stg{tag}"])
            k.dma("sp", sel(F_, pr_s), P_)
        else:
            k.dma("pool", stg, sel(P_, 1 - pr_p), writes=[f"stg{tag}"])
            k.dma("sp", sel(F_, pr_s), sel(P_, pr_s))
        k.coll("AllGather", stg, rcv, PAIRS, reads=[f"stg{tag}"], writes=[f"rcv{tag}"])
        k.dma("sp", sel(F_, 1 - pr_s), sel(rcv3, 1 - pr_s), reads=[f"rcv{tag}"])

    for i in range(depth):
        last = (i == depth - 1)
        xin = x if i == 0 else x1
        lam_init = 0.8 - 0.6 * math.exp(-0.3 * i)
        L = f"L{i}"
        QP = [Sc(f"{L}QP{b_}", [2, 128, HALF], BF16) for b_ in range(2)]
        KP = [Sc(f"{L}KP{b_}", [2, 128, HALF], BF16) for b_ in range(2)]
        VP = [Sc(f"{L}VP{b_}", [2, PV, 256], BF16) for b_ in range(NPV)]
        ZP = [[Sc(f"{L}ZP{j}_{q}", [2, 64, HALF], F32) for q in range(4)] for j in range(3)]
        ZL = [Sc(f"{L}ZL{q}", [64, HALF], F32) for q in range(4)]
        QF = [Sc(f"{L}QF{b_}", [2, 128, HALF], BF16) for b_ in range(2)]
        KF = [Sc(f"{L}KF{b_}", [2, 128, HALF], BF16) for b_ in range(2)]
        VF = [Sc(f"{L}VF{b_}", [2, PV, 256], BF16) for b_ in range(NPV)]
        ZF = [Sc(f"{L}ZF{r}", [2, 64, HALF], F32) for r in range(16)]
        OP = [Sc(f"{L}OP{b_}", [2, 128, HALF], BF16) for b_ in range(2)]
        YP = [Sc(f"{L}YP{b_}", [2, 128, HALF], BF16) for b_ in range(2)]
        OAF = [Sc(f"{L}OAF{b_}", [2, 128, HALF], BF16) for b_ in range(2)]
        YF = [Sc(f"{L}YF{b_}", [2, 128, HALF], BF16) for b_ in range(2)]

        def zcb(cc, hq, t0):
            if cc < 12:
                j, g_, rb = cc // 4, (cc % 4) // 2, cc % 2
                return ZP[j][rb * 2 + hq][g_, :, t0:t0 + TB]
            return ZL[(cc - 12) * 2 + hq][:, t0:t0 + TB]
        ocb = dict(q=lambda cc, t0: QP[cc % 2][cc // 2, :, t0:t0 + TB],
                   k=lambda cc, t0: KP[cc % 2][cc // 2, :, t0:t0 + TB],
                   v=lambda g_, t: VP[t // PV][g_, t % PV:t % PV + 128, :],
                   z=zcb)
        with Alloc(nc) as al:
            emit_A(nc, k, al, HALF, xin, w_in[i], gam[i], ident, None, None, None, None, gT, pfx=f"A{i}", ocb=ocb)
        k.barrier()
        for b_ in range(2):
            exch(f"{L}q{b_}", QP[b_], QF[b_], 128, HALF, BF16)
            exch(f"{L}k{b_}", KP[b_], KF[b_], 128, HALF, BF16)
        for b_ in range(NPV):
            exch(f"{L}v{b_}", VP[b_], VF[b_], PV, 256, BF16)
        for j in range(3):
            for q in range(4):
                exch(f"{L}z{j}_{q}", ZP[j][q], ZF[j * 4 + q], 64, HALF, F32)
        for q in range(4):
            exch(f"{L}zl{q}", ZL[q], ZF[12 + q], 64, HALF, F32, static_src=True)
        k.barrier()

        def hm(T_):
            def fn(r0, r1, a, b_):
                h = a // HALF
                assert (b_ - 1) // HALF == h and r1 - r0 == 128
                return T_[r0 // 128][h, :, a - h * HALF:b_ - h * HALF]
            return fn

        def vfn(a, b_):
            h = a // HALF
            tl = a - h * HALF
            assert (b_ - 1) // HALF == h and tl // PV == (tl + (b_ - a) - 1) // PV
            return VF[tl // PV][h, tl % PV:tl % PV + (b_ - a), :]

        def zfn64(r0, a, b_):
            h = a // HALF
            assert (b_ - 1) // HALF == h
            return ZF[r0 // 64][h, :, a - h * HALF:b_ - h * HALF]
        with Alloc(nc) as al:
            insB = dict(qfn=hm(QF), kfn=hm(KF), ofn=hm(OP), vfn=vfn, G=G, b31=b31, neg=neg,
                        lamv=lamv[i], sgain=sgain[i], ident=ident)
            emit_B(nc, k, al, S, lam_init, insB, None, pfx=f"B{i}")
        k.barrier()
        with Alloc(nc) as al:
            insC = {nm: cpar[nm][i] for nm, shp, dt in CPAR}
            insC.update(ccon)
            insC["zfn64"] = zfn64
            insC["zsplit"] = HALF
            insC["yfn"] = hm(YP)
            emit_C(nc, k, al, S, insC, None, pfx=f"C{i}")
        k.barrier()
        for b_ in range(2):
            exch(f"{L}o{b_}", OP[b_], OAF[b_], 128, HALF, BF16)
            exch(f"{L}y{b_}", YP[b_], YF[b_], 128, HALF, BF16)
        k.barrier()
        with Alloc(nc) as al:
            emit_D1(nc, k, al, HALF, dict(x=xin, gT=gT, woa=woa[i], wor=wor[i], wo=wo[i],
                                          oafn=lambda kc, a, b_: OAF[kc % 2][kc // 2, :, a:b_],
                                          yrfn=lambda kc, a, b_: YF[kc % 2][kc // 2, :, a:b_]), xmid, pfx=f"E{i}")
        k.barrier()
        with Alloc(nc) as al:
            emit_D2(nc, k, al, HALF, last, dict(x=xmid, p=p[i], wg=wg[i], wu=wu[i], wd=wd[i], wple=wple[i], wpg=wpg[i],
                                                gffn=gffn[i], gple=gple[i], gfin=gfin, ident=ident),
                    out if last else x1, wgS, wuS, pfx=f"F{i}")
        k.barrier()
    k.finish("sp")
    return nc


def _pp(v):
    return np.ascontiguousarray(np.asarray(v, np.float32).reshape(-1, 128).T)


def kernel(x, p, rel_bias, norm_mix, w_in, lam_q1, lam_k1, lam_q2, lam_k2, attn_subln,
           rwkv_mu, rwkv_w0, rwkv_w2, rwkv_a0, rwkv_a2, rwkv_g2, rwkv_kk, rwkv_ka, rwkv_rk,
           rwkv_lnx_w, rwkv_lnx_b, w_out_attn, w_out_rwkv, w_out, norm_ffn, w_ffn_gate,
           w_ffn_up, w_ffn_down, norm_ple, w_ple, w_ple_gate, norm_final):
    f32 = np.float32
    A_ = lambda a: np.ascontiguousarray(np.asarray(a, f32))
    x = A_(x); p = A_(p); rel_bias = A_(rel_bias)
    B_, S_, D_ = x.shape
    HALF = S_ // 2
    NC = 2 * B_
    depth = int(np.asarray(w_in).shape[0])
    bc = lambda v, shape: np.ascontiguousarray(np.broadcast_to(v, shape))
    tidx = toeplitz_idx()
    com = dict(
        w_in=A_(w_in), gam=np.stack([_pp(norm_mix[i]) for i in range(depth)]), ident=np.eye(128, dtype=f32),
        neg=neg_mask(),
        lamv=np.stack([bc(np.stack([A_(lam_q1[i]), A_(lam_k1[i]), A_(lam_q2[i]), A_(lam_k2[i])])[None], (128, 4, 64)) for i in range(depth)]),
        sgain=np.stack([bc(A_(attn_subln[i])[None], (128, 128)) for i in range(depth)]),
        woa=A_(w_out_attn), wor=A_(w_out_rwkv), wo=A_(w_out), wg=A_(w_ffn_gate), wu=A_(w_ffn_up), wd=A_(w_ffn_down),
        wple=A_(w_ple), wpg=A_(w_ple_gate),
        gffn=np.stack([_pp(norm_ffn[i]) for i in range(depth)]), gple=np.stack([_pp(norm_ple[i]) for i in range(depth)]),
        gfin=bc(A_(norm_final)[None], (128, D_)),
    )
    cC = consts_C()
    for nm, shp, dt in CCON:
        if nm != "ident":
            com["k_" + nm] = cC[nm]
    blk = (np.arange(128)[:, None] // 64 == np.arange(128)[None, :] // 64)
    grp = []
    for hh in range(2):
        d = dict(G=np.ascontiguousarray(np.stack([rel_bias[tidx, 2 * hh + hl, c] for hl in range(2) for c in range(2)])),
                 b31=bc(np.stack([rel_bias[31, 2 * hh + hl, c] for hl in range(2) for c in range(2)])[None, :], (128, 4)))
        sl = np.arange(hh * 256, (hh + 1) * 256)
        idx = np.concatenate([sl, 512 + sl, 1024 + sl, np.arange(1536, 1792)])
        cp = {nm: [] for nm, shp, dt in CPAR}
        for i in range(depth):
            rkf = A_(rwkv_rk[i]).reshape(-1)[sl]
            rkb = np.stack([np.where(blk, rkf[hp * 128:(hp + 1) * 128][:, None], f32(0)) for hp in range(2)], 1).astype(f32)
            e = dict(mu=_pp(A_(rwkv_mu[i])[idx]), w2c=A_(rwkv_w2[i])[:, sl], w0c=A_(rwkv_w0[i])[None, sl],
                     a2c=A_(rwkv_a2[i])[:, sl], a0c=_pp(A_(rwkv_a0[i])[sl]), g2c=A_(rwkv_g2[i])[:, sl],
                     kkw=_pp(A_(rwkv_kk[i])[sl]), ka=_pp(A_(rwkv_ka[i])[sl]), rkb=rkb,
                     lnw=_pp(A_(rwkv_lnx_w[i])[sl]), lnb=_pp(A_(rwkv_lnx_b[i])[sl]))
            for nm in cp:
                cp[nm].append(np.ascontiguousarray(e[nm]))
        for nm in cp:
            d["c_" + nm] = np.ascontiguousarray(np.stack(cp[nm]))
        grp.append(d)
    nc = build_fused8(HALF, depth)
    in_maps = []
    for c in range(NC):
        b, pi = c // 2, c % 2
        d = dict(com)
        d.update(grp[pi])
        d["x"] = np.ascontiguousarray(x[b, pi * HALF:(pi + 1) * HALF])
        d["p"] = np.ascontiguousarray(p[:, b, pi * HALF:(pi + 1) * HALF])
        in_maps.append(d)
    res = run_bass_kernel_spmd(nc, in_maps, core_ids=list(range(NC)))
    out = np.empty((B_, S_, D_), f32)
    for c in range(NC):
        out[c // 2, (c % 2) * HALF:(c % 2 + 1) * HALF] = np.asarray(res.results[c]["out"], f32)
    return out
```
